# Optimizing a Trainium2 kernel written in Bass

```python
import jax, jax.numpy as jnp
from jax import lax
import numpy as np

D_MODEL = 1024
BATCH = 2
SEQ = 16384
DEPTH = 1
DEC_BATCH = 32
DEC_SEQ = 2048
PAST_LEN = 128

HEAD_DIM = 64
MIX_WIDTH = D_MODEL
N_HEADS_A = (MIX_WIDTH // 2) // HEAD_DIM
N_HEADS_B = (MIX_WIDTH // 2) // HEAD_DIM
N_KV_B = N_HEADS_B // 4
DILATED_PATTERNS = ((128, 1), (512, 4), (2048, 16))
SWA_HALF_WINDOW = 128
SWA_BLOCK = 128
ROPE_THETA = 10000.0
D_FF = -(-8 * D_MODEL // (3 * 256)) * 256
EPS = 1e-6

QA = N_HEADS_A * HEAD_DIM
KA = N_HEADS_A * HEAD_DIM
VA = N_HEADS_A * HEAD_DIM
QB = N_HEADS_B * HEAD_DIM
KB = N_KV_B * HEAD_DIM
VB = N_KV_B * HEAD_DIM
IN_COLS = QA + KA + VA + QB + KB + VB

kernel_name = "hymba_dilated_swa_sink_encoder"


def rmsnorm(x, g):
    x32 = x.astype(jnp.float32)
    y = x32 * lax.rsqrt(jnp.mean(x32 * x32, axis=-1, keepdims=True) + EPS)
    return (y * g.astype(jnp.float32)).astype(x.dtype)


def rope(t, positions):
    half = HEAD_DIM // 2
    inv = ROPE_THETA ** (-2.0 * jnp.arange(half, dtype=jnp.float32) / HEAD_DIM)
    ang = positions.astype(jnp.float32)[:, None] * inv[None, :]
    cos = jnp.cos(ang)[None, :, None, :]
    sin = jnp.sin(ang)[None, :, None, :]
    t32 = t.astype(jnp.float32)
    t1, t2 = t32[..., :half], t32[..., half:]
    return jnp.concatenate([t1 * cos - t2 * sin, t2 * cos + t1 * sin], axis=-1).astype(t.dtype)


def banded_attention(q, k, v, half_window, block, sink=None):
    B, L, Hk, G, Dh = q.shape
    nb = -(-L // block)
    Lp = nb * block
    pad = Lp - L
    q32 = jnp.pad(q.astype(jnp.float32), ((0, 0), (0, pad), (0, 0), (0, 0), (0, 0)))
    kp = jnp.pad(k.astype(jnp.float32), ((0, 0), (block, pad + block), (0, 0), (0, 0)))
    vp = jnp.pad(v.astype(jnp.float32), ((0, 0), (block, pad + block), (0, 0), (0, 0)))
    qb = q32.reshape(B, nb, block, Hk, G, Dh)
    kb = kp.reshape(B, nb + 2, block, Hk, Dh)
    vb = vp.reshape(B, nb + 2, block, Hk, Dh)
    kw = jnp.concatenate([kb[:, :-2], kb[:, 1:-1], kb[:, 2:]], axis=2)
    vw = jnp.concatenate([vb[:, :-2], vb[:, 1:-1], vb[:, 2:]], axis=2)
    rel = jnp.arange(3 * block)[None, :] - block - jnp.arange(block)[:, None]
    band = jnp.abs(rel) <= half_window
    kpos = (jnp.arange(nb)[:, None] - 1) * block + jnp.arange(3 * block)[None, :]
    valid = (kpos >= 0) & (kpos < L)
    mask = band[None] & valid[:, None, :]
    scale = HEAD_DIM ** -0.5
    s = jnp.einsum('bnqhgd,bnkhd->bnhgqk', qb, kw) * scale
    s = jnp.where(mask[None, :, None, None], s, -jnp.inf)
    m = jnp.max(s, axis=-1)
    if sink is not None:
        sk = sink.astype(jnp.float32)[None, None, :, :, None]
        m = jnp.maximum(m, sk)
    p = jnp.exp(s - m[..., None])
    denom = jnp.sum(p, axis=-1)
    if sink is not None:
        denom = denom + jnp.exp(sk - m)
    o = jnp.einsum('bnhgqk,bnkhd->bnqhgd', p, vw)
    denom_t = denom.transpose(0, 1, 4, 2, 3)
    o = o / denom_t[..., None]
    lse = (m.transpose(0, 1, 4, 2, 3) + jnp.log(denom_t))
    o = o.reshape(B, Lp, Hk, G, Dh)[:, :L]
    lse = lse.reshape(B, Lp, Hk, G)[:, :L]
    return o, lse


def dilated_branch(q, k, v, window, dilation):
    B, S, H, Dh = q.shape
    L = S // dilation
    count = (window // 2) // dilation

    def to_sub(t):
        return t.reshape(B, L, dilation, H, Dh).transpose(0, 2, 1, 3, 4).reshape(B * dilation, L, H, Dh)

    o, lse = banded_attention(to_sub(q)[:, :, :, None, :], to_sub(k), to_sub(v), count, count)
    o = o.reshape(B, dilation, L, H, Dh).transpose(0, 2, 1, 3, 4).reshape(B, S, H, Dh)
    lse = lse.reshape(B, dilation, L, H).transpose(0, 2, 1, 3).reshape(B, S, H)
    return o, lse


def encoder_layer(x, attn_norm, w_in, qnorm_a, knorm_a, qnorm_b, knorm_b, sink_b,
                  w_out, ffn_norm, w_gate, w_up, w_down):
    B, S, _ = x.shape
    pos = jnp.arange(S)
    h = rmsnorm(x, attn_norm)
    proj = h @ w_in
    splits = np.cumsum([QA, KA, VA, QB, KB])
    qa, ka, va, qb, kb, vb = jnp.split(proj, splits, axis=-1)
    qa = rope(rmsnorm(qa.reshape(B, S, N_HEADS_A, HEAD_DIM), qnorm_a), pos)
    ka = rope(rmsnorm(ka.reshape(B, S, N_HEADS_A, HEAD_DIM), knorm_a), pos)
    va = va.reshape(B, S, N_HEADS_A, HEAD_DIM)
    qb = rope(rmsnorm(qb.reshape(B, S, N_HEADS_B, HEAD_DIM), qnorm_b), pos)
    kb = rope(rmsnorm(kb.reshape(B, S, N_KV_B, HEAD_DIM), knorm_b), pos)
    vb = vb.reshape(B, S, N_KV_B, HEAD_DIM)

    outs, lses = [], []
    for window, dilation in DILATED_PATTERNS:
        o, l = dilated_branch(qa, ka, va, window, dilation)
        outs.append(o)
        lses.append(l)
    wts = jax.nn.softmax(jnp.stack(lses, axis=0), axis=0)
    out_a = jnp.sum(wts[..., None] * jnp.stack(outs, axis=0), axis=0)
    out_a = out_a.reshape(B, S, QA).astype(x.dtype)

    G = N_HEADS_B // N_KV_B
    out_b, _ = banded_attention(qb.reshape(B, S, N_KV_B, G, HEAD_DIM), kb, vb,
                                SWA_HALF_WINDOW, SWA_BLOCK, sink=sink_b.reshape(N_KV_B, G))
    out_b = out_b.reshape(B, S, QB).astype(x.dtype)

    x = x + jnp.concatenate([out_a, out_b], axis=-1) @ w_out
    h2 = rmsnorm(x, ffn_norm)
    x = x + (jax.nn.silu(h2 @ w_gate) * (h2 @ w_up)) @ w_down
    return x


def setup_inputs(seed: int = 0) -> dict:
    key = jax.random.key(seed)
    ks = jax.random.split(key, 16)
    f32 = jnp.float32

    def nrm(k, shape, scale):
        return jax.random.normal(k, shape, f32) * scale

    return {
        "x_prompt": nrm(ks[0], (BATCH, SEQ, D_MODEL), 1.0),
        "x_sample": nrm(ks[1], (DEC_BATCH, DEC_SEQ, D_MODEL), 1.0),
        "attn_norm": 1.0 + nrm(ks[2], (DEPTH, D_MODEL), 0.02),
        "w_in": nrm(ks[3], (DEPTH, D_MODEL, IN_COLS), D_MODEL ** -0.5),
        "qnorm_a": 1.0 + nrm(ks[4], (DEPTH, HEAD_DIM), 0.02),
        "knorm_a": 1.0 + nrm(ks[5], (DEPTH, HEAD_DIM), 0.02),
        "qnorm_b": 1.0 + nrm(ks[6], (DEPTH, HEAD_DIM), 0.02),
        "knorm_b": 1.0 + nrm(ks[7], (DEPTH, HEAD_DIM), 0.02),
        "sink_b": nrm(ks[8], (DEPTH, N_HEADS_B), 0.5),
        "w_out": nrm(ks[9], (DEPTH, QA + QB, D_MODEL), (QA + QB) ** -0.5),
        "ffn_norm": 1.0 + nrm(ks[10], (DEPTH, D_MODEL), 0.02),
        "w_gate": nrm(ks[11], (DEPTH, D_MODEL, D_FF), D_MODEL ** -0.5),
        "w_up": nrm(ks[12], (DEPTH, D_MODEL, D_FF), D_MODEL ** -0.5),
        "w_down": nrm(ks[13], (DEPTH, D_FF, D_MODEL), D_FF ** -0.5),
    }


def reference(x_prompt, x_sample, attn_norm, w_in, qnorm_a, knorm_a, qnorm_b, knorm_b,
              sink_b, w_out, ffn_norm, w_gate, w_up, w_down):
    y_prompt = x_prompt
    y_sample = x_sample
    for i in range(DEPTH):
        params = (attn_norm[i], w_in[i], qnorm_a[i], knorm_a[i], qnorm_b[i], knorm_b[i],
                  sink_b[i], w_out[i], ffn_norm[i], w_gate[i], w_up[i], w_down[i])
        y_prompt = encoder_layer(y_prompt, *params)
        y_sample = encoder_layer(y_sample, *params)
    return (y_prompt, y_sample)
```

```python
import numpy as np
import ml_dtypes
import concourse.bass as bass
import concourse.mybir as mybir
from concourse.bass_utils import run_bass_kernel_spmd

F32 = mybir.dt.float32
BF16 = mybir.dt.bfloat16
ALU = mybir.AluOpType
AF = mybir.ActivationFunctionType
AX = mybir.AxisListType

D = 1024
NT = 2048
DFF = 2816
NJ = 22
EPS = 1e-6
NCORES = 8


class Src:
    def __init__(self, nc, name, step):
        self.sem = nc.alloc_semaphore(name=name)
        self.count = 0
        self.step = step
        self.name = name


class Buf:
    __slots__ = ("name", "writes", "reads", "dsrc")

    def __init__(self, name):
        self.name = name
        self.writes = []
        self.reads = []
        self.dsrc = None


class Eng:
    def __init__(self, fw, name, eng, is_pe=False):
        self.name = name
        self.eng = eng
        self.src = Src(fw.nc, "e_" + name, 1)
        self.known = {}
        self.is_pe = is_pe


class FW:
    def __init__(self, nc):
        self.nc = nc
        self.pe = Eng(self, "pe", nc.tensor, is_pe=True)
        self.act = Eng(self, "act", nc.scalar)
        self.dve = Eng(self, "dve", nc.vector)
        self.pool = Eng(self, "pool", nc.gpsimd)
        self.sp = Eng(self, "sp", nc.sync)
        self.engs = [self.pe, self.act, self.dve, self.pool, self.sp]
        self.nbuf = 0
        self.dsrcs = []
        self.bufs = {}

    def buf(self, name=None):
        self.nbuf += 1
        if name is None:
            return Buf(f"b{self.nbuf}")
        if name not in self.bufs:
            self.bufs[name] = Buf(name)
        return self.bufs[name]

    def _collect(self, E, reads, writes):
        need = {}

        def add(tok):
            s, c = tok
            if E.is_pe and s is E.src:
                return
            if s.step == 16:
                c = s.count
            if need.get(s, 0) < c:
                need[s] = c
        for b in reads:
            for t in b.writes:
                add(t)
        for b in writes:
            for t in b.writes:
                add(t)
            for t in b.reads:
                add(t)
        for s, c in need.items():
            if E.known.get(s, 0) >= c:
                continue
            E.eng.wait_ge(s.sem, c)
            E.known[s] = c

    def _commit(self, tok, reads, writes, partial):
        for b in writes:
            if partial:
                b.writes.append(tok)
                if len(b.writes) > 24:
                    m = {}
                    for s, c in b.writes:
                        if m.get(s, 0) < c:
                            m[s] = c
                    b.writes = list(m.items())
            else:
                b.writes = [tok]
                b.reads = []
        for b in reads:
            b.reads.append(tok)
            if len(b.reads) > 24:
                m = {}
                for s, c in b.reads:
                    if m.get(s, 0) < c:
                        m[s] = c
                b.reads = list(m.items())

    def op(self, E, fn, reads=(), writes=(), signal=True, partial=False):
        self._collect(E, reads, writes)
        ins = fn()
        if signal:
            E.src.count += 1
            ins.then_inc(E.src.sem, 1)
            tok = (E.src, E.src.count)
        else:
            tok = (E.src, E.src.count + 1)
        self._commit(tok, reads, writes, partial)
        return ins

    def dma(self, out_ap, in_ap, key, reads=(), writes=(), partial=False):
        E = self.sp
        if key.dsrc is None:
            key.dsrc = Src(self.nc, "d_" + key.name, 16)
            self.dsrcs.append(key.dsrc)
        self._collect(E, reads, writes)
        ins = E.eng.dma_start(out=out_ap, in_=in_ap)
        key.dsrc.count += 16
        ins.then_inc(key.dsrc.sem, 16)
        tok = (key.dsrc, key.dsrc.count)
        self._commit(tok, reads, writes, partial)
        return ins

    def barrier(self):
        for E in self.engs:
            for F in self.engs:
                if F is E or F is self.sp:
                    continue
                c = F.src.count
                if c > 0 and E.known.get(F.src, 0) < c:
                    E.eng.wait_ge(F.src.sem, c)
                    E.known[F.src] = c
            for d in self.dsrcs:
                if d.count > 0 and E.known.get(d, 0) < d.count:
                    E.eng.wait_ge(d.sem, d.count)
                    E.known[d] = d.count

    def finish(self):
        E = self.sp
        for F in self.engs:
            if F is E:
                continue
            if F.src.count > 0:
                E.eng.wait_ge(F.src.sem, F.src.count)
        for d in self.dsrcs:
            if d.count > 0:
                E.eng.wait_ge(d.sem, d.count)


def build_program(sel=None, phases="ABC"):
    nc = bass.Bass("TRN2", target_bir_lowering=False)
    fw = FW(nc)
    PE, ACT, DVE, POOL = fw.pe, fw.act, fw.dve, fw.pool

    def din(name, shape, dt=F32):
        return nc.dram_tensor(name, list(shape), dt, kind="ExternalInput").ap()

    xs = din("xs", [4, NT, D])
    xp = din("xp", [6144, D])
    cst = din("cst", [80, 128, 64])
    w_in = din("w_in", [D, 2304])
    w_out = din("w_out", [D, D])
    w_gate = din("w_gate", [D, DFF])
    w_up = din("w_up", [D, DFF])
    w_down = din("w_down", [DFF, D])
    gA_d = din("gA", [128, 8])
    gF_d = din("gF", [128, 8])
    gq_d = din("gq", [128, 4, 64])
    sink_d = din("sinkT", [128, 4])
    mask_d = din("masks", [128, 2, 128], BF16)
    ident_d = din("ident", [128, 128], BF16)
    hv_d = din("hv", [128, 2, 2, 64], BF16)
    ys = nc.dram_tensor("ys", [4, NT, D], F32, kind="ExternalOutput").ap()
    yp = nc.dram_tensor("yp", [4096, D], F32, kind="ExternalOutput").ap()
    QKs = nc.dram_tensor("QKs", [2, 13, 128, NT], BF16, kind="Internal").ap()
    Vs = nc.dram_tensor("Vs", [2, 5, NT, 128], BF16, kind="Internal").ap()
    wgs = nc.dram_tensor("wgs", [NJ, 128, 8, 128], BF16, kind="Internal").ap()
    wus = nc.dram_tensor("wus", [NJ, 128, 8, 128], BF16, kind="Internal").ap()
    wds = nc.dram_tensor("wds", [NJ, 128, D], BF16, kind="Internal").ap()
    b_QKs = [[fw.buf(f"QKs{s}_{c}") for c in range(13)] for s in range(2)]
    b_Vs = [[fw.buf(f"Vs{s}_{c}") for c in range(5)] for s in range(2)]
    b_wsc = fw.buf("wscratch")

    def sb(name, shape, dt):
        return nc.alloc_sbuf_tensor("sb_" + name, list(shape), dt).ap()

    gA = sb("gA", [128, 8], F32)
    gF = sb("gF", [128, 8], F32)
    gq = sb("gq", [128, 4, 64], F32)
    mask = sb("mask", [128, 2, 128], BF16)
    ident = sb("ident", [128, 128], BF16)
    ones64 = sb("ones64", [128, 64], BF16)
    hvb = sb("hvb", [128, 2, 2, 64], BF16)
    es = sb("es", [128, 4], F32)
    eps_t = sb("eps", [128, 1], F32)
    winb = sb("winb", [128, 8, 2304], BF16)
    woutb = sb("woutb", [128, 8, D], BF16)
    AT = sb("AT", [128, 8, NT], BF16)
    b_const = fw.buf("const")
    b_winb = fw.buf("winb")
    b_woutb = fw.buf("woutb")
    b_AT = [fw.buf(f"AT{k}") for k in range(8)]

    PS = nc.alloc_psum_tensor("PS", [128, 4096], F32).ap()
    BK = [fw.buf(f"bank{i}") for i in range(8)]

    def bank(i, n=1):
        return PS[:, i * 512:(i + n) * 512]

    arena0 = (nc.sbuf_base + 63) // 64 * 64
    arena_state = {"off": 0}
    ARENA_BYTES = nc.sbuf_top - arena0

    def arena_reset():
        arena_state["off"] = 0

    uid = [0]

    def ar(name, shape, dt):
        nbytes = int(np.prod(shape[1:])) * (2 if dt == BF16 else 4)
        nbytes = (nbytes + 63) // 64 * 64
        off = arena_state["off"]
        assert off + nbytes <= ARENA_BYTES, (name, off, nbytes, ARENA_BYTES)
        arena_state["off"] = off + nbytes
        uid[0] += 1
        return nc.alloc_sbuf_tensor_at(f"{name}_{uid[0]}", list(shape), dt, offset=arena0 + off).ap()

    for dst, src in ((gA, gA_d), (gF, gF_d), (gq, gq_d), (mask, mask_d), (ident, ident_d),
                     (hvb, hv_d), (es, sink_d)):
        fw.dma(dst, src, key=b_const, writes=[b_const], partial=True)
    fw.op(POOL, lambda: nc.gpsimd.memset(ones64, 1.0), writes=[b_const], partial=True)
    fw.op(POOL, lambda: nc.gpsimd.memset(eps_t, EPS), writes=[b_const], partial=True)
    fw.op(ACT, lambda: nc.scalar.activation(out=es, in_=es, func=AF.Exp), reads=[b_const], writes=[b_const],
          partial=True)

    arena_reset()
    stg = [ar("stg", [128, 2304], F32) for _ in range(3)]
    b_stg = [fw.buf(f"stg{i}") for i in range(3)]
    stb = [ar("stb", [128, 1024], BF16) for _ in range(3)]
    b_stb = [fw.buf(f"stb{i}") for i in range(3)]
    cast_engs = [(ACT, lambda o, i: nc.scalar.copy(out=o, in_=i)),
                 (DVE, lambda o, i: nc.vector.tensor_copy(out=o, in_=i)),
                 (POOL, lambda o, i: nc.gpsimd.tensor_copy(out=o, in_=i))]
    cnt = 0
    for k in range(8):
        i = cnt % 3
        fw.dma(stg[i], w_in[k * 128:(k + 1) * 128, :], key=b_stg[i], writes=[b_stg[i]])
        if i == 0:
            fw.op(ACT, lambda i=i, k=k: nc.scalar.activation(out=winb[:, k, :], in_=stg[i], func=AF.Copy,
                                                             scale=gA[:, k:k + 1]),
                  reads=[b_stg[i], b_const], writes=[b_winb], partial=True)
        else:
            fw.op(DVE, lambda i=i, k=k: nc.vector.tensor_scalar(out=winb[:, k, :], in0=stg[i],
                                                                scalar1=gA[:, k:k + 1], scalar2=None,
                                                                op0=ALU.mult),
                  reads=[b_stg[i], b_const], writes=[b_winb], partial=True)
        cnt += 1
    for k in range(8):
        i = cnt % 3
        fw.dma(stg[i][:, 0:D], w_out[k * 128:(k + 1) * 128, :], key=b_stg[i], writes=[b_stg[i]])
        E, f = cast_engs[i]
        fw.op(E, lambda f=f, i=i, k=k: f(woutb[:, k, :], stg[i][:, 0:D]), reads=[b_stg[i]], writes=[b_woutb],
              partial=True)
        cnt += 1
    for j in range(NJ):
        for (wsrc, wdst) in ((w_gate, wgs), (w_up, wus)):
            i = cnt % 3
            fw.dma(stg[i][:, 0:1024].rearrange("p (k c) -> p k c", k=8),
                   wsrc[:, j * 128:(j + 1) * 128].rearrange("(k p) c -> p k c", p=128),
                   key=b_stg[i], writes=[b_stg[i]])
            ME, me = (POOL, nc.gpsimd) if i == 0 else (DVE, nc.vector)
            fw.op(ME, lambda me=me, i=i: me.tensor_tensor(
                out=stb[i].rearrange("p (k c) -> p k c", k=8), in0=stg[i][:, 0:1024].rearrange("p (k c) -> p k c", k=8),
                in1=gF.unsqueeze(2).to_broadcast([128, 8, 128]), op=ALU.mult),
                reads=[b_stg[i], b_const], writes=[b_stb[i]])
            fw.dma(wdst[j].rearrange("p k c -> p (k c)"), stb[i], key=b_stb[i], reads=[b_stb[i]],
                   writes=[b_wsc], partial=True)
            cnt += 1
        i = cnt % 3
        fw.dma(stg[i][:, 0:1024], w_down[j * 128:(j + 1) * 128, :], key=b_stg[i], writes=[b_stg[i]])
        E, f = cast_engs[i]
        fw.op(E, lambda f=f, i=i: f(stb[i], stg[i][:, 0:1024]), reads=[b_stg[i]], writes=[b_stb[i]])
        fw.dma(wds[j], stb[i], key=b_stb[i], reads=[b_stb[i]], writes=[b_wsc], partial=True)
        cnt += 1
    fw.barrier()

    pieces = []
    for i in range(4):
        pieces.append(dict(halo=False, xsrc=lambda n, i=i: xs[i, n * 128:(n + 1) * 128, :],
                           hsrc=None, cs0=0, csh=None, out=lambda r0, i=i: ys[i, r0:r0 + 128, :], pi=0))
    for pi in range(2):
        mb = 1024 + 2048 * pi

        def hrow(n, mb=mb):
            hidx = n * 128
            return (mb + 2048 + hidx) if hidx < 1024 else (mb + hidx - 2048)
        pieces.append(dict(halo=True, xsrc=lambda n, mb=mb: xp[mb + n * 128: mb + (n + 1) * 128, :],
                           hsrc=lambda n, hrow=hrow: xp[hrow(n):hrow(n) + 128, :],
                           cs0=16 + 32 * pi, csh=32 + 32 * pi,
                           out=lambda r0, pi=pi: yp[2048 * pi + r0: 2048 * pi + r0 + 128, :], pi=pi))

    def phaseA(pc):
        arena_reset()
        NX = 4
        XT = [ar("xt", [128, D], F32) for _ in range(NX)]
        CS = [ar("cs", [128, 64], F32) for _ in range(NX)]
        b_XT = [fw.buf(f"xt{i}") for i in range(NX)]
        b_CS = [fw.buf(f"cs{i}") for i in range(NX)]
        junk = ar("junk", [128, D], BF16)
        b_junk = fw.buf("junk")
        ssq = [ar("ssq", [128, 4], F32) for _ in range(2)]
        b_ssq = [fw.buf(f"ssq{i}") for i in range(2)]
        xn = [ar("xn", [128, D], BF16) for _ in range(2)]
        b_xn = [fw.buf(f"xn{i}") for i in range(2)]
        hT = [ar("hT", [128, 8, 128], BF16) for _ in range(2)]
        b_hT = [fw.buf(f"hT{i}") for i in range(2)]
        vt = [ar("vt", [128, 5, 128], BF16) for _ in range(2)]
        b_vt = [fw.buf(f"vt{i}") for i in range(2)]
        qkt = [ar("qkt", [128, 13, 256], BF16) for _ in range(2)]
        b_qkt = [fw.buf(f"qkt{i}") for i in range(2)]
        SETS = []
        for i in range(2):
            SETS.append(dict(sq=ar("sq", [128, 1664], F32), qn=ar("qn", [128, 1664], F32),
                             rb=ar("rb", [128, 1664], F32), qr=ar("qr", [128, 1664], BF16),
                             s26=ar("s26", [128, 3, 26], F32), gc=ar("gc", [128, 4, 64], F32),
                             gs=ar("gs", [128, 2, 4, 32], F32), b_tab=fw.buf(f"tab{i}"),
                             b_sq=[fw.buf(f"sq{i}_{h}") for h in range(2)],
                             b_qn=[fw.buf(f"qn{i}_{h}") for h in range(2)],
                             b_rb=[fw.buf(f"rb{i}_{h}") for h in range(2)],
                             b_qr=fw.buf(f"qr{i}"), b_s26=[fw.buf(f"s26{i}_{h}") for h in range(2)]))
        P1 = bank(0, 4)
        P2 = bank(4)
        pT = bank(5).bitcast(BF16)
        qkT = bank(6, 2).bitcast(BF16)

        tiles = []
        for s in ([0, 1] if pc["halo"] else [0]):
            for n in range(16):
                tiles.append((s, n))
        T = len(tiles)

        def load(k):
            s, n = tiles[k]
            i = k % NX
            src = pc["xsrc"](n) if s == 0 else pc["hsrc"](n)
            fw.dma(XT[i], src, key=b_XT[i], writes=[b_XT[i]])
            ci = (pc["cs0"] if s == 0 else pc["csh"]) + n
            fw.dma(CS[i], cst[ci], key=b_CS[i], writes=[b_CS[i]])

        def info(k):
            s, n = tiles[k]
            edge = (s == 1 and n in (0, 15))
            if s == 0:
                ranges = [(0, 26)]
                chunks = list(range(13))
            else:
                ranges = [(8, 16)] + ([(24, 26)] if edge else [])
                chunks = [4, 5, 6, 7] + ([12] if edge else [])
            return s, n, edge, ranges, chunks

        def bks_of(c0, c1):
            return BK[c0 // 512:(c1 - 1) // 512 + 1]

        def xnorm(k):
            i = k % NX
            xt = XT[i]
            p = k % 2
            sq_, xn_ = ssq[p], xn[p]
            fw.op(ACT, lambda: nc.scalar.activation(out=junk, in_=xt, func=AF.Square, accum_out=sq_[:, 0:1]),
                  reads=[b_XT[i]], writes=[b_junk, b_ssq[p]])
            fw.op(ACT, lambda: nc.scalar.activation(out=sq_[:, 1:2], in_=sq_[:, 0:1], func=AF.Sqrt, scale=1.0 / D,
                                                    bias=eps_t), reads=[b_ssq[p], b_const], writes=[b_ssq[p]])
            fw.op(DVE, lambda: nc.vector.reciprocal(out=sq_[:, 2:3], in_=sq_[:, 1:2]), reads=[b_ssq[p]],
                  writes=[b_ssq[p]])
            fw.op(ACT, lambda: nc.scalar.activation(out=xn_, in_=xt, func=AF.Copy, scale=sq_[:, 2:3]),
                  reads=[b_XT[i], b_ssq[p]], writes=[b_xn[p]])

        def xT(k):
            p = k % 2
            xn_ = xn[p]
            for kk in range(8):
                fw.op(PE, lambda kk=kk: nc.tensor.transpose(out=pT[:, kk * 128:(kk + 1) * 128],
                                                            in_=xn_[:, kk * 128:(kk + 1) * 128], identity=ident),
                      reads=[b_xn[p], b_const], writes=[BK[5]], signal=(kk == 7), partial=(kk > 0))
            fw.op(ACT, lambda: nc.scalar.copy(out=hT[p].rearrange("p k t -> p (k t)"), in_=pT), reads=[BK[5]],
                  writes=[b_hT[p]])

        def proj(k, which):
            s, n, edge, ranges, chunks = info(k)
            p = k % 2
            if s == 0:
                allg = {"h0": [(0, 512, P1[:, 0:512], [BK[0]]), (512, 512, P1[:, 512:1024], [BK[1]])],
                        "h1": [(1024, 512, P1[:, 1024:1536], [BK[2]]), (1536, 256, P1[:, 1536:1792], [BK[3]])],
                        "va": [(1792, 512, P2, [BK[4]])]}
            else:
                allg = {"h0": [(512, 512, P1[:, 512:1024], [BK[1]])],
                        "h1": ([(1536, 256, P1[:, 1536:1792], [BK[3]])] if edge else []),
                        "va": [(1792, 512, P2, [BK[4]])]}
            for (c0, ncol, outap, bks) in allg[which]:
                for kk in range(8):
                    fw.op(PE, lambda kk=kk, c0=c0, ncol=ncol, outap=outap: nc.tensor.matmul(
                        outap, lhsT=hT[p][:, kk, :], rhs=winb[:, kk, c0:c0 + ncol], start=(kk == 0),
                        stop=(kk == 7)), reads=[b_hT[p], b_winb], writes=bks, signal=(kk == 7), partial=(kk > 0))

        def vevac(k):
            s, n, edge, ranges, chunks = info(k)
            p = k % 2
            fw.op(ACT, lambda: nc.scalar.copy(out=vt[p][:, 0:4, :].rearrange("p c e -> p (c e)"), in_=P2),
                  reads=[BK[4]], writes=[b_vt[p]])
            npair = 4
            if s == 0 or edge:
                npair = 5
                fw.op(ACT, lambda: nc.scalar.copy(out=vt[p][:, 4, :], in_=P1[:, 1664:1792]), reads=[BK[3]],
                      writes=[b_vt[p]], partial=True)
            fw.dma(Vs[s, 0:npair, n * 128:(n + 1) * 128, :].rearrange("c p e -> p c e"), vt[p][:, 0:npair, :],
                   key=b_vt[p], reads=[b_vt[p]], writes=b_Vs[s][0:npair], partial=True)

        def half_ranges(k, which):
            s, n, edge, ranges, chunks = info(k)
            if s == 0:
                return [(0, 16)] if which == "h0" else [(16, 26)]
            if which == "h0":
                return [(8, 16)]
            return [(24, 26)] if edge else []

        def s2a(k, which):
            S_ = SETS[k % 2]
            sq, qn, s26 = S_["sq"], S_["qn"], S_["s26"]
            hx = 0 if which == "h0" else 1
            b_sq, b_qn, b_s26 = S_["b_sq"][hx], S_["b_qn"][hx], S_["b_s26"][hx]
            for (h0, h1) in half_ranges(k, which):
                H = h1 - h0
                c0, c1 = h0 * 64, h1 * 64
                bks = bks_of(c0, c1)
                fw.op(ACT, lambda: nc.scalar.activation(out=sq[:, c0:c1], in_=P1[:, c0:c1], func=AF.Square),
                      reads=bks, writes=[b_sq])
                fw.op(DVE, lambda: nc.vector.tensor_reduce(out=s26[:, 0, h0:h1],
                                                           in_=sq[:, c0:c1].rearrange("p (h d) -> p h d", d=64),
                                                           axis=AX.X, op=ALU.add), reads=[b_sq],
                      writes=[b_s26])
                fw.op(ACT, lambda: nc.scalar.activation(out=s26[:, 1, h0:h1], in_=s26[:, 0, h0:h1], func=AF.Sqrt,
                                                        scale=1.0 / 64, bias=eps_t), reads=[b_s26, b_const],
                      writes=[b_s26])
                fw.op(DVE, lambda: nc.vector.reciprocal(out=s26[:, 2, h0:h1], in_=s26[:, 1, h0:h1]),
                      reads=[b_s26], writes=[b_s26])
                fw.op(DVE, lambda: nc.vector.tensor_tensor(
                    out=qn[:, c0:c1].rearrange("p (h d) -> p h d", d=64),
                    in0=P1[:, c0:c1].rearrange("p (h d) -> p h d", d=64),
                    in1=s26[:, 2, h0:h1].unsqueeze(2).to_broadcast([128, H, 64]), op=ALU.mult),
                    reads=bks + [b_s26], writes=[b_qn])

        TYPES = {"h0": [(0, 0, 8), (1, 8, 16)], "h1": [(2, 16, 24), (3, 24, 26)]}

        def rope_ew(k):
            s, n, edge, ranges, chunks = info(k)
            i = k % NX
            cs, b_cs = CS[i], b_CS[i]
            S_ = SETS[k % 2]
            ra, qn, rb, qr = S_["sq"], S_["qn"], S_["rb"], S_["qr"]
            gc, gs, b_tab = S_["gc"], S_["gs"], S_["b_tab"]
            fw.op(DVE, lambda: nc.vector.tensor_tensor(
                out=gc.rearrange("p y (t d) -> p y t d", t=2), in0=gq.rearrange("p y (t d) -> p y t d", t=2),
                in1=cs[:, 0:32].unsqueeze(1).unsqueeze(1).to_broadcast([128, 4, 2, 32]), op=ALU.mult),
                reads=[b_const, b_cs], writes=[b_tab])
            fw.op(DVE, lambda: nc.vector.tensor_tensor(
                out=gs[:, 0, :, :], in0=gq[:, :, 32:64],
                in1=cs[:, 32:64].unsqueeze(1).to_broadcast([128, 4, 32]), op=ALU.mult),
                reads=[b_const, b_cs], writes=[b_tab], partial=True)
            fw.op(DVE, lambda: nc.vector.tensor_tensor(
                out=gs[:, 1, :, :], in0=gq[:, :, 0:32],
                in1=cs[:, 32:64].unsqueeze(1).to_broadcast([128, 4, 32]), op=ALU.mult),
                reads=[b_const, b_cs], writes=[b_tab], partial=True)
            for hx, which in enumerate(("h0", "h1")):
                b_ra, b_qn, b_rb, b_qr = S_["b_sq"][hx], S_["b_qn"][hx], S_["b_rb"][hx], S_["b_qr"]
                hr = half_ranges(k, which)
                if not hr:
                    continue
                first = True
                for (ty, t0_, t1_) in TYPES[which]:
                    for (h0, h1) in hr:
                        a0, a1 = max(h0, t0_), min(h1, t1_)
                        if a0 >= a1:
                            continue
                        H = a1 - a0
                        c0, c1 = a0 * 64, a1 * 64
                        v4 = lambda t: t[:, c0:c1].rearrange("p (h t d) -> p h t d", t=2, d=32)
                        gcb = gc[:, ty, :].rearrange("p (t d) -> p t d", t=2).unsqueeze(1).to_broadcast(
                            [128, H, 2, 32])
                        g1b = gs[:, 0, ty, :].unsqueeze(1).to_broadcast([128, H, 32])
                        g2b = gs[:, 1, ty, :].unsqueeze(1).to_broadcast([128, H, 32])
                        fw.op(DVE, lambda: nc.vector.tensor_tensor(out=v4(ra), in0=v4(qn), in1=gcb, op=ALU.mult),
                              reads=[b_qn, b_tab], writes=[b_ra], partial=(not first))
                        fw.op(DVE, lambda: nc.vector.tensor_tensor(out=v4(rb)[:, :, 0, :], in0=v4(qn)[:, :, 1, :],
                                                                   in1=g1b, op=ALU.mult), reads=[b_qn, b_tab],
                              writes=[b_rb], partial=(not first))
                        fw.op(DVE, lambda: nc.vector.tensor_tensor(out=v4(rb)[:, :, 1, :], in0=v4(qn)[:, :, 0, :],
                                                                   in1=g2b, op=ALU.mult), reads=[b_qn, b_tab],
                              writes=[b_rb], partial=True)
                        first = False
                for (h0, h1) in hr:
                    c0, c1 = h0 * 64, h1 * 64
                    v4 = lambda t: t[:, c0:c1].rearrange("p (h t d) -> p h t d", t=2, d=32)
                    fw.op(DVE, lambda: nc.vector.tensor_tensor(out=v4(qr)[:, :, 0, :], in0=v4(ra)[:, :, 0, :],
                                                                in1=v4(rb)[:, :, 0, :], op=ALU.subtract),
                          reads=[b_ra, b_rb], writes=[b_qr], partial=True)
                    fw.op(DVE, lambda: nc.vector.tensor_tensor(out=v4(qr)[:, :, 1, :], in0=v4(ra)[:, :, 1, :],
                                                                in1=v4(rb)[:, :, 1, :], op=ALU.add),
                          reads=[b_ra, b_rb], writes=[b_qr], partial=True)

        def rope_T(k):
            s, n, edge, ranges, chunks = info(k)
            S_ = SETS[k % 2]
            qr, b_qr = S_["qr"], S_["b_qr"]
            ti = n % 2
            gi = (k // 2) % 2
            for ci_, c in enumerate(chunks):
                fw.op(PE, lambda c=c: nc.tensor.transpose(out=qkT[:, c * 128:(c + 1) * 128],
                                                          in_=qr[:, c * 128:(c + 1) * 128], identity=ident),
                      reads=[b_qr, b_const], writes=[BK[6], BK[7]], signal=(ci_ == len(chunks) - 1),
                      partial=(ci_ > 0))
            tsl = slice(ti * 128, (ti + 1) * 128)
            if s == 0:
                fw.op(ACT, lambda: nc.scalar.copy(out=qkt[gi][:, :, tsl],
                                                  in_=qkT[:, 0:1664].rearrange("p (c t) -> p c t", t=128)),
                      reads=[BK[6], BK[7]], writes=[b_qkt[gi]], partial=True)
            else:
                fw.op(ACT, lambda: nc.scalar.copy(out=qkt[gi][:, 4:8, tsl],
                                                  in_=qkT[:, 512:1024].rearrange("p (c t) -> p c t", t=128)),
                      reads=[BK[6], BK[7]], writes=[b_qkt[gi]], partial=True)
                if edge:
                    fw.op(ACT, lambda: nc.scalar.copy(out=qkt[gi][:, 12, tsl], in_=qkT[:, 1536:1664]),
                          reads=[BK[6], BK[7]], writes=[b_qkt[gi]], partial=True)
                    fw.dma(QKs[s, 12, :, n * 128:(n + 1) * 128], qkt[gi][:, 12, tsl],
                           key=b_qkt[gi], reads=[b_qkt[gi]], writes=[b_QKs[s][12]], partial=True)
            if ti == 1:
                g0 = (n // 2) * 256
                for (c0, c1) in (((0, 7), (7, 13)) if s == 0 else ((4, 8),)):
                    fw.dma(QKs[s, c0:c1, :, g0:g0 + 256].rearrange("c p t -> p c t"),
                           qkt[gi][:, c0:c1, :], key=b_qkt[gi], reads=[b_qkt[gi]], writes=b_QKs[s][c0:c1],
                           partial=True)

        for k in range(min(3, T)):
            load(k)
        xnorm(0)
        xT(0)
        if T > 1:
            xnorm(1)
        for k in range(T):
            if k + 3 < T:
                load(k + 3)
            proj(k, "h0")
            if k + 1 < T:
                xT(k + 1)
            s2a(k, "h0")
            proj(k, "h1")
            proj(k, "va")
            vevac(k)
            s2a(k, "h1")
            if k + 2 < T:
                xnorm(k + 2)
            if k >= 1:
                rope_T(k - 1)
            rope_ew(k)
        rope_T(T - 1)
        fw.barrier()

    def phaseB(pc):
        arena_reset()
        halo = pc["halo"]
        pi = pc["pi"]
        UB = []
        offs = []
        for u in range(2):
            offs.append(arena_state["off"])
            UB.append(dict(
                qT=ar("qT", [128, NT], BF16), kT=ar("kT", [128, 2, NT], BF16),
                Vn=ar("Vn", [128, 18, 128], BF16), V4=ar("V4", [128, 4, 6, 128], BF16),
                V16=ar("V16", [128, 16, 2, 128], BF16), b=fw.buf(f"ub{u}")))
        off_end = arena_state["off"]
        arena_state["off"] = offs[1]
        qbT = ar("qbT", [128, 4, NT], BF16)
        kbT = ar("kbT", [128, 2304], BF16)
        Vb = ar("Vb", [128, 18, 128], BF16)
        assert arena_state["off"] <= off_end
        arena_state["off"] = off_end
        b_mb = UB[1]["b"]
        acc = ar("acc", [128, 2, NT], F32)
        b_acc = fw.buf("acc")
        NPT = 4
        Pt = [ar("Pt", [128, 2, 2, 128], BF16) for _ in range(NPT)]
        b_Pt = [fw.buf(f"Pt{i}") for i in range(NPT)]
        Ptb = [ar("Ptb", [128, 512], BF16) for _ in range(NPT)]
        b_Ptb = [fw.buf(f"Ptb{i}") for i in range(NPT)]
        dn = [ar("dn", [128, 512], F32) for _ in range(2)]
        b_dn = [fw.buf(f"dn{i}") for i in range(2)]
        LAG = 2

        def load_unitA(c):
            U = UB[c % 2]
            b = U["b"]
            rd = [b_QKs[0][c], b_QKs[0][4 + c], b_Vs[0][c]]
            fw.dma(U["qT"], QKs[0, c], key=b, reads=rd, writes=[b])
            fw.dma(U["kT"][:, 0, :], QKs[0, 4 + c], key=b, reads=rd, writes=[b], partial=True)
            for n4 in range(4):
                fw.dma(U["Vn"][:, 4 * n4:4 * n4 + 4, :],
                       Vs[0, c, 512 * n4:512 * n4 + 512, :].rearrange("(n p) e -> p n e", p=128), key=b, reads=rd,
                       writes=[b], partial=True)
            for tb in range(4):
                fw.dma(U["V4"][:, :, tb, :], Vs[0, c, 512 * tb:512 * tb + 512, :].rearrange("(p r) e -> p r e", r=4),
                       key=b, reads=rd, writes=[b], partial=True)
            fw.dma(U["V16"][:, :, 0, :], Vs[0, c].rearrange("(p r) e -> p r e", r=16), key=b, reads=rd, writes=[b],
                   partial=True)
            if halo:
                rd = [b_QKs[1][4 + c], b_Vs[1][c]]
                fw.dma(U["kT"][:, 1, :], QKs[1, 4 + c], key=b, reads=rd, writes=[b], partial=True)
                fw.dma(U["Vn"][:, 16, :], Vs[1, c, 0:128, :], key=b, reads=rd, writes=[b], partial=True)
                fw.dma(U["Vn"][:, 17, :], Vs[1, c, 1920:2048, :], key=b, reads=rd, writes=[b], partial=True)
                fw.dma(U["V4"][:, :, 4, :], Vs[1, c, 0:512, :].rearrange("(p r) e -> p r e", r=4), key=b, reads=rd,
                       writes=[b], partial=True)
                fw.dma(U["V4"][:, :, 5, :], Vs[1, c, 1536:2048, :].rearrange("(p r) e -> p r e", r=4), key=b,
                       reads=rd, writes=[b], partial=True)
                fw.dma(U["V16"][:, :, 1, :], Vs[1, c].rearrange("(p r) e -> p r e", r=16), key=b, reads=rd,
                       writes=[b], partial=True)

        def load_unitB():
            b = b_mb
            rd = b_QKs[0][8:13] + [b_Vs[0][4]]
            fw.dma(qbT, QKs[0, 8:12].rearrange("c p t -> p c t"), key=b, reads=rd, writes=[b])
            fw.dma(kbT[:, 0:NT], QKs[0, 12], key=b, reads=rd, writes=[b], partial=True)
            for n4 in range(4):
                fw.dma(Vb[:, 4 * n4:4 * n4 + 4, :],
                       Vs[0, 4, 512 * n4:512 * n4 + 512, :].rearrange("(n p) e -> p n e", p=128), key=b, reads=rd,
                       writes=[b], partial=True)
            if halo:
                rd = [b_QKs[1][12], b_Vs[1][4]]
                fw.dma(kbT[:, 2048:2176], QKs[1, 12, :, 0:128], key=b, reads=rd, writes=[b], partial=True)
                fw.dma(kbT[:, 2176:2304], QKs[1, 12, :, 1920:2048], key=b, reads=rd, writes=[b], partial=True)
                fw.dma(Vb[:, 16, :], Vs[1, 4, 0:128, :], key=b, reads=rd, writes=[b], partial=True)
                fw.dma(Vb[:, 17, :], Vs[1, 4, 1920:2048, :], key=b, reads=rd, writes=[b], partial=True)

        def unitA(c):
            U = UB[c % 2]
            b = U["b"]
            qT, kT = U["qT"], U["kT"]
            fw.op(POOL, lambda: nc.gpsimd.memset(acc, 0.0), writes=[b_acc])
            blocks = []
            for (d, Td) in ((1, 16), (4, 4), (16, 1)):
                Ld = NT // d
                for r in range(d):
                    for bb in range(-1, Td):
                        if bb == -1:
                            l0, nq, qi0 = 0, 64, 64
                        elif bb == Td - 1:
                            l0, nq, qi0 = Ld - 64, 64, 0
                        else:
                            l0, nq, qi0 = 128 * bb + 64, 128, 0
                        qsl = slice(d * l0 + r, d * (l0 + nq - 1) + r + 1, d)
                        tl = []
                        for wt, tb in enumerate((bb, bb + 1)):
                            if 0 <= tb < Td:
                                kp = (0, 128)
                                slot, tidx = 0, tb
                                if d == 1:
                                    Vt = U["Vn"][:, tb, :]
                                elif d == 4:
                                    Vt = U["V4"][:, r, tb, :]
                                else:
                                    Vt = U["V16"][:, r, 0, :]
                                dl = ones64
                            elif not halo:
                                continue
                            elif tb == -1:
                                kp = (64, 128)
                                slot, tidx = 1, Td - 1
                                Vt = (U["Vn"][:, 17, :] if d == 1 else U["V4"][:, r, 5, :] if d == 4
                                      else U["V16"][:, r, 1, :])
                                dl = hvb[:, pi, 1, :]
                            else:
                                kp = (0, 64)
                                slot, tidx = 1, 0
                                Vt = (U["Vn"][:, 16, :] if d == 1 else U["V4"][:, r, 4, :] if d == 4
                                      else U["V16"][:, r, 1, :])
                                dl = hvb[:, pi, 0, :]
                            k0 = d * (128 * tidx + kp[0]) + r
                            ksl = slice(k0, k0 + d * (kp[1] - kp[0] - 1) + 1, d)
                            tl.append((wt, kp, slot, ksl, Vt, dl))
                        blocks.append((nq, qi0, qsl, tl))

            def views(bi):
                sbk = 2 * (bi % 3)
                obk = 6 + bi % 2
                S4 = bank(sbk, 2).rearrange("p (h w q) -> p h w q", h=2, w=4)
                O2 = bank(obk)[:, 0:256].rearrange("p (a q) -> p a q", a=2)
                return sbk, obk, S4, O2, bi % NPT

            def front(bi):
                nq, qi0, qsl, tl = blocks[bi]
                sbk, obk, S4, O2, pti = views(bi)
                P4 = Pt[pti]
                nS = len(tl) * 2
                si = 0
                for (wt, kp, slot, ksl, Vt, dl) in tl:
                    for hh in range(2):
                        si += 1
                        fw.op(PE, lambda wt=wt, kp=kp, slot=slot, ksl=ksl, hh=hh: nc.tensor.matmul(
                            S4[kp[0]:kp[1], hh, wt, 0:nq], lhsT=kT[64 * hh:64 * hh + 64, slot, ksl],
                            rhs=qT[64 * hh:64 * hh + 64, qsl], start=True, stop=True),
                            reads=[b], writes=[BK[sbk], BK[sbk + 1]], signal=(si == nS), partial=(si > 1))
                ME, me = (DVE, nc.vector)
                if len(tl) == 2 and all(t[1] == (0, 128) for t in tl):
                    fw.op(ACT, lambda: nc.scalar.activation(
                        out=P4[:, :, :, 0:nq], in_=S4[:, :, 0:2, 0:nq], func=AF.Exp, scale=0.125),
                        reads=[BK[sbk], BK[sbk + 1]], writes=[b_Pt[pti]])
                    fw.op(ME, lambda: me.tensor_tensor(
                        out=P4[:, :, :, 0:nq], in0=P4[:, :, :, 0:nq],
                        in1=mask[:, :, qi0:qi0 + nq].unsqueeze(1).to_broadcast([128, 2, 2, nq]),
                        op=ALU.mult), reads=[b_Pt[pti], b_const], writes=[b_Pt[pti]], partial=True)
                else:
                    for ti_, (wt, kp, slot, ksl, Vt, dl) in enumerate(tl):
                        kpn = kp[1] - kp[0]
                        fw.op(ACT, lambda wt=wt, kp=kp: nc.scalar.activation(
                            out=P4[kp[0]:kp[1], :, wt, 0:nq], in_=S4[kp[0]:kp[1], :, wt, 0:nq], func=AF.Exp,
                            scale=0.125), reads=[BK[sbk], BK[sbk + 1]], writes=[b_Pt[pti]], partial=(ti_ > 0))
                        fw.op(ME, lambda wt=wt, kp=kp, kpn=kpn: me.tensor_tensor(
                            out=P4[kp[0]:kp[1], :, wt, 0:nq], in0=P4[kp[0]:kp[1], :, wt, 0:nq],
                            in1=mask[kp[0]:kp[1], wt, qi0:qi0 + nq].unsqueeze(1).to_broadcast([kpn, 2, nq]),
                            op=ALU.mult), reads=[b_Pt[pti], b_const], writes=[b_Pt[pti]], partial=True)

            def back(bi):
                nq, qi0, qsl, tl = blocks[bi]
                sbk, obk, S4, O2, pti = views(bi)
                P4 = Pt[pti]
                nP = len(tl) * 4
                pi_ = 0
                for ti_, (wt, kp, slot, ksl, Vt, dl) in enumerate(tl):
                    for hh in range(2):
                        pi_ += 1
                        fw.op(PE, lambda wt=wt, kp=kp, Vt=Vt, hh=hh, ti_=ti_: nc.tensor.matmul(
                            O2[64 * hh:64 * hh + 64, 0, 0:nq], lhsT=Vt[kp[0]:kp[1], 64 * hh:64 * hh + 64],
                            rhs=P4[kp[0]:kp[1], hh, wt, 0:nq], start=(ti_ == 0), stop=False,
                            skip_group_check=True),
                            reads=[b, b_Pt[pti]], writes=[BK[obk]], signal=False, partial=(pi_ > 1))
                        pi_ += 1
                        fw.op(PE, lambda wt=wt, kp=kp, dl=dl, hh=hh, ti_=ti_: nc.tensor.matmul(
                            O2[64 * hh:64 * hh + 64, 1, 0:nq], lhsT=dl[kp[0]:kp[1], :],
                            rhs=P4[kp[0]:kp[1], hh, wt, 0:nq], start=False, stop=(ti_ == len(tl) - 1),
                            skip_group_check=True),
                            reads=[b_const, b_Pt[pti]], writes=[BK[obk]], signal=(pi_ == nP), partial=True)
                fw.op(DVE, lambda: nc.vector.tensor_tensor(out=acc[:, :, qsl], in0=O2[:, :, 0:nq],
                                                           in1=acc[:, :, qsl], op=ALU.add),
                      reads=[BK[obk], b_acc], writes=[b_acc], partial=True)

            nb = len(blocks)
            for i in range(nb + LAG):
                if i < nb:
                    front(i)
                if i - LAG >= 0:
                    back(i - LAG)
            fw.op(DVE, lambda: nc.vector.reciprocal(out=acc[:, 1, :], in_=acc[:, 1, :]), reads=[b_acc],
                  writes=[b_acc], partial=True)
            fw.op(POOL, lambda: nc.gpsimd.tensor_tensor(out=AT[:, c, :], in0=acc[:, 0, :], in1=acc[:, 1, :],
                                                        op=ALU.mult), reads=[b_acc], writes=[b_AT[c]])

        def unitB():
            b = b_mb
            items = []
            for bb in range(16):
                for g in range(2):
                    tl = []
                    for wt, tb in ((0, bb - 1), (None, bb), (1, bb + 1)):
                        if 0 <= tb < 16:
                            tl.append((wt, slice(128 * tb, 128 * tb + 128), Vb[:, tb, :], ones64))
                        elif not halo:
                            continue
                        elif tb == -1:
                            tl.append((wt, slice(2176, 2304), Vb[:, 17, :], hvb[:, pi, 1, :]))
                        else:
                            tl.append((wt, slice(2048, 2176), Vb[:, 16, :], hvb[:, pi, 0, :]))
                    for ti_, t in enumerate(tl):
                        items.append((bb, g, ti_, len(tl), t))

            def vw(i, bb):
                sbk = i % 3
                nbk = 3 + 2 * (bb % 2)
                return sbk, bank(sbk), nbk, bank(nbk), bank(nbk + 1), i % NPT

            def front(i):
                bb, g, ti_, ntl, (wt, ksl, Vt, dl) = items[i]
                sbk, Sb, nbk, NB, DB, pbi = vw(i, bb)
                fw.op(PE, lambda: nc.tensor.matmul(
                    Sb, lhsT=kbT[64 * g:64 * g + 64, ksl], rhs=qbT[64 * g:64 * g + 64, :, 128 * bb:128 * bb + 128],
                    start=True, stop=True), reads=[b], writes=[BK[sbk]])
                fw.op(ACT, lambda: nc.scalar.activation(out=Ptb[pbi], in_=Sb, func=AF.Exp, scale=0.125),
                      reads=[BK[sbk]], writes=[b_Ptb[pbi]])
                if wt is not None:
                    ME, me = (DVE, nc.vector)
                    fw.op(ME, lambda: me.tensor_tensor(
                        out=Ptb[pbi].rearrange("p (c q) -> p c q", c=4),
                        in0=Ptb[pbi].rearrange("p (c q) -> p c q", c=4),
                        in1=mask[:, wt, :].unsqueeze(1).to_broadcast([128, 4, 128]), op=ALU.mult),
                        reads=[b_Ptb[pbi], b_const], writes=[b_Ptb[pbi]])

            def back(i):
                bb, g, ti_, ntl, (wt, ksl, Vt, dl) = items[i]
                sbk, Sb, nbk, NB, DB, pbi = vw(i, bb)
                last = (ti_ == ntl - 1)
                first_of_block = (g == 0 and ti_ == 0)
                fw.op(PE, lambda: nc.tensor.matmul(
                    NB[64 * g:64 * g + 64, :], lhsT=Vt[:, 64 * g:64 * g + 64], rhs=Ptb[pbi], start=(ti_ == 0),
                    stop=last), reads=[b, b_Ptb[pbi]], writes=[BK[nbk]], signal=False,
                    partial=(not first_of_block))
                fw.op(PE, lambda: nc.tensor.matmul(
                    DB[64 * g:64 * g + 64, :], lhsT=dl, rhs=Ptb[pbi], start=(ti_ == 0), stop=last),
                    reads=[b_const, b_Ptb[pbi]], writes=[BK[nbk + 1]], signal=True, partial=(not first_of_block))
                if g == 1 and last:
                    di = bb % 2
                    fw.op(DVE, lambda: nc.vector.tensor_tensor(
                        out=dn[di].rearrange("p (c q) -> p c q", c=4), in0=DB.rearrange("p (c q) -> p c q", c=4),
                        in1=es.unsqueeze(2).to_broadcast([128, 4, 128]), op=ALU.add), reads=[BK[nbk + 1], b_const],
                        writes=[b_dn[di]])
                    fw.op(DVE, lambda: nc.vector.reciprocal(out=dn[di], in_=dn[di]), reads=[b_dn[di]],
                          writes=[b_dn[di]])
                    fw.op(DVE, lambda: nc.vector.tensor_tensor(
                        out=AT[:, 4:8, 128 * bb:128 * bb + 128], in0=NB.rearrange("p (c q) -> p c q", c=4),
                        in1=dn[di].rearrange("p (c q) -> p c q", c=4), op=ALU.mult), reads=[BK[nbk], b_dn[di]],
                        writes=b_AT[4:8], partial=True)

            ni = len(items)
            for i in range(ni + LAG):
                if i < ni:
                    front(i)
                if i - LAG >= 0:
                    back(i - LAG)

        load_unitA(0)
        load_unitB()
        unitB()
        for c in range(4):
            if c + 1 < 4:
                load_unitA(c + 1)
            unitA(c)
        fw.barrier()

    def phaseC(pc):
        arena_reset()
        x1 = ar("x1", [128, 4, D], F32)
        b_x1 = [fw.buf(f"x1_{t}") for t in range(4)]
        XR = [ar("xr", [128, D], F32) for _ in range(2)]
        b_XR = [fw.buf(f"xr{i}") for i in range(2)]
        junk = ar("junk", [128, D], BF16)
        b_junk = fw.buf("junkc")
        ssq = ar("ssq", [128, 4], F32)
        b_ssq = fw.buf("ssqc")
        xn2 = ar("xn2", [128, D], BF16)
        b_xn2 = fw.buf("xn2")
        h2T = ar("h2T", [128, 8, 512], BF16)
        b_h2T = fw.buf("h2T")
        actT = ar("actT", [128, NJ, 512], BF16)
        b_actT = [fw.buf(f"actT{j}") for j in range(NJ)]
        sg = [ar("sg", [128, 512], F32) for _ in range(2)]
        b_sg = [fw.buf(f"sg{i}") for i in range(2)]
        WG = [ar("WG", [128, 8, 128], BF16) for _ in range(3)]
        WU = [ar("WU", [128, 8, 128], BF16) for _ in range(3)]
        WD = [ar("WD", [128, D], BF16) for _ in range(3)]
        b_WG = [fw.buf(f"WG{i}") for i in range(3)]
        b_WU = [fw.buf(f"WU{i}") for i in range(3)]
        b_WD = [fw.buf(f"WD{i}") for i in range(3)]
        OS = [ar("OS", [128, D], F32) for _ in range(2)]
        b_OS = [fw.buf(f"OS{i}") for i in range(2)]
        pT = bank(4).bitcast(BF16)

        def ld_gu(j):
            i = j % 3
            fw.dma(WG[i], wgs[j], key=b_WG[i], reads=[b_wsc], writes=[b_WG[i]])
            fw.dma(WU[i], wus[j], key=b_WU[i], reads=[b_wsc], writes=[b_WU[i]])

        def ld_d(j):
            i = j % 3
            fw.dma(WD[i], wds[j], key=b_WD[i], reads=[b_wsc], writes=[b_WD[i]])

        xcnt = [0]
        for sp_ in range(4):
            ld_gu(0)
            ld_gu(1)
            for t in range(4):
                r0 = 512 * sp_ + 128 * t
                tcs = slice(r0, r0 + 128)
                xi = xcnt[0] % 2
                xcnt[0] += 1
                fw.dma(XR[xi], pc["xsrc"](r0 // 128), key=b_XR[xi], writes=[b_XR[xi]])
                yb = 2 * (t % 2)
                for half in range(2):
                    for kk in range(8):
                        fw.op(PE, lambda kk=kk, half=half: nc.tensor.matmul(
                            bank(yb + half), lhsT=AT[:, kk, tcs], rhs=woutb[:, kk, half * 512:(half + 1) * 512],
                            start=(kk == 0), stop=(kk == 7)), reads=[b_AT[kk], b_woutb], writes=[BK[yb + half]],
                            signal=(kk == 7), partial=(kk > 0))
                fw.op(DVE, lambda: nc.vector.tensor_tensor(out=x1[:, t, :], in0=bank(yb, 2), in1=XR[xi], op=ALU.add),
                      reads=[BK[yb], BK[yb + 1], b_XR[xi]], writes=[b_x1[t]])
                fw.op(ACT, lambda: nc.scalar.activation(out=junk, in_=x1[:, t, :], func=AF.Square,
                                                        accum_out=ssq[:, 0:1]), reads=[b_x1[t]],
                      writes=[b_junk, b_ssq])
                fw.op(ACT, lambda: nc.scalar.activation(out=ssq[:, 1:2], in_=ssq[:, 0:1], func=AF.Sqrt,
                                                        scale=1.0 / D, bias=eps_t), reads=[b_ssq, b_const],
                      writes=[b_ssq])
                fw.op(DVE, lambda: nc.vector.reciprocal(out=ssq[:, 2:3], in_=ssq[:, 1:2]), reads=[b_ssq],
                      writes=[b_ssq])
                fw.op(ACT, lambda: nc.scalar.activation(out=xn2, in_=x1[:, t, :], func=AF.Copy, scale=ssq[:, 2:3]),
                      reads=[b_x1[t], b_ssq], writes=[b_xn2])
                for kk in range(8):
                    fw.op(PE, lambda kk=kk: nc.tensor.transpose(out=pT[:, kk * 128:(kk + 1) * 128],
                                                                in_=xn2[:, kk * 128:(kk + 1) * 128],
                                                                identity=ident),
                          reads=[b_xn2, b_const], writes=[BK[4]], signal=(kk == 7), partial=(kk > 0))
                fw.op(ACT, lambda: nc.scalar.copy(out=h2T[:, :, 128 * t:128 * t + 128],
                                                  in_=pT.rearrange("p (k q) -> p k q", k=8)),
                      reads=[BK[4]], writes=[b_h2T], partial=(t > 0))
            for j in range(NJ):
                if j + 2 < NJ:
                    ld_gu(j + 2)
                if j == NJ - 2:
                    ld_d(0)
                if j == NJ - 1:
                    ld_d(1)
                i = j % 3
                gb, ub = 4 + j % 2, 6 + j % 2
                for kk in range(8):
                    fw.op(PE, lambda kk=kk: nc.tensor.matmul(bank(gb), lhsT=WG[i][:, kk, :], rhs=h2T[:, kk, :],
                                                             start=(kk == 0), stop=(kk == 7)),
                          reads=[b_WG[i], b_h2T], writes=[BK[gb]], signal=(kk == 7), partial=(kk > 0))
                for kk in range(8):
                    fw.op(PE, lambda kk=kk: nc.tensor.matmul(bank(ub), lhsT=WU[i][:, kk, :], rhs=h2T[:, kk, :],
                                                             start=(kk == 0), stop=(kk == 7)),
                          reads=[b_WU[i], b_h2T], writes=[BK[ub]], signal=(kk == 7), partial=(kk > 0))
                si = j % 2
                fw.op(ACT, lambda: nc.scalar.activation(out=sg[si], in_=bank(gb), func=AF.Silu), reads=[BK[gb]],
                      writes=[b_sg[si]])
                fw.op(DVE, lambda: nc.vector.tensor_tensor(out=actT[:, j, :], in0=bank(ub), in1=sg[si],
                                                           op=ALU.mult), reads=[BK[ub], b_sg[si]],
                      writes=[b_actT[j]])
            for j in range(NJ):
                if j + 2 < NJ:
                    ld_d(j + 2)
                i = j % 3
                for t in range(4):
                    for half in range(2):
                        fw.op(PE, lambda t=t, half=half: nc.tensor.matmul(
                            bank(2 * t + half), lhsT=actT[:, j, 128 * t:128 * t + 128],
                            rhs=WD[i][:, half * 512:(half + 1) * 512], start=(j == 0), stop=(j == NJ - 1)),
                            reads=[b_actT[j], b_WD[i]], writes=[BK[2 * t + half]],
                            signal=(j == NJ - 1 or (t == 3 and half == 1)), partial=(j > 0))
            for t in range(4):
                r0 = 512 * sp_ + 128 * t
                oi = t % 2
                fw.op(DVE, lambda: nc.vector.tensor_tensor(out=OS[oi], in0=bank(2 * t, 2), in1=x1[:, t, :],
                                                           op=ALU.add),
                      reads=[BK[2 * t], BK[2 * t + 1], b_x1[t]], writes=[b_OS[oi]])
                fw.dma(pc["out"](r0), OS[oi], key=b_OS[oi], reads=[b_OS[oi]])
        fw.barrier()

    for ip, pc in enumerate(pieces):
        if sel is not None and ip not in sel:
            continue
        if "A" in phases:
            phaseA(pc)
        if "B" in phases:
            phaseB(pc)
        if "C" in phases:
            phaseC(pc)
    fw.finish()
    return nc


_CACHE = {}


def _host_consts():
    bf = ml_dtypes.bfloat16
    kj = np.arange(128)[:, None]
    qi = np.arange(128)[None, :]
    masks = np.stack([(kj >= qi), (kj <= qi)], axis=1).astype(np.float32).astype(bf)
    ident = np.eye(128, dtype=np.float32).astype(bf)
    return masks, ident


def _rope_table(pos):
    half = 32
    inv = (np.float32(10000.0) ** (np.float32(-2.0) * np.arange(half, dtype=np.float32) / np.float32(64))).astype(
        np.float32)
    ang = (pos.astype(np.float32)[:, None] * inv[None, :]).astype(np.float32)
    return np.concatenate([np.cos(ang), np.sin(ang)], axis=1).astype(np.float32)


def kernel(x_prompt, x_sample, attn_norm, w_in, qnorm_a, knorm_a, qnorm_b, knorm_b,
           sink_b, w_out, ffn_norm, w_gate, w_up, w_down):
    f32 = np.float32
    x_prompt = np.asarray(x_prompt, f32)
    x_sample = np.asarray(x_sample, f32)
    w_in = np.asarray(w_in, f32)[0]
    w_out = np.asarray(w_out, f32)[0]
    w_gate = np.ascontiguousarray(np.asarray(w_gate, f32)[0])
    w_up = np.ascontiguousarray(np.asarray(w_up, f32)[0])
    w_down = np.ascontiguousarray(np.asarray(w_down, f32)[0])
    QBo, KBo, VBo, VAo = 1536, 2048, 2176, 1024
    cols = list(range(0, 512)) + list(range(512, 1024))
    for c in range(4):
        for g in range(2):
            h = 4 * g + c
            cols += list(range(QBo + h * 64, QBo + h * 64 + 64))
    cols += list(range(KBo, KBo + 128)) + list(range(VBo, VBo + 128)) + list(range(VAo, VAo + 512))
    w_in_p = np.ascontiguousarray(w_in[:, cols])
    rows = list(range(512))
    for c in range(4):
        for g in range(2):
            h = 4 * g + c
            rows += list(range(512 + h * 64, 512 + h * 64 + 64))
    w_out_p = np.ascontiguousarray(w_out[rows, :])
    gA = np.ascontiguousarray(np.asarray(attn_norm, f32)[0].reshape(8, 128).T)
    gF = np.ascontiguousarray(np.asarray(ffn_norm, f32)[0].reshape(8, 128).T)
    gq1 = np.stack([np.asarray(qnorm_a, f32)[0], np.asarray(knorm_a, f32)[0],
                    np.asarray(qnorm_b, f32)[0], np.asarray(knorm_b, f32)[0]])
    gq = np.ascontiguousarray(np.broadcast_to(gq1[None, :, :], (128, 4, 64)))
    sk = np.asarray(sink_b, f32)[0]
    sinkT = np.zeros((128, 4), f32)
    for g in range(2):
        for c in range(4):
            sinkT[64 * g:64 * g + 64, c] = sk[4 * g + c]
    masks, ident = _host_consts()

    if "nc" not in _CACHE:
        _CACHE["nc"] = build_program()
    nc = _CACHE["nc"]

    in_maps = []
    p128 = np.arange(128)
    for core in range(NCORES):
        ps, q = core // 4, core % 4
        lo = q * 4096 - 1024
        xp = np.zeros((6144, D), f32)
        a, b = max(lo, 0), min(lo + 6144, 16384)
        xp[a - lo:b - lo] = x_prompt[ps, a:b]
        cst = np.zeros((80, 128, 64), f32)
        for n in range(16):
            cst[n] = _rope_table(n * 128 + p128)
        for pi in range(2):
            base = q * 4096 + 2048 * pi
            for n in range(16):
                cst[16 + 32 * pi + n] = _rope_table(base + n * 128 + p128)
                hidx = n * 128 + p128
                local = np.where(hidx < 1024, 2048 + hidx, hidx - 2048)
                cst[32 + 32 * pi + n] = _rope_table(np.maximum(base + local, 0))
        hv = np.ones((128, 2, 2, 64), f32)
        if q == 0:
            hv[:, 0, 1, :] = 0.0
        if q == 3:
            hv[:, 1, 0, :] = 0.0
        in_maps.append({
            "xs": np.ascontiguousarray(x_sample[4 * core:4 * core + 4]),
            "xp": xp, "cst": cst, "w_in": w_in_p, "w_out": w_out_p, "w_gate": w_gate, "w_up": w_up,
            "w_down": w_down, "gA": gA, "gF": gF, "gq": gq, "sinkT": sinkT, "masks": masks, "ident": ident,
            "hv": hv.astype(ml_dtypes.bfloat16),
        })
    res = run_bass_kernel_spmd(nc, in_maps, core_ids=list(range(NCORES)))
    y_prompt = np.zeros((2, 16384, D), f32)
    y_sample = np.zeros((32, NT, D), f32)
    for core in range(NCORES):
        r = res.results[core]
        ps, q = core // 4, core % 4
        y_prompt[ps, q * 4096:(q + 1) * 4096] = np.asarray(r["yp"], f32)
        y_sample[4 * core:4 * core + 4] = np.asarray(r["ys"], f32)
    return (y_prompt, y_sample)
```

```python
import numpy as np
import ml_dtypes
import concourse.bass as bass
import concourse.mybir as mybir
from concourse.bass_utils import run_bass_kernel_spmd

F32 = mybir.dt.float32
BF16 = mybir.dt.bfloat16
ALU = mybir.AluOpType
AF = mybir.ActivationFunctionType
AX = mybir.AxisListType

D = 1024
NT = 2048
DFF = 2816
NJ = 22
EPS = 1e-6
NCORES = 8


class Src:
    def __init__(self, nc, name, step):
        self.sem = nc.alloc_semaphore(name=name)
        self.count = 0
        self.step = step
        self.name = name


class Buf:
    __slots__ = ("name", "writes", "reads", "dsrc")

    def __init__(self, name):
        self.name = name
        self.writes = []
        self.reads = []
        self.dsrc = None


class Eng:
    def __init__(self, fw, name, eng, is_pe=False):
        self.name = name
        self.eng = eng
        self.src = Src(fw.nc, "e_" + name, 1)
        self.known = {}
        self.is_pe = is_pe


class FW:
    def __init__(self, nc):
        self.nc = nc
        self.pe = Eng(self, "pe", nc.tensor, is_pe=True)
        self.act = Eng(self, "act", nc.scalar)
        self.dve = Eng(self, "dve", nc.vector)
        self.pool = Eng(self, "pool", nc.gpsimd)
        self.sp = Eng(self, "sp", nc.sync)
        self.engs = [self.pe, self.act, self.dve, self.pool, self.sp]
        self.nbuf = 0
        self.dsrcs = []
        self.bufs = {}

    def buf(self, name=None):
        self.nbuf += 1
        if name is None:
            return Buf(f"b{self.nbuf}")
        if name not in self.bufs:
            self.bufs[name] = Buf(name)
        return self.bufs[name]

    def _collect(self, E, reads, writes):
        need = {}

        def add(tok):
            s, c = tok
            if E.is_pe and s is E.src:
                return
            if s.step == 16:
                c = s.count
            if need.get(s, 0) < c:
                need[s] = c
        for b in reads:
            for t in b.writes:
                add(t)
        for b in writes:
            for t in b.writes:
                add(t)
            for t in b.reads:
                add(t)
        for s, c in need.items():
            if E.known.get(s, 0) >= c:
                continue
            E.eng.wait_ge(s.sem, c)
            E.known[s] = c

    def _commit(self, tok, reads, writes, partial):
        for b in writes:
            if partial:
                b.writes.append(tok)
                if len(b.writes) > 24:
                    m = {}
                    for s, c in b.writes:
                        if m.get(s, 0) < c:
                            m[s] = c
                    b.writes = list(m.items())
            else:
                b.writes = [tok]
                b.reads = []
        for b in reads:
            b.reads.append(tok)
            if len(b.reads) > 24:
                m = {}
                for s, c in b.reads:
                    if m.get(s, 0) < c:
                        m[s] = c
                b.reads = list(m.items())

    def op(self, E, fn, reads=(), writes=(), signal=True, partial=False):
        self._collect(E, reads, writes)
        ins = fn()
        if signal:
            E.src.count += 1
            ins.then_inc(E.src.sem, 1)
            tok = (E.src, E.src.count)
        else:
            tok = (E.src, E.src.count + 1)
        self._commit(tok, reads, writes, partial)
        return ins

    def dma(self, out_ap, in_ap, key, reads=(), writes=(), partial=False):
        E = self.sp
        if key.dsrc is None:
            key.dsrc = Src(self.nc, "d_" + key.name, 16)
            self.dsrcs.append(key.dsrc)
        self._collect(E, reads, writes)
        ins = E.eng.dma_start(out=out_ap, in_=in_ap)
        key.dsrc.count += 16
        ins.then_inc(key.dsrc.sem, 16)
        tok = (key.dsrc, key.dsrc.count)
        self._commit(tok, reads, writes, partial)
        return ins

    def barrier(self):
        for E in self.engs:
            for F in self.engs:
                if F is E or F is self.sp:
                    continue
                c = F.src.count
                if c > 0 and E.known.get(F.src, 0) < c:
                    E.eng.wait_ge(F.src.sem, c)
                    E.known[F.src] = c
            for d in self.dsrcs:
                if d.count > 0 and E.known.get(d, 0) < d.count:
                    E.eng.wait_ge(d.sem, d.count)
                    E.known[d] = d.count

    def finish(self):
        E = self.sp
        for F in self.engs:
            if F is E:
                continue
            if F.src.count > 0:
                E.eng.wait_ge(F.src.sem, F.src.count)
        for d in self.dsrcs:
            if d.count > 0:
                E.eng.wait_ge(d.sem, d.count)


def build_program(sel=None, phases="ABC"):
    nc = bass.Bass("TRN2", target_bir_lowering=False)
    fw = FW(nc)
    PE, ACT, DVE, POOL = fw.pe, fw.act, fw.dve, fw.pool

    def din(name, shape, dt=F32):
        return nc.dram_tensor(name, list(shape), dt, kind="ExternalInput").ap()

    xs = din("xs", [4, NT, D])
    xp = din("xp", [6144, D])
    cst = din("cst", [80, 128, 64])
    w_in = din("w_in", [D, 2304])
    w_out = din("w_out", [D, D])
    w_gate = din("w_gate", [D, DFF])
    w_up = din("w_up", [D, DFF])
    w_down = din("w_down", [DFF, D])
    gA_d = din("gA", [128, 8])
    gF_d = din("gF", [128, 8])
    gq_d = din("gq", [128, 4, 64])
    sink_d = din("sinkT", [128, 4])
    mask_d = din("masks", [128, 2, 128], BF16)
    ident_d = din("ident", [128, 128], BF16)
    hv_d = din("hv", [128, 2, 2, 64], BF16)
    ys = nc.dram_tensor("ys", [4, NT, D], F32, kind="ExternalOutput").ap()
    yp = nc.dram_tensor("yp", [4096, D], F32, kind="ExternalOutput").ap()
    QKs = nc.dram_tensor("QKs", [2, 13, 128, NT], BF16, kind="Internal").ap()
    Vs = nc.dram_tensor("Vs", [2, 5, NT, 128], BF16, kind="Internal").ap()
    wgs = nc.dram_tensor("wgs", [NJ, 128, 8, 128], BF16, kind="Internal").ap()
    wus = nc.dram_tensor("wus", [NJ, 128, 8, 128], BF16, kind="Internal").ap()
    wds = nc.dram_tensor("wds", [NJ, 128, D], BF16, kind="Internal").ap()
    b_QKs = [[fw.buf(f"QKs{s}_{c}") for c in range(13)] for s in range(2)]
    b_Vs = [[fw.buf(f"Vs{s}_{c}") for c in range(5)] for s in range(2)]
    b_wsc = fw.buf("wscratch")

    def sb(name, shape, dt):
        return nc.alloc_sbuf_tensor("sb_" + name, list(shape), dt).ap()

    gA = sb("gA", [128, 8], F32)
    gF = sb("gF", [128, 8], F32)
    gq = sb("gq", [128, 4, 64], F32)
    mask = sb("mask", [128, 2, 128], BF16)
    ident = sb("ident", [128, 128], BF16)
    ones64 = sb("ones64", [128, 64], BF16)
    hvb = sb("hvb", [128, 2, 2, 64], BF16)
    es = sb("es", [128, 4], F32)
    eps_t = sb("eps", [128, 1], F32)
    winb = sb("winb", [128, 8, 2304], BF16)
    woutb = sb("woutb", [128, 8, D], BF16)
    AT = sb("AT", [128, 8, NT], BF16)
    b_const = fw.buf("const")
    b_winb = fw.buf("winb")
    b_woutb = fw.buf("woutb")
    b_AT = [fw.buf(f"AT{k}") for k in range(8)]

    PS = nc.alloc_psum_tensor("PS", [128, 4096], F32).ap()
    BK = [fw.buf(f"bank{i}") for i in range(8)]

    def bank(i, n=1):
        return PS[:, i * 512:(i + n) * 512]

    arena0 = (nc.sbuf_base + 63) // 64 * 64
    arena_state = {"off": 0}
    ARENA_BYTES = nc.sbuf_top - arena0

    def arena_reset():
        arena_state["off"] = 0

    uid = [0]

    def ar(name, shape, dt):
        nbytes = int(np.prod(shape[1:])) * (2 if dt == BF16 else 4)
        nbytes = (nbytes + 63) // 64 * 64
        off = arena_state["off"]
        assert off + nbytes <= ARENA_BYTES, (name, off, nbytes, ARENA_BYTES)
        arena_state["off"] = off + nbytes
        uid[0] += 1
        return nc.alloc_sbuf_tensor_at(f"{name}_{uid[0]}", list(shape), dt, offset=arena0 + off).ap()

    for dst, src in ((gA, gA_d), (gF, gF_d), (gq, gq_d), (mask, mask_d), (ident, ident_d),
                     (hvb, hv_d), (es, sink_d)):
        fw.dma(dst, src, key=b_const, writes=[b_const], partial=True)
    fw.op(POOL, lambda: nc.gpsimd.memset(ones64, 1.0), writes=[b_const], partial=True)
    fw.op(POOL, lambda: nc.gpsimd.memset(eps_t, EPS), writes=[b_const], partial=True)
    fw.op(ACT, lambda: nc.scalar.activation(out=es, in_=es, func=AF.Exp), reads=[b_const], writes=[b_const],
          partial=True)

    arena_reset()
    NB0 = 6
    stg = [ar("stg", [128, 2304], F32) for _ in range(NB0)]
    b_stg = [fw.buf(f"stg{i}") for i in range(NB0)]
    stb = [ar("stb", [128, 1024], BF16) for _ in range(NB0)]
    b_stb = [fw.buf(f"stb{i}") for i in range(NB0)]
    jobs = []
    for k in range(8):
        jobs.append(("win", k))
    for k in range(8):
        jobs.append(("wout", k))
    for j in range(NJ):
        jobs.append(("g", j))
        jobs.append(("u", j))
        jobs.append(("d", j))

    def p0_load(n):
        kind, a_ = jobs[n]
        i = n % NB0
        if kind == "win":
            fw.dma(stg[i], w_in[a_ * 128:(a_ + 1) * 128, :], key=b_stg[i], writes=[b_stg[i]])
        elif kind == "wout":
            fw.dma(stg[i][:, 0:D], w_out[a_ * 128:(a_ + 1) * 128, :], key=b_stg[i], writes=[b_stg[i]])
        elif kind in ("g", "u"):
            wsrc = w_gate if kind == "g" else w_up
            fw.dma(stg[i][:, 0:1024].rearrange("p (k c) -> p k c", k=8),
                   wsrc[:, a_ * 128:(a_ + 1) * 128].rearrange("(k p) c -> p k c", p=128),
                   key=b_stg[i], writes=[b_stg[i]])
        else:
            fw.dma(stg[i][:, 0:1024], w_down[a_ * 128:(a_ + 1) * 128, :], key=b_stg[i], writes=[b_stg[i]])

    def p0_work(n):
        kind, a_ = jobs[n]
        i = n % NB0
        e3 = n % 3
        if kind == "win":
            if e3 == 0:
                fw.op(ACT, lambda: nc.scalar.activation(out=winb[:, a_, :], in_=stg[i], func=AF.Copy,
                                                        scale=gA[:, a_:a_ + 1]),
                      reads=[b_stg[i], b_const], writes=[b_winb], partial=True)
            else:
                fw.op(DVE, lambda: nc.vector.tensor_scalar(out=winb[:, a_, :], in0=stg[i],
                                                           scalar1=gA[:, a_:a_ + 1], scalar2=None, op0=ALU.mult),
                      reads=[b_stg[i], b_const], writes=[b_winb], partial=True)
        elif kind == "wout":
            if e3 == 0:
                fw.op(ACT, lambda: nc.scalar.copy(out=woutb[:, a_, :], in_=stg[i][:, 0:D]), reads=[b_stg[i]],
                      writes=[b_woutb], partial=True)
            else:
                fw.op(DVE, lambda: nc.vector.tensor_copy(out=woutb[:, a_, :], in_=stg[i][:, 0:D]),
                      reads=[b_stg[i]], writes=[b_woutb], partial=True)
        elif kind in ("g", "u"):
            wdst = wgs if kind == "g" else wus
            fw.op(DVE, lambda: nc.vector.tensor_tensor(
                out=stb[i].rearrange("p (k c) -> p k c", k=8),
                in0=stg[i][:, 0:1024].rearrange("p (k c) -> p k c", k=8),
                in1=gF.unsqueeze(2).to_broadcast([128, 8, 128]), op=ALU.mult),
                reads=[b_stg[i], b_const], writes=[b_stb[i]])
            fw.dma(wdst[a_].rearrange("p k c -> p (k c)"), stb[i], key=b_stb[i], reads=[b_stb[i]],
                   writes=[b_wsc], partial=True)
        else:
            fw.op(ACT, lambda: nc.scalar.copy(out=stb[i], in_=stg[i][:, 0:1024]), reads=[b_stg[i]],
                  writes=[b_stb[i]])
            fw.dma(wds[a_], stb[i], key=b_stb[i], reads=[b_stb[i]], writes=[b_wsc], partial=True)

    for n in range(min(NB0 - 1, len(jobs))):
        p0_load(n)
    for n in range(len(jobs)):
        if n + NB0 - 1 < len(jobs):
            p0_load(n + NB0 - 1)
        p0_work(n)
    fw.barrier()

    pieces = []
    for i in range(4):
        pieces.append(dict(halo=False, xsrc=lambda n, i=i: xs[i, n * 128:(n + 1) * 128, :],
                           hsrc=None, cs0=0, csh=None, out=lambda r0, i=i: ys[i, r0:r0 + 128, :], pi=0))
    for pi in range(2):
        mb = 1024 + 2048 * pi

        def hrow(n, mb=mb):
            hidx = n * 128
            return (mb + 2048 + hidx) if hidx < 1024 else (mb + hidx - 2048)
        pieces.append(dict(halo=True, xsrc=lambda n, mb=mb: xp[mb + n * 128: mb + (n + 1) * 128, :],
                           hsrc=lambda n, hrow=hrow: xp[hrow(n):hrow(n) + 128, :],
                           cs0=16 + 32 * pi, csh=32 + 32 * pi,
                           out=lambda r0, pi=pi: yp[2048 * pi + r0: 2048 * pi + r0 + 128, :], pi=pi))

    def phaseA(pc):
        arena_reset()
        NX = 4
        XT = [ar("xt", [128, D], F32) for _ in range(NX)]
        CS = [ar("cs", [128, 64], F32) for _ in range(NX)]
        b_XT = [fw.buf(f"xt{i}") for i in range(NX)]
        b_CS = [fw.buf(f"cs{i}") for i in range(NX)]
        junk = ar("junk", [128, D], BF16)
        b_junk = fw.buf("junk")
        ssq = [ar("ssq", [128, 4], F32) for _ in range(2)]
        b_ssq = [fw.buf(f"ssq{i}") for i in range(2)]
        xn = [ar("xn", [128, D], BF16) for _ in range(2)]
        b_xn = [fw.buf(f"xn{i}") for i in range(2)]
        hT = [ar("hT", [128, 8, 128], BF16) for _ in range(2)]
        b_hT = [fw.buf(f"hT{i}") for i in range(2)]
        vt = [ar("vt", [128, 5, 128], BF16) for _ in range(2)]
        b_vt = [fw.buf(f"vt{i}") for i in range(2)]
        qkt = [ar("qkt", [128, 13, 256], BF16) for _ in range(2)]
        b_qkt = [fw.buf(f"qkt{i}") for i in range(2)]
        SETS = []
        for i in range(2):
            SETS.append(dict(sq=ar("sq", [128, 1664], F32), qn=ar("qn", [128, 1664], F32),
                             rb=ar("rb", [128, 1664], F32), qr=ar("qr", [128, 1664], BF16),
                             s26=ar("s26", [128, 3, 26], F32), gc=ar("gc", [128, 4, 64], F32),
                             gs=ar("gs", [128, 2, 4, 32], F32), b_tab=fw.buf(f"tab{i}"),
                             b_sq=[fw.buf(f"sq{i}_{h}") for h in range(2)],
                             b_qn=[fw.buf(f"qn{i}_{h}") for h in range(2)],
                             b_rb=[fw.buf(f"rb{i}_{h}") for h in range(2)],
                             b_qr=fw.buf(f"qr{i}"), b_s26=[fw.buf(f"s26{i}_{h}") for h in range(2)]))
        P1 = bank(0, 4)
        P2 = bank(4)
        pT = bank(5).bitcast(BF16)
        qkT = bank(6, 2).bitcast(BF16)

        tiles = []
        for s in ([0, 1] if pc["halo"] else [0]):
            for n in range(16):
                tiles.append((s, n))
        T = len(tiles)

        def load(k):
            s, n = tiles[k]
            i = k % NX
            src = pc["xsrc"](n) if s == 0 else pc["hsrc"](n)
            fw.dma(XT[i], src, key=b_XT[i], writes=[b_XT[i]])
            ci = (pc["cs0"] if s == 0 else pc["csh"]) + n
            fw.dma(CS[i], cst[ci], key=b_CS[i], writes=[b_CS[i]])

        def info(k):
            s, n = tiles[k]
            edge = (s == 1 and n in (0, 15))
            if s == 0:
                ranges = [(0, 26)]
                chunks = list(range(13))
            else:
                ranges = [(8, 16)] + ([(24, 26)] if edge else [])
                chunks = [4, 5, 6, 7] + ([12] if edge else [])
            return s, n, edge, ranges, chunks

        def bks_of(c0, c1):
            return BK[c0 // 512:(c1 - 1) // 512 + 1]

        def xnorm(k):
            i = k % NX
            xt = XT[i]
            p = k % 2
            sq_, xn_ = ssq[p], xn[p]
            fw.op(ACT, lambda: nc.scalar.activation(out=junk, in_=xt, func=AF.Square, accum_out=sq_[:, 0:1]),
                  reads=[b_XT[i]], writes=[b_junk, b_ssq[p]])
            fw.op(ACT, lambda: nc.scalar.activation(out=sq_[:, 1:2], in_=sq_[:, 0:1], func=AF.Sqrt, scale=1.0 / D,
                                                    bias=eps_t), reads=[b_ssq[p], b_const], writes=[b_ssq[p]])
            fw.op(DVE, lambda: nc.vector.reciprocal(out=sq_[:, 2:3], in_=sq_[:, 1:2]), reads=[b_ssq[p]],
                  writes=[b_ssq[p]])
            fw.op(ACT, lambda: nc.scalar.activation(out=xn_, in_=xt, func=AF.Copy, scale=sq_[:, 2:3]),
                  reads=[b_XT[i], b_ssq[p]], writes=[b_xn[p]])

        def xT(k):
            p = k % 2
            xn_ = xn[p]
            for kk in range(8):
                fw.op(PE, lambda kk=kk: nc.tensor.transpose(out=pT[:, kk * 128:(kk + 1) * 128],
                                                            in_=xn_[:, kk * 128:(kk + 1) * 128], identity=ident),
                      reads=[b_xn[p], b_const], writes=[BK[5]], signal=(kk == 7), partial=(kk > 0))
            fw.op(ACT, lambda: nc.scalar.copy(out=hT[p].rearrange("p k t -> p (k t)"), in_=pT), reads=[BK[5]],
                  writes=[b_hT[p]])

        def proj(k, which):
            s, n, edge, ranges, chunks = info(k)
            p = k % 2
            if s == 0:
                allg = {"h0": [(0, 512, P1[:, 0:512], [BK[0]]), (512, 512, P1[:, 512:1024], [BK[1]])],
                        "h1": [(1024, 512, P1[:, 1024:1536], [BK[2]]), (1536, 256, P1[:, 1536:1792], [BK[3]])],
                        "va": [(1792, 512, P2, [BK[4]])]}
            else:
                allg = {"h0": [(512, 512, P1[:, 512:1024], [BK[1]])],
                        "h1": ([(1536, 256, P1[:, 1536:1792], [BK[3]])] if edge else []),
                        "va": [(1792, 512, P2, [BK[4]])]}
            for (c0, ncol, outap, bks) in allg[which]:
                for kk in range(8):
                    fw.op(PE, lambda kk=kk, c0=c0, ncol=ncol, outap=outap: nc.tensor.matmul(
                        outap, lhsT=hT[p][:, kk, :], rhs=winb[:, kk, c0:c0 + ncol], start=(kk == 0),
                        stop=(kk == 7)), reads=[b_hT[p], b_winb], writes=bks, signal=(kk == 7), partial=(kk > 0))

        def vevac(k):
            s, n, edge, ranges, chunks = info(k)
            p = k % 2
            fw.op(ACT, lambda: nc.scalar.copy(out=vt[p][:, 0:4, :].rearrange("p c e -> p (c e)"), in_=P2),
                  reads=[BK[4]], writes=[b_vt[p]])
            npair = 4
            if s == 0 or edge:
                npair = 5
                fw.op(ACT, lambda: nc.scalar.copy(out=vt[p][:, 4, :], in_=P1[:, 1664:1792]), reads=[BK[3]],
                      writes=[b_vt[p]], partial=True)
            fw.dma(Vs[s, 0:npair, n * 128:(n + 1) * 128, :].rearrange("c p e -> p c e"), vt[p][:, 0:npair, :],
                   key=b_vt[p], reads=[b_vt[p]], writes=b_Vs[s][0:npair], partial=True)

        def half_ranges(k, which):
            s, n, edge, ranges, chunks = info(k)
            if s == 0:
                return [(0, 16)] if which == "h0" else [(16, 26)]
            if which == "h0":
                return [(8, 16)]
            return [(24, 26)] if edge else []

        def s2a(k, which):
            S_ = SETS[k % 2]
            sq, qn, s26 = S_["sq"], S_["qn"], S_["s26"]
            hx = 0 if which == "h0" else 1
            b_sq, b_qn, b_s26 = S_["b_sq"][hx], S_["b_qn"][hx], S_["b_s26"][hx]
            for (h0, h1) in half_ranges(k, which):
                H = h1 - h0
                c0, c1 = h0 * 64, h1 * 64
                bks = bks_of(c0, c1)
                fw.op(ACT, lambda: nc.scalar.activation(out=sq[:, c0:c1], in_=P1[:, c0:c1], func=AF.Square),
                      reads=bks, writes=[b_sq])
                fw.op(DVE, lambda: nc.vector.tensor_reduce(out=s26[:, 0, h0:h1],
                                                           in_=sq[:, c0:c1].rearrange("p (h d) -> p h d", d=64),
                                                           axis=AX.X, op=ALU.add), reads=[b_sq],
                      writes=[b_s26])
                fw.op(ACT, lambda: nc.scalar.activation(out=s26[:, 1, h0:h1], in_=s26[:, 0, h0:h1], func=AF.Sqrt,
                                                        scale=1.0 / 64, bias=eps_t), reads=[b_s26, b_const],
                      writes=[b_s26])
                fw.op(DVE, lambda: nc.vector.reciprocal(out=s26[:, 2, h0:h1], in_=s26[:, 1, h0:h1]),
                      reads=[b_s26], writes=[b_s26])
                fw.op(DVE, lambda: nc.vector.tensor_tensor(
                    out=qn[:, c0:c1].rearrange("p (h d) -> p h d", d=64),
                    in0=P1[:, c0:c1].rearrange("p (h d) -> p h d", d=64),
                    in1=s26[:, 2, h0:h1].unsqueeze(2).to_broadcast([128, H, 64]), op=ALU.mult),
                    reads=bks + [b_s26], writes=[b_qn])

        TYPES = {"h0": [(0, 0, 8), (1, 8, 16)], "h1": [(2, 16, 24), (3, 24, 26)]}

        def rope_ew(k):
            s, n, edge, ranges, chunks = info(k)
            i = k % NX
            cs, b_cs = CS[i], b_CS[i]
            S_ = SETS[k % 2]
            ra, qn, rb, qr = S_["sq"], S_["qn"], S_["rb"], S_["qr"]
            gc, gs, b_tab = S_["gc"], S_["gs"], S_["b_tab"]
            fw.op(DVE, lambda: nc.vector.tensor_tensor(
                out=gc.rearrange("p y (t d) -> p y t d", t=2), in0=gq.rearrange("p y (t d) -> p y t d", t=2),
                in1=cs[:, 0:32].unsqueeze(1).unsqueeze(1).to_broadcast([128, 4, 2, 32]), op=ALU.mult),
                reads=[b_const, b_cs], writes=[b_tab])
            fw.op(DVE, lambda: nc.vector.tensor_tensor(
                out=gs[:, 0, :, :], in0=gq[:, :, 32:64],
                in1=cs[:, 32:64].unsqueeze(1).to_broadcast([128, 4, 32]), op=ALU.mult),
                reads=[b_const, b_cs], writes=[b_tab], partial=True)
            fw.op(DVE, lambda: nc.vector.tensor_tensor(
                out=gs[:, 1, :, :], in0=gq[:, :, 0:32],
                in1=cs[:, 32:64].unsqueeze(1).to_broadcast([128, 4, 32]), op=ALU.mult),
                reads=[b_const, b_cs], writes=[b_tab], partial=True)
            for hx, which in enumerate(("h0", "h1")):
                b_ra, b_qn, b_rb, b_qr = S_["b_sq"][hx], S_["b_qn"][hx], S_["b_rb"][hx], S_["b_qr"]
                hr = half_ranges(k, which)
                if not hr:
                    continue
                first = True
                for (ty, t0_, t1_) in TYPES[which]:
                    for (h0, h1) in hr:
                        a0, a1 = max(h0, t0_), min(h1, t1_)
                        if a0 >= a1:
                            continue
                        H = a1 - a0
                        c0, c1 = a0 * 64, a1 * 64
                        v4 = lambda t: t[:, c0:c1].rearrange("p (h t d) -> p h t d", t=2, d=32)
                        gcb = gc[:, ty, :].rearrange("p (t d) -> p t d", t=2).unsqueeze(1).to_broadcast(
                            [128, H, 2, 32])
                        g1b = gs[:, 0, ty, :].unsqueeze(1).to_broadcast([128, H, 32])
                        g2b = gs[:, 1, ty, :].unsqueeze(1).to_broadcast([128, H, 32])
                        fw.op(DVE, lambda: nc.vector.tensor_tensor(out=v4(ra), in0=v4(qn), in1=gcb, op=ALU.mult),
                              reads=[b_qn, b_tab], writes=[b_ra], partial=(not first))
                        fw.op(DVE, lambda: nc.vector.tensor_tensor(out=v4(rb)[:, :, 0, :], in0=v4(qn)[:, :, 1, :],
                                                                   in1=g1b, op=ALU.mult), reads=[b_qn, b_tab],
                              writes=[b_rb], partial=(not first))
                        fw.op(DVE, lambda: nc.vector.tensor_tensor(out=v4(rb)[:, :, 1, :], in0=v4(qn)[:, :, 0, :],
                                                                   in1=g2b, op=ALU.mult), reads=[b_qn, b_tab],
                              writes=[b_rb], partial=True)
                        first = False
                for (h0, h1) in hr:
                    c0, c1 = h0 * 64, h1 * 64
                    v4 = lambda t: t[:, c0:c1].rearrange("p (h t d) -> p h t d", t=2, d=32)
                    fw.op(DVE, lambda: nc.vector.tensor_tensor(out=v4(qr)[:, :, 0, :], in0=v4(ra)[:, :, 0, :],
                                                                in1=v4(rb)[:, :, 0, :], op=ALU.subtract),
                          reads=[b_ra, b_rb], writes=[b_qr], partial=True)
                    fw.op(DVE, lambda: nc.vector.tensor_tensor(out=v4(qr)[:, :, 1, :], in0=v4(ra)[:, :, 1, :],
                                                                in1=v4(rb)[:, :, 1, :], op=ALU.add),
                          reads=[b_ra, b_rb], writes=[b_qr], partial=True)

        def rope_T(k):
            s, n, edge, ranges, chunks = info(k)
            S_ = SETS[k % 2]
            qr, b_qr = S_["qr"], S_["b_qr"]
            ti = n % 2
            gi = (k // 2) % 2
            for ci_, c in enumerate(chunks):
                fw.op(PE, lambda c=c: nc.tensor.transpose(out=qkT[:, c * 128:(c + 1) * 128],
                                                          in_=qr[:, c * 128:(c + 1) * 128], identity=ident),
                      reads=[b_qr, b_const], writes=[BK[6], BK[7]], signal=(ci_ == len(chunks) - 1),
                      partial=(ci_ > 0))
            tsl = slice(ti * 128, (ti + 1) * 128)
            if s == 0:
                fw.op(ACT, lambda: nc.scalar.copy(out=qkt[gi][:, :, tsl],
                                                  in_=qkT[:, 0:1664].rearrange("p (c t) -> p c t", t=128)),
                      reads=[BK[6], BK[7]], writes=[b_qkt[gi]], partial=True)
            else:
                fw.op(ACT, lambda: nc.scalar.copy(out=qkt[gi][:, 4:8, tsl],
                                                  in_=qkT[:, 512:1024].rearrange("p (c t) -> p c t", t=128)),
                      reads=[BK[6], BK[7]], writes=[b_qkt[gi]], partial=True)
                if edge:
                    fw.op(ACT, lambda: nc.scalar.copy(out=qkt[gi][:, 12, tsl], in_=qkT[:, 1536:1664]),
                          reads=[BK[6], BK[7]], writes=[b_qkt[gi]], partial=True)
                    fw.dma(QKs[s, 12, :, n * 128:(n + 1) * 128], qkt[gi][:, 12, tsl],
                           key=b_qkt[gi], reads=[b_qkt[gi]], writes=[b_QKs[s][12]], partial=True)
            if ti == 1:
                g0 = (n // 2) * 256
                for (c0, c1) in (((0, 7), (7, 13)) if s == 0 else ((4, 8),)):
                    fw.dma(QKs[s, c0:c1, :, g0:g0 + 256].rearrange("c p t -> p c t"),
                           qkt[gi][:, c0:c1, :], key=b_qkt[gi], reads=[b_qkt[gi]], writes=b_QKs[s][c0:c1],
                           partial=True)

        for k in range(min(3, T)):
            load(k)
        xnorm(0)
        xT(0)
        if T > 1:
            xnorm(1)
        for k in range(T):
            if k + 3 < T:
                load(k + 3)
            proj(k, "h0")
            if k + 1 < T:
                xT(k + 1)
            s2a(k, "h0")
            proj(k, "h1")
            proj(k, "va")
            vevac(k)
            s2a(k, "h1")
            if k + 2 < T:
                xnorm(k + 2)
            if k >= 1:
                rope_T(k - 1)
            rope_ew(k)
        rope_T(T - 1)
        fw.barrier()

    def phaseB(pc):
        arena_reset()
        halo = pc["halo"]
        pi = pc["pi"]
        UB = []
        offs = []
        for u in range(2):
            offs.append(arena_state["off"])
            UB.append(dict(
                qT=ar("qT", [128, NT], BF16), kT=ar("kT", [128, 2, NT], BF16),
                Vn=ar("Vn", [128, 18, 128], BF16), V4=ar("V4", [128, 4, 6, 128], BF16),
                V16=ar("V16", [128, 16, 2, 128], BF16), b=fw.buf(f"ub{u}")))
        off_end = arena_state["off"]
        arena_state["off"] = offs[1]
        qbT = ar("qbT", [128, 4, NT], BF16)
        kbT = ar("kbT", [128, 2304], BF16)
        Vb = ar("Vb", [128, 18, 128], BF16)
        assert arena_state["off"] <= off_end
        arena_state["off"] = off_end
        b_mb = UB[1]["b"]
        acc = ar("acc", [128, 2, NT], F32)
        b_acc = fw.buf("acc")
        NPT = 4
        Pt = [ar("Pt", [128, 2, 2, 128], BF16) for _ in range(NPT)]
        b_Pt = [fw.buf(f"Pt{i}") for i in range(NPT)]
        Ptb = [ar("Ptb", [128, 512], BF16) for _ in range(NPT)]
        b_Ptb = [fw.buf(f"Ptb{i}") for i in range(NPT)]
        dn = [ar("dn", [128, 512], F32) for _ in range(2)]
        b_dn = [fw.buf(f"dn{i}") for i in range(2)]
        LAG = 2

        def load_unitA(c):
            U = UB[c % 2]
            b = U["b"]
            rd = [b_QKs[0][c], b_QKs[0][4 + c], b_Vs[0][c]]
            fw.dma(U["qT"], QKs[0, c], key=b, reads=rd, writes=[b])
            fw.dma(U["kT"][:, 0, :], QKs[0, 4 + c], key=b, reads=rd, writes=[b], partial=True)
            for n4 in range(4):
                fw.dma(U["Vn"][:, 4 * n4:4 * n4 + 4, :],
                       Vs[0, c, 512 * n4:512 * n4 + 512, :].rearrange("(n p) e -> p n e", p=128), key=b, reads=rd,
                       writes=[b], partial=True)
            for tb in range(4):
                fw.dma(U["V4"][:, :, tb, :], Vs[0, c, 512 * tb:512 * tb + 512, :].rearrange("(p r) e -> p r e", r=4),
                       key=b, reads=rd, writes=[b], partial=True)
            fw.dma(U["V16"][:, :, 0, :], Vs[0, c].rearrange("(p r) e -> p r e", r=16), key=b, reads=rd, writes=[b],
                   partial=True)
            if halo:
                rd = [b_QKs[1][4 + c], b_Vs[1][c]]
                fw.dma(U["kT"][:, 1, :], QKs[1, 4 + c], key=b, reads=rd, writes=[b], partial=True)
                fw.dma(U["Vn"][:, 16, :], Vs[1, c, 0:128, :], key=b, reads=rd, writes=[b], partial=True)
                fw.dma(U["Vn"][:, 17, :], Vs[1, c, 1920:2048, :], key=b, reads=rd, writes=[b], partial=True)
                fw.dma(U["V4"][:, :, 4, :], Vs[1, c, 0:512, :].rearrange("(p r) e -> p r e", r=4), key=b, reads=rd,
                       writes=[b], partial=True)
                fw.dma(U["V4"][:, :, 5, :], Vs[1, c, 1536:2048, :].rearrange("(p r) e -> p r e", r=4), key=b,
                       reads=rd, writes=[b], partial=True)
                fw.dma(U["V16"][:, :, 1, :], Vs[1, c].rearrange("(p r) e -> p r e", r=16), key=b, reads=rd,
                       writes=[b], partial=True)

        def load_unitB():
            b = b_mb
            rd = b_QKs[0][8:13] + [b_Vs[0][4]]
            fw.dma(qbT, QKs[0, 8:12].rearrange("c p t -> p c t"), key=b, reads=rd, writes=[b])
            fw.dma(kbT[:, 0:NT], QKs[0, 12], key=b, reads=rd, writes=[b], partial=True)
            for n4 in range(4):
                fw.dma(Vb[:, 4 * n4:4 * n4 + 4, :],
                       Vs[0, 4, 512 * n4:512 * n4 + 512, :].rearrange("(n p) e -> p n e", p=128), key=b, reads=rd,
                       writes=[b], partial=True)
            if halo:
                rd = [b_QKs[1][12], b_Vs[1][4]]
                fw.dma(kbT[:, 2048:2176], QKs[1, 12, :, 0:128], key=b, reads=rd, writes=[b], partial=True)
                fw.dma(kbT[:, 2176:2304], QKs[1, 12, :, 1920:2048], key=b, reads=rd, writes=[b], partial=True)
                fw.dma(Vb[:, 16, :], Vs[1, 4, 0:128, :], key=b, reads=rd, writes=[b], partial=True)
                fw.dma(Vb[:, 17, :], Vs[1, 4, 1920:2048, :], key=b, reads=rd, writes=[b], partial=True)

        def unitA(c):
            U = UB[c % 2]
            b = U["b"]
            qT, kT = U["qT"], U["kT"]
            fw.op(POOL, lambda: nc.gpsimd.memset(acc, 0.0), writes=[b_acc])
            blocks = []
            for (d, Td) in ((1, 16), (4, 4), (16, 1)):
                Ld = NT // d
                for r in range(d):
                    for bb in range(-1, Td):
                        if bb == -1:
                            l0, nq, qi0 = 0, 64, 64
                        elif bb == Td - 1:
                            l0, nq, qi0 = Ld - 64, 64, 0
                        else:
                            l0, nq, qi0 = 128 * bb + 64, 128, 0
                        qsl = slice(d * l0 + r, d * (l0 + nq - 1) + r + 1, d)
                        tl = []
                        for wt, tb in enumerate((bb, bb + 1)):
                            if 0 <= tb < Td:
                                kp = (0, 128)
                                slot, tidx = 0, tb
                                if d == 1:
                                    Vt = U["Vn"][:, tb, :]
                                elif d == 4:
                                    Vt = U["V4"][:, r, tb, :]
                                else:
                                    Vt = U["V16"][:, r, 0, :]
                                dl = ones64
                            elif not halo:
                                continue
                            elif tb == -1:
                                kp = (64, 128)
                                slot, tidx = 1, Td - 1
                                Vt = (U["Vn"][:, 17, :] if d == 1 else U["V4"][:, r, 5, :] if d == 4
                                      else U["V16"][:, r, 1, :])
                                dl = hvb[:, pi, 1, :]
                            else:
                                kp = (0, 64)
                                slot, tidx = 1, 0
                                Vt = (U["Vn"][:, 16, :] if d == 1 else U["V4"][:, r, 4, :] if d == 4
                                      else U["V16"][:, r, 1, :])
                                dl = hvb[:, pi, 0, :]
                            k0 = d * (128 * tidx + kp[0]) + r
                            ksl = slice(k0, k0 + d * (kp[1] - kp[0] - 1) + 1, d)
                            tl.append((wt, kp, slot, ksl, Vt, dl))
                        blocks.append((nq, qi0, qsl, tl))

            def views(bi):
                sbk = 2 * (bi % 3)
                obk = 6 + bi % 2
                S4 = bank(sbk, 2).rearrange("p (h w q) -> p h w q", h=2, w=4)
                O2 = bank(obk)[:, 0:256].rearrange("p (a q) -> p a q", a=2)
                return sbk, obk, S4, O2, bi % NPT

            def front(bi):
                nq, qi0, qsl, tl = blocks[bi]
                sbk, obk, S4, O2, pti = views(bi)
                P4 = Pt[pti]
                nS = len(tl) * 2
                si = 0
                for (wt, kp, slot, ksl, Vt, dl) in tl:
                    for hh in range(2):
                        si += 1
                        fw.op(PE, lambda wt=wt, kp=kp, slot=slot, ksl=ksl, hh=hh: nc.tensor.matmul(
                            S4[kp[0]:kp[1], hh, wt, 0:nq], lhsT=kT[64 * hh:64 * hh + 64, slot, ksl],
                            rhs=qT[64 * hh:64 * hh + 64, qsl], start=True, stop=True),
                            reads=[b], writes=[BK[sbk], BK[sbk + 1]], signal=(si == nS), partial=(si > 1))
                ME, me = (DVE, nc.vector)
                if len(tl) == 2 and all(t[1] == (0, 128) for t in tl):
                    fw.op(ACT, lambda: nc.scalar.activation(
                        out=P4[:, :, :, 0:nq], in_=S4[:, :, 0:2, 0:nq], func=AF.Exp, scale=0.125),
                        reads=[BK[sbk], BK[sbk + 1]], writes=[b_Pt[pti]])
                    fw.op(ME, lambda: me.tensor_tensor(
                        out=P4[:, :, :, 0:nq], in0=P4[:, :, :, 0:nq],
                        in1=mask[:, :, qi0:qi0 + nq].unsqueeze(1).to_broadcast([128, 2, 2, nq]),
                        op=ALU.mult), reads=[b_Pt[pti], b_const], writes=[b_Pt[pti]], partial=True)
                else:
                    for ti_, (wt, kp, slot, ksl, Vt, dl) in enumerate(tl):
                        kpn = kp[1] - kp[0]
                        fw.op(ACT, lambda wt=wt, kp=kp: nc.scalar.activation(
                            out=P4[kp[0]:kp[1], :, wt, 0:nq], in_=S4[kp[0]:kp[1], :, wt, 0:nq], func=AF.Exp,
                            scale=0.125), reads=[BK[sbk], BK[sbk + 1]], writes=[b_Pt[pti]], partial=(ti_ > 0))
                        fw.op(ME, lambda wt=wt, kp=kp, kpn=kpn: me.tensor_tensor(
                            out=P4[kp[0]:kp[1], :, wt, 0:nq], in0=P4[kp[0]:kp[1], :, wt, 0:nq],
                            in1=mask[kp[0]:kp[1], wt, qi0:qi0 + nq].unsqueeze(1).to_broadcast([kpn, 2, nq]),
                            op=ALU.mult), reads=[b_Pt[pti], b_const], writes=[b_Pt[pti]], partial=True)

            def back(bi):
                nq, qi0, qsl, tl = blocks[bi]
                sbk, obk, S4, O2, pti = views(bi)
                P4 = Pt[pti]
                nP = len(tl) * 4
                pi_ = 0
                for ti_, (wt, kp, slot, ksl, Vt, dl) in enumerate(tl):
                    for hh in range(2):
                        pi_ += 1
                        fw.op(PE, lambda wt=wt, kp=kp, Vt=Vt, hh=hh, ti_=ti_: nc.tensor.matmul(
                            O2[64 * hh:64 * hh + 64, 0, 0:nq], lhsT=Vt[kp[0]:kp[1], 64 * hh:64 * hh + 64],
                            rhs=P4[kp[0]:kp[1], hh, wt, 0:nq], start=(ti_ == 0), stop=False,
                            skip_group_check=True),
                            reads=[b, b_Pt[pti]], writes=[BK[obk]], signal=False, partial=(pi_ > 1))
                        pi_ += 1
                        fw.op(PE, lambda wt=wt, kp=kp, dl=dl, hh=hh, ti_=ti_: nc.tensor.matmul(
                            O2[64 * hh:64 * hh + 64, 1, 0:nq], lhsT=dl[kp[0]:kp[1], :],
                            rhs=P4[kp[0]:kp[1], hh, wt, 0:nq], start=False, stop=(ti_ == len(tl) - 1),
                            skip_group_check=True),
                            reads=[b_const, b_Pt[pti]], writes=[BK[obk]], signal=(pi_ == nP), partial=True)
                fw.op(DVE, lambda: nc.vector.tensor_tensor(out=acc[:, :, qsl], in0=O2[:, :, 0:nq],
                                                           in1=acc[:, :, qsl], op=ALU.add),
                      reads=[BK[obk], b_acc], writes=[b_acc], partial=True)

            nb = len(blocks)
            for i in range(nb + LAG):
                if i < nb:
                    front(i)
                if i - LAG >= 0:
                    back(i - LAG)
            fw.op(ACT, lambda: nc.scalar.activation(out=acc[:, 1, :], in_=acc[:, 1, :], func=AF.Ln), reads=[b_acc],
                  writes=[b_acc], partial=True)
            fw.op(ACT, lambda: nc.scalar.activation(out=acc[:, 1, :], in_=acc[:, 1, :], func=AF.Exp, scale=-1.0),
                  reads=[b_acc], writes=[b_acc], partial=True)
            fw.op(POOL, lambda: nc.gpsimd.tensor_tensor(out=AT[:, c, :], in0=acc[:, 0, :], in1=acc[:, 1, :],
                                                        op=ALU.mult), reads=[b_acc], writes=[b_AT[c]])

        def unitB():
            b = b_mb
            items = []
            for bb in range(16):
                for g in range(2):
                    tl = []
                    for wt, tb in ((0, bb - 1), (None, bb), (1, bb + 1)):
                        if 0 <= tb < 16:
                            tl.append((wt, slice(128 * tb, 128 * tb + 128), Vb[:, tb, :], ones64))
                        elif not halo:
                            continue
                        elif tb == -1:
                            tl.append((wt, slice(2176, 2304), Vb[:, 17, :], hvb[:, pi, 1, :]))
                        else:
                            tl.append((wt, slice(2048, 2176), Vb[:, 16, :], hvb[:, pi, 0, :]))
                    for ti_, t in enumerate(tl):
                        items.append((bb, g, ti_, len(tl), t))

            def vw(i, bb):
                sbk = i % 3
                nbk = 3 + 2 * (bb % 2)
                return sbk, bank(sbk), nbk, bank(nbk), bank(nbk + 1), i % NPT

            def front(i):
                bb, g, ti_, ntl, (wt, ksl, Vt, dl) = items[i]
                sbk, Sb, nbk, NB, DB, pbi = vw(i, bb)
                fw.op(PE, lambda: nc.tensor.matmul(
                    Sb, lhsT=kbT[64 * g:64 * g + 64, ksl], rhs=qbT[64 * g:64 * g + 64, :, 128 * bb:128 * bb + 128],
                    start=True, stop=True), reads=[b], writes=[BK[sbk]])
                fw.op(ACT, lambda: nc.scalar.activation(out=Ptb[pbi], in_=Sb, func=AF.Exp, scale=0.125),
                      reads=[BK[sbk]], writes=[b_Ptb[pbi]])
                if wt is not None:
                    ME, me = (DVE, nc.vector)
                    fw.op(ME, lambda: me.tensor_tensor(
                        out=Ptb[pbi].rearrange("p (c q) -> p c q", c=4),
                        in0=Ptb[pbi].rearrange("p (c q) -> p c q", c=4),
                        in1=mask[:, wt, :].unsqueeze(1).to_broadcast([128, 4, 128]), op=ALU.mult),
                        reads=[b_Ptb[pbi], b_const], writes=[b_Ptb[pbi]])

            def back(i):
                bb, g, ti_, ntl, (wt, ksl, Vt, dl) = items[i]
                sbk, Sb, nbk, NB, DB, pbi = vw(i, bb)
                last = (ti_ == ntl - 1)
                first_of_block = (g == 0 and ti_ == 0)
                fw.op(PE, lambda: nc.tensor.matmul(
                    NB[64 * g:64 * g + 64, :], lhsT=Vt[:, 64 * g:64 * g + 64], rhs=Ptb[pbi], start=(ti_ == 0),
                    stop=last), reads=[b, b_Ptb[pbi]], writes=[BK[nbk]], signal=False,
                    partial=(not first_of_block))
                fw.op(PE, lambda: nc.tensor.matmul(
                    DB[64 * g:64 * g + 64, :], lhsT=dl, rhs=Ptb[pbi], start=(ti_ == 0), stop=last),
                    reads=[b_const, b_Ptb[pbi]], writes=[BK[nbk + 1]], signal=True, partial=(not first_of_block))
                if g == 1 and last:
                    di = bb % 2
                    for c4 in range(4):
                        fw.op(ACT, lambda c4=c4: nc.scalar.activation(
                            out=dn[di][:, c4 * 128:(c4 + 1) * 128], in_=DB[:, c4 * 128:(c4 + 1) * 128], func=AF.Ln,
                            bias=es[:, c4:c4 + 1]), reads=[BK[nbk + 1], b_const], writes=[b_dn[di]],
                            partial=(c4 > 0))
                    fw.op(ACT, lambda: nc.scalar.activation(out=dn[di], in_=dn[di], func=AF.Exp, scale=-1.0),
                          reads=[b_dn[di]], writes=[b_dn[di]])
                    fw.op(DVE, lambda: nc.vector.tensor_tensor(
                        out=AT[:, 4:8, 128 * bb:128 * bb + 128], in0=NB.rearrange("p (c q) -> p c q", c=4),
                        in1=dn[di].rearrange("p (c q) -> p c q", c=4), op=ALU.mult), reads=[BK[nbk], b_dn[di]],
                        writes=b_AT[4:8], partial=True)

            ni = len(items)
            for i in range(ni + LAG):
                if i < ni:
                    front(i)
                if i - LAG >= 0:
                    back(i - LAG)

        load_unitA(0)
        load_unitB()
        unitB()
        for c in range(4):
            if c + 1 < 4:
                load_unitA(c + 1)
            unitA(c)
        fw.barrier()

    def phaseC(pc):
        arena_reset()
        x1 = [ar("x1", [128, 4, D], F32) for _ in range(2)]
        b_x1 = [[fw.buf(f"x1_{u}_{t}") for t in range(4)] for u in range(2)]
        XR = [ar("xr", [128, D], F32) for _ in range(2)]
        b_XR = [fw.buf(f"xr{i}") for i in range(2)]
        junk = ar("junk", [128, D], BF16)
        b_junk = fw.buf("junkc")
        ssq = ar("ssq", [128, 4], F32)
        b_ssq = fw.buf("ssqc")
        xn2 = [ar("xn2", [128, D], BF16) for _ in range(2)]
        b_xn2 = [fw.buf(f"xn2_{i}") for i in range(2)]
        h2T = [ar("h2T", [128, 8, 512], BF16) for _ in range(2)]
        b_h2T = [fw.buf(f"h2T{i}") for i in range(2)]
        actT = ar("actT", [128, NJ, 512], BF16)
        b_actT = [fw.buf(f"actT{j}") for j in range(NJ)]
        sg = [ar("sg", [128, 512], F32) for _ in range(2)]
        b_sg = [fw.buf(f"sg{i}") for i in range(2)]
        WG = [ar("WG", [128, 8, 128], BF16) for _ in range(3)]
        WU = [ar("WU", [128, 8, 128], BF16) for _ in range(3)]
        WD = [ar("WD", [128, D], BF16) for _ in range(3)]
        b_WG = [fw.buf(f"WG{i}") for i in range(3)]
        b_WU = [fw.buf(f"WU{i}") for i in range(3)]
        b_WD = [fw.buf(f"WD{i}") for i in range(3)]
        OS = [ar("OS", [128, D], F32) for _ in range(2)]
        b_OS = [fw.buf(f"OS{i}") for i in range(2)]
        pT = bank(2).bitcast(BF16)

        def ld_gu(j):
            i = j % 3
            fw.dma(WG[i], wgs[j], key=b_WG[i], reads=[b_wsc], writes=[b_WG[i]])
            fw.dma(WU[i], wus[j], key=b_WU[i], reads=[b_wsc], writes=[b_WU[i]])

        def ld_d(j):
            i = j % 3
            fw.dma(WD[i], wds[j], key=b_WD[i], reads=[b_wsc], writes=[b_WD[i]])

        xcnt = [0]

        def c1_load(sp_, t):
            r0 = 512 * sp_ + 128 * t
            xi = (4 * sp_ + t) % 2
            fw.dma(XR[xi], pc["xsrc"](r0 // 128), key=b_XR[xi], writes=[b_XR[xi]])

        def c1_a(sp_, t):
            u = sp_ % 2
            r0 = 512 * sp_ + 128 * t
            tcs = slice(r0, r0 + 128)
            xi = (4 * sp_ + t) % 2
            xp_ = t % 2
            for half in range(2):
                for kk in range(8):
                    fw.op(PE, lambda kk=kk, half=half: nc.tensor.matmul(
                        bank(half), lhsT=AT[:, kk, tcs], rhs=woutb[:, kk, half * 512:(half + 1) * 512],
                        start=(kk == 0), stop=(kk == 7)), reads=[b_AT[kk], b_woutb], writes=[BK[half]],
                        signal=(kk == 7), partial=(kk > 0))
            fw.op(DVE, lambda: nc.vector.tensor_tensor(out=x1[u][:, t, :], in0=bank(0, 2), in1=XR[xi], op=ALU.add),
                  reads=[BK[0], BK[1], b_XR[xi]], writes=[b_x1[u][t]])
            fw.op(ACT, lambda: nc.scalar.activation(out=junk, in_=x1[u][:, t, :], func=AF.Square,
                                                    accum_out=ssq[:, 0:1]), reads=[b_x1[u][t]],
                  writes=[b_junk, b_ssq])
            fw.op(ACT, lambda: nc.scalar.activation(out=ssq[:, 1:2], in_=ssq[:, 0:1], func=AF.Sqrt,
                                                    scale=1.0 / D, bias=eps_t), reads=[b_ssq, b_const],
                  writes=[b_ssq])
            fw.op(DVE, lambda: nc.vector.reciprocal(out=ssq[:, 2:3], in_=ssq[:, 1:2]), reads=[b_ssq],
                  writes=[b_ssq])
            fw.op(ACT, lambda: nc.scalar.activation(out=xn2[xp_], in_=x1[u][:, t, :], func=AF.Copy,
                                                    scale=ssq[:, 2:3]),
                  reads=[b_x1[u][t], b_ssq], writes=[b_xn2[xp_]])

        def c1_b(sp_, t):
            u = sp_ % 2
            xp_ = t % 2
            for kk in range(8):
                fw.op(PE, lambda kk=kk: nc.tensor.transpose(out=pT[:, kk * 128:(kk + 1) * 128],
                                                            in_=xn2[xp_][:, kk * 128:(kk + 1) * 128],
                                                            identity=ident),
                      reads=[b_xn2[xp_], b_const], writes=[BK[2]], signal=(kk == 7), partial=(kk > 0))
            fw.op(ACT, lambda: nc.scalar.copy(out=h2T[u][:, :, 128 * t:128 * t + 128],
                                              in_=pT.rearrange("p (k q) -> p k q", k=8)),
                  reads=[BK[2]], writes=[b_h2T[u]], partial=(t > 0))

        def c2(sp_, j):
            u = sp_ % 2
            i = j % 3
            gb, ub = 4 + j % 2, 6 + j % 2
            for kk in range(8):
                fw.op(PE, lambda kk=kk: nc.tensor.matmul(bank(gb), lhsT=WG[i][:, kk, :], rhs=h2T[u][:, kk, :],
                                                         start=(kk == 0), stop=(kk == 7)),
                      reads=[b_WG[i], b_h2T[u]], writes=[BK[gb]], signal=(kk == 7), partial=(kk > 0))
            for kk in range(8):
                fw.op(PE, lambda kk=kk: nc.tensor.matmul(bank(ub), lhsT=WU[i][:, kk, :], rhs=h2T[u][:, kk, :],
                                                         start=(kk == 0), stop=(kk == 7)),
                      reads=[b_WU[i], b_h2T[u]], writes=[BK[ub]], signal=(kk == 7), partial=(kk > 0))
            si = j % 2
            fw.op(ACT, lambda: nc.scalar.activation(out=sg[si], in_=bank(gb), func=AF.Silu), reads=[BK[gb]],
                  writes=[b_sg[si]])
            fw.op(DVE, lambda: nc.vector.tensor_tensor(out=actT[:, j, :], in0=bank(ub), in1=sg[si],
                                                       op=ALU.mult), reads=[BK[ub], b_sg[si]],
                  writes=[b_actT[j]])

        def c3(sp_):
            for j in range(NJ):
                if j + 2 < NJ:
                    ld_d(j + 2)
                i = j % 3
                for t in range(4):
                    for half in range(2):
                        fw.op(PE, lambda t=t, half=half: nc.tensor.matmul(
                            bank(2 * t + half), lhsT=actT[:, j, 128 * t:128 * t + 128],
                            rhs=WD[i][:, half * 512:(half + 1) * 512], start=(j == 0), stop=(j == NJ - 1)),
                            reads=[b_actT[j], b_WD[i]], writes=[BK[2 * t + half]],
                            signal=(j == NJ - 1 or (t == 3 and half == 1)), partial=(j > 0))

        def c4(sp_):
            u = sp_ % 2
            for t in (2, 3, 0, 1):
                r0 = 512 * sp_ + 128 * t
                oi = t % 2
                fw.op(DVE, lambda: nc.vector.tensor_tensor(out=OS[oi], in0=bank(2 * t, 2), in1=x1[u][:, t, :],
                                                           op=ALU.add),
                      reads=[BK[2 * t], BK[2 * t + 1], b_x1[u][t]], writes=[b_OS[oi]])
                fw.dma(pc["out"](r0), OS[oi], key=b_OS[oi], reads=[b_OS[oi]])

        NSP = 4
        c1_load(0, 0)
        c1_load(0, 1)
        ld_gu(0)
        ld_gu(1)
        for t in range(4):
            c1_a(0, t)
            if t + 2 < 4:
                c1_load(0, t + 2)
            if t >= 1:
                c1_b(0, t - 1)
        c1_b(0, 3)
        for sp_ in range(NSP):
            nxt = sp_ + 1 < NSP
            sched_a = {2: 0, 6: 1, 10: 2, 14: 3}
            sched_b = {5: 0, 9: 1, 13: 2, 17: 3}
            if nxt:
                c1_load(sp_ + 1, 0)
                c1_load(sp_ + 1, 1)
            for j in range(NJ):
                if j + 2 < NJ:
                    ld_gu(j + 2)
                if j == NJ - 2:
                    ld_d(0)
                if j == NJ - 1:
                    ld_d(1)
                c2(sp_, j)
                if nxt and j in sched_a:
                    t = sched_a[j]
                    c1_a(sp_ + 1, t)
                    if t + 2 < 4:
                        c1_load(sp_ + 1, t + 2)
                if nxt and j in sched_b:
                    c1_b(sp_ + 1, sched_b[j])
            if nxt:
                ld_gu(0)
                ld_gu(1)
            c3(sp_)
            c4(sp_)
        fw.barrier()

    for ip, pc in enumerate(pieces):
        if sel is not None and ip not in sel:
            continue
        if "A" in phases:
            phaseA(pc)
        if "B" in phases:
            phaseB(pc)
        if "C" in phases:
            phaseC(pc)
    fw.finish()
    return nc


_CACHE = {}


def _host_consts():
    bf = ml_dtypes.bfloat16
    kj = np.arange(128)[:, None]
    qi = np.arange(128)[None, :]
    masks = np.stack([(kj >= qi), (kj <= qi)], axis=1).astype(np.float32).astype(bf)
    ident = np.eye(128, dtype=np.float32).astype(bf)
    return masks, ident


def _rope_table(pos):
    half = 32
    inv = (np.float32(10000.0) ** (np.float32(-2.0) * np.arange(half, dtype=np.float32) / np.float32(64))).astype(
        np.float32)
    ang = (pos.astype(np.float32)[:, None] * inv[None, :]).astype(np.float32)
    return np.concatenate([np.cos(ang), np.sin(ang)], axis=1).astype(np.float32)


def kernel(x_prompt, x_sample, attn_norm, w_in, qnorm_a, knorm_a, qnorm_b, knorm_b,
           sink_b, w_out, ffn_norm, w_gate, w_up, w_down):
    f32 = np.float32
    x_prompt = np.asarray(x_prompt, f32)
    x_sample = np.asarray(x_sample, f32)
    w_in = np.asarray(w_in, f32)[0]
    w_out = np.asarray(w_out, f32)[0]
    w_gate = np.ascontiguousarray(np.asarray(w_gate, f32)[0])
    w_up = np.ascontiguousarray(np.asarray(w_up, f32)[0])
    w_down = np.ascontiguousarray(np.asarray(w_down, f32)[0])
    QBo, KBo, VBo, VAo = 1536, 2048, 2176, 1024
    cols = list(range(0, 512)) + list(range(512, 1024))
    for c in range(4):
        for g in range(2):
            h = 4 * g + c
            cols += list(range(QBo + h * 64, QBo + h * 64 + 64))
    cols += list(range(KBo, KBo + 128)) + list(range(VBo, VBo + 128)) + list(range(VAo, VAo + 512))
    w_in_p = np.ascontiguousarray(w_in[:, cols])
    rows = list(range(512))
    for c in range(4):
        for g in range(2):
            h = 4 * g + c
            rows += list(range(512 + h * 64, 512 + h * 64 + 64))
    w_out_p = np.ascontiguousarray(w_out[rows, :])
    gA = np.ascontiguousarray(np.asarray(attn_norm, f32)[0].reshape(8, 128).T)
    gF = np.ascontiguousarray(np.asarray(ffn_norm, f32)[0].reshape(8, 128).T)
    gq1 = np.stack([np.asarray(qnorm_a, f32)[0], np.asarray(knorm_a, f32)[0],
                    np.asarray(qnorm_b, f32)[0], np.asarray(knorm_b, f32)[0]])
    gq = np.ascontiguousarray(np.broadcast_to(gq1[None, :, :], (128, 4, 64)))
    sk = np.asarray(sink_b, f32)[0]
    sinkT = np.zeros((128, 4), f32)
    for g in range(2):
        for c in range(4):
            sinkT[64 * g:64 * g + 64, c] = sk[4 * g + c]
    masks, ident = _host_consts()

    if "nc" not in _CACHE:
        _CACHE["nc"] = build_program()
    nc = _CACHE["nc"]

    in_maps = []
    p128 = np.arange(128)
    for core in range(NCORES):
        ps, q = core // 4, core % 4
        lo = q * 4096 - 1024
        xp = np.zeros((6144, D), f32)
        a, b = max(lo, 0), min(lo + 6144, 16384)
        xp[a - lo:b - lo] = x_prompt[ps, a:b]
        cst = np.zeros((80, 128, 64), f32)
        for n in range(16):
            cst[n] = _rope_table(n * 128 + p128)
        for pi in range(2):
            base = q * 4096 + 2048 * pi
            for n in range(16):
                cst[16 + 32 * pi + n] = _rope_table(base + n * 128 + p128)
                hidx = n * 128 + p128
                local = np.where(hidx < 1024, 2048 + hidx, hidx - 2048)
                cst[32 + 32 * pi + n] = _rope_table(np.maximum(base + local, 0))
        hv = np.ones((128, 2, 2, 64), f32)
        if q == 0:
            hv[:, 0, 1, :] = 0.0
        if q == 3:
            hv[:, 1, 0, :] = 0.0
        in_maps.append({
            "xs": np.ascontiguousarray(x_sample[4 * core:4 * core + 4]),
            "xp": xp, "cst": cst, "w_in": w_in_p, "w_out": w_out_p, "w_gate": w_gate, "w_up": w_up,
            "w_down": w_down, "gA": gA, "gF": gF, "gq": gq, "sinkT": sinkT, "masks": masks, "ident": ident,
            "hv": hv.astype(ml_dtypes.bfloat16),
        })
    res = run_bass_kernel_spmd(nc, in_maps, core_ids=list(range(NCORES)))
    y_prompt = np.zeros((2, 16384, D), f32)
    y_sample = np.zeros((32, NT, D), f32)
    for core in range(NCORES):
        r = res.results[core]
        ps, q = core // 4, core % 4
        y_prompt[ps, q * 4096:(q + 1) * 4096] = np.asarray(r["yp"], f32)
        y_sample[4 * core:4 * core + 4] = np.asarray(r["ys"], f32)
    return (y_prompt, y_sample)
```

```python
import numpy as np
import ml_dtypes
import concourse.bass as bass
import concourse.mybir as mybir
from concourse.bass_utils import run_bass_kernel_spmd

F32 = mybir.dt.float32
BF16 = mybir.dt.bfloat16
ALU = mybir.AluOpType
AF = mybir.ActivationFunctionType
AX = mybir.AxisListType

D = 1024
NT = 2048
DFF = 2816
NJ = 22
EPS = 1e-6
NCORES = 8


class Src:
    def __init__(self, nc, name, step):
        self.sem = nc.alloc_semaphore(name=name)
        self.count = 0
        self.step = step
        self.name = name


class Buf:
    __slots__ = ("name", "writes", "reads", "dsrc")

    def __init__(self, name):
        self.name = name
        self.writes = []
        self.reads = []
        self.dsrc = None


class Eng:
    def __init__(self, fw, name, eng, is_pe=False):
        self.name = name
        self.eng = eng
        self.src = Src(fw.nc, "e_" + name, 1)
        self.known = {}
        self.is_pe = is_pe


class FW:
    def __init__(self, nc):
        self.nc = nc
        self.pe = Eng(self, "pe", nc.tensor, is_pe=True)
        self.act = Eng(self, "act", nc.scalar)
        self.dve = Eng(self, "dve", nc.vector)
        self.pool = Eng(self, "pool", nc.gpsimd)
        self.sp = Eng(self, "sp", nc.sync)
        self.engs = [self.pe, self.act, self.dve, self.pool, self.sp]
        self.nbuf = 0
        self.dsrcs = []
        self.bufs = {}

    def buf(self, name=None):
        self.nbuf += 1
        if name is None:
            return Buf(f"b{self.nbuf}")
        if name not in self.bufs:
            self.bufs[name] = Buf(name)
        return self.bufs[name]

    def _collect(self, E, reads, writes):
        need = {}

        def add(tok):
            s, c = tok
            if E.is_pe and s is E.src:
                return
            if s.step == 16:
                c = s.count
            if need.get(s, 0) < c:
                need[s] = c
        for b in reads:
            for t in b.writes:
                add(t)
        for b in writes:
            for t in b.writes:
                add(t)
            for t in b.reads:
                add(t)
        for s, c in need.items():
            if E.known.get(s, 0) >= c:
                continue
            E.eng.wait_ge(s.sem, c)
            E.known[s] = c

    def _commit(self, tok, reads, writes, partial):
        for b in writes:
            if partial:
                b.writes.append(tok)
                if len(b.writes) > 24:
                    m = {}
                    for s, c in b.writes:
                        if m.get(s, 0) < c:
                            m[s] = c
                    b.writes = list(m.items())
            else:
                b.writes = [tok]
                b.reads = []
        for b in reads:
            b.reads.append(tok)
            if len(b.reads) > 24:
                m = {}
                for s, c in b.reads:
                    if m.get(s, 0) < c:
                        m[s] = c
                b.reads = list(m.items())

    def op(self, E, fn, reads=(), writes=(), signal=True, partial=False):
        self._collect(E, reads, writes)
        ins = fn()
        if signal:
            E.src.count += 1
            ins.then_inc(E.src.sem, 1)
            tok = (E.src, E.src.count)
        else:
            tok = (E.src, E.src.count + 1)
        self._commit(tok, reads, writes, partial)
        return ins

    def dma(self, out_ap, in_ap, key, reads=(), writes=(), partial=False):
        E = self.sp
        if key.dsrc is None:
            key.dsrc = Src(self.nc, "d_" + key.name, 16)
            self.dsrcs.append(key.dsrc)
        self._collect(E, reads, writes)
        ins = E.eng.dma_start(out=out_ap, in_=in_ap)
        key.dsrc.count += 16
        ins.then_inc(key.dsrc.sem, 16)
        tok = (key.dsrc, key.dsrc.count)
        self._commit(tok, reads, writes, partial)
        return ins

    def barrier(self):
        for E in self.engs:
            for F in self.engs:
                if F is E or F is self.sp:
                    continue
                c = F.src.count
                if c > 0 and E.known.get(F.src, 0) < c:
                    E.eng.wait_ge(F.src.sem, c)
                    E.known[F.src] = c
            for d in self.dsrcs:
                if d.count > 0 and E.known.get(d, 0) < d.count:
                    E.eng.wait_ge(d.sem, d.count)
                    E.known[d] = d.count

    def finish(self):
        E = self.sp
        for F in self.engs:
            if F is E:
                continue
            if F.src.count > 0:
                E.eng.wait_ge(F.src.sem, F.src.count)
        for d in self.dsrcs:
            if d.count > 0:
                E.eng.wait_ge(d.sem, d.count)


def build_program(sel=None, phases="ABC"):
    nc = bass.Bass("TRN2", target_bir_lowering=False)
    fw = FW(nc)
    PE, ACT, DVE, POOL = fw.pe, fw.act, fw.dve, fw.pool

    def din(name, shape, dt=F32):
        return nc.dram_tensor(name, list(shape), dt, kind="ExternalInput").ap()

    xs = din("xs", [4, NT, D])
    xp = din("xp", [6144, D])
    cst = din("cst", [80, 128, 64])
    w_in = din("w_in", [D, 2304])
    w_out = din("w_out", [D, D])
    w_gate = din("w_gate", [D, DFF])
    w_up = din("w_up", [D, DFF])
    w_down = din("w_down", [DFF, D])
    gA_d = din("gA", [128, 8])
    gF_d = din("gF", [128, 8])
    gq_d = din("gq", [128, 4, 64])
    sink_d = din("sinkT", [128, 4])
    mask_d = din("masks", [128, 2, 128], BF16)
    ident_d = din("ident", [128, 128], BF16)
    hv_d = din("hv", [128, 2, 2, 64], BF16)
    ys = nc.dram_tensor("ys", [4, NT, D], F32, kind="ExternalOutput").ap()
    yp = nc.dram_tensor("yp", [4096, D], F32, kind="ExternalOutput").ap()
    QKs = nc.dram_tensor("QKs", [2, 13, 128, NT], BF16, kind="Internal").ap()
    Vs = nc.dram_tensor("Vs", [2, 5, NT, 128], BF16, kind="Internal").ap()
    wgs = nc.dram_tensor("wgs", [NJ, 128, 8, 128], BF16, kind="Internal").ap()
    wus = nc.dram_tensor("wus", [NJ, 128, 8, 128], BF16, kind="Internal").ap()
    wds = nc.dram_tensor("wds", [NJ, 128, D], BF16, kind="Internal").ap()
    b_QKs = [[fw.buf(f"QKs{s}_{c}") for c in range(13)] for s in range(2)]
    b_Vs = [[fw.buf(f"Vs{s}_{c}") for c in range(5)] for s in range(2)]
    b_wsc = fw.buf("wscratch")

    def sb(name, shape, dt):
        return nc.alloc_sbuf_tensor("sb_" + name, list(shape), dt).ap()

    gA = sb("gA", [128, 8], F32)
    gF = sb("gF", [128, 8], F32)
    gq = sb("gq", [128, 4, 64], F32)
    mask = sb("mask", [128, 2, 128], BF16)
    ident = sb("ident", [128, 128], BF16)
    ones64 = sb("ones64", [128, 64], BF16)
    hvb = sb("hvb", [128, 2, 2, 64], BF16)
    es = sb("es", [128, 4], F32)
    eps_t = sb("eps", [128, 1], F32)
    winb = sb("winb", [128, 8, 2304], BF16)
    woutb = sb("woutb", [128, 8, D], BF16)
    AT = sb("AT", [128, 8, NT], BF16)
    b_const = fw.buf("const")
    b_winb = fw.buf("winb")
    b_woutb = fw.buf("woutb")
    b_AT = [fw.buf(f"AT{k}") for k in range(8)]

    PS = nc.alloc_psum_tensor("PS", [128, 4096], F32).ap()
    BK = [fw.buf(f"bank{i}") for i in range(8)]

    def bank(i, n=1):
        return PS[:, i * 512:(i + n) * 512]

    arena0 = (nc.sbuf_base + 63) // 64 * 64
    arena_state = {"off": 0}
    ARENA_BYTES = nc.sbuf_top - arena0

    def arena_reset():
        arena_state["off"] = 0

    uid = [0]

    def ar(name, shape, dt):
        nbytes = int(np.prod(shape[1:])) * (2 if dt == BF16 else 4)
        nbytes = (nbytes + 63) // 64 * 64
        off = arena_state["off"]
        assert off + nbytes <= ARENA_BYTES, (name, off, nbytes, ARENA_BYTES)
        arena_state["off"] = off + nbytes
        uid[0] += 1
        return nc.alloc_sbuf_tensor_at(f"{name}_{uid[0]}", list(shape), dt, offset=arena0 + off).ap()

    for dst, src in ((gA, gA_d), (gF, gF_d), (gq, gq_d), (mask, mask_d), (ident, ident_d),
                     (hvb, hv_d), (es, sink_d)):
        fw.dma(dst, src, key=b_const, writes=[b_const], partial=True)
    fw.op(POOL, lambda: nc.gpsimd.memset(ones64, 1.0), writes=[b_const], partial=True)
    fw.op(POOL, lambda: nc.gpsimd.memset(eps_t, EPS), writes=[b_const], partial=True)
    fw.op(ACT, lambda: nc.scalar.activation(out=es, in_=es, func=AF.Exp), reads=[b_const], writes=[b_const],
          partial=True)

    arena_reset()
    NB0 = 6
    stg = [ar("stg", [128, 2304], F32) for _ in range(NB0)]
    b_stg = [fw.buf(f"stg{i}") for i in range(NB0)]
    stb = [ar("stb", [128, 1024], BF16) for _ in range(NB0)]
    b_stb = [fw.buf(f"stb{i}") for i in range(NB0)]
    jobs = []
    for k in range(8):
        jobs.append(("win", k))
    for k in range(8):
        jobs.append(("wout", k))
    for j in range(NJ):
        jobs.append(("g", j))
        jobs.append(("u", j))
        jobs.append(("d", j))

    def p0_load(n):
        kind, a_ = jobs[n]
        i = n % NB0
        if kind == "win":
            fw.dma(stg[i], w_in[a_ * 128:(a_ + 1) * 128, :], key=b_stg[i], writes=[b_stg[i]])
        elif kind == "wout":
            fw.dma(stg[i][:, 0:D], w_out[a_ * 128:(a_ + 1) * 128, :], key=b_stg[i], writes=[b_stg[i]])
        elif kind in ("g", "u"):
            wsrc = w_gate if kind == "g" else w_up
            fw.dma(stg[i][:, 0:1024].rearrange("p (k c) -> p k c", k=8),
                   wsrc[:, a_ * 128:(a_ + 1) * 128].rearrange("(k p) c -> p k c", p=128),
                   key=b_stg[i], writes=[b_stg[i]])
        else:
            fw.dma(stg[i][:, 0:1024], w_down[a_ * 128:(a_ + 1) * 128, :], key=b_stg[i], writes=[b_stg[i]])

    def p0_work(n):
        kind, a_ = jobs[n]
        i = n % NB0
        e3 = n % 3
        if kind == "win":
            if e3 == 0:
                fw.op(ACT, lambda: nc.scalar.activation(out=winb[:, a_, :], in_=stg[i], func=AF.Copy,
                                                        scale=gA[:, a_:a_ + 1]),
                      reads=[b_stg[i], b_const], writes=[b_winb], partial=True)
            else:
                fw.op(DVE, lambda: nc.vector.tensor_scalar(out=winb[:, a_, :], in0=stg[i],
                                                           scalar1=gA[:, a_:a_ + 1], scalar2=None, op0=ALU.mult),
                      reads=[b_stg[i], b_const], writes=[b_winb], partial=True)
        elif kind == "wout":
            if e3 == 0:
                fw.op(ACT, lambda: nc.scalar.copy(out=woutb[:, a_, :], in_=stg[i][:, 0:D]), reads=[b_stg[i]],
                      writes=[b_woutb], partial=True)
            else:
                fw.op(DVE, lambda: nc.vector.tensor_copy(out=woutb[:, a_, :], in_=stg[i][:, 0:D]),
                      reads=[b_stg[i]], writes=[b_woutb], partial=True)
        elif kind in ("g", "u"):
            wdst = wgs if kind == "g" else wus
            fw.op(DVE, lambda: nc.vector.tensor_tensor(
                out=stb[i].rearrange("p (k c) -> p k c", k=8),
                in0=stg[i][:, 0:1024].rearrange("p (k c) -> p k c", k=8),
                in1=gF.unsqueeze(2).to_broadcast([128, 8, 128]), op=ALU.mult),
                reads=[b_stg[i], b_const], writes=[b_stb[i]])
            fw.dma(wdst[a_].rearrange("p k c -> p (k c)"), stb[i], key=b_stb[i], reads=[b_stb[i]],
                   writes=[b_wsc], partial=True)
        else:
            fw.op(ACT, lambda: nc.scalar.copy(out=stb[i], in_=stg[i][:, 0:1024]), reads=[b_stg[i]],
                  writes=[b_stb[i]])
            fw.dma(wds[a_], stb[i], key=b_stb[i], reads=[b_stb[i]], writes=[b_wsc], partial=True)

    for n in range(min(NB0 - 1, len(jobs))):
        p0_load(n)
    for n in range(len(jobs)):
        if n + NB0 - 1 < len(jobs):
            p0_load(n + NB0 - 1)
        p0_work(n)
    fw.barrier()

    pieces = []
    for i in range(4):
        pieces.append(dict(halo=False, xsrc=lambda n, i=i: xs[i, n * 128:(n + 1) * 128, :],
                           hsrc=None, cs0=0, csh=None, out=lambda r0, i=i: ys[i, r0:r0 + 128, :], pi=0))
    for pi in range(2):
        mb = 1024 + 2048 * pi

        def hrow(n, mb=mb):
            hidx = n * 128
            return (mb + 2048 + hidx) if hidx < 1024 else (mb + hidx - 2048)
        pieces.append(dict(halo=True, xsrc=lambda n, mb=mb: xp[mb + n * 128: mb + (n + 1) * 128, :],
                           hsrc=lambda n, hrow=hrow: xp[hrow(n):hrow(n) + 128, :],
                           cs0=16 + 32 * pi, csh=32 + 32 * pi,
                           out=lambda r0, pi=pi: yp[2048 * pi + r0: 2048 * pi + r0 + 128, :], pi=pi))

    mixb_cache = {}

    def alloc_mixb():
        assert arena_state["off"] == 0
        if "t" not in mixb_cache:
            qbT = ar("qbT", [128, 4, NT], BF16)
            kbT = ar("kbT", [128, 2304], BF16)
            Vb = ar("Vb", [128, 18, 128], BF16)
            mixb_cache["t"] = (qbT, kbT, Vb, fw.buf("mixb"))
            mixb_cache["off"] = arena_state["off"]
        arena_state["off"] = mixb_cache["off"]
        return mixb_cache["t"]

    def phaseA(pc):
        arena_reset()
        qbT, kbT, Vb, b_mb = alloc_mixb()
        NX = 4
        XT = [ar("xt", [128, D], F32) for _ in range(NX)]
        CS = [ar("cs", [128, 64], F32) for _ in range(NX)]
        b_XT = [fw.buf(f"xt{i}") for i in range(NX)]
        b_CS = [fw.buf(f"cs{i}") for i in range(NX)]
        junk = ar("junk", [128, D], BF16)
        b_junk = fw.buf("junk")
        ssq = [ar("ssq", [128, 4], F32) for _ in range(2)]
        b_ssq = [fw.buf(f"ssq{i}") for i in range(2)]
        xn = [ar("xn", [128, D], BF16) for _ in range(2)]
        b_xn = [fw.buf(f"xn{i}") for i in range(2)]
        hT = [ar("hT", [128, 8, 128], BF16) for _ in range(2)]
        b_hT = [fw.buf(f"hT{i}") for i in range(2)]
        vt = [ar("vt", [128, 5, 128], BF16) for _ in range(2)]
        b_vt = [fw.buf(f"vt{i}") for i in range(2)]
        qkt = [ar("qkt", [128, 13, 256], BF16) for _ in range(2)]
        b_qkt = [fw.buf(f"qkt{i}") for i in range(2)]
        SETS = []
        for i in range(2):
            SETS.append(dict(sq=ar("sq", [128, 1664], F32), qn=ar("qn", [128, 1664], F32),
                             rb=ar("rb", [128, 1664], F32), qr=ar("qr", [128, 1664], BF16),
                             s26=ar("s26", [128, 3, 26], F32), gc=ar("gc", [128, 4, 64], F32),
                             gs=ar("gs", [128, 2, 4, 32], F32), b_tab=fw.buf(f"tab{i}"),
                             b_sq=[fw.buf(f"sq{i}_{h}") for h in range(2)],
                             b_qn=[fw.buf(f"qn{i}_{h}") for h in range(2)],
                             b_rb=[fw.buf(f"rb{i}_{h}") for h in range(2)],
                             b_qr=fw.buf(f"qr{i}"), b_s26=[fw.buf(f"s26{i}_{h}") for h in range(2)]))
        P1 = bank(0, 4)
        P2 = bank(4)
        pT = bank(5).bitcast(BF16)
        qkT = bank(6, 2).bitcast(BF16)

        tiles = []
        for s in ([0, 1] if pc["halo"] else [0]):
            for n in range(16):
                tiles.append((s, n))
        T = len(tiles)

        def load(k):
            s, n = tiles[k]
            i = k % NX
            src = pc["xsrc"](n) if s == 0 else pc["hsrc"](n)
            fw.dma(XT[i], src, key=b_XT[i], writes=[b_XT[i]])
            ci = (pc["cs0"] if s == 0 else pc["csh"]) + n
            fw.dma(CS[i], cst[ci], key=b_CS[i], writes=[b_CS[i]])

        def info(k):
            s, n = tiles[k]
            edge = (s == 1 and n in (0, 15))
            if s == 0:
                ranges = [(0, 26)]
                chunks = list(range(13))
            else:
                ranges = [(8, 16)] + ([(24, 26)] if edge else [])
                chunks = [4, 5, 6, 7] + ([12] if edge else [])
            return s, n, edge, ranges, chunks

        def bks_of(c0, c1):
            return BK[c0 // 512:(c1 - 1) // 512 + 1]

        def xnorm(k):
            i = k % NX
            xt = XT[i]
            p = k % 2
            sq_, xn_ = ssq[p], xn[p]
            fw.op(ACT, lambda: nc.scalar.activation(out=junk, in_=xt, func=AF.Square, accum_out=sq_[:, 0:1]),
                  reads=[b_XT[i]], writes=[b_junk, b_ssq[p]])
            fw.op(ACT, lambda: nc.scalar.activation(out=sq_[:, 1:2], in_=sq_[:, 0:1], func=AF.Sqrt, scale=1.0 / D,
                                                    bias=eps_t), reads=[b_ssq[p], b_const], writes=[b_ssq[p]])
            fw.op(DVE, lambda: nc.vector.reciprocal(out=sq_[:, 2:3], in_=sq_[:, 1:2]), reads=[b_ssq[p]],
                  writes=[b_ssq[p]])
            fw.op(ACT, lambda: nc.scalar.activation(out=xn_, in_=xt, func=AF.Copy, scale=sq_[:, 2:3]),
                  reads=[b_XT[i], b_ssq[p]], writes=[b_xn[p]])

        def xT(k):
            p = k % 2
            xn_ = xn[p]
            for kk in range(8):
                fw.op(PE, lambda kk=kk: nc.tensor.transpose(out=pT[:, kk * 128:(kk + 1) * 128],
                                                            in_=xn_[:, kk * 128:(kk + 1) * 128], identity=ident),
                      reads=[b_xn[p], b_const], writes=[BK[5]], signal=(kk == 7), partial=(kk > 0))
            fw.op(ACT, lambda: nc.scalar.copy(out=hT[p].rearrange("p k t -> p (k t)"), in_=pT), reads=[BK[5]],
                  writes=[b_hT[p]])

        def proj(k, which):
            s, n, edge, ranges, chunks = info(k)
            p = k % 2
            if s == 0:
                allg = {"h0": [(0, 512, P1[:, 0:512], [BK[0]]), (512, 512, P1[:, 512:1024], [BK[1]])],
                        "h1": [(1024, 512, P1[:, 1024:1536], [BK[2]]), (1536, 256, P1[:, 1536:1792], [BK[3]])],
                        "va": [(1792, 512, P2, [BK[4]])]}
            else:
                allg = {"h0": [(512, 512, P1[:, 512:1024], [BK[1]])],
                        "h1": ([(1536, 256, P1[:, 1536:1792], [BK[3]])] if edge else []),
                        "va": [(1792, 512, P2, [BK[4]])]}
            for (c0, ncol, outap, bks) in allg[which]:
                for kk in range(8):
                    fw.op(PE, lambda kk=kk, c0=c0, ncol=ncol, outap=outap: nc.tensor.matmul(
                        outap, lhsT=hT[p][:, kk, :], rhs=winb[:, kk, c0:c0 + ncol], start=(kk == 0),
                        stop=(kk == 7)), reads=[b_hT[p], b_winb], writes=bks, signal=(kk == 7), partial=(kk > 0))

        def vevac(k):
            s, n, edge, ranges, chunks = info(k)
            p = k % 2
            fw.op(ACT, lambda: nc.scalar.copy(out=vt[p][:, 0:4, :].rearrange("p c e -> p (c e)"), in_=P2),
                  reads=[BK[4]], writes=[b_vt[p]])
            npair = 4
            if s == 0 or edge:
                vslot = n if s == 0 else (16 if n == 0 else 17)
                fw.op(ACT, lambda: nc.scalar.copy(out=Vb[:, vslot, :], in_=P1[:, 1664:1792]), reads=[BK[3]],
                      writes=[b_mb], partial=True)
            fw.dma(Vs[s, 0:npair, n * 128:(n + 1) * 128, :].rearrange("c p e -> p c e"), vt[p][:, 0:npair, :],
                   key=b_vt[p], reads=[b_vt[p]], writes=b_Vs[s][0:npair], partial=True)

        def half_ranges(k, which):
            s, n, edge, ranges, chunks = info(k)
            if s == 0:
                return [(0, 16)] if which == "h0" else [(16, 26)]
            if which == "h0":
                return [(8, 16)]
            return [(24, 26)] if edge else []

        def s2a(k, which):
            S_ = SETS[k % 2]
            sq, qn, s26 = S_["sq"], S_["qn"], S_["s26"]
            hx = 0 if which == "h0" else 1
            b_sq, b_qn, b_s26 = S_["b_sq"][hx], S_["b_qn"][hx], S_["b_s26"][hx]
            for (h0, h1) in half_ranges(k, which):
                H = h1 - h0
                c0, c1 = h0 * 64, h1 * 64
                bks = bks_of(c0, c1)
                fw.op(ACT, lambda: nc.scalar.activation(out=sq[:, c0:c1], in_=P1[:, c0:c1], func=AF.Square),
                      reads=bks, writes=[b_sq])
                fw.op(DVE, lambda: nc.vector.tensor_reduce(out=s26[:, 0, h0:h1],
                                                           in_=sq[:, c0:c1].rearrange("p (h d) -> p h d", d=64),
                                                           axis=AX.X, op=ALU.add), reads=[b_sq],
                      writes=[b_s26])
                fw.op(ACT, lambda: nc.scalar.activation(out=s26[:, 1, h0:h1], in_=s26[:, 0, h0:h1], func=AF.Sqrt,
                                                        scale=1.0 / 64, bias=eps_t), reads=[b_s26, b_const],
                      writes=[b_s26])
                fw.op(DVE, lambda: nc.vector.reciprocal(out=s26[:, 2, h0:h1], in_=s26[:, 1, h0:h1]),
                      reads=[b_s26], writes=[b_s26])
                fw.op(DVE, lambda: nc.vector.tensor_tensor(
                    out=qn[:, c0:c1].rearrange("p (h d) -> p h d", d=64),
                    in0=P1[:, c0:c1].rearrange("p (h d) -> p h d", d=64),
                    in1=s26[:, 2, h0:h1].unsqueeze(2).to_broadcast([128, H, 64]), op=ALU.mult),
                    reads=bks + [b_s26], writes=[b_qn])

        TYPES = {"h0": [(0, 0, 8), (1, 8, 16)], "h1": [(2, 16, 24), (3, 24, 26)]}

        def rope_ew(k):
            s, n, edge, ranges, chunks = info(k)
            i = k % NX
            cs, b_cs = CS[i], b_CS[i]
            S_ = SETS[k % 2]
            ra, qn, rb, qr = S_["sq"], S_["qn"], S_["rb"], S_["qr"]
            gc, gs, b_tab = S_["gc"], S_["gs"], S_["b_tab"]
            fw.op(DVE, lambda: nc.vector.tensor_tensor(
                out=gc.rearrange("p y (t d) -> p y t d", t=2), in0=gq.rearrange("p y (t d) -> p y t d", t=2),
                in1=cs[:, 0:32].unsqueeze(1).unsqueeze(1).to_broadcast([128, 4, 2, 32]), op=ALU.mult),
                reads=[b_const, b_cs], writes=[b_tab])
            fw.op(DVE, lambda: nc.vector.tensor_tensor(
                out=gs[:, 0, :, :], in0=gq[:, :, 32:64],
                in1=cs[:, 32:64].unsqueeze(1).to_broadcast([128, 4, 32]), op=ALU.mult),
                reads=[b_const, b_cs], writes=[b_tab], partial=True)
            fw.op(DVE, lambda: nc.vector.tensor_tensor(
                out=gs[:, 1, :, :], in0=gq[:, :, 0:32],
                in1=cs[:, 32:64].unsqueeze(1).to_broadcast([128, 4, 32]), op=ALU.mult),
                reads=[b_const, b_cs], writes=[b_tab], partial=True)
            for hx, which in enumerate(("h0", "h1")):
                b_ra, b_qn, b_rb, b_qr = S_["b_sq"][hx], S_["b_qn"][hx], S_["b_rb"][hx], S_["b_qr"]
                hr = half_ranges(k, which)
                if not hr:
                    continue
                first = True
                for (ty, t0_, t1_) in TYPES[which]:
                    for (h0, h1) in hr:
                        a0, a1 = max(h0, t0_), min(h1, t1_)
                        if a0 >= a1:
                            continue
                        H = a1 - a0
                        c0, c1 = a0 * 64, a1 * 64
                        v4 = lambda t: t[:, c0:c1].rearrange("p (h t d) -> p h t d", t=2, d=32)
                        gcb = gc[:, ty, :].rearrange("p (t d) -> p t d", t=2).unsqueeze(1).to_broadcast(
                            [128, H, 2, 32])
                        g1b = gs[:, 0, ty, :].unsqueeze(1).to_broadcast([128, H, 32])
                        g2b = gs[:, 1, ty, :].unsqueeze(1).to_broadcast([128, H, 32])
                        fw.op(DVE, lambda: nc.vector.tensor_tensor(out=v4(ra), in0=v4(qn), in1=gcb, op=ALU.mult),
                              reads=[b_qn, b_tab], writes=[b_ra], partial=(not first))
                        fw.op(DVE, lambda: nc.vector.tensor_tensor(out=v4(rb)[:, :, 0, :], in0=v4(qn)[:, :, 1, :],
                                                                   in1=g1b, op=ALU.mult), reads=[b_qn, b_tab],
                              writes=[b_rb], partial=(not first))
                        fw.op(DVE, lambda: nc.vector.tensor_tensor(out=v4(rb)[:, :, 1, :], in0=v4(qn)[:, :, 0, :],
                                                                   in1=g2b, op=ALU.mult), reads=[b_qn, b_tab],
                              writes=[b_rb], partial=True)
                        first = False
                for (h0, h1) in hr:
                    c0, c1 = h0 * 64, h1 * 64
                    v4 = lambda t: t[:, c0:c1].rearrange("p (h t d) -> p h t d", t=2, d=32)
                    fw.op(DVE, lambda: nc.vector.tensor_tensor(out=v4(qr)[:, :, 0, :], in0=v4(ra)[:, :, 0, :],
                                                                in1=v4(rb)[:, :, 0, :], op=ALU.subtract),
                          reads=[b_ra, b_rb], writes=[b_qr], partial=True)
                    fw.op(DVE, lambda: nc.vector.tensor_tensor(out=v4(qr)[:, :, 1, :], in0=v4(ra)[:, :, 1, :],
                                                                in1=v4(rb)[:, :, 1, :], op=ALU.add),
                          reads=[b_ra, b_rb], writes=[b_qr], partial=True)

        def rope_T(k):
            s, n, edge, ranges, chunks = info(k)
            S_ = SETS[k % 2]
            qr, b_qr = S_["qr"], S_["b_qr"]
            ti = n % 2
            gi = (k // 2) % 2
            for ci_, c in enumerate(chunks):
                fw.op(PE, lambda c=c: nc.tensor.transpose(out=qkT[:, c * 128:(c + 1) * 128],
                                                          in_=qr[:, c * 128:(c + 1) * 128], identity=ident),
                      reads=[b_qr, b_const], writes=[BK[6], BK[7]], signal=(ci_ == len(chunks) - 1),
                      partial=(ci_ > 0))
            tsl = slice(ti * 128, (ti + 1) * 128)
            if s == 0:
                fw.op(ACT, lambda: nc.scalar.copy(out=qkt[gi][:, 0:8, tsl],
                                                  in_=qkT[:, 0:1024].rearrange("p (c t) -> p c t", t=128)),
                      reads=[BK[6], BK[7]], writes=[b_qkt[gi]], partial=True)
                fw.op(ACT, lambda: nc.scalar.copy(out=qbT[:, :, n * 128:(n + 1) * 128],
                                                  in_=qkT[:, 1024:1536].rearrange("p (c t) -> p c t", t=128)),
                      reads=[BK[6], BK[7]], writes=[b_mb], partial=True)
                fw.op(ACT, lambda: nc.scalar.copy(out=kbT[:, n * 128:(n + 1) * 128], in_=qkT[:, 1536:1664]),
                      reads=[BK[6], BK[7]], writes=[b_mb], partial=True)
            else:
                fw.op(ACT, lambda: nc.scalar.copy(out=qkt[gi][:, 4:8, tsl],
                                                  in_=qkT[:, 512:1024].rearrange("p (c t) -> p c t", t=128)),
                      reads=[BK[6], BK[7]], writes=[b_qkt[gi]], partial=True)
                if edge:
                    kc0 = 2048 if n == 0 else 2176
                    fw.op(ACT, lambda: nc.scalar.copy(out=kbT[:, kc0:kc0 + 128], in_=qkT[:, 1536:1664]),
                          reads=[BK[6], BK[7]], writes=[b_mb], partial=True)
            if ti == 1:
                g0 = (n // 2) * 256
                for (c0, c1) in (((0, 4), (4, 8)) if s == 0 else ((4, 8),)):
                    fw.dma(QKs[s, c0:c1, :, g0:g0 + 256].rearrange("c p t -> p c t"),
                           qkt[gi][:, c0:c1, :], key=b_qkt[gi], reads=[b_qkt[gi]], writes=b_QKs[s][c0:c1],
                           partial=True)

        for k in range(min(3, T)):
            load(k)
        xnorm(0)
        xT(0)
        if T > 1:
            xnorm(1)
        for k in range(T):
            if k + 3 < T:
                load(k + 3)
            proj(k, "h0")
            if k + 1 < T:
                xT(k + 1)
            s2a(k, "h0")
            proj(k, "h1")
            proj(k, "va")
            vevac(k)
            s2a(k, "h1")
            if k + 2 < T:
                xnorm(k + 2)
            if k >= 1:
                rope_T(k - 1)
            rope_ew(k)
        rope_T(T - 1)
        fw.barrier()

    def phaseB(pc):
        arena_reset()
        halo = pc["halo"]
        pi = pc["pi"]
        qbT, kbT, Vb, b_mb = alloc_mixb()
        UB = []
        offs = []
        for u in range(2):
            offs.append(arena_state["off"])
            UB.append(dict(
                qT=ar("qT", [128, NT], BF16), kT=ar("kT", [128, 2, NT], BF16),
                Vn=ar("Vn", [128, 18, 128], BF16), V4=ar("V4", [128, 4, 6, 128], BF16),
                V16=ar("V16", [128, 16, 2, 128], BF16), b=fw.buf(f"ub{u}")))
        acc = ar("acc", [128, 2, NT], F32)
        b_acc = fw.buf("acc")
        NPT = 4
        Pt = [ar("Pt", [128, 2, 2, 128], BF16) for _ in range(NPT)]
        b_Pt = [fw.buf(f"Pt{i}") for i in range(NPT)]
        Ptb = [ar("Ptb", [128, 512], BF16) for _ in range(NPT)]
        b_Ptb = [fw.buf(f"Ptb{i}") for i in range(NPT)]
        dn = [ar("dn", [128, 512], F32) for _ in range(2)]
        b_dn = [fw.buf(f"dn{i}") for i in range(2)]
        LAG = 2

        def load_unitA(c):
            U = UB[c % 2]
            b = U["b"]
            rd = [b_QKs[0][c], b_QKs[0][4 + c], b_Vs[0][c]]
            fw.dma(U["qT"], QKs[0, c], key=b, reads=rd, writes=[b])
            fw.dma(U["kT"][:, 0, :], QKs[0, 4 + c], key=b, reads=rd, writes=[b], partial=True)
            for n4 in range(4):
                fw.dma(U["Vn"][:, 4 * n4:4 * n4 + 4, :],
                       Vs[0, c, 512 * n4:512 * n4 + 512, :].rearrange("(n p) e -> p n e", p=128), key=b, reads=rd,
                       writes=[b], partial=True)
            for tb in range(4):
                fw.dma(U["V4"][:, :, tb, :], Vs[0, c, 512 * tb:512 * tb + 512, :].rearrange("(p r) e -> p r e", r=4),
                       key=b, reads=rd, writes=[b], partial=True)
            fw.dma(U["V16"][:, :, 0, :], Vs[0, c].rearrange("(p r) e -> p r e", r=16), key=b, reads=rd, writes=[b],
                   partial=True)
            if halo:
                rd = [b_QKs[1][4 + c], b_Vs[1][c]]
                fw.dma(U["kT"][:, 1, :], QKs[1, 4 + c], key=b, reads=rd, writes=[b], partial=True)
                fw.dma(U["Vn"][:, 16, :], Vs[1, c, 0:128, :], key=b, reads=rd, writes=[b], partial=True)
                fw.dma(U["Vn"][:, 17, :], Vs[1, c, 1920:2048, :], key=b, reads=rd, writes=[b], partial=True)
                fw.dma(U["V4"][:, :, 4, :], Vs[1, c, 0:512, :].rearrange("(p r) e -> p r e", r=4), key=b, reads=rd,
                       writes=[b], partial=True)
                fw.dma(U["V4"][:, :, 5, :], Vs[1, c, 1536:2048, :].rearrange("(p r) e -> p r e", r=4), key=b,
                       reads=rd, writes=[b], partial=True)
                fw.dma(U["V16"][:, :, 1, :], Vs[1, c].rearrange("(p r) e -> p r e", r=16), key=b, reads=rd,
                       writes=[b], partial=True)

        def load_unitB():
            b = b_mb
            rd = b_QKs[0][8:13] + [b_Vs[0][4]]
            fw.dma(qbT, QKs[0, 8:12].rearrange("c p t -> p c t"), key=b, reads=rd, writes=[b])
            fw.dma(kbT[:, 0:NT], QKs[0, 12], key=b, reads=rd, writes=[b], partial=True)
            for n4 in range(4):
                fw.dma(Vb[:, 4 * n4:4 * n4 + 4, :],
                       Vs[0, 4, 512 * n4:512 * n4 + 512, :].rearrange("(n p) e -> p n e", p=128), key=b, reads=rd,
                       writes=[b], partial=True)
            if halo:
                rd = [b_QKs[1][12], b_Vs[1][4]]
                fw.dma(kbT[:, 2048:2176], QKs[1, 12, :, 0:128], key=b, reads=rd, writes=[b], partial=True)
                fw.dma(kbT[:, 2176:2304], QKs[1, 12, :, 1920:2048], key=b, reads=rd, writes=[b], partial=True)
                fw.dma(Vb[:, 16, :], Vs[1, 4, 0:128, :], key=b, reads=rd, writes=[b], partial=True)
                fw.dma(Vb[:, 17, :], Vs[1, 4, 1920:2048, :], key=b, reads=rd, writes=[b], partial=True)

        def unitA(c):
            U = UB[c % 2]
            b = U["b"]
            qT, kT = U["qT"], U["kT"]
            blocks = []
            for (d, Td) in ((1, 16), (4, 4), (16, 1)):
                Ld = NT // d
                for r in range(d):
                    for bb in range(-1, Td):
                        if bb == -1:
                            l0, nq, qi0 = 0, 64, 64
                        elif bb == Td - 1:
                            l0, nq, qi0 = Ld - 64, 64, 0
                        else:
                            l0, nq, qi0 = 128 * bb + 64, 128, 0
                        qsl = slice(d * l0 + r, d * (l0 + nq - 1) + r + 1, d)
                        tl = []
                        for wt, tb in enumerate((bb, bb + 1)):
                            if 0 <= tb < Td:
                                kp = (0, 128)
                                slot, tidx = 0, tb
                                if d == 1:
                                    Vt = U["Vn"][:, tb, :]
                                elif d == 4:
                                    Vt = U["V4"][:, r, tb, :]
                                else:
                                    Vt = U["V16"][:, r, 0, :]
                                dl = ones64
                            elif not halo:
                                continue
                            elif tb == -1:
                                kp = (64, 128)
                                slot, tidx = 1, Td - 1
                                Vt = (U["Vn"][:, 17, :] if d == 1 else U["V4"][:, r, 5, :] if d == 4
                                      else U["V16"][:, r, 1, :])
                                dl = hvb[:, pi, 1, :]
                            else:
                                kp = (0, 64)
                                slot, tidx = 1, 0
                                Vt = (U["Vn"][:, 16, :] if d == 1 else U["V4"][:, r, 4, :] if d == 4
                                      else U["V16"][:, r, 1, :])
                                dl = hvb[:, pi, 0, :]
                            k0 = d * (128 * tidx + kp[0]) + r
                            ksl = slice(k0, k0 + d * (kp[1] - kp[0] - 1) + 1, d)
                            tl.append((wt, kp, slot, ksl, Vt, dl))
                        blocks.append((nq, qi0, qsl, tl, d == 1))

            def views(bi):
                sbk = 2 * (bi % 3)
                obk = 6 + bi % 2
                S4 = bank(sbk, 2).rearrange("p (h w q) -> p h w q", h=2, w=4)
                O2 = bank(obk)[:, 0:256].rearrange("p (a q) -> p a q", a=2)
                return sbk, obk, S4, O2, bi % NPT

            def front(bi):
                nq, qi0, qsl, tl, first_br = blocks[bi]
                sbk, obk, S4, O2, pti = views(bi)
                P4 = Pt[pti]
                nS = len(tl) * 2
                si = 0
                for (wt, kp, slot, ksl, Vt, dl) in tl:
                    for hh in range(2):
                        si += 1
                        fw.op(PE, lambda wt=wt, kp=kp, slot=slot, ksl=ksl, hh=hh: nc.tensor.matmul(
                            S4[kp[0]:kp[1], hh, wt, 0:nq], lhsT=kT[64 * hh:64 * hh + 64, slot, ksl],
                            rhs=qT[64 * hh:64 * hh + 64, qsl], start=True, stop=True),
                            reads=[b], writes=[BK[sbk], BK[sbk + 1]], signal=(si == nS), partial=(si > 1))
                ME, me = (DVE, nc.vector)
                if len(tl) == 2 and all(t[1] == (0, 128) for t in tl):
                    fw.op(ACT, lambda: nc.scalar.activation(
                        out=P4[:, :, :, 0:nq], in_=S4[:, :, 0:2, 0:nq], func=AF.Exp, scale=0.125),
                        reads=[BK[sbk], BK[sbk + 1]], writes=[b_Pt[pti]])
                    fw.op(ME, lambda: me.tensor_tensor(
                        out=P4[:, :, :, 0:nq], in0=P4[:, :, :, 0:nq],
                        in1=mask[:, :, qi0:qi0 + nq].unsqueeze(1).to_broadcast([128, 2, 2, nq]),
                        op=ALU.mult), reads=[b_Pt[pti], b_const], writes=[b_Pt[pti]], partial=True)
                else:
                    for ti_, (wt, kp, slot, ksl, Vt, dl) in enumerate(tl):
                        kpn = kp[1] - kp[0]
                        fw.op(ACT, lambda wt=wt, kp=kp: nc.scalar.activation(
                            out=P4[kp[0]:kp[1], :, wt, 0:nq], in_=S4[kp[0]:kp[1], :, wt, 0:nq], func=AF.Exp,
                            scale=0.125), reads=[BK[sbk], BK[sbk + 1]], writes=[b_Pt[pti]], partial=(ti_ > 0))
                        fw.op(ME, lambda wt=wt, kp=kp, kpn=kpn: me.tensor_tensor(
                            out=P4[kp[0]:kp[1], :, wt, 0:nq], in0=P4[kp[0]:kp[1], :, wt, 0:nq],
                            in1=mask[kp[0]:kp[1], wt, qi0:qi0 + nq].unsqueeze(1).to_broadcast([kpn, 2, nq]),
                            op=ALU.mult), reads=[b_Pt[pti], b_const], writes=[b_Pt[pti]], partial=True)

            def back(bi):
                nq, qi0, qsl, tl, first_br = blocks[bi]
                sbk, obk, S4, O2, pti = views(bi)
                P4 = Pt[pti]
                nP = len(tl) * 4
                pi_ = 0
                for ti_, (wt, kp, slot, ksl, Vt, dl) in enumerate(tl):
                    for hh in range(2):
                        pi_ += 1
                        fw.op(PE, lambda wt=wt, kp=kp, Vt=Vt, hh=hh, ti_=ti_: nc.tensor.matmul(
                            O2[64 * hh:64 * hh + 64, 0, 0:nq], lhsT=Vt[kp[0]:kp[1], 64 * hh:64 * hh + 64],
                            rhs=P4[kp[0]:kp[1], hh, wt, 0:nq], start=(ti_ == 0), stop=False,
                            skip_group_check=True),
                            reads=[b, b_Pt[pti]], writes=[BK[obk]], signal=False, partial=(pi_ > 1))
                        pi_ += 1
                        fw.op(PE, lambda wt=wt, kp=kp, dl=dl, hh=hh, ti_=ti_: nc.tensor.matmul(
                            O2[64 * hh:64 * hh + 64, 1, 0:nq], lhsT=dl[kp[0]:kp[1], :],
                            rhs=P4[kp[0]:kp[1], hh, wt, 0:nq], start=False, stop=(ti_ == len(tl) - 1),
                            skip_group_check=True),
                            reads=[b_const, b_Pt[pti]], writes=[BK[obk]], signal=(pi_ == nP), partial=True)
                if first_br:
                    fw.op(DVE, lambda: nc.vector.tensor_copy(out=acc[:, :, qsl], in_=O2[:, :, 0:nq]),
                          reads=[BK[obk]], writes=[b_acc], partial=True)
                else:
                    fw.op(DVE, lambda: nc.vector.tensor_tensor(out=acc[:, :, qsl], in0=O2[:, :, 0:nq],
                                                               in1=acc[:, :, qsl], op=ALU.add),
                          reads=[BK[obk], b_acc], writes=[b_acc], partial=True)

            nb = len(blocks)
            for i in range(nb + LAG):
                if i < nb:
                    front(i)
                if i - LAG >= 0:
                    back(i - LAG)
            fw.op(ACT, lambda: nc.scalar.activation(out=acc[:, 1, :], in_=acc[:, 1, :], func=AF.Ln), reads=[b_acc],
                  writes=[b_acc], partial=True)
            fw.op(ACT, lambda: nc.scalar.activation(out=acc[:, 1, :], in_=acc[:, 1, :], func=AF.Exp, scale=-1.0),
                  reads=[b_acc], writes=[b_acc], partial=True)
            fw.op(DVE, lambda: nc.vector.tensor_tensor(out=AT[:, c, :], in0=acc[:, 0, :], in1=acc[:, 1, :],
                                                       op=ALU.mult), reads=[b_acc], writes=[b_AT[c]])

        def unitB():
            b = b_mb
            items = []
            for bb in range(16):
                for g in range(2):
                    tl = []
                    for wt, tb in ((0, bb - 1), (None, bb), (1, bb + 1)):
                        if 0 <= tb < 16:
                            tl.append((wt, slice(128 * tb, 128 * tb + 128), Vb[:, tb, :], ones64))
                        elif not halo:
                            continue
                        elif tb == -1:
                            tl.append((wt, slice(2176, 2304), Vb[:, 17, :], hvb[:, pi, 1, :]))
                        else:
                            tl.append((wt, slice(2048, 2176), Vb[:, 16, :], hvb[:, pi, 0, :]))
                    for ti_, t in enumerate(tl):
                        items.append((bb, g, ti_, len(tl), t))

            def vw(i, bb):
                sbk = i % 3
                nbk = 3 + 2 * (bb % 2)
                return sbk, bank(sbk), nbk, bank(nbk), bank(nbk + 1), i % NPT

            def front(i):
                bb, g, ti_, ntl, (wt, ksl, Vt, dl) = items[i]
                sbk, Sb, nbk, NB, DB, pbi = vw(i, bb)
                fw.op(PE, lambda: nc.tensor.matmul(
                    Sb, lhsT=kbT[64 * g:64 * g + 64, ksl], rhs=qbT[64 * g:64 * g + 64, :, 128 * bb:128 * bb + 128],
                    start=True, stop=True), reads=[b], writes=[BK[sbk]])
                fw.op(ACT, lambda: nc.scalar.activation(out=Ptb[pbi], in_=Sb, func=AF.Exp, scale=0.125),
                      reads=[BK[sbk]], writes=[b_Ptb[pbi]])
                if wt is not None:
                    ME, me = (DVE, nc.vector)
                    fw.op(ME, lambda: me.tensor_tensor(
                        out=Ptb[pbi].rearrange("p (c q) -> p c q", c=4),
                        in0=Ptb[pbi].rearrange("p (c q) -> p c q", c=4),
                        in1=mask[:, wt, :].unsqueeze(1).to_broadcast([128, 4, 128]), op=ALU.mult),
                        reads=[b_Ptb[pbi], b_const], writes=[b_Ptb[pbi]])

            def back(i):
                bb, g, ti_, ntl, (wt, ksl, Vt, dl) = items[i]
                sbk, Sb, nbk, NB, DB, pbi = vw(i, bb)
                last = (ti_ == ntl - 1)
                first_of_block = (g == 0 and ti_ == 0)
                fw.op(PE, lambda: nc.tensor.matmul(
                    NB[64 * g:64 * g + 64, :], lhsT=Vt[:, 64 * g:64 * g + 64], rhs=Ptb[pbi], start=(ti_ == 0),
                    stop=last), reads=[b, b_Ptb[pbi]], writes=[BK[nbk]], signal=False,
                    partial=(not first_of_block))
                fw.op(PE, lambda: nc.tensor.matmul(
                    DB[64 * g:64 * g + 64, :], lhsT=dl, rhs=Ptb[pbi], start=(ti_ == 0), stop=last),
                    reads=[b_const, b_Ptb[pbi]], writes=[BK[nbk + 1]], signal=True, partial=(not first_of_block))
                if g == 1 and last:
                    di = bb % 2
                    for c4 in range(4):
                        fw.op(ACT, lambda c4=c4: nc.scalar.activation(
                            out=dn[di][:, c4 * 128:(c4 + 1) * 128], in_=DB[:, c4 * 128:(c4 + 1) * 128], func=AF.Ln,
                            bias=es[:, c4:c4 + 1]), reads=[BK[nbk + 1], b_const], writes=[b_dn[di]],
                            partial=(c4 > 0))
                    fw.op(ACT, lambda: nc.scalar.activation(out=dn[di], in_=dn[di], func=AF.Exp, scale=-1.0),
                          reads=[b_dn[di]], writes=[b_dn[di]])
                    fw.op(DVE, lambda: nc.vector.tensor_tensor(
                        out=AT[:, 4:8, 128 * bb:128 * bb + 128], in0=NB.rearrange("p (c q) -> p c q", c=4),
                        in1=dn[di].rearrange("p (c q) -> p c q", c=4), op=ALU.mult), reads=[BK[nbk], b_dn[di]],
                        writes=b_AT[4:8], partial=True)

            ni = len(items)
            for i in range(ni + LAG):
                if i < ni:
                    front(i)
                if i - LAG >= 0:
                    back(i - LAG)

        load_unitA(0)
        load_unitA(1)
        unitB()
        for c in range(4):
            if c >= 1 and c + 1 < 4:
                load_unitA(c + 1)
            unitA(c)
        fw.barrier()

    def phaseC(pc):
        arena_reset()
        x1 = [ar("x1", [128, 4, D], F32) for _ in range(2)]
        b_x1 = [[fw.buf(f"x1_{u}_{t}") for t in range(4)] for u in range(2)]
        XR = [ar("xr", [128, D], F32) for _ in range(2)]
        b_XR = [fw.buf(f"xr{i}") for i in range(2)]
        junk = ar("junk", [128, D], BF16)
        b_junk = fw.buf("junkc")
        ssq = ar("ssq", [128, 4], F32)
        b_ssq = fw.buf("ssqc")
        xn2 = [ar("xn2", [128, D], BF16) for _ in range(2)]
        b_xn2 = [fw.buf(f"xn2_{i}") for i in range(2)]
        h2T = [ar("h2T", [128, 8, 512], BF16) for _ in range(2)]
        b_h2T = [fw.buf(f"h2T{i}") for i in range(2)]
        actT = ar("actT", [128, NJ, 512], BF16)
        b_actT = [fw.buf(f"actT{j}") for j in range(NJ)]
        sg = [ar("sg", [128, 512], F32) for _ in range(2)]
        b_sg = [fw.buf(f"sg{i}") for i in range(2)]
        WG = [ar("WG", [128, 8, 128], BF16) for _ in range(3)]
        WU = [ar("WU", [128, 8, 128], BF16) for _ in range(3)]
        WD = [ar("WD", [128, D], BF16) for _ in range(3)]
        b_WG = [fw.buf(f"WG{i}") for i in range(3)]
        b_WU = [fw.buf(f"WU{i}") for i in range(3)]
        b_WD = [fw.buf(f"WD{i}") for i in range(3)]
        OS = [ar("OS", [128, D], F32) for _ in range(2)]
        b_OS = [fw.buf(f"OS{i}") for i in range(2)]
        pT = bank(2).bitcast(BF16)

        def ld_gu(j):
            i = j % 3
            fw.dma(WG[i], wgs[j], key=b_WG[i], reads=[b_wsc], writes=[b_WG[i]])
            fw.dma(WU[i], wus[j], key=b_WU[i], reads=[b_wsc], writes=[b_WU[i]])

        def ld_d(j):
            i = j % 3
            fw.dma(WD[i], wds[j], key=b_WD[i], reads=[b_wsc], writes=[b_WD[i]])

        xcnt = [0]

        def c1_load(sp_, t):
            r0 = 512 * sp_ + 128 * t
            xi = (4 * sp_ + t) % 2
            fw.dma(XR[xi], pc["xsrc"](r0 // 128), key=b_XR[xi], writes=[b_XR[xi]])

        def c1_a(sp_, t):
            u = sp_ % 2
            r0 = 512 * sp_ + 128 * t
            tcs = slice(r0, r0 + 128)
            xi = (4 * sp_ + t) % 2
            xp_ = t % 2
            for half in range(2):
                for kk in range(8):
                    fw.op(PE, lambda kk=kk, half=half: nc.tensor.matmul(
                        bank(half), lhsT=AT[:, kk, tcs], rhs=woutb[:, kk, half * 512:(half + 1) * 512],
                        start=(kk == 0), stop=(kk == 7)), reads=[b_AT[kk], b_woutb], writes=[BK[half]],
                        signal=(kk == 7), partial=(kk > 0))
            fw.op(DVE, lambda: nc.vector.tensor_tensor(out=x1[u][:, t, :], in0=bank(0, 2), in1=XR[xi], op=ALU.add),
                  reads=[BK[0], BK[1], b_XR[xi]], writes=[b_x1[u][t]])
            fw.op(ACT, lambda: nc.scalar.activation(out=junk, in_=x1[u][:, t, :], func=AF.Square,
                                                    accum_out=ssq[:, 0:1]), reads=[b_x1[u][t]],
                  writes=[b_junk, b_ssq])
            fw.op(ACT, lambda: nc.scalar.activation(out=ssq[:, 1:2], in_=ssq[:, 0:1], func=AF.Sqrt,
                                                    scale=1.0 / D, bias=eps_t), reads=[b_ssq, b_const],
                  writes=[b_ssq])
            fw.op(DVE, lambda: nc.vector.reciprocal(out=ssq[:, 2:3], in_=ssq[:, 1:2]), reads=[b_ssq],
                  writes=[b_ssq])
            fw.op(ACT, lambda: nc.scalar.activation(out=xn2[xp_], in_=x1[u][:, t, :], func=AF.Copy,
                                                    scale=ssq[:, 2:3]),
                  reads=[b_x1[u][t], b_ssq], writes=[b_xn2[xp_]])

        def c1_b(sp_, t):
            u = sp_ % 2
            xp_ = t % 2
            for kk in range(8):
                fw.op(PE, lambda kk=kk: nc.tensor.transpose(out=pT[:, kk * 128:(kk + 1) * 128],
                                                            in_=xn2[xp_][:, kk * 128:(kk + 1) * 128],
                                                            identity=ident),
                      reads=[b_xn2[xp_], b_const], writes=[BK[2]], signal=(kk == 7), partial=(kk > 0))
            fw.op(ACT, lambda: nc.scalar.copy(out=h2T[u][:, :, 128 * t:128 * t + 128],
                                              in_=pT.rearrange("p (k q) -> p k q", k=8)),
                  reads=[BK[2]], writes=[b_h2T[u]], partial=(t > 0))

        def c2(sp_, j):
            u = sp_ % 2
            i = j % 3
            gb, ub = 4 + j % 2, 6 + j % 2
            for kk in range(8):
                fw.op(PE, lambda kk=kk: nc.tensor.matmul(bank(gb), lhsT=WG[i][:, kk, :], rhs=h2T[u][:, kk, :],
                                                         start=(kk == 0), stop=(kk == 7)),
                      reads=[b_WG[i], b_h2T[u]], writes=[BK[gb]], signal=(kk == 7), partial=(kk > 0))
            for kk in range(8):
                fw.op(PE, lambda kk=kk: nc.tensor.matmul(bank(ub), lhsT=WU[i][:, kk, :], rhs=h2T[u][:, kk, :],
                                                         start=(kk == 0), stop=(kk == 7)),
                      reads=[b_WU[i], b_h2T[u]], writes=[BK[ub]], signal=(kk == 7), partial=(kk > 0))
            si = j % 2
            fw.op(ACT, lambda: nc.scalar.activation(out=sg[si], in_=bank(gb), func=AF.Silu), reads=[BK[gb]],
                  writes=[b_sg[si]])
            fw.op(DVE, lambda: nc.vector.tensor_tensor(out=actT[:, j, :], in0=bank(ub), in1=sg[si],
                                                       op=ALU.mult), reads=[BK[ub], b_sg[si]],
                  writes=[b_actT[j]])

        def c3(sp_):
            for j in range(NJ):
                if j + 2 < NJ:
                    ld_d(j + 2)
                i = j % 3
                for t in range(4):
                    for half in range(2):
                        fw.op(PE, lambda t=t, half=half: nc.tensor.matmul(
                            bank(2 * t + half), lhsT=actT[:, j, 128 * t:128 * t + 128],
                            rhs=WD[i][:, half * 512:(half + 1) * 512], start=(j == 0), stop=(j == NJ - 1)),
                            reads=[b_actT[j], b_WD[i]], writes=[BK[2 * t + half]],
                            signal=(j == NJ - 1 or (t == 3 and half == 1)), partial=(j > 0))

        def c4(sp_):
            u = sp_ % 2
            for t in (2, 3, 0, 1):
                r0 = 512 * sp_ + 128 * t
                oi = t % 2
                fw.op(DVE, lambda: nc.vector.tensor_tensor(out=OS[oi], in0=bank(2 * t, 2), in1=x1[u][:, t, :],
                                                           op=ALU.add),
                      reads=[BK[2 * t], BK[2 * t + 1], b_x1[u][t]], writes=[b_OS[oi]])
                fw.dma(pc["out"](r0), OS[oi], key=b_OS[oi], reads=[b_OS[oi]])

        NSP = 4
        c1_load(0, 0)
        c1_load(0, 1)
        ld_gu(0)
        ld_gu(1)
        for t in range(4):
            c1_a(0, t)
            if t + 2 < 4:
                c1_load(0, t + 2)
            if t >= 1:
                c1_b(0, t - 1)
        c1_b(0, 3)
        for sp_ in range(NSP):
            nxt = sp_ + 1 < NSP
            sched_a = {2: 0, 6: 1, 10: 2, 14: 3}
            sched_b = {5: 0, 9: 1, 13: 2, 17: 3}
            if nxt:
                c1_load(sp_ + 1, 0)
                c1_load(sp_ + 1, 1)
            for j in range(NJ):
                if j + 2 < NJ:
                    ld_gu(j + 2)
                if j == NJ - 2:
                    ld_d(0)
                if j == NJ - 1:
                    ld_d(1)
                c2(sp_, j)
                if nxt and j in sched_a:
                    t = sched_a[j]
                    c1_a(sp_ + 1, t)
                    if t + 2 < 4:
                        c1_load(sp_ + 1, t + 2)
                if nxt and j in sched_b:
                    c1_b(sp_ + 1, sched_b[j])
            if nxt:
                ld_gu(0)
                ld_gu(1)
            c3(sp_)
            c4(sp_)
        fw.barrier()

    for ip, pc in enumerate(pieces):
        if sel is not None and ip not in sel:
            continue
        if "A" in phases:
            phaseA(pc)
        if "B" in phases:
            phaseB(pc)
        if "C" in phases:
            phaseC(pc)
    fw.finish()
    return nc


_CACHE = {}


def _host_consts():
    bf = ml_dtypes.bfloat16
    kj = np.arange(128)[:, None]
    qi = np.arange(128)[None, :]
    masks = np.stack([(kj >= qi), (kj <= qi)], axis=1).astype(np.float32).astype(bf)
    ident = np.eye(128, dtype=np.float32).astype(bf)
    return masks, ident


def _rope_table(pos):
    half = 32
    inv = (np.float32(10000.0) ** (np.float32(-2.0) * np.arange(half, dtype=np.float32) / np.float32(64))).astype(
        np.float32)
    ang = (pos.astype(np.float32)[:, None] * inv[None, :]).astype(np.float32)
    return np.concatenate([np.cos(ang), np.sin(ang)], axis=1).astype(np.float32)


def kernel(x_prompt, x_sample, attn_norm, w_in, qnorm_a, knorm_a, qnorm_b, knorm_b,
           sink_b, w_out, ffn_norm, w_gate, w_up, w_down):
    f32 = np.float32
    x_prompt = np.asarray(x_prompt, f32)
    x_sample = np.asarray(x_sample, f32)
    w_in = np.asarray(w_in, f32)[0]
    w_out = np.asarray(w_out, f32)[0]
    w_gate = np.ascontiguousarray(np.asarray(w_gate, f32)[0])
    w_up = np.ascontiguousarray(np.asarray(w_up, f32)[0])
    w_down = np.ascontiguousarray(np.asarray(w_down, f32)[0])
    QBo, KBo, VBo, VAo = 1536, 2048, 2176, 1024
    cols = list(range(0, 512)) + list(range(512, 1024))
    for c in range(4):
        for g in range(2):
            h = 4 * g + c
            cols += list(range(QBo + h * 64, QBo + h * 64 + 64))
    cols += list(range(KBo, KBo + 128)) + list(range(VBo, VBo + 128)) + list(range(VAo, VAo + 512))
    w_in_p = np.ascontiguousarray(w_in[:, cols])
    rows = list(range(512))
    for c in range(4):
        for g in range(2):
            h = 4 * g + c
            rows += list(range(512 + h * 64, 512 + h * 64 + 64))
    w_out_p = np.ascontiguousarray(w_out[rows, :])
    gA = np.ascontiguousarray(np.asarray(attn_norm, f32)[0].reshape(8, 128).T)
    gF = np.ascontiguousarray(np.asarray(ffn_norm, f32)[0].reshape(8, 128).T)
    gq1 = np.stack([np.asarray(qnorm_a, f32)[0], np.asarray(knorm_a, f32)[0],
                    np.asarray(qnorm_b, f32)[0], np.asarray(knorm_b, f32)[0]])
    gq = np.ascontiguousarray(np.broadcast_to(gq1[None, :, :], (128, 4, 64)))
    sk = np.asarray(sink_b, f32)[0]
    sinkT = np.zeros((128, 4), f32)
    for g in range(2):
        for c in range(4):
            sinkT[64 * g:64 * g + 64, c] = sk[4 * g + c]
    masks, ident = _host_consts()

    if "nc" not in _CACHE:
        _CACHE["nc"] = build_program()
    nc = _CACHE["nc"]

    in_maps = []
    p128 = np.arange(128)
    for core in range(NCORES):
        ps, q = core // 4, core % 4
        lo = q * 4096 - 1024
        xp = np.zeros((6144, D), f32)
        a, b = max(lo, 0), min(lo + 6144, 16384)
        xp[a - lo:b - lo] = x_prompt[ps, a:b]
        cst = np.zeros((80, 128, 64), f32)
        for n in range(16):
            cst[n] = _rope_table(n * 128 + p128)
        for pi in range(2):
            base = q * 4096 + 2048 * pi
            for n in range(16):
                cst[16 + 32 * pi + n] = _rope_table(base + n * 128 + p128)
                hidx = n * 128 + p128
                local = np.where(hidx < 1024, 2048 + hidx, hidx - 2048)
                cst[32 + 32 * pi + n] = _rope_table(np.maximum(base + local, 0))
        hv = np.ones((128, 2, 2, 64), f32)
        if q == 0:
            hv[:, 0, 1, :] = 0.0
        if q == 3:
            hv[:, 1, 0, :] = 0.0
        in_maps.append({
            "xs": np.ascontiguousarray(x_sample[4 * core:4 * core + 4]),
            "xp": xp, "cst": cst, "w_in": w_in_p, "w_out": w_out_p, "w_gate": w_gate, "w_up": w_up,
            "w_down": w_down, "gA": gA, "gF": gF, "gq": gq, "sinkT": sinkT, "masks": masks, "ident": ident,
            "hv": hv.astype(ml_dtypes.bfloat16),
        })
    res = run_bass_kernel_spmd(nc, in_maps, core_ids=list(range(NCORES)))
    y_prompt = np.zeros((2, 16384, D), f32)
    y_sample = np.zeros((32, NT, D), f32)
    for core in range(NCORES):
        r = res.results[core]
        ps, q = core // 4, core % 4
        y_prompt[ps, q * 4096:(q + 1) * 4096] = np.asarray(r["yp"], f32)
        y_sample[4 * core:4 * core + 4] = np.asarray(r["ys"], f32)
    return (y_prompt, y_sample)
```

```python
import numpy as np
import ml_dtypes
import concourse.bass as bass
import concourse.mybir as mybir
from concourse.bass_utils import run_bass_kernel_spmd

F32 = mybir.dt.float32
BF16 = mybir.dt.bfloat16
ALU = mybir.AluOpType
AF = mybir.ActivationFunctionType
AX = mybir.AxisListType

D = 1024
NT = 2048
DFF = 2816
NJ = 22
EPS = 1e-6
NCORES = 8


class Src:
    def __init__(self, nc, name, step):
        self.sem = nc.alloc_semaphore(name=name)
        self.count = 0
        self.step = step
        self.name = name


class Buf:
    __slots__ = ("name", "writes", "reads", "dsrc")

    def __init__(self, name):
        self.name = name
        self.writes = []
        self.reads = []
        self.dsrc = None


class Eng:
    def __init__(self, fw, name, eng, is_pe=False):
        self.name = name
        self.eng = eng
        self.src = Src(fw.nc, "e_" + name, 1)
        self.known = {}
        self.is_pe = is_pe


class FW:
    def __init__(self, nc):
        self.nc = nc
        self.pe = Eng(self, "pe", nc.tensor, is_pe=True)
        self.act = Eng(self, "act", nc.scalar)
        self.dve = Eng(self, "dve", nc.vector)
        self.pool = Eng(self, "pool", nc.gpsimd)
        self.sp = Eng(self, "sp", nc.sync)
        self.engs = [self.pe, self.act, self.dve, self.pool, self.sp]
        self.nbuf = 0
        self.dsrcs = []
        self.bufs = {}

    def buf(self, name=None):
        self.nbuf += 1
        if name is None:
            return Buf(f"b{self.nbuf}")
        if name not in self.bufs:
            self.bufs[name] = Buf(name)
        return self.bufs[name]

    def _collect(self, E, reads, writes):
        need = {}

        def add(tok):
            s, c = tok
            if E.is_pe and s is E.src:
                return
            if s.step == 16:
                c = s.count
            if need.get(s, 0) < c:
                need[s] = c
        for b in reads:
            for t in b.writes:
                add(t)
        for b in writes:
            for t in b.writes:
                add(t)
            for t in b.reads:
                add(t)
        for s, c in need.items():
            if E.known.get(s, 0) >= c:
                continue
            E.eng.wait_ge(s.sem, c)
            E.known[s] = c

    def _commit(self, tok, reads, writes, partial):
        for b in writes:
            if partial:
                b.writes.append(tok)
                if len(b.writes) > 24:
                    m = {}
                    for s, c in b.writes:
                        if m.get(s, 0) < c:
                            m[s] = c
                    b.writes = list(m.items())
            else:
                b.writes = [tok]
                b.reads = []
        for b in reads:
            b.reads.append(tok)
            if len(b.reads) > 24:
                m = {}
                for s, c in b.reads:
                    if m.get(s, 0) < c:
                        m[s] = c
                b.reads = list(m.items())

    def op(self, E, fn, reads=(), writes=(), signal=True, partial=False):
        self._collect(E, reads, writes)
        ins = fn()
        if signal:
            E.src.count += 1
            ins.then_inc(E.src.sem, 1)
            tok = (E.src, E.src.count)
        else:
            tok = (E.src, E.src.count + 1)
        self._commit(tok, reads, writes, partial)
        return ins

    def dma(self, out_ap, in_ap, key, reads=(), writes=(), partial=False):
        E = self.sp
        if key.dsrc is None:
            key.dsrc = Src(self.nc, "d_" + key.name, 16)
            self.dsrcs.append(key.dsrc)
        self._collect(E, reads, writes)
        ins = E.eng.dma_start(out=out_ap, in_=in_ap)
        key.dsrc.count += 16
        ins.then_inc(key.dsrc.sem, 16)
        tok = (key.dsrc, key.dsrc.count)
        self._commit(tok, reads, writes, partial)
        return ins

    def barrier(self):
        for E in self.engs:
            for F in self.engs:
                if F is E or F is self.sp:
                    continue
                c = F.src.count
                if c > 0 and E.known.get(F.src, 0) < c:
                    E.eng.wait_ge(F.src.sem, c)
                    E.known[F.src] = c
            for d in self.dsrcs:
                if d.count > 0 and E.known.get(d, 0) < d.count:
                    E.eng.wait_ge(d.sem, d.count)
                    E.known[d] = d.count

    def finish(self):
        E = self.sp
        for F in self.engs:
            if F is E:
                continue
            if F.src.count > 0:
                E.eng.wait_ge(F.src.sem, F.src.count)
        for d in self.dsrcs:
            if d.count > 0:
                E.eng.wait_ge(d.sem, d.count)


def build_program(sel=None, phases="ABC"):
    nc = bass.Bass("TRN2", target_bir_lowering=False)
    fw = FW(nc)
    PE, ACT, DVE, POOL = fw.pe, fw.act, fw.dve, fw.pool

    def din(name, shape, dt=F32):
        return nc.dram_tensor(name, list(shape), dt, kind="ExternalInput").ap()

    xs = din("xs", [4, NT, D])
    xp = din("xp", [6144, D])
    cst = din("cst", [80, 128, 64])
    w_in = din("w_in", [D, 2304])
    w_out = din("w_out", [D, D])
    w_gate = din("w_gate", [D, DFF])
    w_up = din("w_up", [D, DFF])
    w_down = din("w_down", [DFF, D])
    gA_d = din("gA", [128, 8])
    gF_d = din("gF", [128, 8])
    gq_d = din("gq", [128, 4, 64])
    sink_d = din("sinkT", [128, 4])
    mask_d = din("masks", [128, 2, 128], BF16)
    ident_d = din("ident", [128, 128], BF16)
    hv_d = din("hv", [128, 2, 2, 64], BF16)
    ys = nc.dram_tensor("ys", [4, NT, D], F32, kind="ExternalOutput").ap()
    yp = nc.dram_tensor("yp", [4096, D], F32, kind="ExternalOutput").ap()
    QKs = nc.dram_tensor("QKs", [2, 13, 128, NT], BF16, kind="Internal").ap()
    Vs = nc.dram_tensor("Vs", [2, 5, NT, 128], BF16, kind="Internal").ap()
    wgs = nc.dram_tensor("wgs", [NJ, 128, 8, 128], BF16, kind="Internal").ap()
    wus = nc.dram_tensor("wus", [NJ, 128, 8, 128], BF16, kind="Internal").ap()
    wds = nc.dram_tensor("wds", [NJ, 128, D], BF16, kind="Internal").ap()
    b_QKs = [[fw.buf(f"QKs{s}_{c}") for c in range(13)] for s in range(2)]
    b_Vs = [[fw.buf(f"Vs{s}_{c}") for c in range(5)] for s in range(2)]
    b_wsc = fw.buf("wscratch")

    def sb(name, shape, dt):
        return nc.alloc_sbuf_tensor("sb_" + name, list(shape), dt).ap()

    gA = sb("gA", [128, 8], F32)
    gF = sb("gF", [128, 8], F32)
    gq = sb("gq", [128, 4, 64], F32)
    mask = sb("mask", [128, 2, 128], BF16)
    ident = sb("ident", [128, 128], BF16)
    ones64 = sb("ones64", [128, 64], BF16)
    hvb = sb("hvb", [128, 2, 2, 64], BF16)
    es = sb("es", [128, 4], F32)
    eps_t = sb("eps", [128, 1], F32)
    winb = sb("winb", [128, 8, 2304], BF16)
    woutb = sb("woutb", [128, 8, D], BF16)
    AT = sb("AT", [128, 8, NT], BF16)
    b_const = fw.buf("const")
    b_winb = fw.buf("winb")
    b_woutb = fw.buf("woutb")
    b_AT = [fw.buf(f"AT{k}") for k in range(8)]

    PS = nc.alloc_psum_tensor("PS", [128, 4096], F32).ap()
    BK = [fw.buf(f"bank{i}") for i in range(8)]

    def bank(i, n=1):
        return PS[:, i * 512:(i + n) * 512]

    arena0 = (nc.sbuf_base + 63) // 64 * 64
    arena_state = {"off": 0}
    ARENA_BYTES = nc.sbuf_top - arena0

    def arena_reset():
        arena_state["off"] = 0

    uid = [0]

    def ar(name, shape, dt):
        nbytes = int(np.prod(shape[1:])) * (2 if dt == BF16 else 4)
        nbytes = (nbytes + 63) // 64 * 64
        off = arena_state["off"]
        assert off + nbytes <= ARENA_BYTES, (name, off, nbytes, ARENA_BYTES)
        arena_state["off"] = off + nbytes
        uid[0] += 1
        return nc.alloc_sbuf_tensor_at(f"{name}_{uid[0]}", list(shape), dt, offset=arena0 + off).ap()

    for dst, src in ((gA, gA_d), (gF, gF_d), (gq, gq_d), (mask, mask_d), (ident, ident_d),
                     (hvb, hv_d), (es, sink_d)):
        fw.dma(dst, src, key=b_const, writes=[b_const], partial=True)
    fw.op(POOL, lambda: nc.gpsimd.memset(ones64, 1.0), writes=[b_const], partial=True)
    fw.op(POOL, lambda: nc.gpsimd.memset(eps_t, EPS), writes=[b_const], partial=True)
    fw.op(ACT, lambda: nc.scalar.activation(out=es, in_=es, func=AF.Exp), reads=[b_const], writes=[b_const],
          partial=True)

    arena_reset()
    NB0 = 6
    stg = [ar("stg", [128, 2304], F32) for _ in range(NB0)]
    b_stg = [fw.buf(f"stg{i}") for i in range(NB0)]
    stb = [ar("stb", [128, 1024], BF16) for _ in range(NB0)]
    b_stb = [fw.buf(f"stb{i}") for i in range(NB0)]
    jobs = []
    for k in range(8):
        jobs.append(("win", k))
    for k in range(8):
        jobs.append(("wout", k))
    for j in range(NJ):
        jobs.append(("g", j))
        jobs.append(("u", j))
        jobs.append(("d", j))

    def p0_load(n):
        kind, a_ = jobs[n]
        i = n % NB0
        if kind == "win":
            fw.dma(stg[i], w_in[a_ * 128:(a_ + 1) * 128, :], key=b_stg[i], writes=[b_stg[i]])
        elif kind == "wout":
            fw.dma(stg[i][:, 0:D], w_out[a_ * 128:(a_ + 1) * 128, :], key=b_stg[i], writes=[b_stg[i]])
        elif kind in ("g", "u"):
            wsrc = w_gate if kind == "g" else w_up
            fw.dma(stg[i][:, 0:1024].rearrange("p (k c) -> p k c", k=8),
                   wsrc[:, a_ * 128:(a_ + 1) * 128].rearrange("(k p) c -> p k c", p=128),
                   key=b_stg[i], writes=[b_stg[i]])
        else:
            fw.dma(stg[i][:, 0:1024], w_down[a_ * 128:(a_ + 1) * 128, :], key=b_stg[i], writes=[b_stg[i]])

    def p0_work(n):
        kind, a_ = jobs[n]
        i = n % NB0
        e3 = n % 3
        if kind == "win":
            if e3 == 0:
                fw.op(ACT, lambda: nc.scalar.activation(out=winb[:, a_, :], in_=stg[i], func=AF.Copy,
                                                        scale=gA[:, a_:a_ + 1]),
                      reads=[b_stg[i], b_const], writes=[b_winb], partial=True)
            else:
                fw.op(DVE, lambda: nc.vector.tensor_scalar(out=winb[:, a_, :], in0=stg[i],
                                                           scalar1=gA[:, a_:a_ + 1], scalar2=None, op0=ALU.mult),
                      reads=[b_stg[i], b_const], writes=[b_winb], partial=True)
        elif kind == "wout":
            if e3 == 0:
                fw.op(ACT, lambda: nc.scalar.copy(out=woutb[:, a_, :], in_=stg[i][:, 0:D]), reads=[b_stg[i]],
                      writes=[b_woutb], partial=True)
            else:
                fw.op(DVE, lambda: nc.vector.tensor_copy(out=woutb[:, a_, :], in_=stg[i][:, 0:D]),
                      reads=[b_stg[i]], writes=[b_woutb], partial=True)
        elif kind in ("g", "u"):
            wdst = wgs if kind == "g" else wus
            fw.op(DVE, lambda: nc.vector.tensor_tensor(
                out=stb[i].rearrange("p (k c) -> p k c", k=8),
                in0=stg[i][:, 0:1024].rearrange("p (k c) -> p k c", k=8),
                in1=gF.unsqueeze(2).to_broadcast([128, 8, 128]), op=ALU.mult),
                reads=[b_stg[i], b_const], writes=[b_stb[i]])
            fw.dma(wdst[a_].rearrange("p k c -> p (k c)"), stb[i], key=b_stb[i], reads=[b_stb[i]],
                   writes=[b_wsc], partial=True)
        else:
            fw.op(ACT, lambda: nc.scalar.copy(out=stb[i], in_=stg[i][:, 0:1024]), reads=[b_stg[i]],
                  writes=[b_stb[i]])
            fw.dma(wds[a_], stb[i], key=b_stb[i], reads=[b_stb[i]], writes=[b_wsc], partial=True)

    for n in range(min(NB0 - 1, len(jobs))):
        p0_load(n)
    for n in range(len(jobs)):
        if n + NB0 - 1 < len(jobs):
            p0_load(n + NB0 - 1)
        p0_work(n)
    fw.barrier()

    pieces = []
    for i in range(4):
        pieces.append(dict(halo=False, xsrc=lambda n, i=i: xs[i, n * 128:(n + 1) * 128, :],
                           hsrc=None, cs0=0, csh=None, out=lambda r0, i=i: ys[i, r0:r0 + 128, :], pi=0))
    for pi in range(2):
        mb = 1024 + 2048 * pi

        def hrow(n, mb=mb):
            hidx = n * 128
            return (mb + 2048 + hidx) if hidx < 1024 else (mb + hidx - 2048)
        pieces.append(dict(halo=True, xsrc=lambda n, mb=mb: xp[mb + n * 128: mb + (n + 1) * 128, :],
                           hsrc=lambda n, hrow=hrow: xp[hrow(n):hrow(n) + 128, :],
                           cs0=16 + 32 * pi, csh=32 + 32 * pi,
                           out=lambda r0, pi=pi: yp[2048 * pi + r0: 2048 * pi + r0 + 128, :], pi=pi))

    mixb_cache = {}

    def alloc_mixb():
        assert arena_state["off"] == 0
        if "t" not in mixb_cache:
            qbT = ar("qbT", [128, 4, NT], BF16)
            kbT = ar("kbT", [128, 2304], BF16)
            Vb = ar("Vb", [128, 18, 128], BF16)
            mixb_cache["t"] = (qbT, kbT, Vb, fw.buf("mixb"))
            mixb_cache["off"] = arena_state["off"]
        arena_state["off"] = mixb_cache["off"]
        return mixb_cache["t"]

    def phaseA(pc):
        arena_reset()
        qbT, kbT, Vb, b_mb = alloc_mixb()
        NX = 4
        XT = [ar("xt", [128, D], F32) for _ in range(NX)]
        CS = [ar("cs", [128, 64], F32) for _ in range(NX)]
        b_XT = [fw.buf(f"xt{i}") for i in range(NX)]
        b_CS = [fw.buf(f"cs{i}") for i in range(NX)]
        junk = ar("junk", [128, D], BF16)
        b_junk = fw.buf("junk")
        ssq = [ar("ssq", [128, 4], F32) for _ in range(2)]
        b_ssq = [fw.buf(f"ssq{i}") for i in range(2)]
        xn = [ar("xn", [128, D], BF16) for _ in range(2)]
        b_xn = [fw.buf(f"xn{i}") for i in range(2)]
        hT = [ar("hT", [128, 8, 128], BF16) for _ in range(2)]
        b_hT = [fw.buf(f"hT{i}") for i in range(2)]
        vt = [ar("vt", [128, 5, 128], BF16) for _ in range(2)]
        b_vt = [fw.buf(f"vt{i}") for i in range(2)]
        qkt = [ar("qkt", [128, 8, 256], BF16) for _ in range(2)]
        b_qkt = [fw.buf(f"qkt{i}") for i in range(2)]
        SETS = []
        for i in range(2):
            SETS.append(dict(sq=ar("sq", [128, 1664], F32), qn=ar("qn", [128, 1664], F32),
                             pa=ar("pa", [128, 1664], BF16), pb=ar("pb", [128, 1664], BF16),
                             s26=ar("s26", [128, 3, 26], F32), gc=ar("gc", [128, 4, 64], F32),
                             gs=ar("gs", [128, 2, 4, 32], F32), b_tab=fw.buf(f"tab{i}"),
                             b_sq=[fw.buf(f"sq{i}_{h}") for h in range(2)],
                             b_qn=[fw.buf(f"qn{i}_{h}") for h in range(2)],
                             b_pa=[fw.buf(f"pa{i}_{h}") for h in range(2)],
                             b_pb=[fw.buf(f"pb{i}_{h}") for h in range(2)],
                             b_s26=[fw.buf(f"s26{i}_{h}") for h in range(2)]))
        P1 = bank(0, 4)
        P2 = bank(4)
        pT = bank(5).bitcast(BF16)
        qkT = bank(6, 2)

        tiles = []
        for s in ([0, 1] if pc["halo"] else [0]):
            for n in range(16):
                tiles.append((s, n))
        T = len(tiles)

        def load(k):
            s, n = tiles[k]
            i = k % NX
            src = pc["xsrc"](n) if s == 0 else pc["hsrc"](n)
            fw.dma(XT[i], src, key=b_XT[i], writes=[b_XT[i]])
            ci = (pc["cs0"] if s == 0 else pc["csh"]) + n
            fw.dma(CS[i], cst[ci], key=b_CS[i], writes=[b_CS[i]])

        def info(k):
            s, n = tiles[k]
            edge = (s == 1 and n in (0, 15))
            if s == 0:
                ranges = [(0, 26)]
                chunks = list(range(13))
            else:
                ranges = [(8, 16)] + ([(24, 26)] if edge else [])
                chunks = [4, 5, 6, 7] + ([12] if edge else [])
            return s, n, edge, ranges, chunks

        def bks_of(c0, c1):
            return BK[c0 // 512:(c1 - 1) // 512 + 1]

        def xnorm(k):
            i = k % NX
            xt = XT[i]
            p = k % 2
            sq_, xn_ = ssq[p], xn[p]
            fw.op(ACT, lambda: nc.scalar.activation(out=junk, in_=xt, func=AF.Square, accum_out=sq_[:, 0:1]),
                  reads=[b_XT[i]], writes=[b_junk, b_ssq[p]])
            fw.op(ACT, lambda: nc.scalar.activation(out=sq_[:, 1:2], in_=sq_[:, 0:1], func=AF.Sqrt, scale=1.0 / D,
                                                    bias=eps_t), reads=[b_ssq[p], b_const], writes=[b_ssq[p]])
            fw.op(DVE, lambda: nc.vector.reciprocal(out=sq_[:, 2:3], in_=sq_[:, 1:2]), reads=[b_ssq[p]],
                  writes=[b_ssq[p]])
            fw.op(ACT, lambda: nc.scalar.activation(out=xn_, in_=xt, func=AF.Copy, scale=sq_[:, 2:3]),
                  reads=[b_XT[i], b_ssq[p]], writes=[b_xn[p]])

        def xT(k):
            p = k % 2
            xn_ = xn[p]
            for kk in range(8):
                fw.op(PE, lambda kk=kk: nc.tensor.transpose(out=pT[:, kk * 128:(kk + 1) * 128],
                                                            in_=xn_[:, kk * 128:(kk + 1) * 128], identity=ident),
                      reads=[b_xn[p], b_const], writes=[BK[5]], signal=(kk == 7), partial=(kk > 0))
            fw.op(ACT, lambda: nc.scalar.copy(out=hT[p].rearrange("p k t -> p (k t)"), in_=pT), reads=[BK[5]],
                  writes=[b_hT[p]])

        def proj(k, which):
            s, n, edge, ranges, chunks = info(k)
            p = k % 2
            if s == 0:
                allg = {"h0": [(0, 512, P1[:, 0:512], [BK[0]]), (512, 512, P1[:, 512:1024], [BK[1]])],
                        "h1": [(1024, 512, P1[:, 1024:1536], [BK[2]]), (1536, 256, P1[:, 1536:1792], [BK[3]])],
                        "va": [(1792, 512, P2, [BK[4]])]}
            else:
                allg = {"h0": [(512, 512, P1[:, 512:1024], [BK[1]])],
                        "h1": ([(1536, 256, P1[:, 1536:1792], [BK[3]])] if edge else []),
                        "va": [(1792, 512, P2, [BK[4]])]}
            for (c0, ncol, outap, bks) in allg[which]:
                for kk in range(8):
                    fw.op(PE, lambda kk=kk, c0=c0, ncol=ncol, outap=outap: nc.tensor.matmul(
                        outap, lhsT=hT[p][:, kk, :], rhs=winb[:, kk, c0:c0 + ncol], start=(kk == 0),
                        stop=(kk == 7)), reads=[b_hT[p], b_winb], writes=bks, signal=(kk == 7), partial=(kk > 0))

        def vevac(k):
            s, n, edge, ranges, chunks = info(k)
            p = k % 2
            fw.op(ACT, lambda: nc.scalar.copy(out=vt[p][:, 0:4, :].rearrange("p c e -> p (c e)"), in_=P2),
                  reads=[BK[4]], writes=[b_vt[p]])
            npair = 4
            if s == 0 or edge:
                vslot = n if s == 0 else (16 if n == 0 else 17)
                fw.op(ACT, lambda: nc.scalar.copy(out=Vb[:, vslot, :], in_=P1[:, 1664:1792]), reads=[BK[3]],
                      writes=[b_mb], partial=True)
            fw.dma(Vs[s, 0:npair, n * 128:(n + 1) * 128, :].rearrange("c p e -> p c e"), vt[p][:, 0:npair, :],
                   key=b_vt[p], reads=[b_vt[p]], writes=b_Vs[s][0:npair], partial=True)

        def half_ranges(k, which):
            s, n, edge, ranges, chunks = info(k)
            if s == 0:
                return [(0, 16)] if which == "h0" else [(16, 26)]
            if which == "h0":
                return [(8, 16)]
            return [(24, 26)] if edge else []

        def s2a(k, which):
            S_ = SETS[k % 2]
            sq, qn, s26 = S_["sq"], S_["qn"], S_["s26"]
            hx = 0 if which == "h0" else 1
            b_sq, b_qn, b_s26 = S_["b_sq"][hx], S_["b_qn"][hx], S_["b_s26"][hx]
            for (h0, h1) in half_ranges(k, which):
                H = h1 - h0
                c0, c1 = h0 * 64, h1 * 64
                bks = bks_of(c0, c1)
                fw.op(ACT, lambda: nc.scalar.activation(out=sq[:, c0:c1], in_=P1[:, c0:c1], func=AF.Square),
                      reads=bks, writes=[b_sq])
                fw.op(DVE, lambda: nc.vector.tensor_reduce(out=s26[:, 0, h0:h1],
                                                           in_=sq[:, c0:c1].rearrange("p (h d) -> p h d", d=64),
                                                           axis=AX.X, op=ALU.add), reads=[b_sq],
                      writes=[b_s26])
                fw.op(ACT, lambda: nc.scalar.activation(out=s26[:, 1, h0:h1], in_=s26[:, 0, h0:h1], func=AF.Sqrt,
                                                        scale=1.0 / 64, bias=eps_t), reads=[b_s26, b_const],
                      writes=[b_s26])
                fw.op(DVE, lambda: nc.vector.reciprocal(out=s26[:, 2, h0:h1], in_=s26[:, 1, h0:h1]),
                      reads=[b_s26], writes=[b_s26])
                fw.op(DVE, lambda: nc.vector.tensor_tensor(
                    out=qn[:, c0:c1].rearrange("p (h d) -> p h d", d=64),
                    in0=P1[:, c0:c1].rearrange("p (h d) -> p h d", d=64),
                    in1=s26[:, 2, h0:h1].unsqueeze(2).to_broadcast([128, H, 64]), op=ALU.mult),
                    reads=bks + [b_s26], writes=[b_qn])

        TYPES = {"h0": [(0, 0, 8), (1, 8, 16)], "h1": [(2, 16, 24), (3, 24, 26)]}

        def rope_ew(k):
            s, n, edge, ranges, chunks = info(k)
            i = k % NX
            cs, b_cs = CS[i], b_CS[i]
            S_ = SETS[k % 2]
            qn, pa, pb = S_["qn"], S_["pa"], S_["pb"]
            gc, gs, b_tab = S_["gc"], S_["gs"], S_["b_tab"]
            fw.op(DVE, lambda: nc.vector.tensor_tensor(
                out=gc.rearrange("p y (t d) -> p y t d", t=2), in0=gq.rearrange("p y (t d) -> p y t d", t=2),
                in1=cs[:, 0:32].unsqueeze(1).unsqueeze(1).to_broadcast([128, 4, 2, 32]), op=ALU.mult),
                reads=[b_const, b_cs], writes=[b_tab])
            fw.op(DVE, lambda: nc.vector.scalar_tensor_tensor(
                out=gs[:, 0, :, :], in0=gq[:, :, 32:64], scalar=-1.0,
                in1=cs[:, 32:64].unsqueeze(1).to_broadcast([128, 4, 32]), op0=ALU.mult, op1=ALU.mult),
                reads=[b_const, b_cs], writes=[b_tab], partial=True)
            fw.op(DVE, lambda: nc.vector.tensor_tensor(
                out=gs[:, 1, :, :], in0=gq[:, :, 0:32],
                in1=cs[:, 32:64].unsqueeze(1).to_broadcast([128, 4, 32]), op=ALU.mult),
                reads=[b_const, b_cs], writes=[b_tab], partial=True)
            for hx, which in enumerate(("h0", "h1")):
                b_qn, b_pa, b_pb = S_["b_qn"][hx], S_["b_pa"][hx], S_["b_pb"][hx]
                hr = half_ranges(k, which)
                if not hr:
                    continue
                first = True
                for (ty, t0_, t1_) in TYPES[which]:
                    for (h0, h1) in hr:
                        a0, a1 = max(h0, t0_), min(h1, t1_)
                        if a0 >= a1:
                            continue
                        H = a1 - a0
                        c0, c1 = a0 * 64, a1 * 64
                        v4 = lambda t: t[:, c0:c1].rearrange("p (h t d) -> p h t d", t=2, d=32)
                        gcb = gc[:, ty, :].rearrange("p (t d) -> p t d", t=2).unsqueeze(1).to_broadcast(
                            [128, H, 2, 32])
                        g1b = gs[:, 0, ty, :].unsqueeze(1).to_broadcast([128, H, 32])
                        g2b = gs[:, 1, ty, :].unsqueeze(1).to_broadcast([128, H, 32])
                        fw.op(DVE, lambda: nc.vector.tensor_tensor(out=v4(pa), in0=v4(qn), in1=gcb, op=ALU.mult),
                              reads=[b_qn, b_tab], writes=[b_pa], partial=(not first))
                        fw.op(DVE, lambda: nc.vector.tensor_tensor(out=v4(pb)[:, :, 0, :], in0=v4(qn)[:, :, 1, :],
                                                                   in1=g1b, op=ALU.mult), reads=[b_qn, b_tab],
                              writes=[b_pb], partial=(not first))
                        fw.op(DVE, lambda: nc.vector.tensor_tensor(out=v4(pb)[:, :, 1, :], in0=v4(qn)[:, :, 0, :],
                                                                   in1=g2b, op=ALU.mult), reads=[b_qn, b_tab],
                              writes=[b_pb], partial=True)
                        first = False

        def rope_T(k, which):
            s, n, edge, ranges, chunks = info(k)
            S_ = SETS[k % 2]
            hx = 0 if which == "h0" else 1
            pa, pb, b_pa, b_pb = S_["pa"], S_["pb"], S_["b_pa"][hx], S_["b_pb"][hx]
            ti = n % 2
            gi = (k // 2) % 2
            cl = [c for c in chunks if (c < 8) == (which == "h0")]
            if not cl:
                return
            for ci_, c in enumerate(cl):
                sl = c % 8
                osl = qkT[:, sl * 128:(sl + 1) * 128]
                fw.op(PE, lambda c=c, osl=osl: nc.tensor.matmul(osl, lhsT=pa[:, c * 128:(c + 1) * 128], rhs=ident,
                                                                start=True, stop=False),
                      reads=[b_pa, b_const], writes=[BK[6], BK[7]], signal=False, partial=(ci_ > 0))
                fw.op(PE, lambda c=c, osl=osl: nc.tensor.matmul(osl, lhsT=pb[:, c * 128:(c + 1) * 128], rhs=ident,
                                                                start=False, stop=True),
                      reads=[b_pb, b_const], writes=[BK[6], BK[7]], signal=(ci_ == len(cl) - 1), partial=True)
            tsl = slice(ti * 128, (ti + 1) * 128)
            if s == 0 and which == "h0":
                fw.op(ACT, lambda: nc.scalar.copy(out=qkt[gi][:, 0:8, tsl],
                                                  in_=qkT[:, 0:1024].rearrange("p (c t) -> p c t", t=128)),
                      reads=[BK[6], BK[7]], writes=[b_qkt[gi]], partial=True)
            elif s == 0:
                fw.op(ACT, lambda: nc.scalar.copy(out=qbT[:, :, n * 128:(n + 1) * 128],
                                                  in_=qkT[:, 0:512].rearrange("p (c t) -> p c t", t=128)),
                      reads=[BK[6], BK[7]], writes=[b_mb], partial=True)
                fw.op(ACT, lambda: nc.scalar.copy(out=kbT[:, n * 128:(n + 1) * 128], in_=qkT[:, 512:640]),
                      reads=[BK[6], BK[7]], writes=[b_mb], partial=True)
            elif which == "h0":
                fw.op(ACT, lambda: nc.scalar.copy(out=qkt[gi][:, 4:8, tsl],
                                                  in_=qkT[:, 512:1024].rearrange("p (c t) -> p c t", t=128)),
                      reads=[BK[6], BK[7]], writes=[b_qkt[gi]], partial=True)
            else:
                kc0 = 2048 if n == 0 else 2176
                fw.op(ACT, lambda: nc.scalar.copy(out=kbT[:, kc0:kc0 + 128], in_=qkT[:, 512:640]),
                      reads=[BK[6], BK[7]], writes=[b_mb], partial=True)
            if which == "h0" and ti == 1:
                g0 = (n // 2) * 256
                for (c0, c1) in (((0, 4), (4, 8)) if s == 0 else ((4, 8),)):
                    fw.dma(QKs[s, c0:c1, :, g0:g0 + 256].rearrange("c p t -> p c t"),
                           qkt[gi][:, c0:c1, :], key=b_qkt[gi], reads=[b_qkt[gi]], writes=b_QKs[s][c0:c1],
                           partial=True)

        for k in range(min(3, T)):
            load(k)
        xnorm(0)
        xT(0)
        if T > 1:
            xnorm(1)
        for k in range(T):
            if k + 3 < T:
                load(k + 3)
            proj(k, "h0")
            if k + 1 < T:
                xT(k + 1)
            s2a(k, "h0")
            proj(k, "h1")
            if k >= 1:
                rope_T(k - 1, "h0")
            proj(k, "va")
            vevac(k)
            s2a(k, "h1")
            if k >= 1:
                rope_T(k - 1, "h1")
            if k + 2 < T:
                xnorm(k + 2)
            rope_ew(k)
        rope_T(T - 1, "h0")
        rope_T(T - 1, "h1")
        fw.barrier()

    def phaseB(pc):
        arena_reset()
        halo = pc["halo"]
        pi = pc["pi"]
        qbT, kbT, Vb, b_mb = alloc_mixb()
        UB = []
        offs = []
        for u in range(2):
            offs.append(arena_state["off"])
            UB.append(dict(
                qT=ar("qT", [128, NT], BF16), kT=ar("kT", [128, 2, NT], BF16),
                Vn=ar("Vn", [128, 18, 128], BF16), V4=ar("V4", [128, 4, 6, 128], BF16),
                V16=ar("V16", [128, 16, 2, 128], BF16), b=fw.buf(f"ub{u}")))
        acc = ar("acc", [128, 2, NT], F32)
        b_acc = fw.buf("acc")
        NPT = 4
        Pt = [ar("Pt", [128, 2, 2, 128], BF16) for _ in range(NPT)]
        b_Pt = [fw.buf(f"Pt{i}") for i in range(NPT)]
        Ptb = [ar("Ptb", [128, 512], BF16) for _ in range(NPT)]
        b_Ptb = [fw.buf(f"Ptb{i}") for i in range(NPT)]
        dn = [ar("dn", [128, 512], F32) for _ in range(2)]
        b_dn = [fw.buf(f"dn{i}") for i in range(2)]
        LAG = 2

        def load_unitA(c):
            U = UB[c % 2]
            b = U["b"]
            rd = [b_QKs[0][c], b_QKs[0][4 + c], b_Vs[0][c]]
            fw.dma(U["qT"], QKs[0, c], key=b, reads=rd, writes=[b])
            fw.dma(U["kT"][:, 0, :], QKs[0, 4 + c], key=b, reads=rd, writes=[b], partial=True)
            for n4 in range(4):
                fw.dma(U["Vn"][:, 4 * n4:4 * n4 + 4, :],
                       Vs[0, c, 512 * n4:512 * n4 + 512, :].rearrange("(n p) e -> p n e", p=128), key=b, reads=rd,
                       writes=[b], partial=True)
            for tb in range(4):
                fw.dma(U["V4"][:, :, tb, :], Vs[0, c, 512 * tb:512 * tb + 512, :].rearrange("(p r) e -> p r e", r=4),
                       key=b, reads=rd, writes=[b], partial=True)
            fw.dma(U["V16"][:, :, 0, :], Vs[0, c].rearrange("(p r) e -> p r e", r=16), key=b, reads=rd, writes=[b],
                   partial=True)
            if halo:
                rd = [b_QKs[1][4 + c], b_Vs[1][c]]
                fw.dma(U["kT"][:, 1, :], QKs[1, 4 + c], key=b, reads=rd, writes=[b], partial=True)
                fw.dma(U["Vn"][:, 16, :], Vs[1, c, 0:128, :], key=b, reads=rd, writes=[b], partial=True)
                fw.dma(U["Vn"][:, 17, :], Vs[1, c, 1920:2048, :], key=b, reads=rd, writes=[b], partial=True)
                fw.dma(U["V4"][:, :, 4, :], Vs[1, c, 0:512, :].rearrange("(p r) e -> p r e", r=4), key=b, reads=rd,
                       writes=[b], partial=True)
                fw.dma(U["V4"][:, :, 5, :], Vs[1, c, 1536:2048, :].rearrange("(p r) e -> p r e", r=4), key=b,
                       reads=rd, writes=[b], partial=True)
                fw.dma(U["V16"][:, :, 1, :], Vs[1, c].rearrange("(p r) e -> p r e", r=16), key=b, reads=rd,
                       writes=[b], partial=True)

        def load_unitB():
            b = b_mb
            rd = b_QKs[0][8:13] + [b_Vs[0][4]]
            fw.dma(qbT, QKs[0, 8:12].rearrange("c p t -> p c t"), key=b, reads=rd, writes=[b])
            fw.dma(kbT[:, 0:NT], QKs[0, 12], key=b, reads=rd, writes=[b], partial=True)
            for n4 in range(4):
                fw.dma(Vb[:, 4 * n4:4 * n4 + 4, :],
                       Vs[0, 4, 512 * n4:512 * n4 + 512, :].rearrange("(n p) e -> p n e", p=128), key=b, reads=rd,
                       writes=[b], partial=True)
            if halo:
                rd = [b_QKs[1][12], b_Vs[1][4]]
                fw.dma(kbT[:, 2048:2176], QKs[1, 12, :, 0:128], key=b, reads=rd, writes=[b], partial=True)
                fw.dma(kbT[:, 2176:2304], QKs[1, 12, :, 1920:2048], key=b, reads=rd, writes=[b], partial=True)
                fw.dma(Vb[:, 16, :], Vs[1, 4, 0:128, :], key=b, reads=rd, writes=[b], partial=True)
                fw.dma(Vb[:, 17, :], Vs[1, 4, 1920:2048, :], key=b, reads=rd, writes=[b], partial=True)

        def unitA(c):
            U = UB[c % 2]
            b = U["b"]
            qT, kT = U["qT"], U["kT"]
            blocks = []
            for (d, Td) in ((1, 16), (4, 4), (16, 1)):
                Ld = NT // d
                for r in range(d):
                    for bb in range(-1, Td):
                        if bb == -1:
                            l0, nq, qi0 = 0, 64, 64
                        elif bb == Td - 1:
                            l0, nq, qi0 = Ld - 64, 64, 0
                        else:
                            l0, nq, qi0 = 128 * bb + 64, 128, 0
                        qsl = slice(d * l0 + r, d * (l0 + nq - 1) + r + 1, d)
                        tl = []
                        for wt, tb in enumerate((bb, bb + 1)):
                            if 0 <= tb < Td:
                                kp = (0, 128)
                                slot, tidx = 0, tb
                                if d == 1:
                                    Vt = U["Vn"][:, tb, :]
                                elif d == 4:
                                    Vt = U["V4"][:, r, tb, :]
                                else:
                                    Vt = U["V16"][:, r, 0, :]
                                dl = ones64
                            elif not halo:
                                continue
                            elif tb == -1:
                                kp = (64, 128)
                                slot, tidx = 1, Td - 1
                                Vt = (U["Vn"][:, 17, :] if d == 1 else U["V4"][:, r, 5, :] if d == 4
                                      else U["V16"][:, r, 1, :])
                                dl = hvb[:, pi, 1, :]
                            else:
                                kp = (0, 64)
                                slot, tidx = 1, 0
                                Vt = (U["Vn"][:, 16, :] if d == 1 else U["V4"][:, r, 4, :] if d == 4
                                      else U["V16"][:, r, 1, :])
                                dl = hvb[:, pi, 0, :]
                            k0 = d * (128 * tidx + kp[0]) + r
                            ksl = slice(k0, k0 + d * (kp[1] - kp[0] - 1) + 1, d)
                            tl.append((wt, kp, slot, ksl, Vt, dl))
                        blocks.append((nq, qi0, qsl, tl, d == 1))

            def views(bi):
                sbk = 2 * (bi % 3)
                obk = 6 + bi % 2
                S4 = bank(sbk, 2).rearrange("p (h w q) -> p h w q", h=2, w=4)
                O2 = bank(obk)[:, 0:256].rearrange("p (a q) -> p a q", a=2)
                return sbk, obk, S4, O2, bi % NPT

            def front(bi):
                nq, qi0, qsl, tl, first_br = blocks[bi]
                sbk, obk, S4, O2, pti = views(bi)
                P4 = Pt[pti]
                nS = len(tl) * 2
                si = 0
                for (wt, kp, slot, ksl, Vt, dl) in tl:
                    for hh in range(2):
                        si += 1
                        fw.op(PE, lambda wt=wt, kp=kp, slot=slot, ksl=ksl, hh=hh: nc.tensor.matmul(
                            S4[kp[0]:kp[1], hh, wt, 0:nq], lhsT=kT[64 * hh:64 * hh + 64, slot, ksl],
                            rhs=qT[64 * hh:64 * hh + 64, qsl], start=True, stop=True),
                            reads=[b], writes=[BK[sbk], BK[sbk + 1]], signal=(si == nS), partial=(si > 1))
                ME, me = (DVE, nc.vector)
                if len(tl) == 2 and all(t[1] == (0, 128) for t in tl):
                    fw.op(ACT, lambda: nc.scalar.activation(
                        out=P4[:, :, :, 0:nq], in_=S4[:, :, 0:2, 0:nq], func=AF.Exp, scale=0.125),
                        reads=[BK[sbk], BK[sbk + 1]], writes=[b_Pt[pti]])
                    fw.op(ME, lambda: me.tensor_tensor(
                        out=P4[:, :, :, 0:nq], in0=P4[:, :, :, 0:nq],
                        in1=mask[:, :, qi0:qi0 + nq].unsqueeze(1).to_broadcast([128, 2, 2, nq]),
                        op=ALU.mult), reads=[b_Pt[pti], b_const], writes=[b_Pt[pti]], partial=True)
                else:
                    for ti_, (wt, kp, slot, ksl, Vt, dl) in enumerate(tl):
                        kpn = kp[1] - kp[0]
                        fw.op(ACT, lambda wt=wt, kp=kp: nc.scalar.activation(
                            out=P4[kp[0]:kp[1], :, wt, 0:nq], in_=S4[kp[0]:kp[1], :, wt, 0:nq], func=AF.Exp,
                            scale=0.125), reads=[BK[sbk], BK[sbk + 1]], writes=[b_Pt[pti]], partial=(ti_ > 0))
                        fw.op(ME, lambda wt=wt, kp=kp, kpn=kpn: me.tensor_tensor(
                            out=P4[kp[0]:kp[1], :, wt, 0:nq], in0=P4[kp[0]:kp[1], :, wt, 0:nq],
                            in1=mask[kp[0]:kp[1], wt, qi0:qi0 + nq].unsqueeze(1).to_broadcast([kpn, 2, nq]),
                            op=ALU.mult), reads=[b_Pt[pti], b_const], writes=[b_Pt[pti]], partial=True)

            def back(bi):
                nq, qi0, qsl, tl, first_br = blocks[bi]
                sbk, obk, S4, O2, pti = views(bi)
                P4 = Pt[pti]
                nP = len(tl) * 4
                pi_ = 0
                for ti_, (wt, kp, slot, ksl, Vt, dl) in enumerate(tl):
                    for hh in range(2):
                        pi_ += 1
                        fw.op(PE, lambda wt=wt, kp=kp, Vt=Vt, hh=hh, ti_=ti_: nc.tensor.matmul(
                            O2[64 * hh:64 * hh + 64, 0, 0:nq], lhsT=Vt[kp[0]:kp[1], 64 * hh:64 * hh + 64],
                            rhs=P4[kp[0]:kp[1], hh, wt, 0:nq], start=(ti_ == 0), stop=False,
                            skip_group_check=True),
                            reads=[b, b_Pt[pti]], writes=[BK[obk]], signal=False, partial=(pi_ > 1))
                        pi_ += 1
                        fw.op(PE, lambda wt=wt, kp=kp, dl=dl, hh=hh, ti_=ti_: nc.tensor.matmul(
                            O2[64 * hh:64 * hh + 64, 1, 0:nq], lhsT=dl[kp[0]:kp[1], :],
                            rhs=P4[kp[0]:kp[1], hh, wt, 0:nq], start=False, stop=(ti_ == len(tl) - 1),
                            skip_group_check=True),
                            reads=[b_const, b_Pt[pti]], writes=[BK[obk]], signal=(pi_ == nP), partial=True)
                if first_br:
                    fw.op(DVE, lambda: nc.vector.tensor_copy(out=acc[:, :, qsl], in_=O2[:, :, 0:nq]),
                          reads=[BK[obk]], writes=[b_acc], partial=True)
                else:
                    fw.op(DVE, lambda: nc.vector.tensor_tensor(out=acc[:, :, qsl], in0=O2[:, :, 0:nq],
                                                               in1=acc[:, :, qsl], op=ALU.add),
                          reads=[BK[obk], b_acc], writes=[b_acc], partial=True)

            nb = len(blocks)
            for i in range(nb + LAG):
                if i < nb:
                    front(i)
                if i - LAG >= 0:
                    back(i - LAG)
            fw.op(ACT, lambda: nc.scalar.activation(out=acc[:, 1, :], in_=acc[:, 1, :], func=AF.Ln), reads=[b_acc],
                  writes=[b_acc], partial=True)
            fw.op(ACT, lambda: nc.scalar.activation(out=acc[:, 1, :], in_=acc[:, 1, :], func=AF.Exp, scale=-1.0),
                  reads=[b_acc], writes=[b_acc], partial=True)
            fw.op(DVE, lambda: nc.vector.tensor_tensor(out=AT[:, c, :], in0=acc[:, 0, :], in1=acc[:, 1, :],
                                                       op=ALU.mult), reads=[b_acc], writes=[b_AT[c]])

        def unitB():
            b = b_mb
            items = []
            for bb in range(16):
                for g in range(2):
                    tl = []
                    for wt, tb in ((0, bb - 1), (None, bb), (1, bb + 1)):
                        if 0 <= tb < 16:
                            tl.append((wt, slice(128 * tb, 128 * tb + 128), Vb[:, tb, :], ones64))
                        elif not halo:
                            continue
                        elif tb == -1:
                            tl.append((wt, slice(2176, 2304), Vb[:, 17, :], hvb[:, pi, 1, :]))
                        else:
                            tl.append((wt, slice(2048, 2176), Vb[:, 16, :], hvb[:, pi, 0, :]))
                    for ti_, t in enumerate(tl):
                        items.append((bb, g, ti_, len(tl), t))

            def vw(i, bb):
                sbk = i % 3
                nbk = 3 + 2 * (bb % 2)
                return sbk, bank(sbk), nbk, bank(nbk), bank(nbk + 1), i % NPT

            def front(i):
                bb, g, ti_, ntl, (wt, ksl, Vt, dl) = items[i]
                sbk, Sb, nbk, NB, DB, pbi = vw(i, bb)
                fw.op(PE, lambda: nc.tensor.matmul(
                    Sb, lhsT=kbT[64 * g:64 * g + 64, ksl], rhs=qbT[64 * g:64 * g + 64, :, 128 * bb:128 * bb + 128],
                    start=True, stop=True), reads=[b], writes=[BK[sbk]])
                fw.op(ACT, lambda: nc.scalar.activation(out=Ptb[pbi], in_=Sb, func=AF.Exp, scale=0.125),
                      reads=[BK[sbk]], writes=[b_Ptb[pbi]])
                if wt is not None:
                    ME, me = (DVE, nc.vector)
                    fw.op(ME, lambda: me.tensor_tensor(
                        out=Ptb[pbi].rearrange("p (c q) -> p c q", c=4),
                        in0=Ptb[pbi].rearrange("p (c q) -> p c q", c=4),
                        in1=mask[:, wt, :].unsqueeze(1).to_broadcast([128, 4, 128]), op=ALU.mult),
                        reads=[b_Ptb[pbi], b_const], writes=[b_Ptb[pbi]])

            def back(i):
                bb, g, ti_, ntl, (wt, ksl, Vt, dl) = items[i]
                sbk, Sb, nbk, NB, DB, pbi = vw(i, bb)
                last = (ti_ == ntl - 1)
                first_of_block = (g == 0 and ti_ == 0)
                fw.op(PE, lambda: nc.tensor.matmul(
                    NB[64 * g:64 * g + 64, :], lhsT=Vt[:, 64 * g:64 * g + 64], rhs=Ptb[pbi], start=(ti_ == 0),
                    stop=last), reads=[b, b_Ptb[pbi]], writes=[BK[nbk]], signal=False,
                    partial=(not first_of_block))
                fw.op(PE, lambda: nc.tensor.matmul(
                    DB[64 * g:64 * g + 64, :], lhsT=dl, rhs=Ptb[pbi], start=(ti_ == 0), stop=last),
                    reads=[b_const, b_Ptb[pbi]], writes=[BK[nbk + 1]], signal=True, partial=(not first_of_block))
                if g == 1 and last:
                    di = bb % 2
                    for c4 in range(4):
                        fw.op(ACT, lambda c4=c4: nc.scalar.activation(
                            out=dn[di][:, c4 * 128:(c4 + 1) * 128], in_=DB[:, c4 * 128:(c4 + 1) * 128], func=AF.Ln,
                            bias=es[:, c4:c4 + 1]), reads=[BK[nbk + 1], b_const], writes=[b_dn[di]],
                            partial=(c4 > 0))
                    fw.op(ACT, lambda: nc.scalar.activation(out=dn[di], in_=dn[di], func=AF.Exp, scale=-1.0),
                          reads=[b_dn[di]], writes=[b_dn[di]])
                    fw.op(DVE, lambda: nc.vector.tensor_tensor(
                        out=AT[:, 4:8, 128 * bb:128 * bb + 128], in0=NB.rearrange("p (c q) -> p c q", c=4),
                        in1=dn[di].rearrange("p (c q) -> p c q", c=4), op=ALU.mult), reads=[BK[nbk], b_dn[di]],
                        writes=b_AT[4:8], partial=True)

            ni = len(items)
            for i in range(ni + LAG):
                if i < ni:
                    front(i)
                if i - LAG >= 0:
                    back(i - LAG)

        load_unitA(0)
        load_unitA(1)
        unitB()
        for c in range(4):
            if c >= 1 and c + 1 < 4:
                load_unitA(c + 1)
            unitA(c)
        fw.barrier()

    def phaseC(pc):
        arena_reset()
        x1 = [ar("x1", [128, 4, D], F32) for _ in range(2)]
        b_x1 = [[fw.buf(f"x1_{u}_{t}") for t in range(4)] for u in range(2)]
        XR = [ar("xr", [128, D], F32) for _ in range(2)]
        b_XR = [fw.buf(f"xr{i}") for i in range(2)]
        junk = ar("junk", [128, D], BF16)
        b_junk = fw.buf("junkc")
        ssq = ar("ssq", [128, 4], F32)
        b_ssq = fw.buf("ssqc")
        xn2 = [ar("xn2", [128, D], BF16) for _ in range(2)]
        b_xn2 = [fw.buf(f"xn2_{i}") for i in range(2)]
        h2T = [ar("h2T", [128, 8, 512], BF16) for _ in range(2)]
        b_h2T = [fw.buf(f"h2T{i}") for i in range(2)]
        actT = ar("actT", [128, NJ, 512], BF16)
        b_actT = [fw.buf(f"actT{j}") for j in range(NJ)]
        sg = [ar("sg", [128, 512], F32) for _ in range(2)]
        b_sg = [fw.buf(f"sg{i}") for i in range(2)]
        WG = [ar("WG", [128, 8, 128], BF16) for _ in range(3)]
        WU = [ar("WU", [128, 8, 128], BF16) for _ in range(3)]
        WD = [ar("WD", [128, D], BF16) for _ in range(3)]
        b_WG = [fw.buf(f"WG{i}") for i in range(3)]
        b_WU = [fw.buf(f"WU{i}") for i in range(3)]
        b_WD = [fw.buf(f"WD{i}") for i in range(3)]
        OS = [ar("OS", [128, D], F32) for _ in range(2)]
        b_OS = [fw.buf(f"OS{i}") for i in range(2)]
        pT = bank(2).bitcast(BF16)

        def ld_gu(j):
            i = j % 3
            fw.dma(WG[i], wgs[j], key=b_WG[i], reads=[b_wsc], writes=[b_WG[i]])
            fw.dma(WU[i], wus[j], key=b_WU[i], reads=[b_wsc], writes=[b_WU[i]])

        def ld_d(j):
            i = j % 3
            fw.dma(WD[i], wds[j], key=b_WD[i], reads=[b_wsc], writes=[b_WD[i]])

        xcnt = [0]

        def c1_load(sp_, t):
            r0 = 512 * sp_ + 128 * t
            xi = (4 * sp_ + t) % 2
            fw.dma(XR[xi], pc["xsrc"](r0 // 128), key=b_XR[xi], writes=[b_XR[xi]])

        def c1_a(sp_, t):
            u = sp_ % 2
            r0 = 512 * sp_ + 128 * t
            tcs = slice(r0, r0 + 128)
            xi = (4 * sp_ + t) % 2
            xp_ = t % 2
            for half in range(2):
                for kk in range(8):
                    fw.op(PE, lambda kk=kk, half=half: nc.tensor.matmul(
                        bank(half), lhsT=AT[:, kk, tcs], rhs=woutb[:, kk, half * 512:(half + 1) * 512],
                        start=(kk == 0), stop=(kk == 7)), reads=[b_AT[kk], b_woutb], writes=[BK[half]],
                        signal=(kk == 7), partial=(kk > 0))
            fw.op(DVE, lambda: nc.vector.tensor_tensor(out=x1[u][:, t, :], in0=bank(0, 2), in1=XR[xi], op=ALU.add),
                  reads=[BK[0], BK[1], b_XR[xi]], writes=[b_x1[u][t]])
            fw.op(ACT, lambda: nc.scalar.activation(out=junk, in_=x1[u][:, t, :], func=AF.Square,
                                                    accum_out=ssq[:, 0:1]), reads=[b_x1[u][t]],
                  writes=[b_junk, b_ssq])
            fw.op(ACT, lambda: nc.scalar.activation(out=ssq[:, 1:2], in_=ssq[:, 0:1], func=AF.Sqrt,
                                                    scale=1.0 / D, bias=eps_t), reads=[b_ssq, b_const],
                  writes=[b_ssq])
            fw.op(DVE, lambda: nc.vector.reciprocal(out=ssq[:, 2:3], in_=ssq[:, 1:2]), reads=[b_ssq],
                  writes=[b_ssq])
            fw.op(ACT, lambda: nc.scalar.activation(out=xn2[xp_], in_=x1[u][:, t, :], func=AF.Copy,
                                                    scale=ssq[:, 2:3]),
                  reads=[b_x1[u][t], b_ssq], writes=[b_xn2[xp_]])

        def c1_b(sp_, t):
            u = sp_ % 2
            xp_ = t % 2
            for kk in range(8):
                fw.op(PE, lambda kk=kk: nc.tensor.transpose(out=pT[:, kk * 128:(kk + 1) * 128],
                                                            in_=xn2[xp_][:, kk * 128:(kk + 1) * 128],
                                                            identity=ident),
                      reads=[b_xn2[xp_], b_const], writes=[BK[2]], signal=(kk == 7), partial=(kk > 0))
            fw.op(ACT, lambda: nc.scalar.copy(out=h2T[u][:, :, 128 * t:128 * t + 128],
                                              in_=pT.rearrange("p (k q) -> p k q", k=8)),
                  reads=[BK[2]], writes=[b_h2T[u]], partial=(t > 0))

        def c2(sp_, j):
            u = sp_ % 2
            i = j % 3
            gb, ub = 4 + j % 2, 6 + j % 2
            for kk in range(8):
                fw.op(PE, lambda kk=kk: nc.tensor.matmul(bank(gb), lhsT=WG[i][:, kk, :], rhs=h2T[u][:, kk, :],
                                                         start=(kk == 0), stop=(kk == 7)),
                      reads=[b_WG[i], b_h2T[u]], writes=[BK[gb]], signal=(kk == 7), partial=(kk > 0))
            for kk in range(8):
                fw.op(PE, lambda kk=kk: nc.tensor.matmul(bank(ub), lhsT=WU[i][:, kk, :], rhs=h2T[u][:, kk, :],
                                                         start=(kk == 0), stop=(kk == 7)),
                      reads=[b_WU[i], b_h2T[u]], writes=[BK[ub]], signal=(kk == 7), partial=(kk > 0))
            si = j % 2
            fw.op(ACT, lambda: nc.scalar.activation(out=sg[si], in_=bank(gb), func=AF.Silu), reads=[BK[gb]],
                  writes=[b_sg[si]])
            fw.op(DVE, lambda: nc.vector.tensor_tensor(out=actT[:, j, :], in0=bank(ub), in1=sg[si],
                                                       op=ALU.mult), reads=[BK[ub], b_sg[si]],
                  writes=[b_actT[j]])

        def c3(sp_):
            for j in range(NJ):
                if j + 2 < NJ:
                    ld_d(j + 2)
                i = j % 3
                for t in range(4):
                    for half in range(2):
                        fw.op(PE, lambda t=t, half=half: nc.tensor.matmul(
                            bank(2 * t + half), lhsT=actT[:, j, 128 * t:128 * t + 128],
                            rhs=WD[i][:, half * 512:(half + 1) * 512], start=(j == 0), stop=(j == NJ - 1)),
                            reads=[b_actT[j], b_WD[i]], writes=[BK[2 * t + half]],
                            signal=(j == NJ - 1 or (t == 3 and half == 1)), partial=(j > 0))

        def c4(sp_):
            u = sp_ % 2
            for t in (2, 3, 0, 1):
                r0 = 512 * sp_ + 128 * t
                oi = t % 2
                fw.op(DVE, lambda: nc.vector.tensor_tensor(out=OS[oi], in0=bank(2 * t, 2), in1=x1[u][:, t, :],
                                                           op=ALU.add),
                      reads=[BK[2 * t], BK[2 * t + 1], b_x1[u][t]], writes=[b_OS[oi]])
                fw.dma(pc["out"](r0), OS[oi], key=b_OS[oi], reads=[b_OS[oi]])

        NSP = 4
        c1_load(0, 0)
        c1_load(0, 1)
        ld_gu(0)
        ld_gu(1)
        for t in range(4):
            c1_a(0, t)
            if t + 2 < 4:
                c1_load(0, t + 2)
            if t >= 1:
                c1_b(0, t - 1)
        c1_b(0, 3)
        for sp_ in range(NSP):
            nxt = sp_ + 1 < NSP
            sched_a = {2: 0, 6: 1, 10: 2, 14: 3}
            sched_b = {5: 0, 9: 1, 13: 2, 17: 3}
            if nxt:
                c1_load(sp_ + 1, 0)
                c1_load(sp_ + 1, 1)
            for j in range(NJ):
                if j + 2 < NJ:
                    ld_gu(j + 2)
                if j == NJ - 2:
                    ld_d(0)
                if j == NJ - 1:
                    ld_d(1)
                c2(sp_, j)
                if nxt and j in sched_a:
                    t = sched_a[j]
                    c1_a(sp_ + 1, t)
                    if t + 2 < 4:
                        c1_load(sp_ + 1, t + 2)
                if nxt and j in sched_b:
                    c1_b(sp_ + 1, sched_b[j])
            if nxt:
                ld_gu(0)
                ld_gu(1)
            c3(sp_)
            c4(sp_)
        fw.barrier()

    for ip, pc in enumerate(pieces):
        if sel is not None and ip not in sel:
            continue
        if "A" in phases:
            phaseA(pc)
        if "B" in phases:
            phaseB(pc)
        if "C" in phases:
            phaseC(pc)
    fw.finish()
    return nc


_CACHE = {}


def _host_consts():
    bf = ml_dtypes.bfloat16
    kj = np.arange(128)[:, None]
    qi = np.arange(128)[None, :]
    masks = np.stack([(kj >= qi), (kj <= qi)], axis=1).astype(np.float32).astype(bf)
    ident = np.eye(128, dtype=np.float32).astype(bf)
    return masks, ident


def _rope_table(pos):
    half = 32
    inv = (np.float32(10000.0) ** (np.float32(-2.0) * np.arange(half, dtype=np.float32) / np.float32(64))).astype(
        np.float32)
    ang = (pos.astype(np.float32)[:, None] * inv[None, :]).astype(np.float32)
    return np.concatenate([np.cos(ang), np.sin(ang)], axis=1).astype(np.float32)


def kernel(x_prompt, x_sample, attn_norm, w_in, qnorm_a, knorm_a, qnorm_b, knorm_b,
           sink_b, w_out, ffn_norm, w_gate, w_up, w_down):
    f32 = np.float32
    x_prompt = np.asarray(x_prompt, f32)
    x_sample = np.asarray(x_sample, f32)
    w_in = np.asarray(w_in, f32)[0]
    w_out = np.asarray(w_out, f32)[0]
    w_gate = np.ascontiguousarray(np.asarray(w_gate, f32)[0])
    w_up = np.ascontiguousarray(np.asarray(w_up, f32)[0])
    w_down = np.ascontiguousarray(np.asarray(w_down, f32)[0])
    QBo, KBo, VBo, VAo = 1536, 2048, 2176, 1024
    cols = list(range(0, 512)) + list(range(512, 1024))
    for c in range(4):
        for g in range(2):
            h = 4 * g + c
            cols += list(range(QBo + h * 64, QBo + h * 64 + 64))
    cols += list(range(KBo, KBo + 128)) + list(range(VBo, VBo + 128)) + list(range(VAo, VAo + 512))
    w_in_p = np.ascontiguousarray(w_in[:, cols])
    rows = list(range(512))
    for c in range(4):
        for g in range(2):
            h = 4 * g + c
            rows += list(range(512 + h * 64, 512 + h * 64 + 64))
    w_out_p = np.ascontiguousarray(w_out[rows, :])
    gA = np.ascontiguousarray(np.asarray(attn_norm, f32)[0].reshape(8, 128).T)
    gF = np.ascontiguousarray(np.asarray(ffn_norm, f32)[0].reshape(8, 128).T)
    gq1 = np.stack([np.asarray(qnorm_a, f32)[0], np.asarray(knorm_a, f32)[0],
                    np.asarray(qnorm_b, f32)[0], np.asarray(knorm_b, f32)[0]])
    gq = np.ascontiguousarray(np.broadcast_to(gq1[None, :, :], (128, 4, 64)))
    sk = np.asarray(sink_b, f32)[0]
    sinkT = np.zeros((128, 4), f32)
    for g in range(2):
        for c in range(4):
            sinkT[64 * g:64 * g + 64, c] = sk[4 * g + c]
    masks, ident = _host_consts()

    if "nc" not in _CACHE:
        _CACHE["nc"] = build_program()
    nc = _CACHE["nc"]

    in_maps = []
    p128 = np.arange(128)
    for core in range(NCORES):
        ps, q = core // 4, core % 4
        lo = q * 4096 - 1024
        xp = np.zeros((6144, D), f32)
        a, b = max(lo, 0), min(lo + 6144, 16384)
        xp[a - lo:b - lo] = x_prompt[ps, a:b]
        cst = np.zeros((80, 128, 64), f32)
        for n in range(16):
            cst[n] = _rope_table(n * 128 + p128)
        for pi in range(2):
            base = q * 4096 + 2048 * pi
            for n in range(16):
                cst[16 + 32 * pi + n] = _rope_table(base + n * 128 + p128)
                hidx = n * 128 + p128
                local = np.where(hidx < 1024, 2048 + hidx, hidx - 2048)
                cst[32 + 32 * pi + n] = _rope_table(np.maximum(base + local, 0))
        hv = np.ones((128, 2, 2, 64), f32)
        if q == 0:
            hv[:, 0, 1, :] = 0.0
        if q == 3:
            hv[:, 1, 0, :] = 0.0
        in_maps.append({
            "xs": np.ascontiguousarray(x_sample[4 * core:4 * core + 4]),
            "xp": xp, "cst": cst, "w_in": w_in_p, "w_out": w_out_p, "w_gate": w_gate, "w_up": w_up,
            "w_down": w_down, "gA": gA, "gF": gF, "gq": gq, "sinkT": sinkT, "masks": masks, "ident": ident,
            "hv": hv.astype(ml_dtypes.bfloat16),
        })
    res = run_bass_kernel_spmd(nc, in_maps, core_ids=list(range(NCORES)))
    y_prompt = np.zeros((2, 16384, D), f32)
    y_sample = np.zeros((32, NT, D), f32)
    for core in range(NCORES):
        r = res.results[core]
        ps, q = core // 4, core % 4
        y_prompt[ps, q * 4096:(q + 1) * 4096] = np.asarray(r["yp"], f32)
        y_sample[4 * core:4 * core + 4] = np.asarray(r["ys"], f32)
    return (y_prompt, y_sample)
```

```python
import numpy as np
import ml_dtypes
import concourse.bass as bass
import concourse.mybir as mybir
from concourse.bass_utils import run_bass_kernel_spmd

F32 = mybir.dt.float32
BF16 = mybir.dt.bfloat16
ALU = mybir.AluOpType
AF = mybir.ActivationFunctionType
AX = mybir.AxisListType

D = 1024
NT = 2048
DFF = 2816
NJ = 22
EPS = 1e-6
NCORES = 8


class Src:
    def __init__(self, nc, name, step):
        self.sem = nc.alloc_semaphore(name=name)
        self.count = 0
        self.step = step
        self.name = name


class Buf:
    __slots__ = ("name", "writes", "reads", "dsrc")

    def __init__(self, name):
        self.name = name
        self.writes = []
        self.reads = []
        self.dsrc = None


class Eng:
    def __init__(self, fw, name, eng, is_pe=False):
        self.name = name
        self.eng = eng
        self.src = Src(fw.nc, "e_" + name, 1)
        self.known = {}
        self.is_pe = is_pe


class FW:
    def __init__(self, nc):
        self.nc = nc
        self.pe = Eng(self, "pe", nc.tensor, is_pe=True)
        self.act = Eng(self, "act", nc.scalar)
        self.dve = Eng(self, "dve", nc.vector)
        self.pool = Eng(self, "pool", nc.gpsimd)
        self.sp = Eng(self, "sp", nc.sync)
        self.engs = [self.pe, self.act, self.dve, self.pool, self.sp]
        self.nbuf = 0
        self.dsrcs = []
        self.bufs = {}

    def buf(self, name=None):
        self.nbuf += 1
        if name is None:
            return Buf(f"b{self.nbuf}")
        if name not in self.bufs:
            self.bufs[name] = Buf(name)
        return self.bufs[name]

    def _collect(self, E, reads, writes):
        need = {}

        def add(tok):
            s, c = tok
            if E.is_pe and s is E.src:
                return
            if s.step == 16:
                c = s.count
            if need.get(s, 0) < c:
                need[s] = c
        for b in reads:
            for t in b.writes:
                add(t)
        for b in writes:
            for t in b.writes:
                add(t)
            for t in b.reads:
                add(t)
        for s, c in need.items():
            if E.known.get(s, 0) >= c:
                continue
            E.eng.wait_ge(s.sem, c)
            E.known[s] = c

    def _commit(self, tok, reads, writes, partial):
        for b in writes:
            if partial:
                b.writes.append(tok)
                if len(b.writes) > 24:
                    m = {}
                    for s, c in b.writes:
                        if m.get(s, 0) < c:
                            m[s] = c
                    b.writes = list(m.items())
            else:
                b.writes = [tok]
                b.reads = []
        for b in reads:
            b.reads.append(tok)
            if len(b.reads) > 24:
                m = {}
                for s, c in b.reads:
                    if m.get(s, 0) < c:
                        m[s] = c
                b.reads = list(m.items())

    def op(self, E, fn, reads=(), writes=(), signal=True, partial=False):
        self._collect(E, reads, writes)
        ins = fn()
        if signal:
            E.src.count += 1
            ins.then_inc(E.src.sem, 1)
            tok = (E.src, E.src.count)
        else:
            tok = (E.src, E.src.count + 1)
        self._commit(tok, reads, writes, partial)
        return ins

    def dma(self, out_ap, in_ap, key, reads=(), writes=(), partial=False):
        E = self.sp
        if key.dsrc is None:
            key.dsrc = Src(self.nc, "d_" + key.name, 16)
            self.dsrcs.append(key.dsrc)
        self._collect(E, reads, writes)
        ins = E.eng.dma_start(out=out_ap, in_=in_ap)
        key.dsrc.count += 16
        ins.then_inc(key.dsrc.sem, 16)
        tok = (key.dsrc, key.dsrc.count)
        self._commit(tok, reads, writes, partial)
        return ins

    def barrier(self):
        for E in self.engs:
            for F in self.engs:
                if F is E or F is self.sp:
                    continue
                c = F.src.count
                if c > 0 and E.known.get(F.src, 0) < c:
                    E.eng.wait_ge(F.src.sem, c)
                    E.known[F.src] = c
            for d in self.dsrcs:
                if d.count > 0 and E.known.get(d, 0) < d.count:
                    E.eng.wait_ge(d.sem, d.count)
                    E.known[d] = d.count

    def finish(self):
        E = self.sp
        for F in self.engs:
            if F is E:
                continue
            if F.src.count > 0:
                E.eng.wait_ge(F.src.sem, F.src.count)
        for d in self.dsrcs:
            if d.count > 0:
                E.eng.wait_ge(d.sem, d.count)


def build_program(sel=None, phases="ABC"):
    nc = bass.Bass("TRN2", target_bir_lowering=False)
    fw = FW(nc)
    PE, ACT, DVE, POOL = fw.pe, fw.act, fw.dve, fw.pool

    def din(name, shape, dt=F32):
        return nc.dram_tensor(name, list(shape), dt, kind="ExternalInput").ap()

    xs = din("xs", [4, NT, D])
    xp = din("xp", [6144, D])
    cst = din("cst", [80, 128, 64])
    w_in = din("w_in", [D, 2304])
    w_out = din("w_out", [D, D])
    w_gate = din("w_gate", [D, DFF])
    w_up = din("w_up", [D, DFF])
    w_down = din("w_down", [DFF, D])
    gA_d = din("gA", [128, 8])
    gF_d = din("gF", [128, 8])
    gq_d = din("gq", [128, 4, 64])
    sink_d = din("sinkT", [1, 2, 512])
    esel_d = din("esel", [1, 128], BF16)
    mask_d = din("masks", [128, 2, 128], BF16)
    ident_d = din("ident", [128, 128], BF16)
    hv_d = din("hv", [128, 2, 2, 64], BF16)
    ys = nc.dram_tensor("ys", [4, NT, D], F32, kind="ExternalOutput").ap()
    yp = nc.dram_tensor("yp", [4096, D], F32, kind="ExternalOutput").ap()
    QKs = nc.dram_tensor("QKs", [2, 13, 128, NT], BF16, kind="Internal").ap()
    Vs = nc.dram_tensor("Vs", [2, 5, NT, 128], BF16, kind="Internal").ap()
    wgs = nc.dram_tensor("wgs", [NJ, 128, 8, 128], BF16, kind="Internal").ap()
    wus = nc.dram_tensor("wus", [NJ, 128, 8, 128], BF16, kind="Internal").ap()
    wds = nc.dram_tensor("wds", [NJ, 128, D], BF16, kind="Internal").ap()
    b_QKs = [[fw.buf(f"QKs{s}_{c}") for c in range(13)] for s in range(2)]
    b_Vs = [[fw.buf(f"Vs{s}_{c}") for c in range(5)] for s in range(2)]
    b_wsc = fw.buf("wscratch")

    def sb(name, shape, dt):
        return nc.alloc_sbuf_tensor("sb_" + name, list(shape), dt).ap()

    gA = sb("gA", [128, 8], F32)
    gF = sb("gF", [128, 8], F32)
    gq = sb("gq", [128, 4, 64], F32)
    mask = sb("mask", [128, 2, 128], BF16)
    ident = sb("ident", [128, 128], BF16)
    ones64 = sb("ones64", [128, 64], BF16)
    hvb = sb("hvb", [128, 2, 2, 64], BF16)
    esr = sb("esr", [1, 2, 512], BF16)
    esel = sb("esel", [1, 128], BF16)
    eps_t = sb("eps", [128, 1], F32)
    winb = sb("winb", [128, 8, 2304], BF16)
    woutb = sb("woutb", [128, 8, D], BF16)
    AT = sb("AT", [128, 8, NT], BF16)
    b_const = fw.buf("const")
    b_winb = fw.buf("winb")
    b_woutb = fw.buf("woutb")
    b_AT = [fw.buf(f"AT{k}") for k in range(8)]

    PS = nc.alloc_psum_tensor("PS", [128, 4096], F32).ap()
    BK = [fw.buf(f"bank{i}") for i in range(8)]

    def bank(i, n=1):
        return PS[:, i * 512:(i + n) * 512]

    arena0 = (nc.sbuf_base + 63) // 64 * 64
    arena_state = {"off": 0}
    ARENA_BYTES = nc.sbuf_top - arena0

    def arena_reset():
        arena_state["off"] = 0

    uid = [0]

    def ar(name, shape, dt):
        nbytes = int(np.prod(shape[1:])) * (2 if dt == BF16 else 4)
        nbytes = (nbytes + 63) // 64 * 64
        off = arena_state["off"]
        assert off + nbytes <= ARENA_BYTES, (name, off, nbytes, ARENA_BYTES)
        arena_state["off"] = off + nbytes
        uid[0] += 1
        return nc.alloc_sbuf_tensor_at(f"{name}_{uid[0]}", list(shape), dt, offset=arena0 + off).ap()

    arena_state["off"] = ARENA_BYTES - 4096
    es = ar("es", [1, 2, 512], F32)
    arena_reset()
    for dst, src in ((gA, gA_d), (gF, gF_d), (gq, gq_d), (mask, mask_d), (ident, ident_d),
                     (hvb, hv_d), (es, sink_d), (esel, esel_d)):
        fw.dma(dst, src, key=b_const, writes=[b_const], partial=True)
    fw.op(POOL, lambda: nc.gpsimd.memset(ones64, 1.0), writes=[b_const], partial=True)
    fw.op(POOL, lambda: nc.gpsimd.memset(eps_t, EPS), writes=[b_const], partial=True)
    fw.op(ACT, lambda: nc.scalar.activation(out=esr, in_=es, func=AF.Exp), reads=[b_const], writes=[b_const],
          partial=True)

    arena_reset()
    NB0 = 6
    stg = [ar("stg", [128, 2304], F32) for _ in range(NB0)]
    b_stg = [fw.buf(f"stg{i}") for i in range(NB0)]
    stb = [ar("stb", [128, 1024], BF16) for _ in range(NB0)]
    b_stb = [fw.buf(f"stb{i}") for i in range(NB0)]
    jobs = []
    for k in range(8):
        jobs.append(("win", k))
    for k in range(8):
        jobs.append(("wout", k))
    for j in range(NJ):
        jobs.append(("g", j))
        jobs.append(("u", j))
        jobs.append(("d", j))

    def p0_load(n):
        kind, a_ = jobs[n]
        i = n % NB0
        if kind == "win":
            fw.dma(stg[i], w_in[a_ * 128:(a_ + 1) * 128, :], key=b_stg[i], writes=[b_stg[i]])
        elif kind == "wout":
            fw.dma(stg[i][:, 0:D], w_out[a_ * 128:(a_ + 1) * 128, :], key=b_stg[i], writes=[b_stg[i]])
        elif kind in ("g", "u"):
            wsrc = w_gate if kind == "g" else w_up
            fw.dma(stg[i][:, 0:1024].rearrange("p (k c) -> p k c", k=8),
                   wsrc[:, a_ * 128:(a_ + 1) * 128].rearrange("(k p) c -> p k c", p=128),
                   key=b_stg[i], writes=[b_stg[i]])
        else:
            fw.dma(stg[i][:, 0:1024], w_down[a_ * 128:(a_ + 1) * 128, :], key=b_stg[i], writes=[b_stg[i]])

    def p0_work(n):
        kind, a_ = jobs[n]
        i = n % NB0
        e3 = n % 3
        if kind == "win":
            if e3 == 0:
                fw.op(ACT, lambda: nc.scalar.activation(out=winb[:, a_, :], in_=stg[i], func=AF.Copy,
                                                        scale=gA[:, a_:a_ + 1]),
                      reads=[b_stg[i], b_const], writes=[b_winb], partial=True)
            else:
                fw.op(DVE, lambda: nc.vector.tensor_scalar(out=winb[:, a_, :], in0=stg[i],
                                                           scalar1=gA[:, a_:a_ + 1], scalar2=None, op0=ALU.mult),
                      reads=[b_stg[i], b_const], writes=[b_winb], partial=True)
        elif kind == "wout":
            if e3 == 0:
                fw.op(ACT, lambda: nc.scalar.copy(out=woutb[:, a_, :], in_=stg[i][:, 0:D]), reads=[b_stg[i]],
                      writes=[b_woutb], partial=True)
            else:
                fw.op(DVE, lambda: nc.vector.tensor_copy(out=woutb[:, a_, :], in_=stg[i][:, 0:D]),
                      reads=[b_stg[i]], writes=[b_woutb], partial=True)
        elif kind in ("g", "u"):
            wdst = wgs if kind == "g" else wus
            fw.op(DVE, lambda: nc.vector.tensor_tensor(
                out=stb[i].rearrange("p (k c) -> p k c", k=8),
                in0=stg[i][:, 0:1024].rearrange("p (k c) -> p k c", k=8),
                in1=gF.unsqueeze(2).to_broadcast([128, 8, 128]), op=ALU.mult),
                reads=[b_stg[i], b_const], writes=[b_stb[i]])
            fw.dma(wdst[a_].rearrange("p k c -> p (k c)"), stb[i], key=b_stb[i], reads=[b_stb[i]],
                   writes=[b_wsc], partial=True)
        else:
            fw.op(ACT, lambda: nc.scalar.copy(out=stb[i], in_=stg[i][:, 0:1024]), reads=[b_stg[i]],
                  writes=[b_stb[i]])
            fw.dma(wds[a_], stb[i], key=b_stb[i], reads=[b_stb[i]], writes=[b_wsc], partial=True)

    for n in range(min(NB0 - 1, len(jobs))):
        p0_load(n)
    for n in range(len(jobs)):
        if n + NB0 - 1 < len(jobs):
            p0_load(n + NB0 - 1)
        p0_work(n)
    fw.barrier()

    pieces = []
    for i in range(4):
        pieces.append(dict(halo=False, xsrc=lambda n, i=i: xs[i, n * 128:(n + 1) * 128, :],
                           hsrc=None, cs0=0, csh=None, out=lambda r0, i=i: ys[i, r0:r0 + 128, :], pi=0))
    for pi in range(2):
        mb = 1024 + 2048 * pi

        def hrow(n, mb=mb):
            hidx = n * 128
            return (mb + 2048 + hidx) if hidx < 1024 else (mb + hidx - 2048)
        pieces.append(dict(halo=True, xsrc=lambda n, mb=mb: xp[mb + n * 128: mb + (n + 1) * 128, :],
                           hsrc=lambda n, hrow=hrow: xp[hrow(n):hrow(n) + 128, :],
                           cs0=16 + 32 * pi, csh=32 + 32 * pi,
                           out=lambda r0, pi=pi: yp[2048 * pi + r0: 2048 * pi + r0 + 128, :], pi=pi))

    mixb_cache = {}

    def alloc_mixb():
        assert arena_state["off"] == 0
        if "t" not in mixb_cache:
            qbT = ar("qbT", [128, 4, NT], BF16)
            kbT = ar("kbT", [128, 2304], BF16)
            Vb = ar("Vb", [128, 18, 2, 128], BF16)
            mixb_cache["t"] = (qbT, kbT, Vb, fw.buf("mixb"))
            mixb_cache["off"] = arena_state["off"]
        arena_state["off"] = mixb_cache["off"]
        return mixb_cache["t"]

    def phaseA(pc):
        arena_reset()
        qbT, kbT, Vb, b_mb = alloc_mixb()
        fw.op(POOL, lambda: nc.gpsimd.memset(Vb[:, 0:16, :, 64:128], 1.0), writes=[b_mb])
        if pc["halo"]:
            for (vslot, side) in ((16, 0), (17, 1)):
                for g in range(2):
                    fw.op(POOL, lambda vslot=vslot, side=side, g=g: nc.gpsimd.tensor_copy(
                        out=Vb[:, vslot, g, 64:128], in_=hvb[:, pc["pi"], side, :]), reads=[b_const],
                        writes=[b_mb], partial=True)
        NX = 4
        XT = [ar("xt", [128, D], F32) for _ in range(NX)]
        CS = [ar("cs", [128, 64], F32) for _ in range(NX)]
        b_XT = [fw.buf(f"xt{i}") for i in range(NX)]
        b_CS = [fw.buf(f"cs{i}") for i in range(NX)]
        junk = ar("junk", [128, D], BF16)
        b_junk = fw.buf("junk")
        ssq = [ar("ssq", [128, 4], F32) for _ in range(2)]
        b_ssq = [fw.buf(f"ssq{i}") for i in range(2)]
        xn = [ar("xn", [128, D], BF16) for _ in range(2)]
        b_xn = [fw.buf(f"xn{i}") for i in range(2)]
        hT = [ar("hT", [128, 8, 128], BF16) for _ in range(2)]
        b_hT = [fw.buf(f"hT{i}") for i in range(2)]
        vt = [ar("vt", [128, 5, 128], BF16) for _ in range(2)]
        b_vt = [fw.buf(f"vt{i}") for i in range(2)]
        qkt = [ar("qkt", [128, 8, 256], BF16) for _ in range(2)]
        b_qkt = [fw.buf(f"qkt{i}") for i in range(2)]
        SETS = []
        for i in range(2):
            SETS.append(dict(sq=ar("sq", [128, 1664], F32), qn=ar("qn", [128, 1664], F32),
                             pa=ar("pa", [128, 1664], BF16), pb=ar("pb", [128, 1664], BF16),
                             s26=ar("s26", [128, 3, 26], F32), gc=ar("gc", [128, 4, 64], F32),
                             gs=ar("gs", [128, 2, 4, 32], F32), b_tab=fw.buf(f"tab{i}"),
                             b_sq=[fw.buf(f"sq{i}_{h}") for h in range(2)],
                             b_qn=[fw.buf(f"qn{i}_{h}") for h in range(2)],
                             b_pa=[fw.buf(f"pa{i}_{h}") for h in range(2)],
                             b_pb=[fw.buf(f"pb{i}_{h}") for h in range(2)],
                             b_s26=[fw.buf(f"s26{i}_{h}") for h in range(2)]))
        P1 = bank(0, 4)
        P2 = bank(4)
        pT = bank(5).bitcast(BF16)
        qkT = bank(6, 2)

        tiles = []
        for s in ([0, 1] if pc["halo"] else [0]):
            for n in range(16):
                tiles.append((s, n))
        T = len(tiles)

        def load(k):
            s, n = tiles[k]
            i = k % NX
            src = pc["xsrc"](n) if s == 0 else pc["hsrc"](n)
            fw.dma(XT[i], src, key=b_XT[i], writes=[b_XT[i]])
            ci = (pc["cs0"] if s == 0 else pc["csh"]) + n
            fw.dma(CS[i], cst[ci], key=b_CS[i], writes=[b_CS[i]])

        def info(k):
            s, n = tiles[k]
            edge = (s == 1 and n in (0, 15))
            if s == 0:
                ranges = [(0, 26)]
                chunks = list(range(13))
            else:
                ranges = [(8, 16)] + ([(24, 26)] if edge else [])
                chunks = [4, 5, 6, 7] + ([12] if edge else [])
            return s, n, edge, ranges, chunks

        def bks_of(c0, c1):
            return BK[c0 // 512:(c1 - 1) // 512 + 1]

        def xnorm(k):
            i = k % NX
            xt = XT[i]
            p = k % 2
            sq_, xn_ = ssq[p], xn[p]
            fw.op(ACT, lambda: nc.scalar.activation(out=junk, in_=xt, func=AF.Square, accum_out=sq_[:, 0:1]),
                  reads=[b_XT[i]], writes=[b_junk, b_ssq[p]])
            fw.op(ACT, lambda: nc.scalar.activation(out=sq_[:, 1:2], in_=sq_[:, 0:1], func=AF.Sqrt, scale=1.0 / D,
                                                    bias=eps_t), reads=[b_ssq[p], b_const], writes=[b_ssq[p]])
            fw.op(DVE, lambda: nc.vector.reciprocal(out=sq_[:, 2:3], in_=sq_[:, 1:2]), reads=[b_ssq[p]],
                  writes=[b_ssq[p]])
            fw.op(ACT, lambda: nc.scalar.activation(out=xn_, in_=xt, func=AF.Copy, scale=sq_[:, 2:3]),
                  reads=[b_XT[i], b_ssq[p]], writes=[b_xn[p]])

        def xT(k):
            p = k % 2
            xn_ = xn[p]
            for kk in range(8):
                fw.op(PE, lambda kk=kk: nc.tensor.transpose(out=pT[:, kk * 128:(kk + 1) * 128],
                                                            in_=xn_[:, kk * 128:(kk + 1) * 128], identity=ident),
                      reads=[b_xn[p], b_const], writes=[BK[5]], signal=(kk == 7), partial=(kk > 0))
            fw.op(ACT, lambda: nc.scalar.copy(out=hT[p].rearrange("p k t -> p (k t)"), in_=pT), reads=[BK[5]],
                  writes=[b_hT[p]])

        def proj(k, which):
            s, n, edge, ranges, chunks = info(k)
            p = k % 2
            if s == 0:
                allg = {"h0": [(0, 512, P1[:, 0:512], [BK[0]]), (512, 512, P1[:, 512:1024], [BK[1]])],
                        "h1": [(1024, 512, P1[:, 1024:1536], [BK[2]]), (1536, 256, P1[:, 1536:1792], [BK[3]])],
                        "va": [(1792, 512, P2, [BK[4]])]}
            else:
                allg = {"h0": [(512, 512, P1[:, 512:1024], [BK[1]])],
                        "h1": ([(1536, 256, P1[:, 1536:1792], [BK[3]])] if edge else []),
                        "va": [(1792, 512, P2, [BK[4]])]}
            for (c0, ncol, outap, bks) in allg[which]:
                for kk in range(8):
                    fw.op(PE, lambda kk=kk, c0=c0, ncol=ncol, outap=outap: nc.tensor.matmul(
                        outap, lhsT=hT[p][:, kk, :], rhs=winb[:, kk, c0:c0 + ncol], start=(kk == 0),
                        stop=(kk == 7)), reads=[b_hT[p], b_winb], writes=bks, signal=(kk == 7), partial=(kk > 0))

        def vevac(k):
            s, n, edge, ranges, chunks = info(k)
            p = k % 2
            fw.op(ACT, lambda: nc.scalar.copy(out=vt[p][:, 0:4, :].rearrange("p c e -> p (c e)"), in_=P2),
                  reads=[BK[4]], writes=[b_vt[p]])
            npair = 4
            if s == 0 or edge:
                vslot = n if s == 0 else (16 if n == 0 else 17)
                fw.op(ACT, lambda: nc.scalar.copy(out=Vb[:, vslot, :, 0:64],
                                                  in_=P1[:, 1664:1792].rearrange("p (g d) -> p g d", g=2)),
                      reads=[BK[3]], writes=[b_mb], partial=True)
            fw.dma(Vs[s, 0:npair, n * 128:(n + 1) * 128, :].rearrange("c p e -> p c e"), vt[p][:, 0:npair, :],
                   key=b_vt[p], reads=[b_vt[p]], writes=b_Vs[s][0:npair], partial=True)

        def half_ranges(k, which):
            s, n, edge, ranges, chunks = info(k)
            if s == 0:
                return [(0, 16)] if which == "h0" else [(16, 26)]
            if which == "h0":
                return [(8, 16)]
            return [(24, 26)] if edge else []

        def s2a(k, which):
            S_ = SETS[k % 2]
            sq, qn, s26 = S_["sq"], S_["qn"], S_["s26"]
            hx = 0 if which == "h0" else 1
            b_sq, b_qn, b_s26 = S_["b_sq"][hx], S_["b_qn"][hx], S_["b_s26"][hx]
            for (h0, h1) in half_ranges(k, which):
                H = h1 - h0
                c0, c1 = h0 * 64, h1 * 64
                bks = bks_of(c0, c1)
                fw.op(ACT, lambda: nc.scalar.activation(out=sq[:, c0:c1], in_=P1[:, c0:c1], func=AF.Square),
                      reads=bks, writes=[b_sq])
                fw.op(DVE, lambda: nc.vector.tensor_reduce(out=s26[:, 0, h0:h1],
                                                           in_=sq[:, c0:c1].rearrange("p (h d) -> p h d", d=64),
                                                           axis=AX.X, op=ALU.add), reads=[b_sq],
                      writes=[b_s26])
                fw.op(ACT, lambda: nc.scalar.activation(out=s26[:, 1, h0:h1], in_=s26[:, 0, h0:h1], func=AF.Sqrt,
                                                        scale=1.0 / 64, bias=eps_t), reads=[b_s26, b_const],
                      writes=[b_s26])
                fw.op(DVE, lambda: nc.vector.reciprocal(out=s26[:, 2, h0:h1], in_=s26[:, 1, h0:h1]),
                      reads=[b_s26], writes=[b_s26])
                fw.op(DVE, lambda: nc.vector.tensor_tensor(
                    out=qn[:, c0:c1].rearrange("p (h d) -> p h d", d=64),
                    in0=P1[:, c0:c1].rearrange("p (h d) -> p h d", d=64),
                    in1=s26[:, 2, h0:h1].unsqueeze(2).to_broadcast([128, H, 64]), op=ALU.mult),
                    reads=bks + [b_s26], writes=[b_qn])

        TYPES = {"h0": [(0, 0, 8), (1, 8, 16)], "h1": [(2, 16, 24), (3, 24, 26)]}

        def rope_ew(k):
            s, n, edge, ranges, chunks = info(k)
            i = k % NX
            cs, b_cs = CS[i], b_CS[i]
            S_ = SETS[k % 2]
            qn, pa, pb = S_["qn"], S_["pa"], S_["pb"]
            gc, gs, b_tab = S_["gc"], S_["gs"], S_["b_tab"]
            fw.op(DVE, lambda: nc.vector.tensor_tensor(
                out=gc.rearrange("p y (t d) -> p y t d", t=2), in0=gq.rearrange("p y (t d) -> p y t d", t=2),
                in1=cs[:, 0:32].unsqueeze(1).unsqueeze(1).to_broadcast([128, 4, 2, 32]), op=ALU.mult),
                reads=[b_const, b_cs], writes=[b_tab])
            fw.op(DVE, lambda: nc.vector.scalar_tensor_tensor(
                out=gs[:, 0, :, :], in0=gq[:, :, 32:64], scalar=-1.0,
                in1=cs[:, 32:64].unsqueeze(1).to_broadcast([128, 4, 32]), op0=ALU.mult, op1=ALU.mult),
                reads=[b_const, b_cs], writes=[b_tab], partial=True)
            fw.op(DVE, lambda: nc.vector.tensor_tensor(
                out=gs[:, 1, :, :], in0=gq[:, :, 0:32],
                in1=cs[:, 32:64].unsqueeze(1).to_broadcast([128, 4, 32]), op=ALU.mult),
                reads=[b_const, b_cs], writes=[b_tab], partial=True)
            for hx, which in enumerate(("h0", "h1")):
                b_qn, b_pa, b_pb = S_["b_qn"][hx], S_["b_pa"][hx], S_["b_pb"][hx]
                hr = half_ranges(k, which)
                if not hr:
                    continue
                first = True
                for (ty, t0_, t1_) in TYPES[which]:
                    for (h0, h1) in hr:
                        a0, a1 = max(h0, t0_), min(h1, t1_)
                        if a0 >= a1:
                            continue
                        H = a1 - a0
                        c0, c1 = a0 * 64, a1 * 64
                        v4 = lambda t: t[:, c0:c1].rearrange("p (h t d) -> p h t d", t=2, d=32)
                        gcb = gc[:, ty, :].rearrange("p (t d) -> p t d", t=2).unsqueeze(1).to_broadcast(
                            [128, H, 2, 32])
                        g1b = gs[:, 0, ty, :].unsqueeze(1).to_broadcast([128, H, 32])
                        g2b = gs[:, 1, ty, :].unsqueeze(1).to_broadcast([128, H, 32])
                        fw.op(DVE, lambda: nc.vector.tensor_tensor(out=v4(pa), in0=v4(qn), in1=gcb, op=ALU.mult),
                              reads=[b_qn, b_tab], writes=[b_pa], partial=(not first))
                        fw.op(DVE, lambda: nc.vector.tensor_tensor(out=v4(pb)[:, :, 0, :], in0=v4(qn)[:, :, 1, :],
                                                                   in1=g1b, op=ALU.mult), reads=[b_qn, b_tab],
                              writes=[b_pb], partial=(not first))
                        fw.op(DVE, lambda: nc.vector.tensor_tensor(out=v4(pb)[:, :, 1, :], in0=v4(qn)[:, :, 0, :],
                                                                   in1=g2b, op=ALU.mult), reads=[b_qn, b_tab],
                              writes=[b_pb], partial=True)
                        first = False

        def rope_T(k, which):
            s, n, edge, ranges, chunks = info(k)
            S_ = SETS[k % 2]
            hx = 0 if which == "h0" else 1
            pa, pb, b_pa, b_pb = S_["pa"], S_["pb"], S_["b_pa"][hx], S_["b_pb"][hx]
            ti = n % 2
            gi = (k // 2) % 2
            cl = [c for c in chunks if (c < 8) == (which == "h0")]
            if not cl:
                return
            for ci_, c in enumerate(cl):
                sl = c % 8
                osl = qkT[:, sl * 128:(sl + 1) * 128]
                fw.op(PE, lambda c=c, osl=osl: nc.tensor.matmul(osl, lhsT=pa[:, c * 128:(c + 1) * 128], rhs=ident,
                                                                start=True, stop=False),
                      reads=[b_pa, b_const], writes=[BK[6], BK[7]], signal=False, partial=(ci_ > 0))
                fw.op(PE, lambda c=c, osl=osl: nc.tensor.matmul(osl, lhsT=pb[:, c * 128:(c + 1) * 128], rhs=ident,
                                                                start=False, stop=True),
                      reads=[b_pb, b_const], writes=[BK[6], BK[7]], signal=(ci_ == len(cl) - 1), partial=True)
            tsl = slice(ti * 128, (ti + 1) * 128)
            if s == 0 and which == "h0":
                fw.op(ACT, lambda: nc.scalar.copy(out=qkt[gi][:, 0:8, tsl],
                                                  in_=qkT[:, 0:1024].rearrange("p (c t) -> p c t", t=128)),
                      reads=[BK[6], BK[7]], writes=[b_qkt[gi]], partial=True)
            elif s == 0:
                fw.op(ACT, lambda: nc.scalar.copy(out=qbT[:, :, n * 128:(n + 1) * 128],
                                                  in_=qkT[:, 0:512].rearrange("p (c t) -> p c t", t=128)),
                      reads=[BK[6], BK[7]], writes=[b_mb], partial=True)
                fw.op(ACT, lambda: nc.scalar.copy(out=kbT[:, n * 128:(n + 1) * 128], in_=qkT[:, 512:640]),
                      reads=[BK[6], BK[7]], writes=[b_mb], partial=True)
            elif which == "h0":
                fw.op(ACT, lambda: nc.scalar.copy(out=qkt[gi][:, 4:8, tsl],
                                                  in_=qkT[:, 512:1024].rearrange("p (c t) -> p c t", t=128)),
                      reads=[BK[6], BK[7]], writes=[b_qkt[gi]], partial=True)
            else:
                kc0 = 2048 if n == 0 else 2176
                fw.op(ACT, lambda: nc.scalar.copy(out=kbT[:, kc0:kc0 + 128], in_=qkT[:, 512:640]),
                      reads=[BK[6], BK[7]], writes=[b_mb], partial=True)
            if which == "h0" and ti == 1:
                g0 = (n // 2) * 256
                for (c0, c1) in (((0, 4), (4, 8)) if s == 0 else ((4, 8),)):
                    fw.dma(QKs[s, c0:c1, :, g0:g0 + 256].rearrange("c p t -> p c t"),
                           qkt[gi][:, c0:c1, :], key=b_qkt[gi], reads=[b_qkt[gi]], writes=b_QKs[s][c0:c1],
                           partial=True)

        for k in range(min(3, T)):
            load(k)
        xnorm(0)
        xT(0)
        if T > 1:
            xnorm(1)
        for k in range(T):
            if k + 3 < T:
                load(k + 3)
            proj(k, "h0")
            if k + 1 < T:
                xT(k + 1)
            s2a(k, "h0")
            proj(k, "h1")
            if k >= 1:
                rope_T(k - 1, "h0")
            proj(k, "va")
            vevac(k)
            s2a(k, "h1")
            if k >= 1:
                rope_T(k - 1, "h1")
            if k + 2 < T:
                xnorm(k + 2)
            rope_ew(k)
        rope_T(T - 1, "h0")
        rope_T(T - 1, "h1")
        fw.barrier()

    def phaseB(pc):
        arena_reset()
        halo = pc["halo"]
        pi = pc["pi"]
        qbT, kbT, Vb, b_mb = alloc_mixb()
        UB = []
        offs = []
        for u in range(2):
            offs.append(arena_state["off"])
            UB.append(dict(
                qT=ar("qT", [128, NT], BF16), kT=ar("kT", [128, 2, NT], BF16),
                Vn=ar("Vn", [128, 18, 128], BF16), V4=ar("V4", [128, 4, 6, 128], BF16),
                V16=ar("V16", [128, 16, 2, 128], BF16), b=fw.buf(f"ub{u}")))
        acc = ar("acc", [128, 2, NT], F32)
        b_acc = fw.buf("acc")
        NPT = 4
        Pt = [ar("Pt", [128, 2, 2, 128], BF16) for _ in range(NPT)]
        b_Pt = [fw.buf(f"Pt{i}") for i in range(NPT)]
        Ptb = [ar("Ptb", [128, 512], BF16) for _ in range(NPT)]
        b_Ptb = [fw.buf(f"Ptb{i}") for i in range(NPT)]
        dn = [ar("dn", [128, 512], F32) for _ in range(2)]
        b_dn = [fw.buf(f"dn{i}") for i in range(2)]
        LAG = 2

        def load_unitA(c):
            U = UB[c % 2]
            b = U["b"]
            rd = [b_QKs[0][c], b_QKs[0][4 + c], b_Vs[0][c]]
            fw.dma(U["qT"], QKs[0, c], key=b, reads=rd, writes=[b])
            fw.dma(U["kT"][:, 0, :], QKs[0, 4 + c], key=b, reads=rd, writes=[b], partial=True)
            for n4 in range(4):
                fw.dma(U["Vn"][:, 4 * n4:4 * n4 + 4, :],
                       Vs[0, c, 512 * n4:512 * n4 + 512, :].rearrange("(n p) e -> p n e", p=128), key=b, reads=rd,
                       writes=[b], partial=True)
            for tb in range(4):
                fw.dma(U["V4"][:, :, tb, :], Vs[0, c, 512 * tb:512 * tb + 512, :].rearrange("(p r) e -> p r e", r=4),
                       key=b, reads=rd, writes=[b], partial=True)
            fw.dma(U["V16"][:, :, 0, :], Vs[0, c].rearrange("(p r) e -> p r e", r=16), key=b, reads=rd, writes=[b],
                   partial=True)
            if halo:
                rd = [b_QKs[1][4 + c], b_Vs[1][c]]
                fw.dma(U["kT"][:, 1, :], QKs[1, 4 + c], key=b, reads=rd, writes=[b], partial=True)
                fw.dma(U["Vn"][:, 16, :], Vs[1, c, 0:128, :], key=b, reads=rd, writes=[b], partial=True)
                fw.dma(U["Vn"][:, 17, :], Vs[1, c, 1920:2048, :], key=b, reads=rd, writes=[b], partial=True)
                fw.dma(U["V4"][:, :, 4, :], Vs[1, c, 0:512, :].rearrange("(p r) e -> p r e", r=4), key=b, reads=rd,
                       writes=[b], partial=True)
                fw.dma(U["V4"][:, :, 5, :], Vs[1, c, 1536:2048, :].rearrange("(p r) e -> p r e", r=4), key=b,
                       reads=rd, writes=[b], partial=True)
                fw.dma(U["V16"][:, :, 1, :], Vs[1, c].rearrange("(p r) e -> p r e", r=16), key=b, reads=rd,
                       writes=[b], partial=True)

        def load_unitB():
            b = b_mb
            rd = b_QKs[0][8:13] + [b_Vs[0][4]]
            fw.dma(qbT, QKs[0, 8:12].rearrange("c p t -> p c t"), key=b, reads=rd, writes=[b])
            fw.dma(kbT[:, 0:NT], QKs[0, 12], key=b, reads=rd, writes=[b], partial=True)
            for n4 in range(4):
                fw.dma(Vb[:, 4 * n4:4 * n4 + 4, :],
                       Vs[0, 4, 512 * n4:512 * n4 + 512, :].rearrange("(n p) e -> p n e", p=128), key=b, reads=rd,
                       writes=[b], partial=True)
            if halo:
                rd = [b_QKs[1][12], b_Vs[1][4]]
                fw.dma(kbT[:, 2048:2176], QKs[1, 12, :, 0:128], key=b, reads=rd, writes=[b], partial=True)
                fw.dma(kbT[:, 2176:2304], QKs[1, 12, :, 1920:2048], key=b, reads=rd, writes=[b], partial=True)
                fw.dma(Vb[:, 16, :], Vs[1, 4, 0:128, :], key=b, reads=rd, writes=[b], partial=True)
                fw.dma(Vb[:, 17, :], Vs[1, 4, 1920:2048, :], key=b, reads=rd, writes=[b], partial=True)

        def unitA(c):
            U = UB[c % 2]
            b = U["b"]
            qT, kT = U["qT"], U["kT"]
            blocks = []
            for (d, Td) in ((1, 16), (4, 4), (16, 1)):
                Ld = NT // d
                for r in range(d):
                    for bb in range(-1, Td):
                        if bb == -1:
                            l0, nq, qi0 = 0, 64, 64
                        elif bb == Td - 1:
                            l0, nq, qi0 = Ld - 64, 64, 0
                        else:
                            l0, nq, qi0 = 128 * bb + 64, 128, 0
                        qsl = slice(d * l0 + r, d * (l0 + nq - 1) + r + 1, d)
                        tl = []
                        for wt, tb in enumerate((bb, bb + 1)):
                            if 0 <= tb < Td:
                                kp = (0, 128)
                                slot, tidx = 0, tb
                                if d == 1:
                                    Vt = U["Vn"][:, tb, :]
                                elif d == 4:
                                    Vt = U["V4"][:, r, tb, :]
                                else:
                                    Vt = U["V16"][:, r, 0, :]
                                dl = ones64
                            elif not halo:
                                continue
                            elif tb == -1:
                                kp = (64, 128)
                                slot, tidx = 1, Td - 1
                                Vt = (U["Vn"][:, 17, :] if d == 1 else U["V4"][:, r, 5, :] if d == 4
                                      else U["V16"][:, r, 1, :])
                                dl = hvb[:, pi, 1, :]
                            else:
                                kp = (0, 64)
                                slot, tidx = 1, 0
                                Vt = (U["Vn"][:, 16, :] if d == 1 else U["V4"][:, r, 4, :] if d == 4
                                      else U["V16"][:, r, 1, :])
                                dl = hvb[:, pi, 0, :]
                            k0 = d * (128 * tidx + kp[0]) + r
                            ksl = slice(k0, k0 + d * (kp[1] - kp[0] - 1) + 1, d)
                            tl.append((wt, kp, slot, ksl, Vt, dl))
                        blocks.append((nq, qi0, qsl, tl, d == 1))

            def views(bi):
                sbk = 2 * (bi % 3)
                obk = 6 + bi % 2
                S4 = bank(sbk, 2).rearrange("p (h w q) -> p h w q", h=2, w=4)
                O2 = bank(obk)[:, 0:256].rearrange("p (a q) -> p a q", a=2)
                return sbk, obk, S4, O2, bi % NPT

            def front(bi):
                nq, qi0, qsl, tl, first_br = blocks[bi]
                sbk, obk, S4, O2, pti = views(bi)
                P4 = Pt[pti]
                nS = len(tl) * 2
                si = 0
                for (wt, kp, slot, ksl, Vt, dl) in tl:
                    for hh in range(2):
                        si += 1
                        fw.op(PE, lambda wt=wt, kp=kp, slot=slot, ksl=ksl, hh=hh: nc.tensor.matmul(
                            S4[kp[0]:kp[1], hh, wt, 0:nq], lhsT=kT[64 * hh:64 * hh + 64, slot, ksl],
                            rhs=qT[64 * hh:64 * hh + 64, qsl], start=True, stop=True),
                            reads=[b], writes=[BK[sbk], BK[sbk + 1]], signal=(si == nS), partial=(si > 1))
                ME, me = (DVE, nc.vector)
                if len(tl) == 2 and all(t[1] == (0, 128) for t in tl):
                    fw.op(ACT, lambda: nc.scalar.activation(
                        out=P4[:, :, :, 0:nq], in_=S4[:, :, 0:2, 0:nq], func=AF.Exp, scale=0.125),
                        reads=[BK[sbk], BK[sbk + 1]], writes=[b_Pt[pti]])
                    fw.op(ME, lambda: me.tensor_tensor(
                        out=P4[:, :, :, 0:nq], in0=P4[:, :, :, 0:nq],
                        in1=mask[:, :, qi0:qi0 + nq].unsqueeze(1).to_broadcast([128, 2, 2, nq]),
                        op=ALU.mult), reads=[b_Pt[pti], b_const], writes=[b_Pt[pti]], partial=True)
                else:
                    for ti_, (wt, kp, slot, ksl, Vt, dl) in enumerate(tl):
                        kpn = kp[1] - kp[0]
                        fw.op(ACT, lambda wt=wt, kp=kp: nc.scalar.activation(
                            out=P4[kp[0]:kp[1], :, wt, 0:nq], in_=S4[kp[0]:kp[1], :, wt, 0:nq], func=AF.Exp,
                            scale=0.125), reads=[BK[sbk], BK[sbk + 1]], writes=[b_Pt[pti]], partial=(ti_ > 0))
                        fw.op(ME, lambda wt=wt, kp=kp, kpn=kpn: me.tensor_tensor(
                            out=P4[kp[0]:kp[1], :, wt, 0:nq], in0=P4[kp[0]:kp[1], :, wt, 0:nq],
                            in1=mask[kp[0]:kp[1], wt, qi0:qi0 + nq].unsqueeze(1).to_broadcast([kpn, 2, nq]),
                            op=ALU.mult), reads=[b_Pt[pti], b_const], writes=[b_Pt[pti]], partial=True)

            def back(bi):
                nq, qi0, qsl, tl, first_br = blocks[bi]
                sbk, obk, S4, O2, pti = views(bi)
                P4 = Pt[pti]
                nP = len(tl) * 4
                pi_ = 0
                for ti_, (wt, kp, slot, ksl, Vt, dl) in enumerate(tl):
                    for hh in range(2):
                        pi_ += 1
                        fw.op(PE, lambda wt=wt, kp=kp, Vt=Vt, hh=hh, ti_=ti_: nc.tensor.matmul(
                            O2[64 * hh:64 * hh + 64, 0, 0:nq], lhsT=Vt[kp[0]:kp[1], 64 * hh:64 * hh + 64],
                            rhs=P4[kp[0]:kp[1], hh, wt, 0:nq], start=(ti_ == 0), stop=False,
                            skip_group_check=True),
                            reads=[b, b_Pt[pti]], writes=[BK[obk]], signal=False, partial=(pi_ > 1))
                        pi_ += 1
                        fw.op(PE, lambda wt=wt, kp=kp, dl=dl, hh=hh, ti_=ti_: nc.tensor.matmul(
                            O2[64 * hh:64 * hh + 64, 1, 0:nq], lhsT=dl[kp[0]:kp[1], :],
                            rhs=P4[kp[0]:kp[1], hh, wt, 0:nq], start=False, stop=(ti_ == len(tl) - 1),
                            skip_group_check=True),
                            reads=[b_const, b_Pt[pti]], writes=[BK[obk]], signal=(pi_ == nP), partial=True)
                if first_br:
                    fw.op(DVE, lambda: nc.vector.tensor_copy(out=acc[:, :, qsl], in_=O2[:, :, 0:nq]),
                          reads=[BK[obk]], writes=[b_acc], partial=True)
                else:
                    fw.op(DVE, lambda: nc.vector.tensor_tensor(out=acc[:, :, qsl], in0=O2[:, :, 0:nq],
                                                               in1=acc[:, :, qsl], op=ALU.add),
                          reads=[BK[obk], b_acc], writes=[b_acc], partial=True)

            nb = len(blocks)
            for i in range(nb + LAG):
                if i < nb:
                    front(i)
                if i - LAG >= 0:
                    back(i - LAG)
            fw.op(ACT, lambda: nc.scalar.activation(out=acc[:, 1, :], in_=acc[:, 1, :], func=AF.Ln), reads=[b_acc],
                  writes=[b_acc], partial=True)
            fw.op(ACT, lambda: nc.scalar.activation(out=acc[:, 1, :], in_=acc[:, 1, :], func=AF.Exp, scale=-1.0),
                  reads=[b_acc], writes=[b_acc], partial=True)
            fw.op(DVE, lambda: nc.vector.tensor_tensor(out=AT[:, c, :], in0=acc[:, 0, :], in1=acc[:, 1, :],
                                                       op=ALU.mult), reads=[b_acc], writes=[b_AT[c]])

        def unitB():
            b = b_mb
            items = []
            for bb in range(16):
                for g in range(2):
                    tl = []
                    for wt, tb in ((0, bb - 1), (None, bb), (1, bb + 1)):
                        if 0 <= tb < 16:
                            tl.append((wt, slice(128 * tb, 128 * tb + 128), tb))
                        elif not halo:
                            continue
                        elif tb == -1:
                            tl.append((wt, slice(2176, 2304), 17))
                        else:
                            tl.append((wt, slice(2048, 2176), 16))
                    for ti_, t in enumerate(tl):
                        items.append((bb, g, ti_, len(tl), t))

            def vw(i, bb, g):
                sbk = i % 3
                nbk = 3 + 2 * (bb % 2) + g
                return sbk, bank(sbk), nbk, bank(nbk), i % NPT

            def front(i):
                bb, g, ti_, ntl, (wt, ksl, vslot) = items[i]
                sbk, Sb, nbk, NB, pbi = vw(i, bb, g)
                fw.op(PE, lambda: nc.tensor.matmul(
                    Sb, lhsT=kbT[64 * g:64 * g + 64, ksl], rhs=qbT[64 * g:64 * g + 64, :, 128 * bb:128 * bb + 128],
                    start=True, stop=True), reads=[b], writes=[BK[sbk]])
                fw.op(ACT, lambda: nc.scalar.activation(out=Ptb[pbi], in_=Sb, func=AF.Exp, scale=0.125),
                      reads=[BK[sbk]], writes=[b_Ptb[pbi]])
                if wt is not None:
                    fw.op(DVE, lambda: nc.vector.tensor_tensor(
                        out=Ptb[pbi].rearrange("p (c q) -> p c q", c=4),
                        in0=Ptb[pbi].rearrange("p (c q) -> p c q", c=4),
                        in1=mask[:, wt, :].unsqueeze(1).to_broadcast([128, 4, 128]), op=ALU.mult),
                        reads=[b_Ptb[pbi], b_const], writes=[b_Ptb[pbi]])

            def back(i):
                bb, g, ti_, ntl, (wt, ksl, vslot) = items[i]
                sbk, Sb, nbk, NB, pbi = vw(i, bb, g)
                last = (ti_ == ntl - 1)
                if ti_ == 0:
                    fw.op(PE, lambda: nc.tensor.matmul(NB, lhsT=esel[0:1, :], rhs=esr[0:1, g, :], start=True,
                                                       stop=False, skip_group_check=True),
                          reads=[b_const], writes=[BK[nbk]], signal=False)
                fw.op(PE, lambda: nc.tensor.matmul(NB, lhsT=Vb[:, vslot, g, :], rhs=Ptb[pbi], start=False, stop=last,
                                                   skip_group_check=True),
                      reads=[b, b_Ptb[pbi]], writes=[BK[nbk]], signal=True, partial=True)
                if last:
                    fw.op(ACT, lambda: nc.scalar.activation(out=dn[g][0:64, :], in_=NB[64:128, :], func=AF.Ln),
                          reads=[BK[nbk]], writes=[b_dn[g]])
                    fw.op(ACT, lambda: nc.scalar.activation(out=dn[g][0:64, :], in_=dn[g][0:64, :], func=AF.Exp,
                                                            scale=-1.0), reads=[b_dn[g]], writes=[b_dn[g]])
                    fw.op(DVE, lambda: nc.vector.tensor_tensor(
                        out=AT[64 * g:64 * g + 64, 4:8, 128 * bb:128 * bb + 128],
                        in0=NB[0:64, :].rearrange("p (c q) -> p c q", c=4),
                        in1=dn[g][0:64, :].rearrange("p (c q) -> p c q", c=4), op=ALU.mult),
                        reads=[BK[nbk], b_dn[g]], writes=b_AT[4:8], partial=True)

            ni = len(items)
            for i in range(ni + LAG):
                if i < ni:
                    front(i)
                if i - LAG >= 0:
                    back(i - LAG)

        load_unitA(0)
        load_unitA(1)
        unitB()
        for c in range(4):
            if c >= 1 and c + 1 < 4:
                load_unitA(c + 1)
            unitA(c)
        fw.barrier()

    def phaseC(pc):
        arena_reset()
        x1 = [ar("x1", [128, 4, D], F32) for _ in range(2)]
        b_x1 = [[fw.buf(f"x1_{u}_{t}") for t in range(4)] for u in range(2)]
        XR = [ar("xr", [128, D], F32) for _ in range(2)]
        b_XR = [fw.buf(f"xr{i}") for i in range(2)]
        junk = ar("junk", [128, D], BF16)
        b_junk = fw.buf("junkc")
        ssq = ar("ssq", [128, 4], F32)
        b_ssq = fw.buf("ssqc")
        xn2 = [ar("xn2", [128, D], BF16) for _ in range(2)]
        b_xn2 = [fw.buf(f"xn2_{i}") for i in range(2)]
        h2T = [ar("h2T", [128, 8, 512], BF16) for _ in range(2)]
        b_h2T = [fw.buf(f"h2T{i}") for i in range(2)]
        actT = ar("actT", [128, NJ, 512], BF16)
        b_actT = [fw.buf(f"actT{j}") for j in range(NJ)]
        sg = [ar("sg", [128, 512], F32) for _ in range(2)]
        b_sg = [fw.buf(f"sg{i}") for i in range(2)]
        WG = [ar("WG", [128, 8, 128], BF16) for _ in range(3)]
        WU = [ar("WU", [128, 8, 128], BF16) for _ in range(3)]
        WD = [ar("WD", [128, D], BF16) for _ in range(3)]
        b_WG = [fw.buf(f"WG{i}") for i in range(3)]
        b_WU = [fw.buf(f"WU{i}") for i in range(3)]
        b_WD = [fw.buf(f"WD{i}") for i in range(3)]
        OS = [ar("OS", [128, D], F32) for _ in range(2)]
        b_OS = [fw.buf(f"OS{i}") for i in range(2)]
        pT = bank(2).bitcast(BF16)

        def ld_gu(j):
            i = j % 3
            fw.dma(WG[i], wgs[j], key=b_WG[i], reads=[b_wsc], writes=[b_WG[i]])
            fw.dma(WU[i], wus[j], key=b_WU[i], reads=[b_wsc], writes=[b_WU[i]])

        def ld_d(j):
            i = j % 3
            fw.dma(WD[i], wds[j], key=b_WD[i], reads=[b_wsc], writes=[b_WD[i]])

        xcnt = [0]

        def c1_load(sp_, t):
            r0 = 512 * sp_ + 128 * t
            xi = (4 * sp_ + t) % 2
            fw.dma(XR[xi], pc["xsrc"](r0 // 128), key=b_XR[xi], writes=[b_XR[xi]])

        def c1_a(sp_, t):
            u = sp_ % 2
            r0 = 512 * sp_ + 128 * t
            tcs = slice(r0, r0 + 128)
            xi = (4 * sp_ + t) % 2
            xp_ = t % 2
            for half in range(2):
                for kk in range(8):
                    fw.op(PE, lambda kk=kk, half=half: nc.tensor.matmul(
                        bank(half), lhsT=AT[:, kk, tcs], rhs=woutb[:, kk, half * 512:(half + 1) * 512],
                        start=(kk == 0), stop=(kk == 7)), reads=[b_AT[kk], b_woutb], writes=[BK[half]],
                        signal=(kk == 7), partial=(kk > 0))
            fw.op(DVE, lambda: nc.vector.tensor_tensor(out=x1[u][:, t, :], in0=bank(0, 2), in1=XR[xi], op=ALU.add),
                  reads=[BK[0], BK[1], b_XR[xi]], writes=[b_x1[u][t]])
            fw.op(ACT, lambda: nc.scalar.activation(out=junk, in_=x1[u][:, t, :], func=AF.Square,
                                                    accum_out=ssq[:, 0:1]), reads=[b_x1[u][t]],
                  writes=[b_junk, b_ssq])
            fw.op(ACT, lambda: nc.scalar.activation(out=ssq[:, 1:2], in_=ssq[:, 0:1], func=AF.Sqrt,
                                                    scale=1.0 / D, bias=eps_t), reads=[b_ssq, b_const],
                  writes=[b_ssq])
            fw.op(DVE, lambda: nc.vector.reciprocal(out=ssq[:, 2:3], in_=ssq[:, 1:2]), reads=[b_ssq],
                  writes=[b_ssq])
            fw.op(ACT, lambda: nc.scalar.activation(out=xn2[xp_], in_=x1[u][:, t, :], func=AF.Copy,
                                                    scale=ssq[:, 2:3]),
                  reads=[b_x1[u][t], b_ssq], writes=[b_xn2[xp_]])

        def c1_b(sp_, t):
            u = sp_ % 2
            xp_ = t % 2
            for kk in range(8):
                fw.op(PE, lambda kk=kk: nc.tensor.transpose(out=pT[:, kk * 128:(kk + 1) * 128],
                                                            in_=xn2[xp_][:, kk * 128:(kk + 1) * 128],
                                                            identity=ident),
                      reads=[b_xn2[xp_], b_const], writes=[BK[2]], signal=(kk == 7), partial=(kk > 0))
            fw.op(ACT, lambda: nc.scalar.copy(out=h2T[u][:, :, 128 * t:128 * t + 128],
                                              in_=pT.rearrange("p (k q) -> p k q", k=8)),
                  reads=[BK[2]], writes=[b_h2T[u]], partial=(t > 0))

        def c2(sp_, j):
            u = sp_ % 2
            i = j % 3
            gb, ub = 4 + j % 2, 6 + j % 2
            for kk in range(8):
                fw.op(PE, lambda kk=kk: nc.tensor.matmul(bank(gb), lhsT=WG[i][:, kk, :], rhs=h2T[u][:, kk, :],
                                                         start=(kk == 0), stop=(kk == 7)),
                      reads=[b_WG[i], b_h2T[u]], writes=[BK[gb]], signal=(kk == 7), partial=(kk > 0))
            for kk in range(8):
                fw.op(PE, lambda kk=kk: nc.tensor.matmul(bank(ub), lhsT=WU[i][:, kk, :], rhs=h2T[u][:, kk, :],
                                                         start=(kk == 0), stop=(kk == 7)),
                      reads=[b_WU[i], b_h2T[u]], writes=[BK[ub]], signal=(kk == 7), partial=(kk > 0))
            si = j % 2
            fw.op(ACT, lambda: nc.scalar.activation(out=sg[si], in_=bank(gb), func=AF.Silu), reads=[BK[gb]],
                  writes=[b_sg[si]])
            fw.op(DVE, lambda: nc.vector.tensor_tensor(out=actT[:, j, :], in0=bank(ub), in1=sg[si],
                                                       op=ALU.mult), reads=[BK[ub], b_sg[si]],
                  writes=[b_actT[j]])

        def c3(sp_):
            for j in range(NJ):
                if j + 2 < NJ:
                    ld_d(j + 2)
                i = j % 3
                for t in range(4):
                    for half in range(2):
                        fw.op(PE, lambda t=t, half=half: nc.tensor.matmul(
                            bank(2 * t + half), lhsT=actT[:, j, 128 * t:128 * t + 128],
                            rhs=WD[i][:, half * 512:(half + 1) * 512], start=(j == 0), stop=(j == NJ - 1)),
                            reads=[b_actT[j], b_WD[i]], writes=[BK[2 * t + half]],
                            signal=(j == NJ - 1 or (t == 3 and half == 1)), partial=(j > 0))

        def c4(sp_):
            u = sp_ % 2
            for t in (2, 3, 0, 1):
                r0 = 512 * sp_ + 128 * t
                oi = t % 2
                fw.op(DVE, lambda: nc.vector.tensor_tensor(out=OS[oi], in0=bank(2 * t, 2), in1=x1[u][:, t, :],
                                                           op=ALU.add),
                      reads=[BK[2 * t], BK[2 * t + 1], b_x1[u][t]], writes=[b_OS[oi]])
                fw.dma(pc["out"](r0), OS[oi], key=b_OS[oi], reads=[b_OS[oi]])

        NSP = 4
        c1_load(0, 0)
        c1_load(0, 1)
        ld_gu(0)
        ld_gu(1)
        for t in range(4):
            c1_a(0, t)
            if t + 2 < 4:
                c1_load(0, t + 2)
            if t >= 1:
                c1_b(0, t - 1)
        c1_b(0, 3)
        for sp_ in range(NSP):
            nxt = sp_ + 1 < NSP
            sched_a = {2: 0, 6: 1, 10: 2, 14: 3}
            sched_b = {5: 0, 9: 1, 13: 2, 17: 3}
            if nxt:
                c1_load(sp_ + 1, 0)
                c1_load(sp_ + 1, 1)
            for j in range(NJ):
                if j + 2 < NJ:
                    ld_gu(j + 2)
                if j == NJ - 2:
                    ld_d(0)
                if j == NJ - 1:
                    ld_d(1)
                c2(sp_, j)
                if nxt and j in sched_a:
                    t = sched_a[j]
                    c1_a(sp_ + 1, t)
                    if t + 2 < 4:
                        c1_load(sp_ + 1, t + 2)
                if nxt and j in sched_b:
                    c1_b(sp_ + 1, sched_b[j])
            if nxt:
                ld_gu(0)
                ld_gu(1)
            c3(sp_)
            c4(sp_)
        fw.barrier()

    for ip, pc in enumerate(pieces):
        if sel is not None and ip not in sel:
            continue
        if "A" in phases:
            phaseA(pc)
        if "B" in phases:
            phaseB(pc)
        if "C" in phases:
            phaseC(pc)
    fw.finish()
    return nc


_CACHE = {}


def _host_consts():
    bf = ml_dtypes.bfloat16
    kj = np.arange(128)[:, None]
    qi = np.arange(128)[None, :]
    masks = np.stack([(kj >= qi), (kj <= qi)], axis=1).astype(np.float32).astype(bf)
    ident = np.eye(128, dtype=np.float32).astype(bf)
    return masks, ident


def _rope_table(pos):
    half = 32
    inv = (np.float32(10000.0) ** (np.float32(-2.0) * np.arange(half, dtype=np.float32) / np.float32(64))).astype(
        np.float32)
    ang = (pos.astype(np.float32)[:, None] * inv[None, :]).astype(np.float32)
    return np.concatenate([np.cos(ang), np.sin(ang)], axis=1).astype(np.float32)


def kernel(x_prompt, x_sample, attn_norm, w_in, qnorm_a, knorm_a, qnorm_b, knorm_b,
           sink_b, w_out, ffn_norm, w_gate, w_up, w_down):
    f32 = np.float32
    x_prompt = np.asarray(x_prompt, f32)
    x_sample = np.asarray(x_sample, f32)
    w_in = np.asarray(w_in, f32)[0]
    w_out = np.asarray(w_out, f32)[0]
    w_gate = np.ascontiguousarray(np.asarray(w_gate, f32)[0])
    w_up = np.ascontiguousarray(np.asarray(w_up, f32)[0])
    w_down = np.ascontiguousarray(np.asarray(w_down, f32)[0])
    QBo, KBo, VBo, VAo = 1536, 2048, 2176, 1024
    cols = list(range(0, 512)) + list(range(512, 1024))
    for c in range(4):
        for g in range(2):
            h = 4 * g + c
            cols += list(range(QBo + h * 64, QBo + h * 64 + 64))
    cols += list(range(KBo, KBo + 128)) + list(range(VBo, VBo + 128)) + list(range(VAo, VAo + 512))
    w_in_p = np.ascontiguousarray(w_in[:, cols])
    rows = list(range(512))
    for c in range(4):
        for g in range(2):
            h = 4 * g + c
            rows += list(range(512 + h * 64, 512 + h * 64 + 64))
    w_out_p = np.ascontiguousarray(w_out[rows, :])
    gA = np.ascontiguousarray(np.asarray(attn_norm, f32)[0].reshape(8, 128).T)
    gF = np.ascontiguousarray(np.asarray(ffn_norm, f32)[0].reshape(8, 128).T)
    gq1 = np.stack([np.asarray(qnorm_a, f32)[0], np.asarray(knorm_a, f32)[0],
                    np.asarray(qnorm_b, f32)[0], np.asarray(knorm_b, f32)[0]])
    gq = np.ascontiguousarray(np.broadcast_to(gq1[None, :, :], (128, 4, 64)))
    sk = np.asarray(sink_b, f32)[0]
    sinkT = np.zeros((1, 2, 512), f32)
    for g in range(2):
        for c in range(4):
            sinkT[0, g, 128 * c:128 * c + 128] = sk[4 * g + c]
    esel = np.zeros((1, 128), f32)
    esel[0, 64:] = 1.0
    esel = esel.astype(ml_dtypes.bfloat16)
    masks, ident = _host_consts()

    if "nc" not in _CACHE:
        _CACHE["nc"] = build_program()
    nc = _CACHE["nc"]

    in_maps = []
    p128 = np.arange(128)
    for core in range(NCORES):
        ps, q = core // 4, core % 4
        lo = q * 4096 - 1024
        xp = np.zeros((6144, D), f32)
        a, b = max(lo, 0), min(lo + 6144, 16384)
        xp[a - lo:b - lo] = x_prompt[ps, a:b]
        cst = np.zeros((80, 128, 64), f32)
        for n in range(16):
            cst[n] = _rope_table(n * 128 + p128)
        for pi in range(2):
            base = q * 4096 + 2048 * pi
            for n in range(16):
                cst[16 + 32 * pi + n] = _rope_table(base + n * 128 + p128)
                hidx = n * 128 + p128
                local = np.where(hidx < 1024, 2048 + hidx, hidx - 2048)
                cst[32 + 32 * pi + n] = _rope_table(np.maximum(base + local, 0))
        hv = np.ones((128, 2, 2, 64), f32)
        if q == 0:
            hv[:, 0, 1, :] = 0.0
        if q == 3:
            hv[:, 1, 0, :] = 0.0
        in_maps.append({
            "xs": np.ascontiguousarray(x_sample[4 * core:4 * core + 4]),
            "xp": xp, "cst": cst, "w_in": w_in_p, "w_out": w_out_p, "w_gate": w_gate, "w_up": w_up,
            "w_down": w_down, "gA": gA, "gF": gF, "gq": gq, "sinkT": sinkT, "esel": esel, "masks": masks, "ident": ident,
            "hv": hv.astype(ml_dtypes.bfloat16),
        })
    res = run_bass_kernel_spmd(nc, in_maps, core_ids=list(range(NCORES)))
    y_prompt = np.zeros((2, 16384, D), f32)
    y_sample = np.zeros((32, NT, D), f32)
    for core in range(NCORES):
        r = res.results[core]
        ps, q = core // 4, core % 4
        y_prompt[ps, q * 4096:(q + 1) * 4096] = np.asarray(r["yp"], f32)
        y_sample[4 * core:4 * core + 4] = np.asarray(r["ys"], f32)
    return (y_prompt, y_sample)
```

```python
import numpy as np
import ml_dtypes
import concourse.bass as bass
import concourse.mybir as mybir
from concourse.bass_utils import run_bass_kernel_spmd

F32 = mybir.dt.float32
BF16 = mybir.dt.bfloat16
ALU = mybir.AluOpType
AF = mybir.ActivationFunctionType
AX = mybir.AxisListType

D = 1024
NT = 2048
DFF = 2816
NJ = 22
EPS = 1e-6
NCORES = 8


class Src:
    def __init__(self, nc, name, step):
        self.sem = nc.alloc_semaphore(name=name)
        self.count = 0
        self.step = step
        self.name = name


class Buf:
    __slots__ = ("name", "writes", "reads", "dsrc")

    def __init__(self, name):
        self.name = name
        self.writes = []
        self.reads = []
        self.dsrc = None


class Eng:
    def __init__(self, fw, name, eng, is_pe=False):
        self.name = name
        self.eng = eng
        self.src = Src(fw.nc, "e_" + name, 1)
        self.known = {}
        self.is_pe = is_pe


class FW:
    def __init__(self, nc):
        self.nc = nc
        self.pe = Eng(self, "pe", nc.tensor, is_pe=True)
        self.act = Eng(self, "act", nc.scalar)
        self.dve = Eng(self, "dve", nc.vector)
        self.pool = Eng(self, "pool", nc.gpsimd)
        self.sp = Eng(self, "sp", nc.sync)
        self.engs = [self.pe, self.act, self.dve, self.pool, self.sp]
        self.nbuf = 0
        self.dsrcs = []
        self.bufs = {}

    def buf(self, name=None):
        self.nbuf += 1
        if name is None:
            return Buf(f"b{self.nbuf}")
        if name not in self.bufs:
            self.bufs[name] = Buf(name)
        return self.bufs[name]

    def _collect(self, E, reads, writes):
        need = {}

        def add(tok):
            s, c = tok
            if E.is_pe and s is E.src:
                return
            if s.step == 16:
                c = s.count
            if need.get(s, 0) < c:
                need[s] = c
        for b in reads:
            for t in b.writes:
                add(t)
        for b in writes:
            for t in b.writes:
                add(t)
            for t in b.reads:
                add(t)
        for s, c in need.items():
            if E.known.get(s, 0) >= c:
                continue
            E.eng.wait_ge(s.sem, c)
            E.known[s] = c

    def _commit(self, tok, reads, writes, partial):
        for b in writes:
            if partial:
                b.writes.append(tok)
                if len(b.writes) > 24:
                    m = {}
                    for s, c in b.writes:
                        if m.get(s, 0) < c:
                            m[s] = c
                    b.writes = list(m.items())
            else:
                b.writes = [tok]
                b.reads = []
        for b in reads:
            b.reads.append(tok)
            if len(b.reads) > 24:
                m = {}
                for s, c in b.reads:
                    if m.get(s, 0) < c:
                        m[s] = c
                b.reads = list(m.items())

    def op(self, E, fn, reads=(), writes=(), signal=True, partial=False):
        self._collect(E, reads, writes)
        ins = fn()
        if signal:
            E.src.count += 1
            ins.then_inc(E.src.sem, 1)
            tok = (E.src, E.src.count)
        else:
            tok = (E.src, E.src.count + 1)
        self._commit(tok, reads, writes, partial)
        return ins

    def dma(self, out_ap, in_ap, key, reads=(), writes=(), partial=False):
        E = self.sp
        if key.dsrc is None:
            key.dsrc = Src(self.nc, "d_" + key.name, 16)
            self.dsrcs.append(key.dsrc)
        self._collect(E, reads, writes)
        ins = E.eng.dma_start(out=out_ap, in_=in_ap)
        key.dsrc.count += 16
        ins.then_inc(key.dsrc.sem, 16)
        tok = (key.dsrc, key.dsrc.count)
        self._commit(tok, reads, writes, partial)
        return ins

    def barrier(self):
        for E in self.engs:
            for F in self.engs:
                if F is E or F is self.sp:
                    continue
                c = F.src.count
                if c > 0 and E.known.get(F.src, 0) < c:
                    E.eng.wait_ge(F.src.sem, c)
                    E.known[F.src] = c
            for d in self.dsrcs:
                if d.count > 0 and E.known.get(d, 0) < d.count:
                    E.eng.wait_ge(d.sem, d.count)
                    E.known[d] = d.count

    def finish(self):
        E = self.sp
        for F in self.engs:
            if F is E:
                continue
            if F.src.count > 0:
                E.eng.wait_ge(F.src.sem, F.src.count)
        for d in self.dsrcs:
            if d.count > 0:
                E.eng.wait_ge(d.sem, d.count)


def build_program(sel=None, phases="ABC"):
    nc = bass.Bass("TRN2", target_bir_lowering=False)
    fw = FW(nc)
    PE, ACT, DVE, POOL = fw.pe, fw.act, fw.dve, fw.pool

    def din(name, shape, dt=F32):
        return nc.dram_tensor(name, list(shape), dt, kind="ExternalInput").ap()

    xs = din("xs", [4, NT, D])
    xp = din("xp", [6144, D])
    cst = din("cst", [80, 128, 64])
    w_in = din("w_in", [D, 2304])
    w_out = din("w_out", [D, D])
    w_gate = din("w_gate", [D, DFF])
    w_up = din("w_up", [D, DFF])
    w_down = din("w_down", [DFF, D])
    gA_d = din("gA", [128, 8])
    gF_d = din("gF", [128, 8])
    gq_d = din("gq", [128, 4, 64])
    sink_d = din("sinkT", [1, 2, 512])
    esel_d = din("esel", [1, 128], BF16)
    mask_d = din("masks", [128, 3, 128], BF16)
    ident_d = din("ident", [128, 128], BF16)
    hv_d = din("hv", [128, 2, 2, 64], BF16)
    ys = nc.dram_tensor("ys", [4, NT, D], F32, kind="ExternalOutput").ap()
    yp = nc.dram_tensor("yp", [4096, D], F32, kind="ExternalOutput").ap()
    QKs = nc.dram_tensor("QKs", [2, 13, 128, NT], BF16, kind="Internal").ap()
    Vs = nc.dram_tensor("Vs", [2, 5, NT, 128], BF16, kind="Internal").ap()
    wgs = nc.dram_tensor("wgs", [NJ, 128, 8, 128], BF16, kind="Internal").ap()
    wus = nc.dram_tensor("wus", [NJ, 128, 8, 128], BF16, kind="Internal").ap()
    wds = nc.dram_tensor("wds", [NJ, 128, D], BF16, kind="Internal").ap()
    b_QKs = [[fw.buf(f"QKs{s}_{c}") for c in range(13)] for s in range(2)]
    b_Vs = [[fw.buf(f"Vs{s}_{c}") for c in range(5)] for s in range(2)]
    b_wsc = fw.buf("wscratch")

    def sb(name, shape, dt):
        return nc.alloc_sbuf_tensor("sb_" + name, list(shape), dt).ap()

    gA = sb("gA", [128, 8], F32)
    gF = sb("gF", [128, 8], F32)
    gq = sb("gq", [128, 4, 64], F32)
    mask = sb("mask", [128, 3, 128], BF16)
    ident = sb("ident", [128, 128], BF16)
    ones64 = sb("ones64", [128, 64], BF16)
    hvb = sb("hvb", [128, 2, 2, 64], BF16)
    esr = sb("esr", [1, 2, 512], BF16)
    esel = sb("esel", [1, 128], BF16)
    eps_t = sb("eps", [128, 1], F32)
    winb = sb("winb", [128, 8, 2304], BF16)
    woutb = sb("woutb", [128, 8, D], BF16)
    AT = sb("AT", [128, 8, NT], BF16)
    b_const = fw.buf("const")
    b_winb = fw.buf("winb")
    b_woutb = fw.buf("woutb")
    b_AT = [fw.buf(f"AT{k}") for k in range(8)]

    PS = nc.alloc_psum_tensor("PS", [128, 4096], F32).ap()
    BK = [fw.buf(f"bank{i}") for i in range(8)]

    def bank(i, n=1):
        return PS[:, i * 512:(i + n) * 512]

    arena0 = (nc.sbuf_base + 63) // 64 * 64
    arena_state = {"off": 0}
    ARENA_BYTES = nc.sbuf_top - arena0

    def arena_reset():
        arena_state["off"] = 0

    uid = [0]

    def ar(name, shape, dt):
        nbytes = int(np.prod(shape[1:])) * (2 if dt == BF16 else 4)
        nbytes = (nbytes + 63) // 64 * 64
        off = arena_state["off"]
        assert off + nbytes <= ARENA_BYTES, (name, off, nbytes, ARENA_BYTES)
        arena_state["off"] = off + nbytes
        uid[0] += 1
        return nc.alloc_sbuf_tensor_at(f"{name}_{uid[0]}", list(shape), dt, offset=arena0 + off).ap()

    arena_state["off"] = ARENA_BYTES - 4096
    es = ar("es", [1, 2, 512], F32)
    arena_reset()
    for dst, src in ((gA, gA_d), (gF, gF_d), (gq, gq_d), (mask, mask_d), (ident, ident_d),
                     (hvb, hv_d), (es, sink_d), (esel, esel_d)):
        fw.dma(dst, src, key=b_const, writes=[b_const], partial=True)
    fw.op(POOL, lambda: nc.gpsimd.memset(ones64, 1.0), writes=[b_const], partial=True)
    fw.op(POOL, lambda: nc.gpsimd.memset(eps_t, EPS), writes=[b_const], partial=True)
    fw.op(ACT, lambda: nc.scalar.activation(out=esr, in_=es, func=AF.Exp), reads=[b_const], writes=[b_const],
          partial=True)

    arena_reset()
    NB0 = 6
    stg = [ar("stg", [128, 2304], F32) for _ in range(NB0)]
    b_stg = [fw.buf(f"stg{i}") for i in range(NB0)]
    stb = [ar("stb", [128, 1024], BF16) for _ in range(NB0)]
    b_stb = [fw.buf(f"stb{i}") for i in range(NB0)]
    jobs = []
    for k in range(8):
        jobs.append(("win", k))
    for k in range(8):
        jobs.append(("wout", k))
    for j in range(NJ):
        jobs.append(("g", j))
        jobs.append(("u", j))
        jobs.append(("d", j))

    def p0_load(n):
        kind, a_ = jobs[n]
        i = n % NB0
        if kind == "win":
            fw.dma(stg[i], w_in[a_ * 128:(a_ + 1) * 128, :], key=b_stg[i], writes=[b_stg[i]])
        elif kind == "wout":
            fw.dma(stg[i][:, 0:D], w_out[a_ * 128:(a_ + 1) * 128, :], key=b_stg[i], writes=[b_stg[i]])
        elif kind in ("g", "u"):
            wsrc = w_gate if kind == "g" else w_up
            fw.dma(stg[i][:, 0:1024].rearrange("p (k c) -> p k c", k=8),
                   wsrc[:, a_ * 128:(a_ + 1) * 128].rearrange("(k p) c -> p k c", p=128),
                   key=b_stg[i], writes=[b_stg[i]])
        else:
            fw.dma(stg[i][:, 0:1024], w_down[a_ * 128:(a_ + 1) * 128, :], key=b_stg[i], writes=[b_stg[i]])

    def p0_work(n):
        kind, a_ = jobs[n]
        i = n % NB0
        e3 = n % 3
        if kind == "win":
            if e3 == 0:
                fw.op(ACT, lambda: nc.scalar.activation(out=winb[:, a_, :], in_=stg[i], func=AF.Copy,
                                                        scale=gA[:, a_:a_ + 1]),
                      reads=[b_stg[i], b_const], writes=[b_winb], partial=True)
            else:
                fw.op(DVE, lambda: nc.vector.tensor_scalar(out=winb[:, a_, :], in0=stg[i],
                                                           scalar1=gA[:, a_:a_ + 1], scalar2=None, op0=ALU.mult),
                      reads=[b_stg[i], b_const], writes=[b_winb], partial=True)
        elif kind == "wout":
            if e3 == 0:
                fw.op(ACT, lambda: nc.scalar.copy(out=woutb[:, a_, :], in_=stg[i][:, 0:D]), reads=[b_stg[i]],
                      writes=[b_woutb], partial=True)
            else:
                fw.op(DVE, lambda: nc.vector.tensor_copy(out=woutb[:, a_, :], in_=stg[i][:, 0:D]),
                      reads=[b_stg[i]], writes=[b_woutb], partial=True)
        elif kind in ("g", "u"):
            wdst = wgs if kind == "g" else wus
            fw.op(DVE, lambda: nc.vector.tensor_tensor(
                out=stb[i].rearrange("p (k c) -> p k c", k=8),
                in0=stg[i][:, 0:1024].rearrange("p (k c) -> p k c", k=8),
                in1=gF.unsqueeze(2).to_broadcast([128, 8, 128]), op=ALU.mult),
                reads=[b_stg[i], b_const], writes=[b_stb[i]])
            fw.dma(wdst[a_].rearrange("p k c -> p (k c)"), stb[i], key=b_stb[i], reads=[b_stb[i]],
                   writes=[b_wsc], partial=True)
        else:
            fw.op(ACT, lambda: nc.scalar.copy(out=stb[i], in_=stg[i][:, 0:1024]), reads=[b_stg[i]],
                  writes=[b_stb[i]])
            fw.dma(wds[a_], stb[i], key=b_stb[i], reads=[b_stb[i]], writes=[b_wsc], partial=True)

    for n in range(min(NB0 - 1, len(jobs))):
        p0_load(n)
    for n in range(len(jobs)):
        if n + NB0 - 1 < len(jobs):
            p0_load(n + NB0 - 1)
        p0_work(n)
    fw.barrier()

    pieces = []
    for i in range(4):
        pieces.append(dict(halo=False, xsrc=lambda n, i=i: xs[i, n * 128:(n + 1) * 128, :],
                           hsrc=None, cs0=0, csh=None, out=lambda r0, i=i: ys[i, r0:r0 + 128, :], pi=0))
    for pi in range(2):
        mb = 1024 + 2048 * pi

        def hrow(n, mb=mb):
            hidx = n * 128
            return (mb + 2048 + hidx) if hidx < 1024 else (mb + hidx - 2048)
        pieces.append(dict(halo=True, xsrc=lambda n, mb=mb: xp[mb + n * 128: mb + (n + 1) * 128, :],
                           hsrc=lambda n, hrow=hrow: xp[hrow(n):hrow(n) + 128, :],
                           cs0=16 + 32 * pi, csh=32 + 32 * pi,
                           out=lambda r0, pi=pi: yp[2048 * pi + r0: 2048 * pi + r0 + 128, :], pi=pi))

    mixb_cache = {}

    def alloc_mixb():
        assert arena_state["off"] == 0
        if "t" not in mixb_cache:
            qbT = ar("qbT", [128, 4, NT], BF16)
            kbT = ar("kbT", [128, 2304], BF16)
            Vb = ar("Vb", [128, 18, 2, 128], BF16)
            mixb_cache["t"] = (qbT, kbT, Vb, fw.buf("mixb"))
            mixb_cache["off"] = arena_state["off"]
        arena_state["off"] = mixb_cache["off"]
        return mixb_cache["t"]

    def phaseA(pc):
        arena_reset()
        qbT, kbT, Vb, b_mb = alloc_mixb()
        fw.op(POOL, lambda: nc.gpsimd.memset(Vb[:, 0:16, :, 64:128], 1.0), writes=[b_mb])
        if pc["halo"]:
            for (vslot, side) in ((16, 0), (17, 1)):
                for g in range(2):
                    fw.op(POOL, lambda vslot=vslot, side=side, g=g: nc.gpsimd.tensor_copy(
                        out=Vb[:, vslot, g, 64:128], in_=hvb[:, pc["pi"], side, :]), reads=[b_const],
                        writes=[b_mb], partial=True)
        NX = 4
        XT = [ar("xt", [128, D], F32) for _ in range(NX)]
        CS = [ar("cs", [128, 64], F32) for _ in range(NX)]
        b_XT = [fw.buf(f"xt{i}") for i in range(NX)]
        b_CS = [fw.buf(f"cs{i}") for i in range(NX)]
        junk = ar("junk", [128, D], BF16)
        b_junk = fw.buf("junk")
        ssq = [ar("ssq", [128, 4], F32) for _ in range(2)]
        b_ssq = [fw.buf(f"ssq{i}") for i in range(2)]
        xn = [ar("xn", [128, D], BF16) for _ in range(2)]
        b_xn = [fw.buf(f"xn{i}") for i in range(2)]
        hT = [ar("hT", [128, 8, 128], BF16) for _ in range(2)]
        b_hT = [fw.buf(f"hT{i}") for i in range(2)]
        vt = [ar("vt", [128, 5, 128], BF16) for _ in range(2)]
        b_vt = [fw.buf(f"vt{i}") for i in range(2)]
        qkt = [ar("qkt", [128, 8, 256], BF16) for _ in range(2)]
        b_qkt = [fw.buf(f"qkt{i}") for i in range(2)]
        SETS = []
        for i in range(2):
            SETS.append(dict(sq=ar("sq", [128, 1664], F32), qn=ar("qn", [128, 1664], F32),
                             pa=ar("pa", [128, 1664], BF16), pb=ar("pb", [128, 1664], BF16),
                             s26=ar("s26", [128, 3, 26], F32), gc=ar("gc", [128, 4, 64], F32),
                             gs=ar("gs", [128, 2, 4, 32], F32), b_tab=fw.buf(f"tab{i}"),
                             b_sq=[fw.buf(f"sq{i}_{h}") for h in range(2)],
                             b_qn=[fw.buf(f"qn{i}_{h}") for h in range(2)],
                             b_pa=[fw.buf(f"pa{i}_{h}") for h in range(2)],
                             b_pb=[fw.buf(f"pb{i}_{h}") for h in range(2)],
                             b_s26=[fw.buf(f"s26{i}_{h}") for h in range(2)]))
        P1 = bank(0, 4)
        P2 = bank(4)
        pT = bank(5).bitcast(BF16)
        qkT = bank(6, 2)

        tiles = []
        for s in ([0, 1] if pc["halo"] else [0]):
            for n in range(16):
                tiles.append((s, n))
        T = len(tiles)

        def load(k):
            s, n = tiles[k]
            i = k % NX
            src = pc["xsrc"](n) if s == 0 else pc["hsrc"](n)
            fw.dma(XT[i], src, key=b_XT[i], writes=[b_XT[i]])
            ci = (pc["cs0"] if s == 0 else pc["csh"]) + n
            fw.dma(CS[i], cst[ci], key=b_CS[i], writes=[b_CS[i]])

        def info(k):
            s, n = tiles[k]
            edge = (s == 1 and n in (0, 15))
            if s == 0:
                ranges = [(0, 26)]
                chunks = list(range(13))
            else:
                ranges = [(8, 16)] + ([(24, 26)] if edge else [])
                chunks = [4, 5, 6, 7] + ([12] if edge else [])
            return s, n, edge, ranges, chunks

        def bks_of(c0, c1):
            return BK[c0 // 512:(c1 - 1) // 512 + 1]

        def xnorm(k):
            i = k % NX
            xt = XT[i]
            p = k % 2
            sq_, xn_ = ssq[p], xn[p]
            fw.op(ACT, lambda: nc.scalar.activation(out=junk, in_=xt, func=AF.Square, accum_out=sq_[:, 0:1]),
                  reads=[b_XT[i]], writes=[b_junk, b_ssq[p]])
            fw.op(ACT, lambda: nc.scalar.activation(out=sq_[:, 1:2], in_=sq_[:, 0:1], func=AF.Sqrt, scale=1.0 / D,
                                                    bias=eps_t), reads=[b_ssq[p], b_const], writes=[b_ssq[p]])
            fw.op(DVE, lambda: nc.vector.reciprocal(out=sq_[:, 2:3], in_=sq_[:, 1:2]), reads=[b_ssq[p]],
                  writes=[b_ssq[p]])
            fw.op(ACT, lambda: nc.scalar.activation(out=xn_, in_=xt, func=AF.Copy, scale=sq_[:, 2:3]),
                  reads=[b_XT[i], b_ssq[p]], writes=[b_xn[p]])

        def xT(k):
            p = k % 2
            xn_ = xn[p]
            for kk in range(8):
                fw.op(PE, lambda kk=kk: nc.tensor.transpose(out=pT[:, kk * 128:(kk + 1) * 128],
                                                            in_=xn_[:, kk * 128:(kk + 1) * 128], identity=ident),
                      reads=[b_xn[p], b_const], writes=[BK[5]], signal=(kk == 7), partial=(kk > 0))
            fw.op(ACT, lambda: nc.scalar.copy(out=hT[p].rearrange("p k t -> p (k t)"), in_=pT), reads=[BK[5]],
                  writes=[b_hT[p]])

        def proj(k, which):
            s, n, edge, ranges, chunks = info(k)
            p = k % 2
            if s == 0:
                allg = {"h0": [(0, 512, P1[:, 0:512], [BK[0]]), (512, 512, P1[:, 512:1024], [BK[1]])],
                        "h1": [(1024, 512, P1[:, 1024:1536], [BK[2]]), (1536, 256, P1[:, 1536:1792], [BK[3]])],
                        "va": [(1792, 512, P2, [BK[4]])]}
            else:
                allg = {"h0": [(512, 512, P1[:, 512:1024], [BK[1]])],
                        "h1": ([(1536, 256, P1[:, 1536:1792], [BK[3]])] if edge else []),
                        "va": [(1792, 512, P2, [BK[4]])]}
            for (c0, ncol, outap, bks) in allg[which]:
                for kk in range(8):
                    fw.op(PE, lambda kk=kk, c0=c0, ncol=ncol, outap=outap: nc.tensor.matmul(
                        outap, lhsT=hT[p][:, kk, :], rhs=winb[:, kk, c0:c0 + ncol], start=(kk == 0),
                        stop=(kk == 7)), reads=[b_hT[p], b_winb], writes=bks, signal=(kk == 7), partial=(kk > 0))

        def vevac(k):
            s, n, edge, ranges, chunks = info(k)
            p = k % 2
            fw.op(ACT, lambda: nc.scalar.copy(out=vt[p][:, 0:4, :].rearrange("p c e -> p (c e)"), in_=P2),
                  reads=[BK[4]], writes=[b_vt[p]])
            npair = 4
            if s == 0 or edge:
                vslot = n if s == 0 else (16 if n == 0 else 17)
                fw.op(ACT, lambda: nc.scalar.copy(out=Vb[:, vslot, :, 0:64],
                                                  in_=P1[:, 1664:1792].rearrange("p (g d) -> p g d", g=2)),
                      reads=[BK[3]], writes=[b_mb], partial=True)
            fw.dma(Vs[s, 0:npair, n * 128:(n + 1) * 128, :].rearrange("c p e -> p c e"), vt[p][:, 0:npair, :],
                   key=b_vt[p], reads=[b_vt[p]], writes=b_Vs[s][0:npair], partial=True)

        def half_ranges(k, which):
            s, n, edge, ranges, chunks = info(k)
            if s == 0:
                return [(0, 16)] if which == "h0" else [(16, 26)]
            if which == "h0":
                return [(8, 16)]
            return [(24, 26)] if edge else []

        def s2a(k, which):
            S_ = SETS[k % 2]
            sq, qn, s26 = S_["sq"], S_["qn"], S_["s26"]
            hx = 0 if which == "h0" else 1
            b_sq, b_qn, b_s26 = S_["b_sq"][hx], S_["b_qn"][hx], S_["b_s26"][hx]
            for (h0, h1) in half_ranges(k, which):
                H = h1 - h0
                c0, c1 = h0 * 64, h1 * 64
                bks = bks_of(c0, c1)
                fw.op(ACT, lambda: nc.scalar.activation(out=sq[:, c0:c1], in_=P1[:, c0:c1], func=AF.Square),
                      reads=bks, writes=[b_sq])
                fw.op(DVE, lambda: nc.vector.tensor_reduce(out=s26[:, 0, h0:h1],
                                                           in_=sq[:, c0:c1].rearrange("p (h d) -> p h d", d=64),
                                                           axis=AX.X, op=ALU.add), reads=[b_sq],
                      writes=[b_s26])
                fw.op(ACT, lambda: nc.scalar.activation(out=s26[:, 1, h0:h1], in_=s26[:, 0, h0:h1], func=AF.Sqrt,
                                                        scale=1.0 / 64, bias=eps_t), reads=[b_s26, b_const],
                      writes=[b_s26])
                fw.op(DVE, lambda: nc.vector.reciprocal(out=s26[:, 2, h0:h1], in_=s26[:, 1, h0:h1]),
                      reads=[b_s26], writes=[b_s26])
                fw.op(DVE, lambda: nc.vector.tensor_tensor(
                    out=qn[:, c0:c1].rearrange("p (h d) -> p h d", d=64),
                    in0=P1[:, c0:c1].rearrange("p (h d) -> p h d", d=64),
                    in1=s26[:, 2, h0:h1].unsqueeze(2).to_broadcast([128, H, 64]), op=ALU.mult),
                    reads=bks + [b_s26], writes=[b_qn])

        TYPES = {"h0": [(0, 0, 8), (1, 8, 16)], "h1": [(2, 16, 24), (3, 24, 26)]}

        def rope_ew(k):
            s, n, edge, ranges, chunks = info(k)
            i = k % NX
            cs, b_cs = CS[i], b_CS[i]
            S_ = SETS[k % 2]
            qn, pa, pb = S_["qn"], S_["pa"], S_["pb"]
            gc, gs, b_tab = S_["gc"], S_["gs"], S_["b_tab"]
            fw.op(DVE, lambda: nc.vector.tensor_tensor(
                out=gc.rearrange("p y (t d) -> p y t d", t=2), in0=gq.rearrange("p y (t d) -> p y t d", t=2),
                in1=cs[:, 0:32].unsqueeze(1).unsqueeze(1).to_broadcast([128, 4, 2, 32]), op=ALU.mult),
                reads=[b_const, b_cs], writes=[b_tab])
            fw.op(DVE, lambda: nc.vector.scalar_tensor_tensor(
                out=gs[:, 0, :, :], in0=gq[:, :, 32:64], scalar=-1.0,
                in1=cs[:, 32:64].unsqueeze(1).to_broadcast([128, 4, 32]), op0=ALU.mult, op1=ALU.mult),
                reads=[b_const, b_cs], writes=[b_tab], partial=True)
            fw.op(DVE, lambda: nc.vector.tensor_tensor(
                out=gs[:, 1, :, :], in0=gq[:, :, 0:32],
                in1=cs[:, 32:64].unsqueeze(1).to_broadcast([128, 4, 32]), op=ALU.mult),
                reads=[b_const, b_cs], writes=[b_tab], partial=True)
            for hx, which in enumerate(("h0", "h1")):
                b_qn, b_pa, b_pb = S_["b_qn"][hx], S_["b_pa"][hx], S_["b_pb"][hx]
                hr = half_ranges(k, which)
                if not hr:
                    continue
                first = True
                for (ty, t0_, t1_) in TYPES[which]:
                    for (h0, h1) in hr:
                        a0, a1 = max(h0, t0_), min(h1, t1_)
                        if a0 >= a1:
                            continue
                        H = a1 - a0
                        c0, c1 = a0 * 64, a1 * 64
                        v4 = lambda t: t[:, c0:c1].rearrange("p (h t d) -> p h t d", t=2, d=32)
                        gcb = gc[:, ty, :].rearrange("p (t d) -> p t d", t=2).unsqueeze(1).to_broadcast(
                            [128, H, 2, 32])
                        g1b = gs[:, 0, ty, :].unsqueeze(1).to_broadcast([128, H, 32])
                        g2b = gs[:, 1, ty, :].unsqueeze(1).to_broadcast([128, H, 32])
                        fw.op(DVE, lambda: nc.vector.tensor_tensor(out=v4(pa), in0=v4(qn), in1=gcb, op=ALU.mult),
                              reads=[b_qn, b_tab], writes=[b_pa], partial=(not first))
                        fw.op(DVE, lambda: nc.vector.tensor_tensor(out=v4(pb)[:, :, 0, :], in0=v4(qn)[:, :, 1, :],
                                                                   in1=g1b, op=ALU.mult), reads=[b_qn, b_tab],
                              writes=[b_pb], partial=(not first))
                        fw.op(DVE, lambda: nc.vector.tensor_tensor(out=v4(pb)[:, :, 1, :], in0=v4(qn)[:, :, 0, :],
                                                                   in1=g2b, op=ALU.mult), reads=[b_qn, b_tab],
                              writes=[b_pb], partial=True)
                        first = False

        def rope_T(k, which):
            s, n, edge, ranges, chunks = info(k)
            S_ = SETS[k % 2]
            hx = 0 if which == "h0" else 1
            pa, pb, b_pa, b_pb = S_["pa"], S_["pb"], S_["b_pa"][hx], S_["b_pb"][hx]
            ti = n % 2
            gi = (k // 2) % 2
            cl = [c for c in chunks if (c < 8) == (which == "h0")]
            if not cl:
                return
            for ci_, c in enumerate(cl):
                sl = c % 8
                osl = qkT[:, sl * 128:(sl + 1) * 128]
                fw.op(PE, lambda c=c, osl=osl: nc.tensor.matmul(osl, lhsT=pa[:, c * 128:(c + 1) * 128], rhs=ident,
                                                                start=True, stop=False),
                      reads=[b_pa, b_const], writes=[BK[6], BK[7]], signal=False, partial=(ci_ > 0))
                fw.op(PE, lambda c=c, osl=osl: nc.tensor.matmul(osl, lhsT=pb[:, c * 128:(c + 1) * 128], rhs=ident,
                                                                start=False, stop=True),
                      reads=[b_pb, b_const], writes=[BK[6], BK[7]], signal=(ci_ == len(cl) - 1), partial=True)
            tsl = slice(ti * 128, (ti + 1) * 128)
            if s == 0 and which == "h0":
                fw.op(ACT, lambda: nc.scalar.copy(out=qkt[gi][:, 0:8, tsl],
                                                  in_=qkT[:, 0:1024].rearrange("p (c t) -> p c t", t=128)),
                      reads=[BK[6], BK[7]], writes=[b_qkt[gi]], partial=True)
            elif s == 0:
                fw.op(ACT, lambda: nc.scalar.copy(out=qbT[:, :, n * 128:(n + 1) * 128],
                                                  in_=qkT[:, 0:512].rearrange("p (c t) -> p c t", t=128)),
                      reads=[BK[6], BK[7]], writes=[b_mb], partial=True)
                fw.op(ACT, lambda: nc.scalar.copy(out=kbT[:, n * 128:(n + 1) * 128], in_=qkT[:, 512:640]),
                      reads=[BK[6], BK[7]], writes=[b_mb], partial=True)
            elif which == "h0":
                fw.op(ACT, lambda: nc.scalar.copy(out=qkt[gi][:, 4:8, tsl],
                                                  in_=qkT[:, 512:1024].rearrange("p (c t) -> p c t", t=128)),
                      reads=[BK[6], BK[7]], writes=[b_qkt[gi]], partial=True)
            else:
                kc0 = 2048 if n == 0 else 2176
                fw.op(ACT, lambda: nc.scalar.copy(out=kbT[:, kc0:kc0 + 128], in_=qkT[:, 512:640]),
                      reads=[BK[6], BK[7]], writes=[b_mb], partial=True)
            if which == "h0" and ti == 1:
                g0 = (n // 2) * 256
                for (c0, c1) in (((0, 4), (4, 8)) if s == 0 else ((4, 8),)):
                    fw.dma(QKs[s, c0:c1, :, g0:g0 + 256].rearrange("c p t -> p c t"),
                           qkt[gi][:, c0:c1, :], key=b_qkt[gi], reads=[b_qkt[gi]], writes=b_QKs[s][c0:c1],
                           partial=True)

        for k in range(min(3, T)):
            load(k)
        xnorm(0)
        xT(0)
        if T > 1:
            xnorm(1)
        for k in range(T):
            if k + 3 < T:
                load(k + 3)
            proj(k, "h0")
            if k + 1 < T:
                xT(k + 1)
            s2a(k, "h0")
            proj(k, "h1")
            if k >= 1:
                rope_T(k - 1, "h0")
            proj(k, "va")
            vevac(k)
            s2a(k, "h1")
            if k >= 1:
                rope_T(k - 1, "h1")
            if k + 2 < T:
                xnorm(k + 2)
            rope_ew(k)
        rope_T(T - 1, "h0")
        rope_T(T - 1, "h1")
        fw.barrier()

    def phaseB(pc):
        arena_reset()
        halo = pc["halo"]
        pi = pc["pi"]
        qbT, kbT, Vb, b_mb = alloc_mixb()
        UB = []
        offs = []
        for u in range(2):
            offs.append(arena_state["off"])
            UB.append(dict(
                qT=ar("qT", [128, NT], BF16), kT=ar("kT", [128, 2, NT], BF16),
                Vn=ar("Vn", [128, 18, 128], BF16), V4=ar("V4", [128, 4, 6, 128], BF16),
                V16=ar("V16", [128, 16, 2, 128], BF16), b=fw.buf(f"ub{u}")))
        acc = ar("acc", [128, 2, NT], F32)
        b_acc = fw.buf("acc")
        NPT = 4
        Pt = [ar("Pt", [128, 2, 2, 128], BF16) for _ in range(NPT)]
        b_Pt = [fw.buf(f"Pt{i}") for i in range(NPT)]
        Ptb = [ar("Ptb", [128, 512], BF16) for _ in range(NPT)]
        b_Ptb = [fw.buf(f"Ptb{i}") for i in range(NPT)]
        dn = [ar("dn", [128, 512], F32) for _ in range(2)]
        b_dn = [fw.buf(f"dn{i}") for i in range(2)]
        LAG = 2

        def load_unitA(c):
            U = UB[c % 2]
            b = U["b"]
            rd = [b_QKs[0][c], b_QKs[0][4 + c], b_Vs[0][c]]
            fw.dma(U["qT"], QKs[0, c], key=b, reads=rd, writes=[b])
            fw.dma(U["kT"][:, 0, :], QKs[0, 4 + c], key=b, reads=rd, writes=[b], partial=True)
            for n4 in range(4):
                fw.dma(U["Vn"][:, 4 * n4:4 * n4 + 4, :],
                       Vs[0, c, 512 * n4:512 * n4 + 512, :].rearrange("(n p) e -> p n e", p=128), key=b, reads=rd,
                       writes=[b], partial=True)
            for tb in range(4):
                fw.dma(U["V4"][:, :, tb, :], Vs[0, c, 512 * tb:512 * tb + 512, :].rearrange("(p r) e -> p r e", r=4),
                       key=b, reads=rd, writes=[b], partial=True)
            fw.dma(U["V16"][:, :, 0, :], Vs[0, c].rearrange("(p r) e -> p r e", r=16), key=b, reads=rd, writes=[b],
                   partial=True)
            if halo:
                rd = [b_QKs[1][4 + c], b_Vs[1][c]]
                fw.dma(U["kT"][:, 1, :], QKs[1, 4 + c], key=b, reads=rd, writes=[b], partial=True)
                fw.dma(U["Vn"][:, 16, :], Vs[1, c, 0:128, :], key=b, reads=rd, writes=[b], partial=True)
                fw.dma(U["Vn"][:, 17, :], Vs[1, c, 1920:2048, :], key=b, reads=rd, writes=[b], partial=True)
                fw.dma(U["V4"][:, :, 4, :], Vs[1, c, 0:512, :].rearrange("(p r) e -> p r e", r=4), key=b, reads=rd,
                       writes=[b], partial=True)
                fw.dma(U["V4"][:, :, 5, :], Vs[1, c, 1536:2048, :].rearrange("(p r) e -> p r e", r=4), key=b,
                       reads=rd, writes=[b], partial=True)
                fw.dma(U["V16"][:, :, 1, :], Vs[1, c].rearrange("(p r) e -> p r e", r=16), key=b, reads=rd,
                       writes=[b], partial=True)

        def load_unitB():
            b = b_mb
            rd = b_QKs[0][8:13] + [b_Vs[0][4]]
            fw.dma(qbT, QKs[0, 8:12].rearrange("c p t -> p c t"), key=b, reads=rd, writes=[b])
            fw.dma(kbT[:, 0:NT], QKs[0, 12], key=b, reads=rd, writes=[b], partial=True)
            for n4 in range(4):
                fw.dma(Vb[:, 4 * n4:4 * n4 + 4, :],
                       Vs[0, 4, 512 * n4:512 * n4 + 512, :].rearrange("(n p) e -> p n e", p=128), key=b, reads=rd,
                       writes=[b], partial=True)
            if halo:
                rd = [b_QKs[1][12], b_Vs[1][4]]
                fw.dma(kbT[:, 2048:2176], QKs[1, 12, :, 0:128], key=b, reads=rd, writes=[b], partial=True)
                fw.dma(kbT[:, 2176:2304], QKs[1, 12, :, 1920:2048], key=b, reads=rd, writes=[b], partial=True)
                fw.dma(Vb[:, 16, :], Vs[1, 4, 0:128, :], key=b, reads=rd, writes=[b], partial=True)
                fw.dma(Vb[:, 17, :], Vs[1, 4, 1920:2048, :], key=b, reads=rd, writes=[b], partial=True)

        def unitA(c):
            U = UB[c % 2]
            b = U["b"]
            qT, kT = U["qT"], U["kT"]
            blocks = []
            def d16_blocks():
                for r in range(16):
                    qsl = slice(r, 16 * 127 + r + 1, 16)
                    tl = [(0, (0, 128), 0, qsl, U["V16"][:, r, 0, :], ones64, 0, 128, 2, 0)]
                    if halo:
                        tl.insert(0, (1, (64, 128), 1, slice(16 * 64 + r, 16 * 127 + r + 1, 16),
                                      U["V16"][:, r, 1, :], hvb[:, pi, 1, :], 0, 64, 0, 64))
                        tl.append((1, (0, 64), 1, slice(r, 16 * 63 + r + 1, 16), U["V16"][:, r, 1, :],
                                   hvb[:, pi, 0, :], 64, 64, 1, 0))
                    blocks.append((128, 0, qsl, tl, False, (16, r, 0)))
            for (d, Td) in ((1, 16), (4, 4)):
                Ld = NT // d
                for r in range(d):
                    for bb in range(-1, Td):
                        if bb == -1:
                            l0, nq, qi0 = 0, 64, 64
                        elif bb == Td - 1:
                            l0, nq, qi0 = Ld - 64, 64, 0
                        else:
                            l0, nq, qi0 = 128 * bb + 64, 128, 0
                        qsl = slice(d * l0 + r, d * (l0 + nq - 1) + r + 1, d)
                        tl = []
                        for wt, tb in enumerate((bb, bb + 1)):
                            if 0 <= tb < Td:
                                kp = (0, 128)
                                slot, tidx = 0, tb
                                if d == 1:
                                    Vt = U["Vn"][:, tb, :]
                                elif d == 4:
                                    Vt = U["V4"][:, r, tb, :]
                                else:
                                    Vt = U["V16"][:, r, 0, :]
                                dl = ones64
                            elif not halo:
                                continue
                            elif tb == -1:
                                kp = (64, 128)
                                slot, tidx = 1, Td - 1
                                Vt = (U["Vn"][:, 17, :] if d == 1 else U["V4"][:, r, 5, :] if d == 4
                                      else U["V16"][:, r, 1, :])
                                dl = hvb[:, pi, 1, :]
                            else:
                                kp = (0, 64)
                                slot, tidx = 1, 0
                                Vt = (U["Vn"][:, 16, :] if d == 1 else U["V4"][:, r, 4, :] if d == 4
                                      else U["V16"][:, r, 1, :])
                                dl = hvb[:, pi, 0, :]
                            k0 = d * (128 * tidx + kp[0]) + r
                            ksl = slice(k0, k0 + d * (kp[1] - kp[0] - 1) + 1, d)
                            tl.append((wt, kp, slot, ksl, Vt, dl, 0, nq, wt, qi0))
                        blocks.append((nq, qi0, qsl, tl, d == 1, (d, r, l0)))
            d16_blocks()

            def views(bi):
                sbk = 2 * (bi % 3)
                obk = 6 + bi % 2
                S4 = bank(sbk, 2).rearrange("p (h w q) -> p h w q", h=2, w=4)
                O2 = bank(obk)[:, 0:256].rearrange("p (a q) -> p a q", a=2)
                return sbk, obk, S4, O2, bi % NPT

            def qsub(blk, q0, qn_):
                d, r, l0 = blk
                return slice(d * (l0 + q0) + r, d * (l0 + q0 + qn_ - 1) + r + 1, d)

            def front(bi):
                nq, qi0, qsl, tl, first_br, blk = blocks[bi]
                sbk, obk, S4, O2, pti = views(bi)
                P4 = Pt[pti]
                nS = len(tl) * 2
                si = 0
                for (wt, kp, slot, ksl, Vt, dl, q0, qn_, mty, mq0) in tl:
                    for hh in range(2):
                        si += 1
                        fw.op(PE, lambda wt=wt, kp=kp, slot=slot, ksl=ksl, hh=hh, q0=q0, qn_=qn_: nc.tensor.matmul(
                            S4[kp[0]:kp[1], hh, wt, q0:q0 + qn_], lhsT=kT[64 * hh:64 * hh + 64, slot, ksl],
                            rhs=qT[64 * hh:64 * hh + 64, qsub(blk, q0, qn_)], start=True, stop=True),
                            reads=[b], writes=[BK[sbk], BK[sbk + 1]], signal=(si == nS), partial=(si > 1))
                if (len(tl) == 2 and all(t[1] == (0, 128) and t[6] == 0 and t[7] == nq for t in tl)
                        and tl[0][8] == 0 and tl[1][8] == 1):
                    fw.op(ACT, lambda: nc.scalar.activation(
                        out=P4[:, :, :, 0:nq], in_=S4[:, :, 0:2, 0:nq], func=AF.Exp, scale=0.125),
                        reads=[BK[sbk], BK[sbk + 1]], writes=[b_Pt[pti]])
                    fw.op(DVE, lambda: nc.vector.tensor_tensor(
                        out=P4[:, :, :, 0:nq], in0=P4[:, :, :, 0:nq],
                        in1=mask[:, 0:2, qi0:qi0 + nq].unsqueeze(1).to_broadcast([128, 2, 2, nq]),
                        op=ALU.mult), reads=[b_Pt[pti], b_const], writes=[b_Pt[pti]], partial=True)
                else:
                    for ti_, (wt, kp, slot, ksl, Vt, dl, q0, qn_, mty, mq0) in enumerate(tl):
                        kpn = kp[1] - kp[0]
                        fw.op(ACT, lambda wt=wt, kp=kp, q0=q0, qn_=qn_: nc.scalar.activation(
                            out=P4[kp[0]:kp[1], :, wt, q0:q0 + qn_], in_=S4[kp[0]:kp[1], :, wt, q0:q0 + qn_],
                            func=AF.Exp, scale=0.125), reads=[BK[sbk], BK[sbk + 1]], writes=[b_Pt[pti]],
                            partial=(ti_ > 0))
                        fw.op(DVE, lambda wt=wt, kp=kp, kpn=kpn, q0=q0, qn_=qn_, mty=mty, mq0=mq0:
                              nc.vector.tensor_tensor(
                            out=P4[kp[0]:kp[1], :, wt, q0:q0 + qn_], in0=P4[kp[0]:kp[1], :, wt, q0:q0 + qn_],
                            in1=mask[kp[0]:kp[1], mty, mq0:mq0 + qn_].unsqueeze(1).to_broadcast([kpn, 2, qn_]),
                            op=ALU.mult), reads=[b_Pt[pti], b_const], writes=[b_Pt[pti]], partial=True)

            def back(bi):
                nq, qi0, qsl, tl, first_br, blk = blocks[bi]
                sbk, obk, S4, O2, pti = views(bi)
                P4 = Pt[pti]
                nP = len(tl) * 4
                pi_ = 0
                for ti_, (wt, kp, slot, ksl, Vt, dl, q0, qn_, mty, mq0) in enumerate(tl):
                    for hh in range(2):
                        pi_ += 1
                        fw.op(PE, lambda wt=wt, kp=kp, Vt=Vt, hh=hh, ti_=ti_, q0=q0, qn_=qn_: nc.tensor.matmul(
                            O2[64 * hh:64 * hh + 64, 0, q0:q0 + qn_], lhsT=Vt[kp[0]:kp[1], 64 * hh:64 * hh + 64],
                            rhs=P4[kp[0]:kp[1], hh, wt, q0:q0 + qn_], start=(ti_ == 0), stop=False,
                            skip_group_check=True),
                            reads=[b, b_Pt[pti]], writes=[BK[obk]], signal=False, partial=(pi_ > 1))
                        pi_ += 1
                        fw.op(PE, lambda wt=wt, kp=kp, dl=dl, hh=hh, ti_=ti_, q0=q0, qn_=qn_: nc.tensor.matmul(
                            O2[64 * hh:64 * hh + 64, 1, q0:q0 + qn_], lhsT=dl[kp[0]:kp[1], :],
                            rhs=P4[kp[0]:kp[1], hh, wt, q0:q0 + qn_], start=False, stop=(ti_ == len(tl) - 1),
                            skip_group_check=True),
                            reads=[b_const, b_Pt[pti]], writes=[BK[obk]], signal=(pi_ == nP), partial=True)
                if first_br:
                    fw.op(ACT, lambda: nc.scalar.copy(out=acc[:, :, qsl], in_=O2[:, :, 0:nq]),
                          reads=[BK[obk]], writes=[b_acc], partial=True)
                else:
                    fw.op(DVE, lambda: nc.vector.tensor_tensor(out=acc[:, :, qsl], in0=O2[:, :, 0:nq],
                                                               in1=acc[:, :, qsl], op=ALU.add),
                          reads=[BK[obk], b_acc], writes=[b_acc], partial=True)

            nb = len(blocks)
            for i in range(nb + LAG):
                if i < nb:
                    front(i)
                if i - LAG >= 0:
                    back(i - LAG)
            fw.op(ACT, lambda: nc.scalar.activation(out=acc[:, 1, :], in_=acc[:, 1, :], func=AF.Ln), reads=[b_acc],
                  writes=[b_acc], partial=True)
            fw.op(ACT, lambda: nc.scalar.activation(out=acc[:, 1, :], in_=acc[:, 1, :], func=AF.Exp, scale=-1.0),
                  reads=[b_acc], writes=[b_acc], partial=True)
            fw.op(DVE, lambda: nc.vector.tensor_tensor(out=AT[:, c, :], in0=acc[:, 0, :], in1=acc[:, 1, :],
                                                       op=ALU.mult), reads=[b_acc], writes=[b_AT[c]])

        def unitB():
            b = b_mb
            items = []
            for bb in range(16):
                for g in range(2):
                    tl = []
                    for wt, tb in ((0, bb - 1), (None, bb), (1, bb + 1)):
                        if 0 <= tb < 16:
                            tl.append((wt, slice(128 * tb, 128 * tb + 128), tb))
                        elif not halo:
                            continue
                        elif tb == -1:
                            tl.append((wt, slice(2176, 2304), 17))
                        else:
                            tl.append((wt, slice(2048, 2176), 16))
                    for ti_, t in enumerate(tl):
                        items.append((bb, g, ti_, len(tl), t))

            def vw(i, bb, g):
                sbk = i % 3
                nbk = 3 + 2 * (bb % 2) + g
                return sbk, bank(sbk), nbk, bank(nbk), i % NPT

            def front(i):
                bb, g, ti_, ntl, (wt, ksl, vslot) = items[i]
                sbk, Sb, nbk, NB, pbi = vw(i, bb, g)
                fw.op(PE, lambda: nc.tensor.matmul(
                    Sb, lhsT=kbT[64 * g:64 * g + 64, ksl], rhs=qbT[64 * g:64 * g + 64, :, 128 * bb:128 * bb + 128],
                    start=True, stop=True), reads=[b], writes=[BK[sbk]])
                fw.op(ACT, lambda: nc.scalar.activation(out=Ptb[pbi], in_=Sb, func=AF.Exp, scale=0.125),
                      reads=[BK[sbk]], writes=[b_Ptb[pbi]])
                if wt is not None:
                    fw.op(DVE, lambda: nc.vector.tensor_tensor(
                        out=Ptb[pbi].rearrange("p (c q) -> p c q", c=4),
                        in0=Ptb[pbi].rearrange("p (c q) -> p c q", c=4),
                        in1=mask[:, wt, :].unsqueeze(1).to_broadcast([128, 4, 128]), op=ALU.mult),
                        reads=[b_Ptb[pbi], b_const], writes=[b_Ptb[pbi]])

            def back(i):
                bb, g, ti_, ntl, (wt, ksl, vslot) = items[i]
                sbk, Sb, nbk, NB, pbi = vw(i, bb, g)
                last = (ti_ == ntl - 1)
                if ti_ == 0:
                    fw.op(PE, lambda: nc.tensor.matmul(NB, lhsT=esel[0:1, :], rhs=esr[0:1, g, :], start=True,
                                                       stop=False, skip_group_check=True),
                          reads=[b_const], writes=[BK[nbk]], signal=False)
                fw.op(PE, lambda: nc.tensor.matmul(NB, lhsT=Vb[:, vslot, g, :], rhs=Ptb[pbi], start=False, stop=last,
                                                   skip_group_check=True),
                      reads=[b, b_Ptb[pbi]], writes=[BK[nbk]], signal=True, partial=True)
                if last:
                    fw.op(ACT, lambda: nc.scalar.activation(out=dn[g][0:64, :], in_=NB[64:128, :], func=AF.Ln),
                          reads=[BK[nbk]], writes=[b_dn[g]])
                    fw.op(ACT, lambda: nc.scalar.activation(out=dn[g][0:64, :], in_=dn[g][0:64, :], func=AF.Exp,
                                                            scale=-1.0), reads=[b_dn[g]], writes=[b_dn[g]])
                    fw.op(DVE, lambda: nc.vector.tensor_tensor(
                        out=AT[64 * g:64 * g + 64, 4:8, 128 * bb:128 * bb + 128],
                        in0=NB[0:64, :].rearrange("p (c q) -> p c q", c=4),
                        in1=dn[g][0:64, :].rearrange("p (c q) -> p c q", c=4), op=ALU.mult),
                        reads=[BK[nbk], b_dn[g]], writes=b_AT[4:8], partial=True)

            ni = len(items)
            for i in range(ni + LAG):
                if i < ni:
                    front(i)
                if i - LAG >= 0:
                    back(i - LAG)

        load_unitA(0)
        load_unitA(1)
        unitB()
        for c in range(4):
            if c >= 1 and c + 1 < 4:
                load_unitA(c + 1)
            unitA(c)
        fw.barrier()

    def phaseC(pc):
        arena_reset()
        x1 = [ar("x1", [128, 4, D], F32) for _ in range(2)]
        b_x1 = [[fw.buf(f"x1_{u}_{t}") for t in range(4)] for u in range(2)]
        XR = [ar("xr", [128, D], F32) for _ in range(2)]
        b_XR = [fw.buf(f"xr{i}") for i in range(2)]
        junk = ar("junk", [128, D], BF16)
        b_junk = fw.buf("junkc")
        ssq = ar("ssq", [128, 4], F32)
        b_ssq = fw.buf("ssqc")
        xn2 = [ar("xn2", [128, D], BF16) for _ in range(2)]
        b_xn2 = [fw.buf(f"xn2_{i}") for i in range(2)]
        h2T = [ar("h2T", [128, 8, 512], BF16) for _ in range(2)]
        b_h2T = [fw.buf(f"h2T{i}") for i in range(2)]
        actT = ar("actT", [128, NJ, 512], BF16)
        b_actT = [fw.buf(f"actT{j}") for j in range(NJ)]
        sg = [ar("sg", [128, 512], F32) for _ in range(2)]
        b_sg = [fw.buf(f"sg{i}") for i in range(2)]
        WG = [ar("WG", [128, 8, 128], BF16) for _ in range(3)]
        WU = [ar("WU", [128, 8, 128], BF16) for _ in range(3)]
        WD = [ar("WD", [128, D], BF16) for _ in range(3)]
        b_WG = [fw.buf(f"WG{i}") for i in range(3)]
        b_WU = [fw.buf(f"WU{i}") for i in range(3)]
        b_WD = [fw.buf(f"WD{i}") for i in range(3)]
        OS = [ar("OS", [128, D], F32) for _ in range(2)]
        b_OS = [fw.buf(f"OS{i}") for i in range(2)]
        pT = bank(2).bitcast(BF16)

        def ld_gu(j):
            i = j % 3
            fw.dma(WG[i], wgs[j], key=b_WG[i], reads=[b_wsc], writes=[b_WG[i]])
            fw.dma(WU[i], wus[j], key=b_WU[i], reads=[b_wsc], writes=[b_WU[i]])

        def ld_d(j):
            i = j % 3
            fw.dma(WD[i], wds[j], key=b_WD[i], reads=[b_wsc], writes=[b_WD[i]])

        xcnt = [0]

        def c1_load(sp_, t):
            r0 = 512 * sp_ + 128 * t
            xi = (4 * sp_ + t) % 2
            fw.dma(XR[xi], pc["xsrc"](r0 // 128), key=b_XR[xi], writes=[b_XR[xi]])

        def c1_a(sp_, t):
            u = sp_ % 2
            r0 = 512 * sp_ + 128 * t
            tcs = slice(r0, r0 + 128)
            xi = (4 * sp_ + t) % 2
            xp_ = t % 2
            for half in range(2):
                for kk in range(8):
                    fw.op(PE, lambda kk=kk, half=half: nc.tensor.matmul(
                        bank(half), lhsT=AT[:, kk, tcs], rhs=woutb[:, kk, half * 512:(half + 1) * 512],
                        start=(kk == 0), stop=(kk == 7)), reads=[b_AT[kk], b_woutb], writes=[BK[half]],
                        signal=(kk == 7), partial=(kk > 0))
            fw.op(DVE, lambda: nc.vector.tensor_tensor(out=x1[u][:, t, :], in0=bank(0, 2), in1=XR[xi], op=ALU.add),
                  reads=[BK[0], BK[1], b_XR[xi]], writes=[b_x1[u][t]])
            fw.op(ACT, lambda: nc.scalar.activation(out=junk, in_=x1[u][:, t, :], func=AF.Square,
                                                    accum_out=ssq[:, 0:1]), reads=[b_x1[u][t]],
                  writes=[b_junk, b_ssq])
            fw.op(ACT, lambda: nc.scalar.activation(out=ssq[:, 1:2], in_=ssq[:, 0:1], func=AF.Sqrt,
                                                    scale=1.0 / D, bias=eps_t), reads=[b_ssq, b_const],
                  writes=[b_ssq])
            fw.op(DVE, lambda: nc.vector.reciprocal(out=ssq[:, 2:3], in_=ssq[:, 1:2]), reads=[b_ssq],
                  writes=[b_ssq])
            fw.op(ACT, lambda: nc.scalar.activation(out=xn2[xp_], in_=x1[u][:, t, :], func=AF.Copy,
                                                    scale=ssq[:, 2:3]),
                  reads=[b_x1[u][t], b_ssq], writes=[b_xn2[xp_]])

        def c1_b(sp_, t):
            u = sp_ % 2
            xp_ = t % 2
            for kk in range(8):
                fw.op(PE, lambda kk=kk: nc.tensor.transpose(out=pT[:, kk * 128:(kk + 1) * 128],
                                                            in_=xn2[xp_][:, kk * 128:(kk + 1) * 128],
                                                            identity=ident),
                      reads=[b_xn2[xp_], b_const], writes=[BK[2]], signal=(kk == 7), partial=(kk > 0))
            fw.op(ACT, lambda: nc.scalar.copy(out=h2T[u][:, :, 128 * t:128 * t + 128],
                                              in_=pT.rearrange("p (k q) -> p k q", k=8)),
                  reads=[BK[2]], writes=[b_h2T[u]], partial=(t > 0))

        def c2(sp_, j):
            u = sp_ % 2
            i = j % 3
            gb, ub = 4 + j % 2, 6 + j % 2
            for kk in range(8):
                fw.op(PE, lambda kk=kk: nc.tensor.matmul(bank(gb), lhsT=WG[i][:, kk, :], rhs=h2T[u][:, kk, :],
                                                         start=(kk == 0), stop=(kk == 7)),
                      reads=[b_WG[i], b_h2T[u]], writes=[BK[gb]], signal=(kk == 7), partial=(kk > 0))
            for kk in range(8):
                fw.op(PE, lambda kk=kk: nc.tensor.matmul(bank(ub), lhsT=WU[i][:, kk, :], rhs=h2T[u][:, kk, :],
                                                         start=(kk == 0), stop=(kk == 7)),
                      reads=[b_WU[i], b_h2T[u]], writes=[BK[ub]], signal=(kk == 7), partial=(kk > 0))
            si = j % 2
            fw.op(ACT, lambda: nc.scalar.activation(out=sg[si], in_=bank(gb), func=AF.Silu), reads=[BK[gb]],
                  writes=[b_sg[si]])
            fw.op(DVE, lambda: nc.vector.tensor_tensor(out=actT[:, j, :], in0=bank(ub), in1=sg[si],
                                                       op=ALU.mult), reads=[BK[ub], b_sg[si]],
                  writes=[b_actT[j]])

        def c3(sp_):
            for j in range(NJ):
                if j + 2 < NJ:
                    ld_d(j + 2)
                i = j % 3
                for t in range(4):
                    for half in range(2):
                        fw.op(PE, lambda t=t, half=half: nc.tensor.matmul(
                            bank(2 * t + half), lhsT=actT[:, j, 128 * t:128 * t + 128],
                            rhs=WD[i][:, half * 512:(half + 1) * 512], start=(j == 0), stop=(j == NJ - 1)),
                            reads=[b_actT[j], b_WD[i]], writes=[BK[2 * t + half]],
                            signal=(j == NJ - 1 or (t == 3 and half == 1)), partial=(j > 0))

        def c4(sp_):
            u = sp_ % 2
            for t in (2, 3, 0, 1):
                r0 = 512 * sp_ + 128 * t
                oi = t % 2
                fw.op(DVE, lambda: nc.vector.tensor_tensor(out=OS[oi], in0=bank(2 * t, 2), in1=x1[u][:, t, :],
                                                           op=ALU.add),
                      reads=[BK[2 * t], BK[2 * t + 1], b_x1[u][t]], writes=[b_OS[oi]])
                fw.dma(pc["out"](r0), OS[oi], key=b_OS[oi], reads=[b_OS[oi]])

        NSP = 4
        c1_load(0, 0)
        c1_load(0, 1)
        ld_gu(0)
        ld_gu(1)
        for t in range(4):
            c1_a(0, t)
            if t + 2 < 4:
                c1_load(0, t + 2)
            if t >= 1:
                c1_b(0, t - 1)
        c1_b(0, 3)
        for sp_ in range(NSP):
            nxt = sp_ + 1 < NSP
            sched_a = {2: 0, 6: 1, 10: 2, 14: 3}
            sched_b = {5: 0, 9: 1, 13: 2, 17: 3}
            if nxt:
                c1_load(sp_ + 1, 0)
                c1_load(sp_ + 1, 1)
            for j in range(NJ):
                if j + 2 < NJ:
                    ld_gu(j + 2)
                if j == NJ - 2:
                    ld_d(0)
                if j == NJ - 1:
                    ld_d(1)
                c2(sp_, j)
                if nxt and j in sched_a:
                    t = sched_a[j]
                    c1_a(sp_ + 1, t)
                    if t + 2 < 4:
                        c1_load(sp_ + 1, t + 2)
                if nxt and j in sched_b:
                    c1_b(sp_ + 1, sched_b[j])
            if nxt:
                ld_gu(0)
                ld_gu(1)
            c3(sp_)
            c4(sp_)
        fw.barrier()

    for ip, pc in enumerate(pieces):
        if sel is not None and ip not in sel:
            continue
        if "A" in phases:
            phaseA(pc)
        if "B" in phases:
            phaseB(pc)
        if "C" in phases:
            phaseC(pc)
    fw.finish()
    return nc


_CACHE = {}


def _host_consts():
    bf = ml_dtypes.bfloat16
    kj = np.arange(128)[:, None]
    qi = np.arange(128)[None, :]
    masks = np.stack([(kj >= qi), (kj <= qi), (np.abs(kj - qi) <= 64)], axis=1).astype(np.float32).astype(bf)
    ident = np.eye(128, dtype=np.float32).astype(bf)
    return masks, ident


def _rope_table(pos):
    half = 32
    inv = 10000.0 ** (-2.0 * np.arange(half, dtype=np.float64) / 64.0)
    ang = pos.astype(np.float64)[:, None] * inv[None, :]
    return np.concatenate([np.cos(ang), np.sin(ang)], axis=1).astype(np.float32)


def kernel(x_prompt, x_sample, attn_norm, w_in, qnorm_a, knorm_a, qnorm_b, knorm_b,
           sink_b, w_out, ffn_norm, w_gate, w_up, w_down):
    f32 = np.float32
    x_prompt = np.asarray(x_prompt, f32)
    x_sample = np.asarray(x_sample, f32)
    w_in = np.asarray(w_in, f32)[0]
    w_out = np.asarray(w_out, f32)[0]
    w_gate = np.ascontiguousarray(np.asarray(w_gate, f32)[0])
    w_up = np.ascontiguousarray(np.asarray(w_up, f32)[0])
    w_down = np.ascontiguousarray(np.asarray(w_down, f32)[0])
    QBo, KBo, VBo, VAo = 1536, 2048, 2176, 1024
    cols = list(range(0, 512)) + list(range(512, 1024))
    for c in range(4):
        for g in range(2):
            h = 4 * g + c
            cols += list(range(QBo + h * 64, QBo + h * 64 + 64))
    cols += list(range(KBo, KBo + 128)) + list(range(VBo, VBo + 128)) + list(range(VAo, VAo + 512))
    w_in_p = np.ascontiguousarray(w_in[:, cols])
    rows = list(range(512))
    for c in range(4):
        for g in range(2):
            h = 4 * g + c
            rows += list(range(512 + h * 64, 512 + h * 64 + 64))
    w_out_p = np.ascontiguousarray(w_out[rows, :])
    gA = np.ascontiguousarray(np.asarray(attn_norm, f32)[0].reshape(8, 128).T)
    gF = np.ascontiguousarray(np.asarray(ffn_norm, f32)[0].reshape(8, 128).T)
    gq1 = np.stack([np.asarray(qnorm_a, f32)[0], np.asarray(knorm_a, f32)[0],
                    np.asarray(qnorm_b, f32)[0], np.asarray(knorm_b, f32)[0]])
    gq = np.ascontiguousarray(np.broadcast_to(gq1[None, :, :], (128, 4, 64)))
    sk = np.asarray(sink_b, f32)[0]
    sinkT = np.zeros((1, 2, 512), f32)
    for g in range(2):
        for c in range(4):
            sinkT[0, g, 128 * c:128 * c + 128] = sk[4 * g + c]
    esel = np.zeros((1, 128), f32)
    esel[0, 64:] = 1.0
    esel = esel.astype(ml_dtypes.bfloat16)
    masks, ident = _host_consts()

    if "nc" not in _CACHE:
        _CACHE["nc"] = build_program()
    nc = _CACHE["nc"]

    in_maps = []
    p128 = np.arange(128)
    for core in range(NCORES):
        ps, q = core // 4, core % 4
        lo = q * 4096 - 1024
        xp = np.zeros((6144, D), f32)
        a, b = max(lo, 0), min(lo + 6144, 16384)
        xp[a - lo:b - lo] = x_prompt[ps, a:b]
        cst = np.zeros((80, 128, 64), f32)
        for n in range(16):
            cst[n] = _rope_table(n * 128 + p128)
        for pi in range(2):
            base = q * 4096 + 2048 * pi
            for n in range(16):
                cst[16 + 32 * pi + n] = _rope_table(base + n * 128 + p128)
                hidx = n * 128 + p128
                local = np.where(hidx < 1024, 2048 + hidx, hidx - 2048)
                cst[32 + 32 * pi + n] = _rope_table(np.maximum(base + local, 0))
        hv = np.ones((128, 2, 2, 64), f32)
        if q == 0:
            hv[:, 0, 1, :] = 0.0
        if q == 3:
            hv[:, 1, 0, :] = 0.0
        in_maps.append({
            "xs": np.ascontiguousarray(x_sample[4 * core:4 * core + 4]),
            "xp": xp, "cst": cst, "w_in": w_in_p, "w_out": w_out_p, "w_gate": w_gate, "w_up": w_up,
            "w_down": w_down, "gA": gA, "gF": gF, "gq": gq, "sinkT": sinkT, "esel": esel, "masks": masks, "ident": ident,
            "hv": hv.astype(ml_dtypes.bfloat16),
        })
    res = run_bass_kernel_spmd(nc, in_maps, core_ids=list(range(NCORES)))
    y_prompt = np.zeros((2, 16384, D), f32)
    y_sample = np.zeros((32, NT, D), f32)
    for core in range(NCORES):
        r = res.results[core]
        ps, q = core // 4, core % 4
        y_prompt[ps, q * 4096:(q + 1) * 4096] = np.asarray(r["yp"], f32)
        y_sample[4 * core:4 * core + 4] = np.asarray(r["ys"], f32)
    return (y_prompt, y_sample)
```

```python
import numpy as np
import ml_dtypes
import concourse.bass as bass
import concourse.mybir as mybir
from concourse.bass_utils import run_bass_kernel_spmd

F32 = mybir.dt.float32
BF16 = mybir.dt.bfloat16
ALU = mybir.AluOpType
AF = mybir.ActivationFunctionType
AX = mybir.AxisListType

D = 1024
NT = 2048
DFF = 2816
NJ = 22
EPS = 1e-6
NCORES = 8


class Src:
    def __init__(self, nc, name, step):
        self.sem = nc.alloc_semaphore(name=name)
        self.count = 0
        self.step = step
        self.name = name


class Buf:
    __slots__ = ("name", "writes", "reads", "dsrc")

    def __init__(self, name):
        self.name = name
        self.writes = []
        self.reads = []
        self.dsrc = None


class Eng:
    def __init__(self, fw, name, eng, is_pe=False):
        self.name = name
        self.eng = eng
        self.src = Src(fw.nc, "e_" + name, 1)
        self.known = {}
        self.is_pe = is_pe


class FW:
    def __init__(self, nc):
        self.nc = nc
        self.pe = Eng(self, "pe", nc.tensor, is_pe=True)
        self.act = Eng(self, "act", nc.scalar)
        self.dve = Eng(self, "dve", nc.vector)
        self.pool = Eng(self, "pool", nc.gpsimd)
        self.sp = Eng(self, "sp", nc.sync)
        self.engs = [self.pe, self.act, self.dve, self.pool, self.sp]
        self.nbuf = 0
        self.dsrcs = []
        self.bufs = {}

    def buf(self, name=None):
        self.nbuf += 1
        if name is None:
            return Buf(f"b{self.nbuf}")
        if name not in self.bufs:
            self.bufs[name] = Buf(name)
        return self.bufs[name]

    def _collect(self, E, reads, writes):
        need = {}

        def add(tok):
            s, c = tok
            if E.is_pe and s is E.src:
                return
            if s.step == 16:
                c = s.count
            if need.get(s, 0) < c:
                need[s] = c
        for b in reads:
            for t in b.writes:
                add(t)
        for b in writes:
            for t in b.writes:
                add(t)
            for t in b.reads:
                add(t)
        for s, c in need.items():
            if E.known.get(s, 0) >= c:
                continue
            E.eng.wait_ge(s.sem, c)
            E.known[s] = c

    def _commit(self, tok, reads, writes, partial):
        for b in writes:
            if partial:
                b.writes.append(tok)
                if len(b.writes) > 24:
                    m = {}
                    for s, c in b.writes:
                        if m.get(s, 0) < c:
                            m[s] = c
                    b.writes = list(m.items())
            else:
                b.writes = [tok]
                b.reads = []
        for b in reads:
            b.reads.append(tok)
            if len(b.reads) > 24:
                m = {}
                for s, c in b.reads:
                    if m.get(s, 0) < c:
                        m[s] = c
                b.reads = list(m.items())

    def op(self, E, fn, reads=(), writes=(), signal=True, partial=False):
        self._collect(E, reads, writes)
        ins = fn()
        if signal:
            E.src.count += 1
            ins.then_inc(E.src.sem, 1)
            tok = (E.src, E.src.count)
        else:
            tok = (E.src, E.src.count + 1)
        self._commit(tok, reads, writes, partial)
        return ins

    def dma(self, out_ap, in_ap, key, reads=(), writes=(), partial=False):
        E = self.sp
        if key.dsrc is None:
            key.dsrc = Src(self.nc, "d_" + key.name, 16)
            self.dsrcs.append(key.dsrc)
        self._collect(E, reads, writes)
        ins = E.eng.dma_start(out=out_ap, in_=in_ap)
        key.dsrc.count += 16
        ins.then_inc(key.dsrc.sem, 16)
        tok = (key.dsrc, key.dsrc.count)
        self._commit(tok, reads, writes, partial)
        return ins

    def barrier(self):
        for E in self.engs:
            for F in self.engs:
                if F is E or F is self.sp:
                    continue
                c = F.src.count
                if c > 0 and E.known.get(F.src, 0) < c:
                    E.eng.wait_ge(F.src.sem, c)
                    E.known[F.src] = c
            for d in self.dsrcs:
                if d.count > 0 and E.known.get(d, 0) < d.count:
                    E.eng.wait_ge(d.sem, d.count)
                    E.known[d] = d.count

    def finish(self):
        E = self.sp
        for F in self.engs:
            if F is E:
                continue
            if F.src.count > 0:
                E.eng.wait_ge(F.src.sem, F.src.count)
        for d in self.dsrcs:
            if d.count > 0:
                E.eng.wait_ge(d.sem, d.count)


def build_program(sel=None, phases="ABC"):
    nc = bass.Bass("TRN2", target_bir_lowering=False)
    fw = FW(nc)
    PE, ACT, DVE, POOL = fw.pe, fw.act, fw.dve, fw.pool

    def din(name, shape, dt=F32):
        return nc.dram_tensor(name, list(shape), dt, kind="ExternalInput").ap()

    xs = din("xs", [4, NT, D])
    xp = din("xp", [6144, D])
    cst = din("cst", [80, 128, 64])
    w_in = din("w_in", [D, 2304])
    w_out = din("w_out", [D, D])
    w_gate = din("w_gate", [D, DFF])
    w_up = din("w_up", [D, DFF])
    w_down = din("w_down", [DFF, D])
    gA_d = din("gA", [128, 8])
    gF_d = din("gF", [128, 8])
    gq_d = din("gq", [128, 4, 64])
    sink_d = din("sinkT", [1, 2, 512])
    esel_d = din("esel", [1, 128], BF16)
    mask_d = din("masks", [128, 4, 128], BF16)
    ident_d = din("ident", [128, 128], BF16)
    hv_d = din("hv", [128, 2, 2, 64], BF16)
    hvc_d = din("hvc", [128, 2, 64], BF16)
    ys = nc.dram_tensor("ys", [4, NT, D], F32, kind="ExternalOutput").ap()
    yp = nc.dram_tensor("yp", [4096, D], F32, kind="ExternalOutput").ap()
    QKs = nc.dram_tensor("QKs", [2, 13, 128, NT], BF16, kind="Internal").ap()
    Vs = nc.dram_tensor("Vs", [2, 5, NT, 128], BF16, kind="Internal").ap()
    wgs = nc.dram_tensor("wgs", [NJ, 128, 8, 128], BF16, kind="Internal").ap()
    wus = nc.dram_tensor("wus", [NJ, 128, 8, 128], BF16, kind="Internal").ap()
    wds = nc.dram_tensor("wds", [NJ, 128, D], BF16, kind="Internal").ap()
    b_QKs = [[fw.buf(f"QKs{s}_{c}") for c in range(13)] for s in range(2)]
    b_Vs = [[fw.buf(f"Vs{s}_{c}") for c in range(5)] for s in range(2)]
    b_wsc = fw.buf("wscratch")

    def sb(name, shape, dt):
        return nc.alloc_sbuf_tensor("sb_" + name, list(shape), dt).ap()

    gA = sb("gA", [128, 8], F32)
    gF = sb("gF", [128, 8], F32)
    gq = sb("gq", [128, 4, 64], F32)
    mask = sb("mask", [128, 4, 128], BF16)
    ident = sb("ident", [128, 128], BF16)
    ones64 = sb("ones64", [128, 64], BF16)
    hvb = sb("hvb", [128, 2, 2, 64], BF16)
    hvc = sb("hvc", [128, 2, 64], BF16)
    esr = sb("esr", [1, 2, 512], BF16)
    esel = sb("esel", [1, 128], BF16)
    eps_t = sb("eps", [128, 1], F32)
    winb = sb("winb", [128, 8, 2304], BF16)
    woutb = sb("woutb", [128, 8, D], BF16)
    AT = sb("AT", [128, 8, NT], BF16)
    b_const = fw.buf("const")
    b_winb = fw.buf("winb")
    b_woutb = fw.buf("woutb")
    b_AT = [fw.buf(f"AT{k}") for k in range(8)]

    PS = nc.alloc_psum_tensor("PS", [128, 4096], F32).ap()
    BK = [fw.buf(f"bank{i}") for i in range(8)]

    def bank(i, n=1):
        return PS[:, i * 512:(i + n) * 512]

    arena0 = (nc.sbuf_base + 63) // 64 * 64
    arena_state = {"off": 0}
    ARENA_BYTES = nc.sbuf_top - arena0

    def arena_reset():
        arena_state["off"] = 0

    uid = [0]

    def ar(name, shape, dt):
        nbytes = int(np.prod(shape[1:])) * (2 if dt == BF16 else 4)
        nbytes = (nbytes + 63) // 64 * 64
        off = arena_state["off"]
        assert off + nbytes <= ARENA_BYTES, (name, off, nbytes, ARENA_BYTES)
        arena_state["off"] = off + nbytes
        uid[0] += 1
        return nc.alloc_sbuf_tensor_at(f"{name}_{uid[0]}", list(shape), dt, offset=arena0 + off).ap()

    arena_state["off"] = ARENA_BYTES - 4096
    es = ar("es", [1, 2, 512], F32)
    arena_reset()
    for dst, src in ((gA, gA_d), (gF, gF_d), (gq, gq_d), (mask, mask_d), (ident, ident_d),
                     (hvb, hv_d), (hvc, hvc_d), (es, sink_d), (esel, esel_d)):
        fw.dma(dst, src, key=b_const, writes=[b_const], partial=True)
    fw.op(POOL, lambda: nc.gpsimd.memset(ones64, 1.0), writes=[b_const], partial=True)
    fw.op(POOL, lambda: nc.gpsimd.memset(eps_t, EPS), writes=[b_const], partial=True)
    fw.op(ACT, lambda: nc.scalar.activation(out=esr, in_=es, func=AF.Exp), reads=[b_const], writes=[b_const],
          partial=True)

    arena_reset()
    NB0 = 6
    stg = [ar("stg", [128, 2304], F32) for _ in range(NB0)]
    b_stg = [fw.buf(f"stg{i}") for i in range(NB0)]
    stb = [ar("stb", [128, 1024], BF16) for _ in range(NB0)]
    b_stb = [fw.buf(f"stb{i}") for i in range(NB0)]
    jobs = []
    for k in range(8):
        jobs.append(("win", k))
    for k in range(8):
        jobs.append(("wout", k))
    for j in range(NJ):
        jobs.append(("g", j))
        jobs.append(("u", j))
        jobs.append(("d", j))

    def p0_load(n):
        kind, a_ = jobs[n]
        i = n % NB0
        if kind == "win":
            fw.dma(stg[i], w_in[a_ * 128:(a_ + 1) * 128, :], key=b_stg[i], writes=[b_stg[i]])
        elif kind == "wout":
            fw.dma(stg[i][:, 0:D], w_out[a_ * 128:(a_ + 1) * 128, :], key=b_stg[i], writes=[b_stg[i]])
        elif kind in ("g", "u"):
            wsrc = w_gate if kind == "g" else w_up
            fw.dma(stg[i][:, 0:1024].rearrange("p (k c) -> p k c", k=8),
                   wsrc[:, a_ * 128:(a_ + 1) * 128].rearrange("(k p) c -> p k c", p=128),
                   key=b_stg[i], writes=[b_stg[i]])
        else:
            fw.dma(stg[i][:, 0:1024], w_down[a_ * 128:(a_ + 1) * 128, :], key=b_stg[i], writes=[b_stg[i]])

    def p0_work(n):
        kind, a_ = jobs[n]
        i = n % NB0
        e3 = n % 3
        if kind == "win":
            if e3 == 0:
                fw.op(ACT, lambda: nc.scalar.activation(out=winb[:, a_, :], in_=stg[i], func=AF.Copy,
                                                        scale=gA[:, a_:a_ + 1]),
                      reads=[b_stg[i], b_const], writes=[b_winb], partial=True)
            else:
                fw.op(DVE, lambda: nc.vector.tensor_scalar(out=winb[:, a_, :], in0=stg[i],
                                                           scalar1=gA[:, a_:a_ + 1], scalar2=None, op0=ALU.mult),
                      reads=[b_stg[i], b_const], writes=[b_winb], partial=True)
        elif kind == "wout":
            if e3 == 0:
                fw.op(ACT, lambda: nc.scalar.copy(out=woutb[:, a_, :], in_=stg[i][:, 0:D]), reads=[b_stg[i]],
                      writes=[b_woutb], partial=True)
            else:
                fw.op(DVE, lambda: nc.vector.tensor_copy(out=woutb[:, a_, :], in_=stg[i][:, 0:D]),
                      reads=[b_stg[i]], writes=[b_woutb], partial=True)
        elif kind in ("g", "u"):
            wdst = wgs if kind == "g" else wus
            fw.op(DVE, lambda: nc.vector.tensor_tensor(
                out=stb[i].rearrange("p (k c) -> p k c", k=8),
                in0=stg[i][:, 0:1024].rearrange("p (k c) -> p k c", k=8),
                in1=gF.unsqueeze(2).to_broadcast([128, 8, 128]), op=ALU.mult),
                reads=[b_stg[i], b_const], writes=[b_stb[i]])
            fw.dma(wdst[a_].rearrange("p k c -> p (k c)"), stb[i], key=b_stb[i], reads=[b_stb[i]],
                   writes=[b_wsc], partial=True)
        else:
            fw.op(ACT, lambda: nc.scalar.copy(out=stb[i], in_=stg[i][:, 0:1024]), reads=[b_stg[i]],
                  writes=[b_stb[i]])
            fw.dma(wds[a_], stb[i], key=b_stb[i], reads=[b_stb[i]], writes=[b_wsc], partial=True)

    for n in range(min(NB0 - 1, len(jobs))):
        p0_load(n)
    for n in range(len(jobs)):
        if n + NB0 - 1 < len(jobs):
            p0_load(n + NB0 - 1)
        p0_work(n)
    fw.barrier()

    pieces = []
    for i in range(4):
        pieces.append(dict(halo=False, xsrc=lambda n, i=i: xs[i, n * 128:(n + 1) * 128, :],
                           hsrc=None, cs0=0, csh=None, out=lambda r0, i=i: ys[i, r0:r0 + 128, :], pi=0))
    for pi in range(2):
        mb = 1024 + 2048 * pi

        def hrow(n, mb=mb):
            hidx = n * 128
            return (mb + 2048 + hidx) if hidx < 1024 else (mb + hidx - 2048)
        pieces.append(dict(halo=True, xsrc=lambda n, mb=mb: xp[mb + n * 128: mb + (n + 1) * 128, :],
                           hsrc=lambda n, hrow=hrow: xp[hrow(n):hrow(n) + 128, :],
                           cs0=16 + 32 * pi, csh=32 + 32 * pi,
                           out=lambda r0, pi=pi: yp[2048 * pi + r0: 2048 * pi + r0 + 128, :], pi=pi))

    mixb_cache = {}

    def alloc_mixb():
        assert arena_state["off"] == 0
        if "t" not in mixb_cache:
            qbT = ar("qbT", [128, 4, NT], BF16)
            kbT = ar("kbT", [128, 2304], BF16)
            Vb = ar("Vb", [128, 18, 2, 128], BF16)
            mixb_cache["t"] = (qbT, kbT, Vb, fw.buf("mixb"))
            mixb_cache["off"] = arena_state["off"]
        arena_state["off"] = mixb_cache["off"]
        return mixb_cache["t"]

    def phaseA(pc):
        arena_reset()
        qbT, kbT, Vb, b_mb = alloc_mixb()
        fw.op(POOL, lambda: nc.gpsimd.memset(Vb[:, 0:16, :, 64:128], 1.0), writes=[b_mb])
        if pc["halo"]:
            for (vslot, side) in ((16, 0), (17, 1)):
                for g in range(2):
                    fw.op(POOL, lambda vslot=vslot, side=side, g=g: nc.gpsimd.tensor_copy(
                        out=Vb[:, vslot, g, 64:128], in_=hvb[:, pc["pi"], side, :]), reads=[b_const],
                        writes=[b_mb], partial=True)
        NX = 4
        XT = [ar("xt", [128, D], F32) for _ in range(NX)]
        CS = [ar("cs", [128, 64], F32) for _ in range(NX)]
        b_XT = [fw.buf(f"xt{i}") for i in range(NX)]
        b_CS = [fw.buf(f"cs{i}") for i in range(NX)]
        junk = ar("junk", [128, D], BF16)
        b_junk = fw.buf("junk")
        ssq = [ar("ssq", [128, 4], F32) for _ in range(2)]
        b_ssq = [fw.buf(f"ssq{i}") for i in range(2)]
        xn = [ar("xn", [128, D], BF16) for _ in range(2)]
        b_xn = [fw.buf(f"xn{i}") for i in range(2)]
        hT = [ar("hT", [128, 8, 128], BF16) for _ in range(2)]
        b_hT = [fw.buf(f"hT{i}") for i in range(2)]
        vt = [ar("vt", [128, 5, 128], BF16) for _ in range(2)]
        b_vt = [fw.buf(f"vt{i}") for i in range(2)]
        qkt = [ar("qkt", [128, 8, 256], BF16) for _ in range(2)]
        b_qkt = [fw.buf(f"qkt{i}") for i in range(2)]
        SETS = []
        for i in range(2):
            SETS.append(dict(sq=ar("sq", [128, 1664], F32), qn=ar("qn", [128, 1664], F32),
                             pa=ar("pa", [128, 1664], BF16), pb=ar("pb", [128, 1664], BF16),
                             s26=ar("s26", [128, 3, 26], F32), gc=ar("gc", [128, 4, 64], F32),
                             gs=ar("gs", [128, 2, 4, 32], F32), b_tab=fw.buf(f"tab{i}"),
                             b_sq=[fw.buf(f"sq{i}_{h}") for h in range(2)],
                             b_qn=[fw.buf(f"qn{i}_{h}") for h in range(2)],
                             b_pa=[fw.buf(f"pa{i}_{h}") for h in range(2)],
                             b_pb=[fw.buf(f"pb{i}_{h}") for h in range(2)],
                             b_s26=[fw.buf(f"s26{i}_{h}") for h in range(2)]))
        P1 = bank(0, 4)
        P2 = bank(4)
        pT = bank(5).bitcast(BF16)
        qkT = bank(6, 2)

        tiles = []
        for s in ([0, 1] if pc["halo"] else [0]):
            for n in range(16):
                tiles.append((s, n))
        T = len(tiles)

        def load(k):
            s, n = tiles[k]
            i = k % NX
            src = pc["xsrc"](n) if s == 0 else pc["hsrc"](n)
            fw.dma(XT[i], src, key=b_XT[i], writes=[b_XT[i]])
            ci = (pc["cs0"] if s == 0 else pc["csh"]) + n
            fw.dma(CS[i], cst[ci], key=b_CS[i], writes=[b_CS[i]])

        def info(k):
            s, n = tiles[k]
            edge = (s == 1 and n in (0, 15))
            if s == 0:
                ranges = [(0, 26)]
                chunks = list(range(13))
            else:
                ranges = [(8, 16)] + ([(24, 26)] if edge else [])
                chunks = [4, 5, 6, 7] + ([12] if edge else [])
            return s, n, edge, ranges, chunks

        def bks_of(c0, c1):
            return BK[c0 // 512:(c1 - 1) // 512 + 1]

        def xnorm(k):
            i = k % NX
            xt = XT[i]
            p = k % 2
            sq_, xn_ = ssq[p], xn[p]
            fw.op(ACT, lambda: nc.scalar.activation(out=junk, in_=xt, func=AF.Square, accum_out=sq_[:, 0:1]),
                  reads=[b_XT[i]], writes=[b_junk, b_ssq[p]])
            fw.op(ACT, lambda: nc.scalar.activation(out=sq_[:, 1:2], in_=sq_[:, 0:1], func=AF.Sqrt, scale=1.0 / D,
                                                    bias=eps_t), reads=[b_ssq[p], b_const], writes=[b_ssq[p]])
            fw.op(DVE, lambda: nc.vector.reciprocal(out=sq_[:, 2:3], in_=sq_[:, 1:2]), reads=[b_ssq[p]],
                  writes=[b_ssq[p]])
            fw.op(ACT, lambda: nc.scalar.activation(out=xn_, in_=xt, func=AF.Copy, scale=sq_[:, 2:3]),
                  reads=[b_XT[i], b_ssq[p]], writes=[b_xn[p]])

        def xT(k):
            p = k % 2
            xn_ = xn[p]
            for kk in range(8):
                fw.op(PE, lambda kk=kk: nc.tensor.transpose(out=pT[:, kk * 128:(kk + 1) * 128],
                                                            in_=xn_[:, kk * 128:(kk + 1) * 128], identity=ident),
                      reads=[b_xn[p], b_const], writes=[BK[5]], signal=(kk == 7), partial=(kk > 0))
            fw.op(ACT, lambda: nc.scalar.copy(out=hT[p].rearrange("p k t -> p (k t)"), in_=pT), reads=[BK[5]],
                  writes=[b_hT[p]])

        def proj(k, which):
            s, n, edge, ranges, chunks = info(k)
            p = k % 2
            if s == 0:
                allg = {"h0": [(0, 512, P1[:, 0:512], [BK[0]]), (512, 512, P1[:, 512:1024], [BK[1]])],
                        "h1": [(1024, 512, P1[:, 1024:1536], [BK[2]]), (1536, 256, P1[:, 1536:1792], [BK[3]])],
                        "va": [(1792, 512, P2, [BK[4]])]}
            else:
                allg = {"h0": [(512, 512, P1[:, 512:1024], [BK[1]])],
                        "h1": ([(1536, 256, P1[:, 1536:1792], [BK[3]])] if edge else []),
                        "va": [(1792, 512, P2, [BK[4]])]}
            for (c0, ncol, outap, bks) in allg[which]:
                for kk in range(8):
                    fw.op(PE, lambda kk=kk, c0=c0, ncol=ncol, outap=outap: nc.tensor.matmul(
                        outap, lhsT=hT[p][:, kk, :], rhs=winb[:, kk, c0:c0 + ncol], start=(kk == 0),
                        stop=(kk == 7)), reads=[b_hT[p], b_winb], writes=bks, signal=(kk == 7), partial=(kk > 0))

        def vevac(k):
            s, n, edge, ranges, chunks = info(k)
            p = k % 2
            fw.op(ACT, lambda: nc.scalar.copy(out=vt[p][:, 0:4, :].rearrange("p c e -> p (c e)"), in_=P2),
                  reads=[BK[4]], writes=[b_vt[p]])
            npair = 4
            if s == 0 or edge:
                vslot = n if s == 0 else (16 if n == 0 else 17)
                fw.op(ACT, lambda: nc.scalar.copy(out=Vb[:, vslot, :, 0:64],
                                                  in_=P1[:, 1664:1792].rearrange("p (g d) -> p g d", g=2)),
                      reads=[BK[3]], writes=[b_mb], partial=True)
            fw.dma(Vs[s, 0:npair, n * 128:(n + 1) * 128, :].rearrange("c p e -> p c e"), vt[p][:, 0:npair, :],
                   key=b_vt[p], reads=[b_vt[p]], writes=b_Vs[s][0:npair], partial=True)

        def half_ranges(k, which):
            s, n, edge, ranges, chunks = info(k)
            if s == 0:
                return [(0, 16)] if which == "h0" else [(16, 26)]
            if which == "h0":
                return [(8, 16)]
            return [(24, 26)] if edge else []

        def s2a(k, which):
            S_ = SETS[k % 2]
            sq, qn, s26 = S_["sq"], S_["qn"], S_["s26"]
            hx = 0 if which == "h0" else 1
            b_sq, b_qn, b_s26 = S_["b_sq"][hx], S_["b_qn"][hx], S_["b_s26"][hx]
            for (h0, h1) in half_ranges(k, which):
                H = h1 - h0
                c0, c1 = h0 * 64, h1 * 64
                bks = bks_of(c0, c1)
                fw.op(ACT, lambda: nc.scalar.activation(out=sq[:, c0:c1], in_=P1[:, c0:c1], func=AF.Square),
                      reads=bks, writes=[b_sq])
                fw.op(DVE, lambda: nc.vector.tensor_reduce(out=s26[:, 0, h0:h1],
                                                           in_=sq[:, c0:c1].rearrange("p (h d) -> p h d", d=64),
                                                           axis=AX.X, op=ALU.add), reads=[b_sq],
                      writes=[b_s26])
                fw.op(ACT, lambda: nc.scalar.activation(out=s26[:, 1, h0:h1], in_=s26[:, 0, h0:h1], func=AF.Sqrt,
                                                        scale=1.0 / 64, bias=eps_t), reads=[b_s26, b_const],
                      writes=[b_s26])
                fw.op(DVE, lambda: nc.vector.reciprocal(out=s26[:, 2, h0:h1], in_=s26[:, 1, h0:h1]),
                      reads=[b_s26], writes=[b_s26])
                fw.op(DVE, lambda: nc.vector.tensor_tensor(
                    out=qn[:, c0:c1].rearrange("p (h d) -> p h d", d=64),
                    in0=P1[:, c0:c1].rearrange("p (h d) -> p h d", d=64),
                    in1=s26[:, 2, h0:h1].unsqueeze(2).to_broadcast([128, H, 64]), op=ALU.mult),
                    reads=bks + [b_s26], writes=[b_qn])

        TYPES = {"h0": [(0, 0, 8), (1, 8, 16)], "h1": [(2, 16, 24), (3, 24, 26)]}

        def rope_ew(k):
            s, n, edge, ranges, chunks = info(k)
            i = k % NX
            cs, b_cs = CS[i], b_CS[i]
            S_ = SETS[k % 2]
            qn, pa, pb = S_["qn"], S_["pa"], S_["pb"]
            gc, gs, b_tab = S_["gc"], S_["gs"], S_["b_tab"]
            fw.op(DVE, lambda: nc.vector.tensor_tensor(
                out=gc.rearrange("p y (t d) -> p y t d", t=2), in0=gq.rearrange("p y (t d) -> p y t d", t=2),
                in1=cs[:, 0:32].unsqueeze(1).unsqueeze(1).to_broadcast([128, 4, 2, 32]), op=ALU.mult),
                reads=[b_const, b_cs], writes=[b_tab])
            fw.op(DVE, lambda: nc.vector.scalar_tensor_tensor(
                out=gs[:, 0, :, :], in0=gq[:, :, 32:64], scalar=-1.0,
                in1=cs[:, 32:64].unsqueeze(1).to_broadcast([128, 4, 32]), op0=ALU.mult, op1=ALU.mult),
                reads=[b_const, b_cs], writes=[b_tab], partial=True)
            fw.op(DVE, lambda: nc.vector.tensor_tensor(
                out=gs[:, 1, :, :], in0=gq[:, :, 0:32],
                in1=cs[:, 32:64].unsqueeze(1).to_broadcast([128, 4, 32]), op=ALU.mult),
                reads=[b_const, b_cs], writes=[b_tab], partial=True)
            for hx, which in enumerate(("h0", "h1")):
                b_qn, b_pa, b_pb = S_["b_qn"][hx], S_["b_pa"][hx], S_["b_pb"][hx]
                hr = half_ranges(k, which)
                if not hr:
                    continue
                first = True
                for (ty, t0_, t1_) in TYPES[which]:
                    for (h0, h1) in hr:
                        a0, a1 = max(h0, t0_), min(h1, t1_)
                        if a0 >= a1:
                            continue
                        H = a1 - a0
                        c0, c1 = a0 * 64, a1 * 64
                        v4 = lambda t: t[:, c0:c1].rearrange("p (h t d) -> p h t d", t=2, d=32)
                        gcb = gc[:, ty, :].rearrange("p (t d) -> p t d", t=2).unsqueeze(1).to_broadcast(
                            [128, H, 2, 32])
                        g1b = gs[:, 0, ty, :].unsqueeze(1).to_broadcast([128, H, 32])
                        g2b = gs[:, 1, ty, :].unsqueeze(1).to_broadcast([128, H, 32])
                        fw.op(DVE, lambda: nc.vector.tensor_tensor(out=v4(pa), in0=v4(qn), in1=gcb, op=ALU.mult),
                              reads=[b_qn, b_tab], writes=[b_pa], partial=(not first))
                        fw.op(DVE, lambda: nc.vector.tensor_tensor(out=v4(pb)[:, :, 0, :], in0=v4(qn)[:, :, 1, :],
                                                                   in1=g1b, op=ALU.mult), reads=[b_qn, b_tab],
                              writes=[b_pb], partial=(not first))
                        fw.op(DVE, lambda: nc.vector.tensor_tensor(out=v4(pb)[:, :, 1, :], in0=v4(qn)[:, :, 0, :],
                                                                   in1=g2b, op=ALU.mult), reads=[b_qn, b_tab],
                              writes=[b_pb], partial=True)
                        first = False

        def rope_T(k, which):
            s, n, edge, ranges, chunks = info(k)
            S_ = SETS[k % 2]
            hx = 0 if which == "h0" else 1
            pa, pb, b_pa, b_pb = S_["pa"], S_["pb"], S_["b_pa"][hx], S_["b_pb"][hx]
            ti = n % 2
            gi = (k // 2) % 2
            cl = [c for c in chunks if (c < 8) == (which == "h0")]
            if not cl:
                return
            for ci_, c in enumerate(cl):
                sl = c % 8
                osl = qkT[:, sl * 128:(sl + 1) * 128]
                fw.op(PE, lambda c=c, osl=osl: nc.tensor.matmul(osl, lhsT=pa[:, c * 128:(c + 1) * 128], rhs=ident,
                                                                start=True, stop=False),
                      reads=[b_pa, b_const], writes=[BK[6], BK[7]], signal=False, partial=(ci_ > 0))
                fw.op(PE, lambda c=c, osl=osl: nc.tensor.matmul(osl, lhsT=pb[:, c * 128:(c + 1) * 128], rhs=ident,
                                                                start=False, stop=True),
                      reads=[b_pb, b_const], writes=[BK[6], BK[7]], signal=(ci_ == len(cl) - 1), partial=True)
            tsl = slice(ti * 128, (ti + 1) * 128)
            if s == 0 and which == "h0":
                fw.op(ACT, lambda: nc.scalar.copy(out=qkt[gi][:, 0:8, tsl],
                                                  in_=qkT[:, 0:1024].rearrange("p (c t) -> p c t", t=128)),
                      reads=[BK[6], BK[7]], writes=[b_qkt[gi]], partial=True)
            elif s == 0:
                fw.op(ACT, lambda: nc.scalar.copy(out=qbT[:, :, n * 128:(n + 1) * 128],
                                                  in_=qkT[:, 0:512].rearrange("p (c t) -> p c t", t=128)),
                      reads=[BK[6], BK[7]], writes=[b_mb], partial=True)
                fw.op(ACT, lambda: nc.scalar.copy(out=kbT[:, n * 128:(n + 1) * 128], in_=qkT[:, 512:640]),
                      reads=[BK[6], BK[7]], writes=[b_mb], partial=True)
            elif which == "h0":
                fw.op(ACT, lambda: nc.scalar.copy(out=qkt[gi][:, 4:8, tsl],
                                                  in_=qkT[:, 512:1024].rearrange("p (c t) -> p c t", t=128)),
                      reads=[BK[6], BK[7]], writes=[b_qkt[gi]], partial=True)
            else:
                kc0 = 2048 if n == 0 else 2176
                fw.op(ACT, lambda: nc.scalar.copy(out=kbT[:, kc0:kc0 + 128], in_=qkT[:, 512:640]),
                      reads=[BK[6], BK[7]], writes=[b_mb], partial=True)
            if which == "h0" and ti == 1:
                g0 = (n // 2) * 256
                for (c0, c1) in (((0, 4), (4, 8)) if s == 0 else ((4, 8),)):
                    fw.dma(QKs[s, c0:c1, :, g0:g0 + 256].rearrange("c p t -> p c t"),
                           qkt[gi][:, c0:c1, :], key=b_qkt[gi], reads=[b_qkt[gi]], writes=b_QKs[s][c0:c1],
                           partial=True)

        for k in range(min(3, T)):
            load(k)
        xnorm(0)
        xT(0)
        if T > 1:
            xnorm(1)
        for k in range(T):
            if k + 3 < T:
                load(k + 3)
            proj(k, "h0")
            if k + 1 < T:
                xT(k + 1)
            s2a(k, "h0")
            proj(k, "h1")
            if k >= 1:
                rope_T(k - 1, "h0")
            proj(k, "va")
            vevac(k)
            s2a(k, "h1")
            if k >= 1:
                rope_T(k - 1, "h1")
            if k + 2 < T:
                xnorm(k + 2)
            rope_ew(k)
        rope_T(T - 1, "h0")
        rope_T(T - 1, "h1")
        fw.barrier()

    def phaseB(pc):
        arena_reset()
        halo = pc["halo"]
        pi = pc["pi"]
        qbT, kbT, Vb, b_mb = alloc_mixb()
        UB = []
        offs = []
        for u in range(2):
            offs.append(arena_state["off"])
            UB.append(dict(
                qT=ar("qT", [128, NT], BF16), kT=ar("kT", [128, 2, NT], BF16),
                Vn=ar("Vn", [128, 18, 128], BF16), V4=ar("V4", [128, 4, 6, 128], BF16),
                V16=ar("V16", [128, 16, 2, 128], BF16), b=fw.buf(f"ub{u}")))
        acc = ar("acc", [128, 2, NT], F32)
        b_acc = fw.buf("acc")
        NPT = 4
        Pt = [ar("Pt", [128, 2, 2, 128], BF16) for _ in range(NPT)]
        b_Pt = [fw.buf(f"Pt{i}") for i in range(NPT)]
        NPTB = 3
        Ptb = [ar("Ptb", [128, 512], BF16) for _ in range(NPTB)]
        b_Ptb = [fw.buf(f"Ptb{i}") for i in range(NPTB)]
        dn = [ar("dn", [128, 512], F32) for _ in range(2)]
        b_dn = [fw.buf(f"dn{i}") for i in range(2)]
        LAG = 2

        def load_unitA(c):
            U = UB[c % 2]
            b = U["b"]
            rd = [b_QKs[0][c], b_QKs[0][4 + c], b_Vs[0][c]]
            fw.dma(U["qT"], QKs[0, c], key=b, reads=rd, writes=[b])
            fw.dma(U["kT"][:, 0, :], QKs[0, 4 + c], key=b, reads=rd, writes=[b], partial=True)
            for n4 in range(4):
                fw.dma(U["Vn"][:, 4 * n4:4 * n4 + 4, :],
                       Vs[0, c, 512 * n4:512 * n4 + 512, :].rearrange("(n p) e -> p n e", p=128), key=b, reads=rd,
                       writes=[b], partial=True)
            for tb in range(4):
                fw.dma(U["V4"][:, :, tb, :], Vs[0, c, 512 * tb:512 * tb + 512, :].rearrange("(p r) e -> p r e", r=4),
                       key=b, reads=rd, writes=[b], partial=True)
            fw.dma(U["V16"][:, :, 0, :], Vs[0, c].rearrange("(p r) e -> p r e", r=16), key=b, reads=rd, writes=[b],
                   partial=True)
            if halo:
                rd = [b_QKs[1][4 + c], b_Vs[1][c]]
                fw.dma(U["kT"][:, 1, :], QKs[1, 4 + c], key=b, reads=rd, writes=[b], partial=True)
                fw.dma(U["Vn"][:, 16, :], Vs[1, c, 0:128, :], key=b, reads=rd, writes=[b], partial=True)
                fw.dma(U["Vn"][:, 17, :], Vs[1, c, 1920:2048, :], key=b, reads=rd, writes=[b], partial=True)
                fw.dma(U["V4"][:, :, 4, :], Vs[1, c, 0:512, :].rearrange("(p r) e -> p r e", r=4), key=b, reads=rd,
                       writes=[b], partial=True)
                fw.dma(U["V4"][:, :, 5, :], Vs[1, c, 1536:2048, :].rearrange("(p r) e -> p r e", r=4), key=b,
                       reads=rd, writes=[b], partial=True)
                fw.dma(U["V16"][:, :, 1, :], Vs[1, c].rearrange("(p r) e -> p r e", r=16), key=b, reads=rd,
                       writes=[b], partial=True)

        def load_unitB():
            b = b_mb
            rd = b_QKs[0][8:13] + [b_Vs[0][4]]
            fw.dma(qbT, QKs[0, 8:12].rearrange("c p t -> p c t"), key=b, reads=rd, writes=[b])
            fw.dma(kbT[:, 0:NT], QKs[0, 12], key=b, reads=rd, writes=[b], partial=True)
            for n4 in range(4):
                fw.dma(Vb[:, 4 * n4:4 * n4 + 4, :],
                       Vs[0, 4, 512 * n4:512 * n4 + 512, :].rearrange("(n p) e -> p n e", p=128), key=b, reads=rd,
                       writes=[b], partial=True)
            if halo:
                rd = [b_QKs[1][12], b_Vs[1][4]]
                fw.dma(kbT[:, 2048:2176], QKs[1, 12, :, 0:128], key=b, reads=rd, writes=[b], partial=True)
                fw.dma(kbT[:, 2176:2304], QKs[1, 12, :, 1920:2048], key=b, reads=rd, writes=[b], partial=True)
                fw.dma(Vb[:, 16, :], Vs[1, 4, 0:128, :], key=b, reads=rd, writes=[b], partial=True)
                fw.dma(Vb[:, 17, :], Vs[1, 4, 1920:2048, :], key=b, reads=rd, writes=[b], partial=True)

        def unitA(c):
            U = UB[c % 2]
            b = U["b"]
            qT, kT = U["qT"], U["kT"]
            blocks = []
            def d16_blocks():
                for r in range(16):
                    qsl = slice(r, 16 * 127 + r + 1, 16)
                    tl = [(0, (0, 128), 0, qsl, U["V16"][:, r, 0, :], ones64, 0, 128, 2, 0)]
                    if halo:
                        tl.append((1, (0, 128), 1, qsl, U["V16"][:, r, 1, :], hvc[:, pi, :], 0, 128, 3, 0))
                    blocks.append((128, 0, qsl, tl, False, (16, r, 0)))
            for (d, Td) in ((1, 16), (4, 4)):
                Ld = NT // d
                for r in range(d):
                    for bb in range(-1, Td):
                        if bb == -1:
                            l0, nq, qi0 = 0, 64, 64
                        elif bb == Td - 1:
                            l0, nq, qi0 = Ld - 64, 64, 0
                        else:
                            l0, nq, qi0 = 128 * bb + 64, 128, 0
                        qsl = slice(d * l0 + r, d * (l0 + nq - 1) + r + 1, d)
                        tl = []
                        for wt, tb in enumerate((bb, bb + 1)):
                            if 0 <= tb < Td:
                                kp = (0, 128)
                                slot, tidx = 0, tb
                                if d == 1:
                                    Vt = U["Vn"][:, tb, :]
                                elif d == 4:
                                    Vt = U["V4"][:, r, tb, :]
                                else:
                                    Vt = U["V16"][:, r, 0, :]
                                dl = ones64
                            elif not halo:
                                continue
                            elif tb == -1:
                                kp = (64, 128)
                                slot, tidx = 1, Td - 1
                                Vt = (U["Vn"][:, 17, :] if d == 1 else U["V4"][:, r, 5, :] if d == 4
                                      else U["V16"][:, r, 1, :])
                                dl = hvb[:, pi, 1, :]
                            else:
                                kp = (0, 64)
                                slot, tidx = 1, 0
                                Vt = (U["Vn"][:, 16, :] if d == 1 else U["V4"][:, r, 4, :] if d == 4
                                      else U["V16"][:, r, 1, :])
                                dl = hvb[:, pi, 0, :]
                            k0 = d * (128 * tidx + kp[0]) + r
                            ksl = slice(k0, k0 + d * (kp[1] - kp[0] - 1) + 1, d)
                            tl.append((wt, kp, slot, ksl, Vt, dl, 0, nq, wt, qi0))
                        blocks.append((nq, qi0, qsl, tl, d == 1, (d, r, l0)))
            d16_blocks()

            def views(bi):
                sbk = 2 * (bi % 3)
                obk = 6 + bi % 2
                S4 = bank(sbk, 2).rearrange("p (h w q) -> p h w q", h=2, w=4)
                O2 = bank(obk)[:, 0:256].rearrange("p (a q) -> p a q", a=2)
                return sbk, obk, S4, O2, bi % NPT

            def qsub(blk, q0, qn_):
                d, r, l0 = blk
                return slice(d * (l0 + q0) + r, d * (l0 + q0 + qn_ - 1) + r + 1, d)

            def front(bi):
                nq, qi0, qsl, tl, first_br, blk = blocks[bi]
                sbk, obk, S4, O2, pti = views(bi)
                P4 = Pt[pti]
                nS = len(tl) * 2
                si = 0
                for (wt, kp, slot, ksl, Vt, dl, q0, qn_, mty, mq0) in tl:
                    for hh in range(2):
                        si += 1
                        fw.op(PE, lambda wt=wt, kp=kp, slot=slot, ksl=ksl, hh=hh, q0=q0, qn_=qn_: nc.tensor.matmul(
                            S4[kp[0]:kp[1], hh, wt, q0:q0 + qn_], lhsT=kT[64 * hh:64 * hh + 64, slot, ksl],
                            rhs=qT[64 * hh:64 * hh + 64, qsub(blk, q0, qn_)], start=True, stop=True),
                            reads=[b], writes=[BK[sbk], BK[sbk + 1]], signal=(si == nS), partial=(si > 1))
                if (len(tl) == 2 and all(t[1] == (0, 128) and t[6] == 0 and t[7] == nq for t in tl)
                        and tl[0][0] == 0 and tl[1][0] == 1 and tl[1][8] == tl[0][8] + 1
                        and tl[0][9] == tl[1][9]):
                    mt0, mq_ = tl[0][8], tl[0][9]
                    fw.op(ACT, lambda: nc.scalar.activation(
                        out=P4[:, :, :, 0:nq], in_=S4[:, :, 0:2, 0:nq], func=AF.Exp, scale=0.125),
                        reads=[BK[sbk], BK[sbk + 1]], writes=[b_Pt[pti]])
                    fw.op(DVE, lambda: nc.vector.tensor_tensor(
                        out=P4[:, :, :, 0:nq], in0=P4[:, :, :, 0:nq],
                        in1=mask[:, mt0:mt0 + 2, mq_:mq_ + nq].unsqueeze(1).to_broadcast([128, 2, 2, nq]),
                        op=ALU.mult), reads=[b_Pt[pti], b_const], writes=[b_Pt[pti]], partial=True)
                else:
                    for ti_, (wt, kp, slot, ksl, Vt, dl, q0, qn_, mty, mq0) in enumerate(tl):
                        kpn = kp[1] - kp[0]
                        fw.op(ACT, lambda wt=wt, kp=kp, q0=q0, qn_=qn_: nc.scalar.activation(
                            out=P4[kp[0]:kp[1], :, wt, q0:q0 + qn_], in_=S4[kp[0]:kp[1], :, wt, q0:q0 + qn_],
                            func=AF.Exp, scale=0.125), reads=[BK[sbk], BK[sbk + 1]], writes=[b_Pt[pti]],
                            partial=(ti_ > 0))
                        fw.op(DVE, lambda wt=wt, kp=kp, kpn=kpn, q0=q0, qn_=qn_, mty=mty, mq0=mq0:
                              nc.vector.tensor_tensor(
                            out=P4[kp[0]:kp[1], :, wt, q0:q0 + qn_], in0=P4[kp[0]:kp[1], :, wt, q0:q0 + qn_],
                            in1=mask[kp[0]:kp[1], mty, mq0:mq0 + qn_].unsqueeze(1).to_broadcast([kpn, 2, qn_]),
                            op=ALU.mult), reads=[b_Pt[pti], b_const], writes=[b_Pt[pti]], partial=True)

            def back(bi):
                nq, qi0, qsl, tl, first_br, blk = blocks[bi]
                sbk, obk, S4, O2, pti = views(bi)
                P4 = Pt[pti]
                nP = len(tl) * 4
                pi_ = 0
                for ti_, (wt, kp, slot, ksl, Vt, dl, q0, qn_, mty, mq0) in enumerate(tl):
                    for hh in range(2):
                        pi_ += 1
                        fw.op(PE, lambda wt=wt, kp=kp, Vt=Vt, hh=hh, ti_=ti_, q0=q0, qn_=qn_: nc.tensor.matmul(
                            O2[64 * hh:64 * hh + 64, 0, q0:q0 + qn_], lhsT=Vt[kp[0]:kp[1], 64 * hh:64 * hh + 64],
                            rhs=P4[kp[0]:kp[1], hh, wt, q0:q0 + qn_], start=(ti_ == 0), stop=False,
                            skip_group_check=True),
                            reads=[b, b_Pt[pti]], writes=[BK[obk]], signal=False, partial=(pi_ > 1))
                        pi_ += 1
                        fw.op(PE, lambda wt=wt, kp=kp, dl=dl, hh=hh, ti_=ti_, q0=q0, qn_=qn_: nc.tensor.matmul(
                            O2[64 * hh:64 * hh + 64, 1, q0:q0 + qn_], lhsT=dl[kp[0]:kp[1], :],
                            rhs=P4[kp[0]:kp[1], hh, wt, q0:q0 + qn_], start=False, stop=(ti_ == len(tl) - 1),
                            skip_group_check=True),
                            reads=[b_const, b_Pt[pti]], writes=[BK[obk]], signal=(pi_ == nP), partial=True)
                if first_br:
                    fw.op(DVE, lambda: nc.vector.tensor_copy(out=acc[:, :, qsl], in_=O2[:, :, 0:nq]),
                          reads=[BK[obk]], writes=[b_acc], partial=True)
                else:
                    fw.op(DVE, lambda: nc.vector.tensor_tensor(out=acc[:, :, qsl], in0=O2[:, :, 0:nq],
                                                               in1=acc[:, :, qsl], op=ALU.add),
                          reads=[BK[obk], b_acc], writes=[b_acc], partial=True)

            nb = len(blocks)
            for i in range(nb + LAG):
                if i < nb:
                    front(i)
                if i - LAG >= 0:
                    back(i - LAG)
            fw.op(ACT, lambda: nc.scalar.activation(out=acc[:, 1, :], in_=acc[:, 1, :], func=AF.Ln), reads=[b_acc],
                  writes=[b_acc], partial=True)
            fw.op(ACT, lambda: nc.scalar.activation(out=acc[:, 1, :], in_=acc[:, 1, :], func=AF.Exp, scale=-1.0),
                  reads=[b_acc], writes=[b_acc], partial=True)
            fw.op(DVE, lambda: nc.vector.tensor_tensor(out=AT[:, c, :], in0=acc[:, 0, :], in1=acc[:, 1, :],
                                                       op=ALU.mult), reads=[b_acc], writes=[b_AT[c]])

        def unitB():
            b = b_mb
            items = []
            for bb in range(16):
                for g in range(2):
                    tl = []
                    for wt, tb in ((0, bb - 1), (None, bb), (1, bb + 1)):
                        if 0 <= tb < 16:
                            tl.append((wt, slice(128 * tb, 128 * tb + 128), tb))
                        elif not halo:
                            continue
                        elif tb == -1:
                            tl.append((wt, slice(2176, 2304), 17))
                        else:
                            tl.append((wt, slice(2048, 2176), 16))
                    for ti_, t in enumerate(tl):
                        items.append((bb, g, ti_, len(tl), t))

            def vw(i, bb, g):
                sbk = i % 3
                nbk = 3 + 2 * (bb % 2) + g
                return sbk, bank(sbk), nbk, bank(nbk), i % NPTB

            def front(i):
                bb, g, ti_, ntl, (wt, ksl, vslot) = items[i]
                sbk, Sb, nbk, NB, pbi = vw(i, bb, g)
                fw.op(PE, lambda: nc.tensor.matmul(
                    Sb, lhsT=kbT[64 * g:64 * g + 64, ksl], rhs=qbT[64 * g:64 * g + 64, :, 128 * bb:128 * bb + 128],
                    start=True, stop=True), reads=[b], writes=[BK[sbk]])
                fw.op(ACT, lambda: nc.scalar.activation(out=Ptb[pbi], in_=Sb, func=AF.Exp, scale=0.125),
                      reads=[BK[sbk]], writes=[b_Ptb[pbi]])
                if wt is not None:
                    fw.op(DVE, lambda: nc.vector.tensor_tensor(
                        out=Ptb[pbi].rearrange("p (c q) -> p c q", c=4),
                        in0=Ptb[pbi].rearrange("p (c q) -> p c q", c=4),
                        in1=mask[:, wt, :].unsqueeze(1).to_broadcast([128, 4, 128]), op=ALU.mult),
                        reads=[b_Ptb[pbi], b_const], writes=[b_Ptb[pbi]])

            def back(i):
                bb, g, ti_, ntl, (wt, ksl, vslot) = items[i]
                sbk, Sb, nbk, NB, pbi = vw(i, bb, g)
                last = (ti_ == ntl - 1)
                if ti_ == 0:
                    fw.op(PE, lambda: nc.tensor.matmul(NB, lhsT=esel[0:1, :], rhs=esr[0:1, g, :], start=True,
                                                       stop=False, skip_group_check=True),
                          reads=[b_const], writes=[BK[nbk]], signal=False)
                fw.op(PE, lambda: nc.tensor.matmul(NB, lhsT=Vb[:, vslot, g, :], rhs=Ptb[pbi], start=False, stop=last,
                                                   skip_group_check=True),
                      reads=[b, b_Ptb[pbi]], writes=[BK[nbk]], signal=True, partial=True)
                if last:
                    fw.op(ACT, lambda: nc.scalar.activation(out=dn[g][0:64, :], in_=NB[64:128, :], func=AF.Ln),
                          reads=[BK[nbk]], writes=[b_dn[g]])
                    fw.op(ACT, lambda: nc.scalar.activation(out=dn[g][0:64, :], in_=dn[g][0:64, :], func=AF.Exp,
                                                            scale=-1.0), reads=[b_dn[g]], writes=[b_dn[g]])
                    fw.op(DVE, lambda: nc.vector.tensor_tensor(
                        out=AT[64 * g:64 * g + 64, 4:8, 128 * bb:128 * bb + 128],
                        in0=NB[0:64, :].rearrange("p (c q) -> p c q", c=4),
                        in1=dn[g][0:64, :].rearrange("p (c q) -> p c q", c=4), op=ALU.mult),
                        reads=[BK[nbk], b_dn[g]], writes=b_AT[4:8], partial=True)

            ni = len(items)
            for i in range(ni + LAG):
                if i < ni:
                    front(i)
                if i - LAG >= 0:
                    back(i - LAG)

        load_unitA(0)
        load_unitA(1)
        unitB()
        for c in range(4):
            if c >= 1 and c + 1 < 4:
                load_unitA(c + 1)
            unitA(c)
        fw.barrier()

    def phaseC(pc):
        arena_reset()
        x1 = [ar("x1", [128, 4, D], F32) for _ in range(2)]
        b_x1 = [[fw.buf(f"x1_{u}_{t}") for t in range(4)] for u in range(2)]
        XR = [ar("xr", [128, D], F32) for _ in range(2)]
        b_XR = [fw.buf(f"xr{i}") for i in range(2)]
        junk = ar("junk", [128, D], BF16)
        b_junk = fw.buf("junkc")
        ssq = ar("ssq", [128, 4], F32)
        b_ssq = fw.buf("ssqc")
        xn2 = [ar("xn2", [128, D], BF16) for _ in range(2)]
        b_xn2 = [fw.buf(f"xn2_{i}") for i in range(2)]
        h2T = [ar("h2T", [128, 8, 512], BF16) for _ in range(2)]
        b_h2T = [fw.buf(f"h2T{i}") for i in range(2)]
        actT = ar("actT", [128, NJ, 512], BF16)
        b_actT = [fw.buf(f"actT{j}") for j in range(NJ)]
        sg = [ar("sg", [128, 512], F32) for _ in range(2)]
        b_sg = [fw.buf(f"sg{i}") for i in range(2)]
        WG = [ar("WG", [128, 8, 128], BF16) for _ in range(3)]
        WU = [ar("WU", [128, 8, 128], BF16) for _ in range(3)]
        WD = [ar("WD", [128, D], BF16) for _ in range(3)]
        b_WG = [fw.buf(f"WG{i}") for i in range(3)]
        b_WU = [fw.buf(f"WU{i}") for i in range(3)]
        b_WD = [fw.buf(f"WD{i}") for i in range(3)]
        OS = [ar("OS", [128, D], F32) for _ in range(2)]
        b_OS = [fw.buf(f"OS{i}") for i in range(2)]
        pT = bank(2).bitcast(BF16)

        def ld_gu(j):
            i = j % 3
            fw.dma(WG[i], wgs[j], key=b_WG[i], reads=[b_wsc], writes=[b_WG[i]])
            fw.dma(WU[i], wus[j], key=b_WU[i], reads=[b_wsc], writes=[b_WU[i]])

        def ld_d(j):
            i = j % 3
            fw.dma(WD[i], wds[j], key=b_WD[i], reads=[b_wsc], writes=[b_WD[i]])

        xcnt = [0]

        def c1_load(sp_, t):
            r0 = 512 * sp_ + 128 * t
            xi = (4 * sp_ + t) % 2
            fw.dma(XR[xi], pc["xsrc"](r0 // 128), key=b_XR[xi], writes=[b_XR[xi]])

        def c1_a(sp_, t):
            u = sp_ % 2
            r0 = 512 * sp_ + 128 * t
            tcs = slice(r0, r0 + 128)
            xi = (4 * sp_ + t) % 2
            xp_ = t % 2
            for half in range(2):
                for kk in range(8):
                    fw.op(PE, lambda kk=kk, half=half: nc.tensor.matmul(
                        bank(half), lhsT=AT[:, kk, tcs], rhs=woutb[:, kk, half * 512:(half + 1) * 512],
                        start=(kk == 0), stop=(kk == 7)), reads=[b_AT[kk], b_woutb], writes=[BK[half]],
                        signal=(kk == 7), partial=(kk > 0))
            fw.op(DVE, lambda: nc.vector.tensor_tensor(out=x1[u][:, t, :], in0=bank(0, 2), in1=XR[xi], op=ALU.add),
                  reads=[BK[0], BK[1], b_XR[xi]], writes=[b_x1[u][t]])
            fw.op(ACT, lambda: nc.scalar.activation(out=junk, in_=x1[u][:, t, :], func=AF.Square,
                                                    accum_out=ssq[:, 0:1]), reads=[b_x1[u][t]],
                  writes=[b_junk, b_ssq])
            fw.op(ACT, lambda: nc.scalar.activation(out=ssq[:, 1:2], in_=ssq[:, 0:1], func=AF.Sqrt,
                                                    scale=1.0 / D, bias=eps_t), reads=[b_ssq, b_const],
                  writes=[b_ssq])
            fw.op(DVE, lambda: nc.vector.reciprocal(out=ssq[:, 2:3], in_=ssq[:, 1:2]), reads=[b_ssq],
                  writes=[b_ssq])
            fw.op(ACT, lambda: nc.scalar.activation(out=xn2[xp_], in_=x1[u][:, t, :], func=AF.Copy,
                                                    scale=ssq[:, 2:3]),
                  reads=[b_x1[u][t], b_ssq], writes=[b_xn2[xp_]])

        def c1_b(sp_, t):
            u = sp_ % 2
            xp_ = t % 2
            for kk in range(8):
                fw.op(PE, lambda kk=kk: nc.tensor.transpose(out=pT[:, kk * 128:(kk + 1) * 128],
                                                            in_=xn2[xp_][:, kk * 128:(kk + 1) * 128],
                                                            identity=ident),
                      reads=[b_xn2[xp_], b_const], writes=[BK[2]], signal=(kk == 7), partial=(kk > 0))
            fw.op(ACT, lambda: nc.scalar.copy(out=h2T[u][:, :, 128 * t:128 * t + 128],
                                              in_=pT.rearrange("p (k q) -> p k q", k=8)),
                  reads=[BK[2]], writes=[b_h2T[u]], partial=(t > 0))

        def c2(sp_, j):
            u = sp_ % 2
            i = j % 3
            gb, ub = 4 + j % 2, 6 + j % 2
            for kk in range(8):
                fw.op(PE, lambda kk=kk: nc.tensor.matmul(bank(gb), lhsT=WG[i][:, kk, :], rhs=h2T[u][:, kk, :],
                                                         start=(kk == 0), stop=(kk == 7)),
                      reads=[b_WG[i], b_h2T[u]], writes=[BK[gb]], signal=(kk == 7), partial=(kk > 0))
            for kk in range(8):
                fw.op(PE, lambda kk=kk: nc.tensor.matmul(bank(ub), lhsT=WU[i][:, kk, :], rhs=h2T[u][:, kk, :],
                                                         start=(kk == 0), stop=(kk == 7)),
                      reads=[b_WU[i], b_h2T[u]], writes=[BK[ub]], signal=(kk == 7), partial=(kk > 0))
            si = j % 2
            fw.op(ACT, lambda: nc.scalar.activation(out=sg[si], in_=bank(gb), func=AF.Silu), reads=[BK[gb]],
                  writes=[b_sg[si]])
            fw.op(DVE, lambda: nc.vector.tensor_tensor(out=actT[:, j, :], in0=bank(ub), in1=sg[si],
                                                       op=ALU.mult), reads=[BK[ub], b_sg[si]],
                  writes=[b_actT[j]])

        def c3(sp_):
            for j in range(NJ):
                if j + 2 < NJ:
                    ld_d(j + 2)
                i = j % 3
                for t in range(4):
                    for half in range(2):
                        fw.op(PE, lambda t=t, half=half: nc.tensor.matmul(
                            bank(2 * t + half), lhsT=actT[:, j, 128 * t:128 * t + 128],
                            rhs=WD[i][:, half * 512:(half + 1) * 512], start=(j == 0), stop=(j == NJ - 1)),
                            reads=[b_actT[j], b_WD[i]], writes=[BK[2 * t + half]],
                            signal=(j == NJ - 1 or (t == 3 and half == 1)), partial=(j > 0))

        def c4(sp_):
            u = sp_ % 2
            for t in (2, 3, 0, 1):
                r0 = 512 * sp_ + 128 * t
                oi = t % 2
                fw.op(DVE, lambda: nc.vector.tensor_tensor(out=OS[oi], in0=bank(2 * t, 2), in1=x1[u][:, t, :],
                                                           op=ALU.add),
                      reads=[BK[2 * t], BK[2 * t + 1], b_x1[u][t]], writes=[b_OS[oi]])
                fw.dma(pc["out"](r0), OS[oi], key=b_OS[oi], reads=[b_OS[oi]])

        NSP = 4
        c1_load(0, 0)
        c1_load(0, 1)
        ld_gu(0)
        ld_gu(1)
        for t in range(4):
            c1_a(0, t)
            if t + 2 < 4:
                c1_load(0, t + 2)
            if t >= 1:
                c1_b(0, t - 1)
        c1_b(0, 3)
        for sp_ in range(NSP):
            nxt = sp_ + 1 < NSP
            sched_a = {2: 0, 6: 1, 10: 2, 14: 3}
            sched_b = {5: 0, 9: 1, 13: 2, 17: 3}
            if nxt:
                c1_load(sp_ + 1, 0)
                c1_load(sp_ + 1, 1)
            for j in range(NJ):
                if j + 2 < NJ:
                    ld_gu(j + 2)
                if j == NJ - 2:
                    ld_d(0)
                if j == NJ - 1:
                    ld_d(1)
                c2(sp_, j)
                if nxt and j in sched_a:
                    t = sched_a[j]
                    c1_a(sp_ + 1, t)
                    if t + 2 < 4:
                        c1_load(sp_ + 1, t + 2)
                if nxt and j in sched_b:
                    c1_b(sp_ + 1, sched_b[j])
            if nxt:
                ld_gu(0)
                ld_gu(1)
            c3(sp_)
            c4(sp_)
        fw.barrier()

    for ip, pc in enumerate(pieces):
        if sel is not None and ip not in sel:
            continue
        if "A" in phases:
            phaseA(pc)
        if "B" in phases:
            phaseB(pc)
        if "C" in phases:
            phaseC(pc)
    fw.finish()
    return nc


_CACHE = {}


def _host_consts():
    bf = ml_dtypes.bfloat16
    kj = np.arange(128)[:, None]
    qi = np.arange(128)[None, :]
    masks = np.stack([(kj >= qi), (kj <= qi), (np.abs(kj - qi) <= 64),
                      ((kj >= qi + 64) | (kj <= qi - 64))], axis=1).astype(np.float32).astype(bf)
    ident = np.eye(128, dtype=np.float32).astype(bf)
    return masks, ident


def _rope_table(pos):
    half = 32
    inv = 10000.0 ** (-2.0 * np.arange(half, dtype=np.float64) / 64.0)
    ang = pos.astype(np.float64)[:, None] * inv[None, :]
    return np.concatenate([np.cos(ang), np.sin(ang)], axis=1).astype(np.float32)


def kernel(x_prompt, x_sample, attn_norm, w_in, qnorm_a, knorm_a, qnorm_b, knorm_b,
           sink_b, w_out, ffn_norm, w_gate, w_up, w_down):
    f32 = np.float32
    x_prompt = np.asarray(x_prompt, f32)
    x_sample = np.asarray(x_sample, f32)
    w_in = np.asarray(w_in, f32)[0]
    w_out = np.asarray(w_out, f32)[0]
    w_gate = np.ascontiguousarray(np.asarray(w_gate, f32)[0])
    w_up = np.ascontiguousarray(np.asarray(w_up, f32)[0])
    w_down = np.ascontiguousarray(np.asarray(w_down, f32)[0])
    QBo, KBo, VBo, VAo = 1536, 2048, 2176, 1024
    cols = list(range(0, 512)) + list(range(512, 1024))
    for c in range(4):
        for g in range(2):
            h = 4 * g + c
            cols += list(range(QBo + h * 64, QBo + h * 64 + 64))
    cols += list(range(KBo, KBo + 128)) + list(range(VBo, VBo + 128)) + list(range(VAo, VAo + 512))
    w_in_p = np.ascontiguousarray(w_in[:, cols])
    rows = list(range(512))
    for c in range(4):
        for g in range(2):
            h = 4 * g + c
            rows += list(range(512 + h * 64, 512 + h * 64 + 64))
    w_out_p = np.ascontiguousarray(w_out[rows, :])
    gA = np.ascontiguousarray(np.asarray(attn_norm, f32)[0].reshape(8, 128).T)
    gF = np.ascontiguousarray(np.asarray(ffn_norm, f32)[0].reshape(8, 128).T)
    gq1 = np.stack([np.asarray(qnorm_a, f32)[0], np.asarray(knorm_a, f32)[0],
                    np.asarray(qnorm_b, f32)[0], np.asarray(knorm_b, f32)[0]])
    gq = np.ascontiguousarray(np.broadcast_to(gq1[None, :, :], (128, 4, 64)))
    sk = np.asarray(sink_b, f32)[0]
    sinkT = np.zeros((1, 2, 512), f32)
    for g in range(2):
        for c in range(4):
            sinkT[0, g, 128 * c:128 * c + 128] = sk[4 * g + c]
    esel = np.zeros((1, 128), f32)
    esel[0, 64:] = 1.0
    esel = esel.astype(ml_dtypes.bfloat16)
    masks, ident = _host_consts()

    if "nc" not in _CACHE:
        _CACHE["nc"] = build_program()
    nc = _CACHE["nc"]

    in_maps = []
    p128 = np.arange(128)
    for core in range(NCORES):
        ps, q = core // 4, core % 4
        lo = q * 4096 - 1024
        xp = np.zeros((6144, D), f32)
        a, b = max(lo, 0), min(lo + 6144, 16384)
        xp[a - lo:b - lo] = x_prompt[ps, a:b]
        cst = np.zeros((80, 128, 64), f32)
        for n in range(16):
            cst[n] = _rope_table(n * 128 + p128)
        for pi in range(2):
            base = q * 4096 + 2048 * pi
            for n in range(16):
                cst[16 + 32 * pi + n] = _rope_table(base + n * 128 + p128)
                hidx = n * 128 + p128
                local = np.where(hidx < 1024, 2048 + hidx, hidx - 2048)
                cst[32 + 32 * pi + n] = _rope_table(np.maximum(base + local, 0))
        hv = np.ones((128, 2, 2, 64), f32)
        if q == 0:
            hv[:, 0, 1, :] = 0.0
        if q == 3:
            hv[:, 1, 0, :] = 0.0
        in_maps.append({
            "xs": np.ascontiguousarray(x_sample[4 * core:4 * core + 4]),
            "xp": xp, "cst": cst, "w_in": w_in_p, "w_out": w_out_p, "w_gate": w_gate, "w_up": w_up,
            "w_down": w_down, "gA": gA, "gF": gF, "gq": gq, "sinkT": sinkT, "esel": esel, "masks": masks, "ident": ident,
            "hv": hv.astype(ml_dtypes.bfloat16),
            "hvc": np.concatenate([hv[0:64, :, 0, :], hv[64:128, :, 1, :]], axis=0).astype(ml_dtypes.bfloat16),
        })
    res = run_bass_kernel_spmd(nc, in_maps, core_ids=list(range(NCORES)))
    y_prompt = np.zeros((2, 16384, D), f32)
    y_sample = np.zeros((32, NT, D), f32)
    for core in range(NCORES):
        r = res.results[core]
        ps, q = core // 4, core % 4
        y_prompt[ps, q * 4096:(q + 1) * 4096] = np.asarray(r["yp"], f32)
        y_sample[4 * core:4 * core + 4] = np.asarray(r["ys"], f32)
    return (y_prompt, y_sample)
```
